# Optimizing a Trainium2 kernel written in Bass

```python
import math
import jax, jax.numpy as jnp
from jax import lax
import numpy as np

D_MODEL = 1024
BATCH = 2
SEQ = 8192
DEPTH = 1
DEC_BATCH = 128
DEC_SEQ = 4
PAST_LEN = 8192
PAGE_SIZE = 128

N_HEADS = 8
QK_NOPE = 64
QK_ROPE = 32
QK_HEAD = QK_NOPE + QK_ROPE
V_HEAD = 64
Q_LORA = 384
KV_LORA = 256
ATTN_WIDTH = N_HEADS * V_HEAD
ROPE_THETA = 10000.0
SCALE = QK_HEAD ** -0.5
Q_BLOCK = 128
NEG_INF = -1e30
SSM_WIDTH = 512
GROUP = 16
N_GROUPS = SSM_WIDTH // GROUP
STATE = 64
DT_MIN = 1e-3
DT_MAX = 1e-1
D_FF = 2816
CONV_W = 3
PLE_DIM = 256
EPS = 1e-6
OFF_CKV = Q_LORA
OFF_KR = OFF_CKV + KV_LORA
OFF_U = OFF_KR + QK_ROPE
OFF_GA = OFF_U + SSM_WIDTH
OFF_GS = OFF_GA + D_MODEL
IN_COLS = OFF_GS + D_MODEL
SPLITS = (OFF_CKV, OFF_KR, OFF_U, OFF_GA, OFF_GS)

kernel_name = 'hybrid_mla_s5_convffn_ple_step'


def rmsnorm(x, g):
    xf = x.astype(jnp.float32)
    xf = xf * lax.rsqrt(jnp.mean(xf * xf, axis=-1, keepdims=True) + EPS)
    return (xf * g.astype(jnp.float32)).astype(x.dtype)


def rope_angles(pos):
    inv_freq = jnp.power(ROPE_THETA, -jnp.arange(0, QK_ROPE, 2, dtype=jnp.float32) / QK_ROPE)
    ang = pos.astype(jnp.float32)[:, None] * inv_freq[None, :]
    return jnp.cos(ang)[:, None, :], jnp.sin(ang)[:, None, :]


def apply_rope(x, cos, sin):
    half = QK_ROPE // 2
    x1, x2 = x[..., :half], x[..., half:]
    cos = cos.astype(x.dtype)
    sin = sin.astype(x.dtype)
    return jnp.concatenate([x1 * cos - x2 * sin, x1 * sin + x2 * cos], axis=-1)


def head_queries(cq, w_uq, g_q, cos, sin):
    q = rmsnorm(jnp.einsum('...sc,chd->...shd', cq, w_uq), g_q)
    return jnp.concatenate([q[..., :QK_NOPE], apply_rope(q[..., QK_NOPE:], cos, sin)], axis=-1)


def head_keys(ckv, kr, w_uk, g_k, cos, sin):
    k_nope = jnp.einsum('...sc,chd->...shd', ckv, w_uk)
    k_rope = jnp.broadcast_to(kr[..., None, :], k_nope.shape[:-1] + (QK_ROPE,))
    k = rmsnorm(jnp.concatenate([k_nope, k_rope], axis=-1), g_k)
    return jnp.concatenate([k[..., :QK_NOPE], apply_rope(k[..., QK_NOPE:], cos, sin)], axis=-1)


def mla_prompt(cq, ckv, kr, lw):
    b, s, _ = cq.shape
    cos, sin = rope_angles(jnp.arange(s))
    q = head_queries(cq, lw['w_uq'], lw['g_q'], cos, sin)
    k = head_keys(ckv, kr, lw['w_uk'], lw['g_k'], cos, sin)
    nb = s // Q_BLOCK
    qb = q.reshape(b, nb, Q_BLOCK, N_HEADS, QK_HEAD).transpose(1, 0, 2, 3, 4)
    kpos = jnp.arange(s)

    def block(args):
        q_blk, blk = args
        qpos = blk * Q_BLOCK + jnp.arange(Q_BLOCK)
        sc = jnp.einsum('bqhd,bkhd->bhqk', q_blk, k).astype(jnp.float32) * SCALE
        sc = jnp.where(kpos[None, :] <= qpos[:, None], sc, NEG_INF)
        pr = jax.nn.softmax(sc, axis=-1).astype(ckv.dtype)
        o_lat = jnp.einsum('bhqk,bkc->bhqc', pr, ckv)
        return jnp.einsum('bhqc,chd->bqhd', o_lat, lw['w_uv'])

    o = lax.map(block, (qb, jnp.arange(nb)))
    return o.transpose(1, 0, 2, 3, 4).reshape(b, s, ATTN_WIDTH)


def mla_sample(cq, ckv, kr, cache_ckv, cache_kr, page_table, layer, lw):
    b, t, _ = cq.shape
    cos, sin = rope_angles(PAST_LEN + jnp.arange(t))
    q = head_queries(cq, lw['w_uq'], lw['g_q'], cos, sin)
    k_new = head_keys(ckv, kr, lw['w_uk'], lw['g_k'], cos, sin)
    causal = jnp.arange(t)[None, :] <= jnp.arange(t)[:, None]
    sc = jnp.einsum('bthd,bshd->bhts', q, k_new).astype(jnp.float32) * SCALE
    sc = jnp.where(causal, sc, NEG_INF)
    m0 = sc.max(axis=-1)
    pr = jnp.exp(sc - m0[..., None])
    l0 = pr.sum(axis=-1)
    acc0 = jnp.einsum('bhts,bsc->bhtc', pr, ckv.astype(jnp.float32))

    def page_step(carry, j):
        m, l, acc = carry
        phys = page_table[:, j]
        c = cache_ckv[layer, phys]
        r = cache_kr[layer, phys]
        pc, ps = rope_angles(j * PAGE_SIZE + jnp.arange(PAGE_SIZE))
        kp = head_keys(c, r, lw['w_uk'], lw['g_k'], pc, ps)
        s_p = jnp.einsum('bthd,bphd->bhtp', q, kp).astype(jnp.float32) * SCALE
        m_new = jnp.maximum(m, s_p.max(axis=-1))
        corr = jnp.exp(m - m_new)
        p_p = jnp.exp(s_p - m_new[..., None])
        l = l * corr + p_p.sum(axis=-1)
        acc = acc * corr[..., None] + jnp.einsum('bhtp,bpc->bhtc', p_p, c.astype(jnp.float32))
        return (m_new, l, acc), None

    (m, l, acc), _ = lax.scan(page_step, (m0, l0, acc0), jnp.arange(page_table.shape[1]))
    o_lat = (acc / l[..., None]).astype(cq.dtype)
    return jnp.einsum('bhtc,chd->bthd', o_lat, lw['w_uv']).reshape(b, t, ATTN_WIDTH)


def ssm_discretize(lw):
    a_re = lw['a_re'].astype(jnp.float32)
    a_im = lw['a_im'].astype(jnp.float32)
    dt = jnp.exp(lw['log_dt'].astype(jnp.float32))[:, None]
    mag = jnp.exp(dt * a_re)
    ab_re = mag * jnp.cos(dt * a_im)
    ab_im = mag * jnp.sin(dt * a_im)
    den = a_re * a_re + a_im * a_im
    nr = ab_re - 1.0
    f_re = (nr * a_re + ab_im * a_im) / den
    f_im = (ab_im * a_re - nr * a_im) / den
    b_re = lw['b_re'].astype(jnp.float32)
    b_im = lw['b_im'].astype(jnp.float32)
    bb_re = f_re[..., None] * b_re - f_im[..., None] * b_im
    bb_im = f_re[..., None] * b_im + f_im[..., None] * b_re
    return ab_re, ab_im, bb_re, bb_im


def complex_affine_combine(e1, e2):
    a1r, a1i, b1r, b1i = e1
    a2r, a2i, b2r, b2i = e2
    return (a2r * a1r - a2i * a1i, a2r * a1i + a2i * a1r,
            a2r * b1r - a2i * b1i + b2r, a2r * b1i + a2i * b1r + b2i)


def ssm_scan(u, h0_re, h0_im, lw):
    ab_re, ab_im, bb_re, bb_im = ssm_discretize(lw)
    b, s, _ = u.shape
    ug = u.astype(jnp.float32).reshape(b, s, N_GROUPS, GROUP)
    bu_re = jnp.einsum('bsgi,gpi->bsgp', ug, bb_re)
    bu_im = jnp.einsum('bsgi,gpi->bsgp', ug, bb_im)
    h0_re = h0_re.astype(jnp.float32)
    h0_im = h0_im.astype(jnp.float32)
    bu_re = bu_re.at[:, 0].add(ab_re * h0_re - ab_im * h0_im)
    bu_im = bu_im.at[:, 0].add(ab_re * h0_im + ab_im * h0_re)
    a_re = jnp.broadcast_to(ab_re, bu_re.shape)
    a_im = jnp.broadcast_to(ab_im, bu_im.shape)
    _, _, h_re, h_im = lax.associative_scan(complex_affine_combine, (a_re, a_im, bu_re, bu_im), axis=1)
    c_re = lw['c_re'].astype(jnp.float32)
    c_im = lw['c_im'].astype(jnp.float32)
    y = jnp.einsum('bsgp,gip->bsgi', h_re, c_re) - jnp.einsum('bsgp,gip->bsgi', h_im, c_im)
    y = y + lw['d_skip'].astype(jnp.float32).reshape(N_GROUPS, GROUP) * ug
    return y.reshape(b, s, SSM_WIDTH).astype(u.dtype), h_re[:, -1], h_im[:, -1]


def conv_ffn(x, buf, lw):
    up = rmsnorm(x, lw['g_ffn']) @ lw['w_up']
    s = up.shape[1]
    full = jnp.concatenate([buf.astype(up.dtype), up], axis=1)
    conv = lw['conv_b'] + full[:, 0:s] * lw['conv_w'][0]
    for tap in range(1, CONV_W):
        conv = conv + full[:, tap:tap + s] * lw['conv_w'][tap]
    a, v = jnp.split(conv, 2, axis=-1)
    return (jax.nn.gelu(a) * v) @ lw['w_down'], full[:, s:]


def trunk_layer(x, p, attend, h0_re, h0_im, conv_buf, lw):
    xn = rmsnorm(x, lw['g_mix'])
    cq, ckv, kr, u, ga, gs = jnp.split(xn @ lw['w_in'], SPLITS, axis=-1)
    cq = rmsnorm(cq, lw['g_cq'])
    ckv = rmsnorm(ckv, lw['g_ckv'])
    att = attend(cq, ckv, kr)
    ys, h_re, h_im = ssm_scan(u, h0_re, h0_im, lw)
    gv, gg = jnp.split(jax.nn.gelu(ys) @ lw['w_glu'], 2, axis=-1)
    ys = gv * jax.nn.sigmoid(gg)
    mixed = jax.nn.sigmoid(ga) * (att @ lw['w_oa']) + jax.nn.sigmoid(gs) * (ys @ lw['w_os'])
    x = x + mixed @ lw['w_out']
    f, new_buf = conv_ffn(x, conv_buf, lw)
    x = x + f
    x = x + jax.nn.sigmoid(rmsnorm(x, lw['g_ple']) @ lw['w_ple_gate']) * (p @ lw['w_ple_proj'])
    return x, ckv, kr, h_re, h_im, new_buf


def setup_inputs(seed: int = 0) -> dict:
    key = jax.random.key(seed)
    keys = iter(jax.random.split(key, 64))
    f32 = jnp.float32

    def nrm(shape, scale):
        return jax.random.normal(next(keys), shape, f32) * scale

    def gain(n):
        return 1.0 + nrm((DEPTH, n), 0.02)

    n_pages = PAST_LEN // PAGE_SIZE
    n_used = DEC_BATCH * n_pages
    n_pool = n_used + n_used // 4
    page_table = jax.random.permutation(next(keys), n_pool)[:n_used].reshape(DEC_BATCH, n_pages).astype(jnp.int32)
    a_im = jnp.pi * jnp.arange(STATE, dtype=f32)[None, None, :] + nrm((DEPTH, N_GROUPS, STATE), 0.01)
    log_dt = jax.random.uniform(next(keys), (DEPTH, N_GROUPS), f32, math.log(DT_MIN), math.log(DT_MAX))
    return {
        'x_prompt': nrm((BATCH, SEQ, D_MODEL), 1.0),
        'x_sample': nrm((DEC_BATCH, DEC_SEQ, D_MODEL), 1.0),
        'p_prompt': nrm((DEPTH, BATCH, SEQ, PLE_DIM), 1.0),
        'p_sample': nrm((DEPTH, DEC_BATCH, DEC_SEQ, PLE_DIM), 1.0),
        'cache_ckv': nrm((DEPTH, n_pool, PAGE_SIZE, KV_LORA), 1.0),
        'cache_kr': nrm((DEPTH, n_pool, PAGE_SIZE, QK_ROPE), 1.0),
        'page_table': page_table,
        'state_ssm_re': nrm((DEPTH, DEC_BATCH, N_GROUPS, STATE), 0.5),
        'state_ssm_im': nrm((DEPTH, DEC_BATCH, N_GROUPS, STATE), 0.5),
        'state_conv': nrm((DEPTH, DEC_BATCH, CONV_W - 1, 2 * D_FF), 1.0),
        'g_mix': gain(D_MODEL),
        'w_in': nrm((DEPTH, D_MODEL, IN_COLS), D_MODEL ** -0.5),
        'g_cq': gain(Q_LORA),
        'g_ckv': gain(KV_LORA),
        'w_uq': nrm((DEPTH, Q_LORA, N_HEADS, QK_HEAD), Q_LORA ** -0.5),
        'w_uk': nrm((DEPTH, KV_LORA, N_HEADS, QK_NOPE), KV_LORA ** -0.5),
        'w_uv': nrm((DEPTH, KV_LORA, N_HEADS, V_HEAD), KV_LORA ** -0.5),
        'g_q': gain(QK_HEAD),
        'g_k': gain(QK_HEAD),
        'a_re': -0.5 + nrm((DEPTH, N_GROUPS, STATE), 0.01),
        'a_im': a_im,
        'log_dt': log_dt,
        'b_re': nrm((DEPTH, N_GROUPS, STATE, GROUP), (2 * GROUP) ** -0.5),
        'b_im': nrm((DEPTH, N_GROUPS, STATE, GROUP), (2 * GROUP) ** -0.5),
        'c_re': nrm((DEPTH, N_GROUPS, GROUP, STATE), STATE ** -0.5),
        'c_im': nrm((DEPTH, N_GROUPS, GROUP, STATE), STATE ** -0.5),
        'd_skip': nrm((DEPTH, SSM_WIDTH), 1.0),
        'w_glu': nrm((DEPTH, SSM_WIDTH, 2 * SSM_WIDTH), SSM_WIDTH ** -0.5),
        'w_oa': nrm((DEPTH, ATTN_WIDTH, D_MODEL), ATTN_WIDTH ** -0.5),
        'w_os': nrm((DEPTH, SSM_WIDTH, D_MODEL), SSM_WIDTH ** -0.5),
        'w_out': nrm((DEPTH, D_MODEL, D_MODEL), D_MODEL ** -0.5),
        'g_ffn': gain(D_MODEL),
        'w_up': nrm((DEPTH, D_MODEL, 2 * D_FF), D_MODEL ** -0.5),
        'conv_w': nrm((DEPTH, CONV_W, 2 * D_FF), CONV_W ** -0.5),
        'conv_b': nrm((DEPTH, 2 * D_FF), 0.01),
        'w_down': nrm((DEPTH, D_FF, D_MODEL), D_FF ** -0.5),
        'g_ple': gain(D_MODEL),
        'w_ple_gate': nrm((DEPTH, D_MODEL, D_MODEL), D_MODEL ** -0.5),
        'w_ple_proj': nrm((DEPTH, PLE_DIM, D_MODEL), PLE_DIM ** -0.5),
    }


def reference(x_prompt, x_sample, p_prompt, p_sample, cache_ckv, cache_kr, page_table,
              state_ssm_re, state_ssm_im, state_conv, g_mix, w_in, g_cq, g_ckv, w_uq, w_uk, w_uv,
              g_q, g_k, a_re, a_im, log_dt, b_re, b_im, c_re, c_im, d_skip, w_glu, w_oa, w_os,
              w_out, g_ffn, w_up, conv_w, conv_b, w_down, g_ple, w_ple_gate, w_ple_proj):
    yp, ys = x_prompt, x_sample
    bp = x_prompt.shape[0]
    ckv_p, kr_p, ckv_s, kr_s = [], [], [], []
    sre_p, sim_p, sre_s, sim_s, cv_p, cv_s = [], [], [], [], [], []
    for i in range(DEPTH):
        lw = dict(g_mix=g_mix[i], w_in=w_in[i], g_cq=g_cq[i], g_ckv=g_ckv[i], w_uq=w_uq[i],
                  w_uk=w_uk[i], w_uv=w_uv[i], g_q=g_q[i], g_k=g_k[i], a_re=a_re[i], a_im=a_im[i],
                  log_dt=log_dt[i], b_re=b_re[i], b_im=b_im[i], c_re=c_re[i], c_im=c_im[i],
                  d_skip=d_skip[i], w_glu=w_glu[i], w_oa=w_oa[i], w_os=w_os[i], w_out=w_out[i],
                  g_ffn=g_ffn[i], w_up=w_up[i], conv_w=conv_w[i], conv_b=conv_b[i], w_down=w_down[i],
                  g_ple=g_ple[i], w_ple_gate=w_ple_gate[i], w_ple_proj=w_ple_proj[i])
        zero_h = jnp.zeros((bp, N_GROUPS, STATE), jnp.float32)
        zero_buf = jnp.zeros((bp, CONV_W - 1, 2 * D_FF), yp.dtype)
        yp, ckv, kr, hr, hi, buf = trunk_layer(
            yp, p_prompt[i], lambda cq, c, r: mla_prompt(cq, c, r, lw), zero_h, zero_h, zero_buf, lw)
        ckv_p.append(ckv); kr_p.append(kr); sre_p.append(hr); sim_p.append(hi); cv_p.append(buf)
        ys, ckv, kr, hr, hi, buf = trunk_layer(
            ys, p_sample[i],
            lambda cq, c, r: mla_sample(cq, c, r, cache_ckv, cache_kr, page_table, i, lw),
            state_ssm_re[i], state_ssm_im[i], state_conv[i], lw)
        ckv_s.append(ckv); kr_s.append(kr); sre_s.append(hr); sim_s.append(hi); cv_s.append(buf)
    ckv_prompt = jnp.stack(ckv_p)
    kr_prompt = jnp.stack(kr_p)
    ckv_sample = jnp.stack(ckv_s)
    kr_sample = jnp.stack(kr_s)
    ssm_re_prompt = jnp.stack(sre_p)
    ssm_im_prompt = jnp.stack(sim_p)
    ssm_re_sample = jnp.stack(sre_s)
    ssm_im_sample = jnp.stack(sim_s)
    conv_prompt = jnp.stack(cv_p)
    conv_sample = jnp.stack(cv_s)
    return (yp, ys, ckv_prompt, kr_prompt, ckv_sample, kr_sample, ssm_re_prompt, ssm_im_prompt,
            ssm_re_sample, ssm_im_sample, conv_prompt, conv_sample)
```

```python
import contextlib
import numpy as np
import concourse.bass as bass
import concourse.mybir as mybir
from concourse.bass_utils import run_bass_kernel_spmd

F32 = mybir.dt.float32
BF16 = mybir.dt.bfloat16
I32 = mybir.dt.int32
AF = mybir.ActivationFunctionType
ALU = mybir.AluOpType
AX = mybir.AxisListType

D = 1024
NPT = 2048
NST = 64
T = NPT + NST
KD = D // 128
N_HEADS = 8
QK_NOPE, QK_ROPE, QK_HEAD, V_HEAD = 64, 32, 96, 64
Q_LORA, KV_LORA = 384, 256
SSM_W, GROUP, N_GROUPS, STATE = 512, 16, 32, 64
D_FF = 2816
PLE = 256
EPS = 1e-6
OFF_CKV = Q_LORA
OFF_KR = OFF_CKV + KV_LORA
OFF_U = OFF_KR + QK_ROPE
OFF_GA = OFF_U + SSM_W
OFF_GS = OFF_GA + D
IN_COLS = OFF_GS + D
SCALE = QK_HEAD ** -0.5
PAST = 8192
PAGE = 128
NPAGES = 64
SEQ_PER_CORE = 16


class Buf:
    __slots__ = ("name", "last_w", "readers")

    def __init__(self, name, fence=()):
        self.name = name
        self.last_w = None
        self.readers = list(fence)


class Op:
    __slots__ = ("eng", "fn", "deps", "dma", "sem", "value", "marked", "idx")

    def __init__(self, eng, fn, dma):
        self.eng = eng
        self.fn = fn
        self.dma = dma
        self.deps = ()
        self.sem = None
        self.value = 0
        self.marked = False
        self.idx = 0


class Prog:
    ENGS = ("pe", "act", "dve", "pool", "sp")

    def __init__(self, nc):
        self.nc = nc
        self.ops = {e: [] for e in self.ENGS}
        self.stack = contextlib.ExitStack()
        self.dma_sems = {}
        self.nbuf = 0
        self.fence = []
        self.live = []

    def sbuf(self, name, shape, dtype):
        return self.stack.enter_context(self.nc.sbuf_tensor(name, list(shape), dtype))

    def psum(self, name, shape, dtype=F32):
        return self.stack.enter_context(self.nc.psum_tensor(name, list(shape), dtype))

    def sem(self, name):
        return self.stack.enter_context(self.nc.semaphore(name))

    def buf(self, name=None):
        self.nbuf += 1
        b = Buf(name or f"b{self.nbuf}", self.fence)
        self.live.append(b)
        return b

    def new_phase(self):
        f = []
        for b in self.live:
            if b.last_w is not None:
                f.append(b.last_w)
            f.extend(b.readers)
        self.fence = list(dict.fromkeys(f))[-64:] if False else list(dict.fromkeys(f))
        self.live = []

    def add(self, eng, fn, reads=(), writes=(), dma=False, semkey=None, after=()):
        op = Op(eng, fn, dma)
        deps = set(after)
        for b in reads:
            if b.last_w is not None:
                deps.add(b.last_w)
        for b in writes:
            if b.last_w is not None:
                deps.add(b.last_w)
            deps.update(b.readers)
        op.deps = tuple(deps)
        for b in reads:
            b.readers.append(op)
        for b in writes:
            b.last_w = op
            b.readers = []
        op.idx = len(self.ops[eng])
        self.ops[eng].append(op)
        if dma:
            key = semkey if semkey is not None else id(op)
            if key not in self.dma_sems:
                self.dma_sems[key] = [self.sem(f"dq{len(self.dma_sems)}"), 0]
            ent = self.dma_sems[key]
            ent[1] += (1 if dma == "cc" else 16)
            op.sem = ent[0]
            op.value = ent[1]
            op.marked = True
        return op

    def finalize(self):
        nc = self.nc
        esem = {e: self.sem(f"eng_{e}") for e in self.ENGS}
        for e in self.ENGS:
            for op in self.ops[e]:
                for d in op.deps:
                    if d.dma:
                        continue
                    if d.eng != e:
                        d.marked = True
                    elif e != "pe" and (op.idx - d.idx) <= 2:
                        d.marked = True
        for e in self.ENGS:
            c = 0
            for op in self.ops[e]:
                if op.dma:
                    continue
                if op.marked:
                    c += 1
                    op.sem = esem[e]
                    op.value = c
        self.stats = {e: len(self.ops[e]) for e in self.ENGS}

        def emit(ename, eng):
            waited = {}
            for op in self.ops[ename]:
                need = {}
                for d in op.deps:
                    if not d.marked:
                        continue
                    if (not d.dma) and d.eng == ename and (ename == "pe" or (op.idx - d.idx) > 2):
                        continue
                    k = id(d.sem)
                    if waited.get(k, 0) >= d.value:
                        continue
                    if k not in need or need[k][1] < d.value:
                        need[k] = (d.sem, d.value)
                for k, (s, v) in need.items():
                    eng.wait_ge(s, v)
                    waited[k] = v
                ins = op.fn(eng)
                if op.marked and ins is not None:
                    if op.dma == "cc":
                        ins.then_inc(op.sem, 1)
                    elif op.dma:
                        ins.then_inc(op.sem, 16)
                    else:
                        ins.then_inc(op.sem, 1)

        with nc.Block() as block:
            @block.tensor
            def _(e):
                emit("pe", e)

            @block.scalar
            def _(e):
                emit("act", e)

            @block.vector
            def _(e):
                emit("dve", e)

            @block.gpsimd
            def _(e):
                emit("pool", e)

            @block.sync
            def _(e):
                emit("sp", e)
        self.stack.close()


class Arena:
    def __init__(self, P, name, nbytes):
        self.t = P.sbuf(name, [128, nbytes // 4], F32)
        self.off = 0
        self.cap = nbytes
        self.peak = 0

    def alloc(self, shape, dtype=F32):
        esz = 2 if dtype == BF16 else 4
        n = int(np.prod(shape[1:]))
        nb = (n * esz + 31) // 32 * 32
        assert self.off + nb <= self.cap, ("arena overflow", self.off, nb, self.cap)
        v = self.t[0:shape[0], self.off // 4:(self.off + nb) // 4]
        self.off += nb
        self.peak = max(self.peak, self.off)
        if dtype != F32:
            v = v.bitcast(dtype)
        v = v[:, 0:n]
        if len(shape) > 2:
            names = "abcde"[:len(shape) - 1]
            pat = "p (" + " ".join(names) + ") -> p " + " ".join(names)
            v = v.rearrange(pat, **{c: int(d) for c, d in zip(names[1:], shape[2:])})
        return v

    def mark(self):
        return self.off

    def release(self, m):
        self.off = m


def _small_layout():
    lay = {}
    off = 0

    def put(name, n):
        nonlocal off
        lay[name] = (off, n)
        off += n
    put("g_mix", 8)
    put("g_cq", 3)
    put("g_ckv", 2)
    put("g_ffn", 8)
    put("g_ple", 8)
    put("conv_w", 3 * 44)
    put("conv_b", 44)
    put("g_q", 1)
    put("g_k", 1)
    put("d_skip", 4)
    put("vis", 4)
    put("full", 4)
    put("ident", 128)
    put("hsel", 4)
    lay["_n"] = off
    return lay


SL = _small_layout()


def _token_chunks():
    return [(0, 510), (510, 512), (1022, 512), (1534, 290), (1824, 288)]


def _pack_layout(items):
    lay, off = {}, 0
    for name, n in items:
        lay[name] = (off, n)
        off += n
    lay["_n"] = off
    return lay


SSL = _pack_layout([("a_re", 16), ("a_im", 16), ("logdt", 16), ("c_re", 512), ("c_im", 512), ("b_re", 512),
                    ("b_im", 512), ("h0_re", 256), ("h0_im", 256), ("msk", 12), ("blk", 4)])
SRL = _pack_layout([("a_re", 256), ("a_im", 256), ("logdt", 256), ("b_re", 512), ("b_im", 512), ("ident", 128)])
PWS = [1, 2, 3, 4, 8, 12, 16]
PWR = [1, 2, 3]
TWO_PI = 6.283185307179586


def build_ssm(P, AR, nc, din, dout, dint, stores, next_bank, uT, b_uT, smc, b_sm, sm, w_glu_b, b_w_glu_b, chunks):
    LOOP_ENG = "pool"
    ss_d = din("ssm_s", [128, SSL["_n"]])
    sr_d = din("ssm_r", [128, SRL["_n"]])
    o_hp = dout("o_hp", [128, 2, 16])
    o_hs = dout("o_hs", [128, 2, 16, 16])
    cc_e_in = dint("cc_e_in", [128, 32], F32)
    cc_e_out = dint("cc_e_out", [512, 32], F32)

    AR.release(0)
    P.new_phase()
    sst = AR.alloc([128, SSL["_n"]], F32)
    Wc = AR.alloc([128, 16, 4, 2, 32], BF16)
    Ktab = AR.alloc([128, 4, 4, 128], BF16)
    A4w = AR.alloc([128, 4, 4, 2, 128], BF16)
    nlim = AR.alloc([128, len(PWS), 16], F32)
    LS_pre = True
    b_sst, b_srt = P.buf("sst"), P.buf("srt")
    P.add("sp", lambda e: e.dma_start(out=sst[:], in_=ss_d), writes=[b_sst], dma=True, semkey="sst")

    def S(name, a=None):
        o, n = SSL[name]
        v = sst[:, o:o + n]
        return v if a is None else v.rearrange("p (a b) -> p a b", a=a)

    def R(name, a=None):
        o, n = SRL[name]
        v = srt[:, o:o + n]
        return v if a is None else v.rearrange("p (a b) -> p a b", a=a)

    def tt(eng, out, a, b, op, rd, wr):
        return P.add(eng, lambda e: e.tensor_tensor(out=out, in0=a, in1=b, op=op), reads=rd, writes=wr)

    def tss(eng, out, a, scalar, op, rd, wr):
        return P.add(eng, lambda e: e.tensor_single_scalar(out=out, in_=a, scalar=scalar, op=op), reads=rd, writes=wr)

    def stt(eng, out, a, scalar, b, op0, op1, rd, wr):
        return P.add("dve", lambda e: e.scalar_tensor_tensor(out=out, in0=a, scalar=scalar, in1=b, op0=op0, op1=op1),
                     reads=rd, writes=wr)

    def act(out, in_, func, rd, wr, scale=1.0, bias=0.0):
        return P.add("act", lambda e: e.activation(out=out, in_=in_, func=func, scale=scale, bias=bias), reads=rd, writes=wr)

    def cp(eng, out, in_, rd, wr):
        return P.add(eng, lambda e: e.tensor_copy(out=out, in_=in_), reads=rd, writes=wr)

    MUL, ADD, SUB = ALU.mult, ALU.add, ALU.subtract

    def lam_pow(pfx, a_re, a_im, logdt, Fd, powers, b_src, eng):
        npw = len(powers)
        bt = P.buf(pfx + "_t")
        mk = lambda nm, sh, dt_=F32: AR.alloc(sh, dt_)
        dt = mk("dt", [128, Fd]); dre = mk("dre", [128, Fd]); dim = mk("dim", [128, Fd])
        ang = mk("ang", [128, npw, Fd]); angc = mk("angc", [128, npw, Fd]); mag = mk("mag", [128, npw, Fd])
        ki = mk("ki", [128, npw, Fd], I32); kf = mk("kf", [128, npw, Fd])
        sn = mk("sn", [128, npw, Fd]); cs = mk("cs", [128, npw, Fd])
        lre = mk("lre", [128, npw, Fd]); lim = mk("lim", [128, npw, Fd])
        act(dt[:], logdt, AF.Exp, [b_src], [bt])
        tt(eng, dre[:], dt[:], a_re, MUL, [bt, b_src], [bt])
        tt(eng, dim[:], dt[:], a_im, MUL, [bt, b_src], [bt])
        for i, n in enumerate(powers):
            tss(eng, ang[:, i, :], dim[:], n / TWO_PI, MUL, [bt], [bt])
            act(mag[:, i, :], dre[:], AF.Exp, [bt], [bt], scale=float(n))
        tss(eng, angc[:], ang[:], 0.25, ADD, [bt], [bt])
        for src, dst in ((ang, sn), (angc, cs)):
            cp("dve", ki[:], src[:], [bt], [bt])
            cp("dve", kf[:], ki[:], [bt], [bt])
            tt(eng, kf[:], src[:], kf[:], SUB, [bt], [bt])
            act(dst[:], kf[:], AF.Sin, [bt], [bt], scale=6.28318)
        tt(eng, lre[:], mag[:], cs[:], MUL, [bt], [bt])
        tt(eng, lim[:], mag[:], sn[:], MUL, [bt], [bt])
        den = mk("den", [128, Fd]); t1 = mk("t1", [128, Fd]); t2 = mk("t2", [128, Fd]); nr = mk("nr", [128, Fd])
        fre = mk("fre", [128, Fd]); fim = mk("fim", [128, Fd])
        tt(eng, den[:], a_re, a_re, MUL, [b_src], [bt])
        tt(eng, t1[:], a_im, a_im, MUL, [b_src], [bt])
        tt(eng, den[:], den[:], t1[:], ADD, [bt], [bt])
        P.add("dve", lambda e: e.reciprocal(out=den[:], in_=den[:]), reads=[bt], writes=[bt])
        tss(eng, nr[:], lre[:, 0, :], -1.0, ADD, [bt], [bt])
        tt(eng, t1[:], nr[:], a_re, MUL, [bt, b_src], [bt])
        tt(eng, t2[:], lim[:, 0, :], a_im, MUL, [bt, b_src], [bt])
        tt(eng, t1[:], t1[:], t2[:], ADD, [bt], [bt])
        tt(eng, fre[:], t1[:], den[:], MUL, [bt], [bt])
        tt(eng, t1[:], lim[:, 0, :], a_re, MUL, [bt, b_src], [bt])
        tt(eng, t2[:], nr[:], a_im, MUL, [bt, b_src], [bt])
        tt(eng, t1[:], t1[:], t2[:], SUB, [bt], [bt])
        tt(eng, fim[:], t1[:], den[:], MUL, [bt], [bt])
        return dict(lre=lre, lim=lim, fre=fre, fim=fim, b=bt)

    def cmul(eng, ore, oim, are, aim, bre, bim, t1, t2, rd, wr):
        tt(eng, t1, are, bre, MUL, rd, wr)
        tt(eng, t2, aim, bim, MUL, rd, wr)
        tt(eng, ore, t1, t2, SUB, rd, wr)
        tt(eng, t1, are, bim, MUL, rd, wr)
        tt(eng, t2, aim, bre, MUL, rd, wr)
        tt(eng, oim, t1, t2, ADD, rd, wr)

    ENG = "pool"
    LS = lam_pow("ls", S("a_re"), S("a_im"), S("logdt"), 16, PWS, b_sst, ENG)
    mB0 = AR.mark()
    srt = AR.alloc([128, SRL["_n"]], F32)
    P.add("sp", lambda e: e.dma_start(out=srt[:], in_=sr_d), writes=[b_srt], dma=True, semkey="srt")
    LR = lam_pow("lr", R("a_re"), R("a_im"), R("logdt"), 256, PWR, b_srt, ENG)
    bS, bR = LS["b"], LR["b"]
    pi = {n: i for i, n in enumerate(PWS)}

    def bc(ap2, n):
        return ap2.unsqueeze(2).broadcast_to([128, 16, n])

    bbs_re = AR.alloc([128, 16, 32], F32); bbs_im = AR.alloc([128, 16, 32], F32)
    x_re = AR.alloc([128, 16, 32], F32); x_im = AR.alloc([128, 16, 32], F32)
    u1 = AR.alloc([128, 16, 32], F32); u2 = AR.alloc([128, 16, 32], F32)
    negc_im = AR.alloc([128, 16, 32], F32)
    cmul(ENG, bbs_re[:], bbs_im[:], bc(LS["fre"][:], 32), bc(LS["fim"][:], 32), S("b_re", 16), S("b_im", 16),
         u1[:], u2[:], [bS, b_sst], [bS])
    tss(ENG, negc_im[:], S("c_im", 16), -1.0, MUL, [b_sst], [bS])

    b_Wc = P.buf("Wc")
    Wc_v = Wc[:].rearrange("p (kk k4) s c n -> p kk k4 s c n", k4=4)
    u1_v = u1[:].rearrange("p (kk k4) n -> p kk k4 n", k4=4)
    u2_v = u2[:].rearrange("p (kk k4) n -> p kk k4 n", k4=4)
    for s in range(4):
        lr_, li_ = LS["lre"][:, pi[s + 1], :], LS["lim"][:, pi[s + 1], :]
        tt(ENG, u1[:], S("c_re", 16), bc(lr_, 32), MUL, [b_sst, bS], [bS])
        tt(ENG, u2[:], S("c_im", 16), bc(li_, 32), MUL, [b_sst, bS], [bS])
        for k4 in range(4):
            tt(ENG, Wc_v[:, :, k4, s, 0, :], u1_v[:, :, k4, :], u2_v[:, :, k4, :], SUB, [bS], [bS, b_Wc])
        tt(ENG, u1[:], S("c_re", 16), bc(li_, 32), MUL, [b_sst, bS], [bS])
        tt(ENG, u2[:], S("c_im", 16), bc(lr_, 32), MUL, [b_sst, bS], [bS])
        for k4 in range(4):
            stt(ENG, Wc_v[:, :, k4, s, 1, :], u1_v[:, :, k4, :], -1.0, u2_v[:, :, k4, :], MUL, SUB,
                [bS], [bS, b_Wc])

    b_Kt = P.buf("Ktab")
    Kc = AR.alloc([128, 4, 32], F32)
    Kf = AR.alloc([128, 4, 128], F32)
    b_Kc, b_Kf = P.buf("Kc"), P.buf("Kf")
    xs_re = AR.alloc([128, 4, 16, 32], F32); xs_im = AR.alloc([128, 4, 16, 32], F32)
    b_xs = P.buf("xs")
    cp(ENG, xs_re[:, 0], bbs_re[:], [bS], [b_xs])
    cp(ENG, xs_im[:, 0], bbs_im[:], [bS], [b_xs])
    for tau in range(1, 4):
        cmul(ENG, xs_re[:, tau], xs_im[:, tau], bc(LS["lre"][:, pi[tau], :], 32), bc(LS["lim"][:, pi[tau], :], 32),
             bbs_re[:], bbs_im[:], u1[:], u2[:], [bS], [bS, b_xs])
    for kk in range(4):
        ps, b_ps = next_bank()
        for k4 in range(4):
            k = 4 * kk + k4
            for tau in range(4):
                o = ps[32 * k4:32 * k4 + 32, tau * 32:tau * 32 + 32]
                P.add("pe", lambda e, o=o, k=k, tau=tau, k4=k4: e.matmul(
                    o, lhsT=xs_re[:, tau, k, :], rhs=S("c_re", 16)[:, k, :], start=True, stop=False,
                    tile_position=(0, 32 * k4)), reads=[b_xs, b_sst], writes=[b_ps])
                P.add("pe", lambda e, o=o, k=k, tau=tau, k4=k4: e.matmul(
                    o, lhsT=xs_im[:, tau, k, :], rhs=negc_im[:, k, :], start=False, stop=True,
                    tile_position=(0, 32 * k4)), reads=[b_xs, bS], writes=[b_ps])
        cp("dve", Kc[:], ps[:, 0:128].rearrange("p (t n) -> p t n", t=4), [b_ps], [b_Kc])
        for k4 in range(4):
            tss("dve", Kf[:, :, 32 * k4:32 * k4 + 32], Kc[:], S("blk")[:, k4:k4 + 1], MUL, [b_Kc, b_sst], [b_Kf])
        stt("dve", Kf[:, 0, :], R("ident"), smc("d_skip", kk), Kf[:, 0, :], MUL, ADD, [b_srt, b_sm, b_Kf], [b_Kf])
        cp("dve", Ktab[:, kk], Kf[:], [b_Kf], [b_Kt])

    b_A4 = P.buf("A4w")
    bbr_re = AR.alloc([128, 4, 2, 64], F32); bbr_im = AR.alloc([128, 4, 2, 64], F32)
    r1 = AR.alloc([128, 4, 2, 64], F32); r2 = AR.alloc([128, 4, 2, 64], F32)
    bcr = lambda t: t.rearrange("p (kk q) -> p kk q", kk=4).unsqueeze(2).broadcast_to([128, 4, 2, 64])
    v4 = lambda t: t.rearrange("p (kk g q) -> p kk g q", kk=4, g=2)
    cmul(ENG, bbr_re[:], bbr_im[:], bcr(LR["fre"][:]), bcr(LR["fim"][:]), v4(R("b_re")), v4(R("b_im")), r1[:], r2[:],
         [bR, b_srt], [bR])
    a4v = lambda s_, c_: A4w[:, :, s_, c_, :].rearrange("p kk (g q) -> p kk g q", g=2)
    cp(ENG, a4v(3, 0), bbr_re[:], [bR], [b_A4])
    cp(ENG, a4v(3, 1), bbr_im[:], [bR], [b_A4])
    pir = {n: i for i, n in enumerate(PWR)}
    for s in range(3):
        n = 3 - s
        cmul(ENG, a4v(s, 0), a4v(s, 1), bcr(LR["lre"][:, pir[n], :]), bcr(LR["lim"][:, pir[n], :]),
             bbr_re[:], bbr_im[:], r1[:], r2[:], [bR], [bR, b_A4])

    tss(ENG, nlim[:], LS["lim"][:], -1.0, MUL, [bS], [bS])
    Lre = lambda n, k: LS["lre"][:, pi[n], k:k + 1]
    Lim = lambda n, k: LS["lim"][:, pi[n], k:k + 1]
    nLim = lambda n, k: nlim[:, pi[n], k:k + 1]

    AR.release(mB0)
    P.new_phase()
    uP = [uT[:, kk, 0:NPT].rearrange("p (c e) -> p e c", e=16) for kk in range(4)]
    uS = [uT[:, kk, NPT:T].rearrange("p (q s) -> p s q", s=4) for kk in range(4)]

    S16 = [AR.alloc([128, 16, 128], F32) for c in range(2)]
    b_S16 = P.buf("S16")
    H16 = [AR.alloc([128, 16, 129], F32) for c in range(2)]
    b_H = [P.buf("H16re"), P.buf("H16im")]
    pr = [AR.alloc([128, 2, 128], F32) for i in range(2)]
    b_pr = [P.buf("pr0"), P.buf("pr1")]

    def s4_matmuls(k, n_c, usrc, nj):
        kk, k4 = divmod(k, 4)
        out = []
        for comp in range(2):
            ps, b_ps = next_bank()
            for j in range(nj):
                for s in range(4):
                    rhs = usrc[kk][32 * k4:32 * k4 + 32, 4 * j + s, :]
                    P.add("pe", lambda e, ps=ps, j=j, s=s, rhs=rhs, comp=comp, kk=kk, k4=k4: e.matmul(
                        ps[:, j * n_c:(j + 1) * n_c], lhsT=A4w[32 * k4:32 * k4 + 32, kk, s, comp, :], rhs=rhs,
                        start=(s == 0), stop=(s == 3), tile_position=(32 * k4, 0)),
                        reads=[b_A4, b_uT], writes=[b_ps])
            out.append((ps, b_ps))
        return out

    def prefix_step(k, j, src, b_src, dst, b_dst, Sre, Sim, b_sre, b_sim, n_c, o_re=None, o_im=None, b_o=()):
        ore = dst[:, 0, 0:n_c] if o_re is None else o_re
        oim = dst[:, 1, 0:n_c] if o_im is None else o_im
        wr = [b_dst] + list(b_o)
        stt("dve", dst[:, 0, 0:n_c], src[:, 0, 0:n_c], Lre(4, k), Sre, MUL, ADD, [b_src, bS, b_sre], [b_dst])
        stt("dve", ore, src[:, 1, 0:n_c], nLim(4, k), dst[:, 0, 0:n_c], MUL, ADD, [b_src, bS, b_dst], wr)
        stt("dve", dst[:, 1, 0:n_c], src[:, 0, 0:n_c], Lim(4, k), Sim, MUL, ADD, [b_src, bS, b_sim], [b_dst])
        stt("dve", oim, src[:, 1, 0:n_c], Lre(4, k), dst[:, 1, 0:n_c], MUL, ADD, [b_src, bS, b_dst], wr)

    for k in range(16):
        (pre, b_pre), (pim, b_pim) = s4_matmuls(k, 128, uP, 4)
        act(pr[0][:, 0, :], pre[:, 0:128], AF.Copy, [b_pre], [b_pr[0]])
        act(pr[0][:, 1, :], pim[:, 0:128], AF.Copy, [b_pim], [b_pr[0]])
        cur = 0
        for j in range(1, 4):
            last = (j == 3)
            prefix_step(k, j, pr[cur], b_pr[cur], pr[1 - cur], b_pr[1 - cur], pre[:, j * 128:(j + 1) * 128],
                        pim[:, j * 128:(j + 1) * 128], b_pre, b_pim, 128,
                        o_re=S16[0][:, k, :] if last else None, o_im=S16[1][:, k, :] if last else None,
                        b_o=[b_S16] if last else ())
            cur = 1 - cur

    lt = [AR.alloc([128, 16], F32) for i in range(6)]
    b_lt = [P.buf(f"lt{i}") for i in range(6)]
    L16re, L16im = LS["lre"][:, pi[16], :], LS["lim"][:, pi[16], :]

    def run_loop():
        for c in range(128):
            hre, him = H16[0][:, :, c], H16[1][:, :, c]
            tt(LOOP_ENG, lt[0][:], L16re, hre, MUL, [bS, b_H[0]], [b_lt[0]])
            tt(LOOP_ENG, lt[1][:], L16im, him, MUL, [bS, b_H[1]], [b_lt[1]])
            tt(LOOP_ENG, lt[3][:], L16re, him, MUL, [bS, b_H[1]], [b_lt[3]])
            tt(LOOP_ENG, lt[4][:], L16im, hre, MUL, [bS, b_H[0]], [b_lt[4]])
            tt(LOOP_ENG, lt[2][:], lt[0][:], lt[1][:], SUB, [b_lt[0], b_lt[1]], [b_lt[2]])
            tt(LOOP_ENG, lt[5][:], lt[3][:], lt[4][:], ADD, [b_lt[3], b_lt[4]], [b_lt[5]])
            tt(LOOP_ENG, H16[0][:, :, c + 1], lt[2][:], S16[0][:, :, c], ADD, [b_lt[2], b_S16], [b_H[0]])
            tt(LOOP_ENG, H16[1][:, :, c + 1], lt[5][:], S16[1][:, :, c], ADD, [b_lt[5], b_S16], [b_H[1]])
            yield

    P.add(LOOP_ENG, lambda e: e.memset(H16[0][:, :, 0], 0.0), writes=[b_H[0]])
    P.add(LOOP_ENG, lambda e: e.memset(H16[1][:, :, 0], 0.0), writes=[b_H[1]])
    for _ in run_loop():
        pass
    Eloc = AR.alloc([128, 2, 16], F32)
    b_E = P.buf("Eloc")
    cp(LOOP_ENG, Eloc[:, 0, :], H16[0][:, :, 128], [b_H[0]], [b_E])
    cp(LOOP_ENG, Eloc[:, 1, :], H16[1][:, :, 128], [b_H[1]], [b_E])
    b_cci, b_cco = P.buf("cc_e_in"), P.buf("cc_e_out")
    P.add("sp", lambda e: e.dma_start(out=cc_e_in, in_=Eloc[:].rearrange("p a b -> p (a b)")), reads=[b_E], writes=[b_cci],
          dma=True, semkey="cce1")
    P.add("pool", lambda e: e.collective_compute("AllGather", ALU.bypass, replica_groups=[[0, 1, 2, 3], [4, 5, 6, 7]],
                                                 ins=[cc_e_in.opt()], outs=[cc_e_out.opt()]),
          reads=[b_cci], writes=[b_cco], dma="cc", semkey="cce2")
    Eg = AR.alloc([128, 4, 2, 16], F32)
    b_Eg = P.buf("Eg")
    P.add("sp", lambda e: e.dma_start(out=Eg[:].rearrange("p r a b -> p r (a b)"),
                                      in_=cc_e_out.rearrange("(r p) f -> p r f", p=128)),
          reads=[b_cco], writes=[b_Eg], dma=True, semkey="cce3")
    Lp = AR.alloc([128, 3, 2, 16], F32)
    b_Lp = P.buf("Lp")
    sq_a = AR.alloc([128, 2, 16], F32); sq_b = AR.alloc([128, 2, 16], F32)
    q1 = AR.alloc([128, 16], F32); q2 = AR.alloc([128, 16], F32)
    b_q = P.buf("sqtmp")
    CE = "dve"
    cp(CE, sq_a[:, 0, :], L16re, [bS], [b_q])
    cp(CE, sq_a[:, 1, :], L16im, [bS], [b_q])
    src_, dst_ = sq_a, sq_b
    for it in range(7):
        o = Lp[:, 0] if it == 6 else dst_
        tt(CE, q1[:], src_[:, 0, :], src_[:, 0, :], MUL, [b_q], [b_q])
        tt(CE, q2[:], src_[:, 1, :], src_[:, 1, :], MUL, [b_q], [b_q])
        tt(CE, o[:, 0, :], q1[:], q2[:], SUB, [b_q], [b_q, b_Lp])
        stt(CE, o[:, 1, :], src_[:, 0, :], 2.0, src_[:, 1, :], MUL, MUL, [b_q], [b_q, b_Lp])
        src_, dst_ = dst_, src_
    cmul(CE, Lp[:, 1, 0, :], Lp[:, 1, 1, :], Lp[:, 0, 0, :], Lp[:, 0, 1, :], Lp[:, 0, 0, :], Lp[:, 0, 1, :], q1[:], q2[:],
         [b_Lp, b_q], [b_Lp, b_q])
    cmul(CE, Lp[:, 2, 0, :], Lp[:, 2, 1, :], Lp[:, 1, 0, :], Lp[:, 1, 1, :], Lp[:, 0, 0, :], Lp[:, 0, 1, :], q1[:], q2[:],
         [b_Lp, b_q], [b_Lp, b_q])
    hin = AR.alloc([128, 2, 16], F32)
    cf = AR.alloc([128, 2, 16], F32)
    b_hin, b_cf = P.buf("hin"), P.buf("cf")
    P.add(CE, lambda e: e.memset(hin[:], 0.0), writes=[b_hin])
    msk = S("msk")
    for i in range(4):
        m0, m1, m2 = (msk[:, 3 * i + n:3 * i + n + 1] for n in range(3))
        tss(CE, cf[:, 0, :], Lp[:, 0, 0, :], m1, MUL, [b_Lp, b_sst], [b_cf])
        stt(CE, cf[:, 0, :], Lp[:, 1, 0, :], m2, cf[:, 0, :], MUL, ADD, [b_Lp, b_sst, b_cf], [b_cf])
        tss(CE, cf[:, 0, :], cf[:, 0, :], m0, ADD, [b_cf, b_sst], [b_cf])
        tss(CE, cf[:, 1, :], Lp[:, 0, 1, :], m1, MUL, [b_Lp, b_sst], [b_cf])
        stt(CE, cf[:, 1, :], Lp[:, 1, 1, :], m2, cf[:, 1, :], MUL, ADD, [b_Lp, b_sst, b_cf], [b_cf])
        ere, eim = Eg[:, i, 0, :], Eg[:, i, 1, :]
        tt(CE, q1[:], cf[:, 0, :], ere, MUL, [b_cf, b_Eg], [b_q])
        tt(CE, hin[:, 0, :], hin[:, 0, :], q1[:], ADD, [b_q, b_hin], [b_hin])
        tt(CE, q1[:], cf[:, 1, :], eim, MUL, [b_cf, b_Eg], [b_q])
        tt(CE, hin[:, 0, :], hin[:, 0, :], q1[:], SUB, [b_q, b_hin], [b_hin])
        tt(CE, q1[:], cf[:, 0, :], eim, MUL, [b_cf, b_Eg], [b_q])
        tt(CE, hin[:, 1, :], hin[:, 1, :], q1[:], ADD, [b_q, b_hin], [b_hin])
        tt(CE, q1[:], cf[:, 1, :], ere, MUL, [b_cf, b_Eg], [b_q])
        tt(CE, hin[:, 1, :], hin[:, 1, :], q1[:], ADD, [b_q, b_hin], [b_hin])
    cp(LOOP_ENG, H16[0][:, :, 0], hin[:, 0, :], [b_hin], [b_H[0]])
    cp(LOOP_ENG, H16[1][:, :, 0], hin[:, 1, :], [b_hin], [b_H[1]])
    for _ in run_loop():
        pass
    hp = AR.alloc([128, 2, 16], F32)
    b_hp = P.buf("hp")
    cp(LOOP_ENG, hp[:, 0, :], H16[0][:, :, 128], [b_H[0]], [b_hp])
    cp(LOOP_ENG, hp[:, 1, :], H16[1][:, :, 128], [b_H[1]], [b_hp])
    stores.append(P.add("sp", lambda e: e.dma_start(out=o_hp, in_=hp[:]), reads=[b_hp], dma=True, semkey="st_hp"))

    yT = AR.alloc([128, 4, T], BF16)
    b_yT = P.buf("yT")
    H4 = AR.alloc([128, 4, 4, 2, 128], BF16)
    b_H4 = P.buf("H4")
    h0b = AR.alloc([128, 2, 16, 16], BF16)
    b_h0b = P.buf("h0b")
    cp("dve", h0b[:, 0], S("h0_re", 16), [b_sst], [b_h0b])
    cp("dve", h0b[:, 1], S("h0_im", 16), [b_sst], [b_h0b])
    hs = AR.alloc([128, 2, 16, 16], F32)
    b_hs = P.buf("hs")
    htmp = AR.alloc([128, 2, 128], F32)
    b_htmp = P.buf("htmp")
    yP = [yT[:, kk, 0:NPT].rearrange("p (c j s) -> p j s c", j=4, s=4) for kk in range(4)]
    yS = [yT[:, kk, NPT:T].rearrange("p (q s) -> p s q", s=4) for kk in range(4)]

    def out_stage(kk, j, n_c, Hsrc, b_hsrc, usrc, dst):
        ps, b_ps = next_bank()
        for s_lo in range(4):
            o = ps[:, s_lo * n_c:(s_lo + 1) * n_c]
            nmm = 8 + s_lo + 1
            idx = 0
            for k4 in range(4):
                k = 4 * kk + k4
                for comp in range(2):
                    o4 = ps[32 * k4:32 * k4 + 32, s_lo * n_c:(s_lo + 1) * n_c]
                    P.add("pe", lambda e, o4=o4, k=k, s_lo=s_lo, comp=comp, k4=k4, idx=idx, nmm=nmm: e.matmul(
                        o4, lhsT=Wc[:, k, s_lo, comp, :], rhs=Hsrc(k, k4, comp), start=(comp == 0), stop=False,
                        tile_position=(0, 32 * k4)),
                        reads=[b_Wc, b_hsrc], writes=[b_ps])
                    idx += 1
            for tau in range(s_lo + 1):
                rhs = usrc[kk][:, 4 * j + s_lo - tau, :]
                P.add("pe", lambda e, o=o, tau=tau, rhs=rhs, idx=idx, nmm=nmm, kk=kk: e.matmul(
                    o, lhsT=Ktab[:, kk, tau, :], rhs=rhs, start=False, stop=(idx == nmm - 1)),
                    reads=[b_Kt, b_uT], writes=[b_ps])
                idx += 1
        act(dst, ps[:, 0:4 * n_c].rearrange("p (s c) -> p s c", s=4), AF.Gelu_apprx_tanh, [b_ps], [b_yT])

    for kk in range(4):
        for k4 in range(4):
            k = 4 * kk + k4
            (pre, b_pre), (pim, b_pim) = s4_matmuls(k, 128, uP, 3)
            cp("dve", H4[:, k4, 0, 0, :], H16[0][:, k, 0:128], [b_H[0]], [b_H4])
            cp("dve", H4[:, k4, 0, 1, :], H16[1][:, k, 0:128], [b_H[1]], [b_H4])
            act(pr[0][:, 0, :], pre[:, 0:128], AF.Copy, [b_pre], [b_pr[0]])
            act(pr[0][:, 1, :], pim[:, 0:128], AF.Copy, [b_pim], [b_pr[0]])
            cur = 0
            for j in range(1, 4):
                n = 4 * j
                stt("dve", htmp[:, 0, :], H16[0][:, k, 0:128], Lre(n, k), pr[cur][:, 0, :], MUL, ADD,
                    [b_H[0], bS, b_pr[cur]], [b_htmp])
                stt("dve", H4[:, k4, j, 0, :], H16[1][:, k, 0:128], nLim(n, k), htmp[:, 0, :], MUL, ADD,
                    [b_H[1], bS, b_htmp], [b_H4])
                stt("dve", htmp[:, 1, :], H16[0][:, k, 0:128], Lim(n, k), pr[cur][:, 1, :], MUL, ADD,
                    [b_H[0], bS, b_pr[cur]], [b_htmp])
                stt("dve", H4[:, k4, j, 1, :], H16[1][:, k, 0:128], Lre(n, k), htmp[:, 1, :], MUL, ADD,
                    [b_H[1], bS, b_htmp], [b_H4])
                if j < 3:
                    prefix_step(k, j, pr[cur], b_pr[cur], pr[1 - cur], b_pr[1 - cur], pre[:, j * 128:(j + 1) * 128],
                                pim[:, j * 128:(j + 1) * 128], b_pre, b_pim, 128)
                    cur = 1 - cur
            (sre, b_sre), (sim, b_sim) = s4_matmuls(k, 16, uS, 1)
            h0r, h0i = S("h0_re", 16)[:, k, :], S("h0_im", 16)[:, k, :]
            stt("dve", hs[:, 0, k, :], h0r, Lre(4, k), sre[:, 0:16], MUL, ADD, [b_sst, bS, b_sre], [b_hs])
            stt("dve", hs[:, 0, k, :], h0i, nLim(4, k), hs[:, 0, k, :], MUL, ADD, [b_sst, bS, b_hs], [b_hs])
            stt("dve", hs[:, 1, k, :], h0r, Lim(4, k), sim[:, 0:16], MUL, ADD, [b_sst, bS, b_sim], [b_hs])
            stt("dve", hs[:, 1, k, :], h0i, Lre(4, k), hs[:, 1, k, :], MUL, ADD, [b_sst, bS, b_hs], [b_hs])
        for j in range(4):
            out_stage(kk, j, 128, lambda k, k4, comp, j=j: H4[:, k4, j, comp, :], b_H4, uP, yP[kk][:, j])
        out_stage(kk, 0, 16, lambda k, k4, comp: h0b[:, comp, k, :], b_h0b, uS, yS[kk])
    stores.append(P.add("sp", lambda e: e.dma_start(out=o_hs, in_=hs[:]), reads=[b_hs], dma=True, semkey="st_hs"))

    wg = AR.alloc([128, 4, 1024], BF16)
    b_wg = P.buf("wg")
    P.add("sp", lambda e: e.dma_start(out=wg[:], in_=w_glu_b.rearrange("(k p) m -> p k m", p=128)), reads=[b_w_glu_b],
          writes=[b_wg], dma=True, semkey="wg")
    ysT, b_ys = uT, b_uT
    sg = AR.alloc([128, 512], F32)
    b_sg = P.buf("sg")
    for (n0, n) in chunks:
        for m in range(4):
            pv, b_pv = next_bank()
            pg, b_pg = next_bank()
            for k in range(4):
                P.add("pe", lambda e, pv=pv, k=k, m=m, n0=n0, n=n: e.matmul(
                    pv[:, 0:n], lhsT=wg[:, k, 128 * m:128 * m + 128], rhs=yT[:, k, n0:n0 + n], start=(k == 0), stop=(k == 3)),
                    reads=[b_wg, b_yT], writes=[b_pv])
            for k in range(4):
                P.add("pe", lambda e, pg=pg, k=k, m=m, n0=n0, n=n: e.matmul(
                    pg[:, 0:n], lhsT=wg[:, k, 512 + 128 * m:512 + 128 * m + 128], rhs=yT[:, k, n0:n0 + n], start=(k == 0),
                    stop=(k == 3)), reads=[b_wg, b_yT], writes=[b_pg])
            act(sg[:, 0:n], pg[:, 0:n], AF.Sigmoid, [b_pg], [b_sg])
            tt("dve", ysT[:, m, n0:n0 + n], pv[:, 0:n], sg[:, 0:n], MUL, [b_pv, b_sg], [b_ys])
    return dict(ysT=ysT, b_ys=b_ys)


def build_attn(P, AR, nc, din, dout, dint, stores, banks, bank_bufs, cast_w, cqn, b_cqn, ckvn, b_ckvn, krK, b_krK,
               sm, b_sm, smc, ones_bf, b_ones, rope_c, rope_s, attT, b_attT, chunks, kS, b_kS):
    MUL, ADD = ALU.mult, ALU.add
    w_uq = din("w_uq", [Q_LORA, N_HEADS * QK_HEAD])
    w_uk = din("w_uk", [KV_LORA, N_HEADS * QK_NOPE])
    w_uv = din("w_uv", [KV_LORA, N_HEADS * V_HEAD])
    maskd_d = din("maskd", [128, 4, 128])
    rotm_d = din("rotm", [96, 96])
    w_uq_b = dint("w_uq_b", [Q_LORA, N_HEADS * QK_HEAD], BF16)
    w_uk_b = dint("w_uk_b", [KV_LORA, N_HEADS * QK_NOPE], BF16)
    w_uv_b = dint("w_uv_b", [KV_LORA, N_HEADS * V_HEAD], BF16)
    b_wqb = cast_w(w_uq_b, w_uq, Q_LORA, 768, "c_wuq")
    b_wkb = cast_w(w_uk_b, w_uk, KV_LORA, 512, "c_wuk")
    b_wvb = cast_w(w_uv_b, w_uv, KV_LORA, 512, "c_wuv")
    K_own = [dint(f"K_own{h}", [QK_HEAD, NPT], BF16) for h in range(N_HEADS)]
    K_all = [dint(f"K_all{h}", [4 * QK_HEAD, NPT], BF16) for h in range(N_HEADS)]
    V_own = [dint(f"V_own{h}", [128, 16 * 65], BF16) for h in range(N_HEADS)]
    V_all = [dint(f"V_all{h}", [512, 16 * 65], BF16) for h in range(N_HEADS)]
    RG = [[0, 1, 2, 3], [4, 5, 6, 7]]

    AR.release(0)
    P.new_phase()
    qT = AR.alloc([96, 8, T], BF16)
    maskd = AR.alloc([128, 4, 128], BF16)
    sel65 = AR.alloc([65, 64], F32)
    mC1 = AR.mark()
    wq = AR.alloc([128, 3, 768], BF16)
    wk = AR.alloc([128, 2, 8, 96], BF16)
    wv = AR.alloc([128, 2, 512], BF16)
    rc = AR.alloc([96, T], F32)
    rs = AR.alloc([96, T], F32)
    rotm = AR.alloc([96, 96], F32)
    maskf = AR.alloc([128, 4, 128], F32)
    b_wq, b_wk, b_wv, b_rc, b_rs, b_rot, b_mk, b_mkf, b_sel, b_qT = [P.buf(n) for n in
        ("wq", "wk", "wv", "rc", "rs", "rotm", "maskd", "maskf", "sel65", "qT")]
    P.add("sp", lambda e: e.dma_start(out=wq[:], in_=w_uq_b.rearrange("(k p) m -> p k m", p=128)), reads=[b_wqb], writes=[b_wq],
          dma=True, semkey="wq")
    P.add("pool", lambda e: e.memset(wk[:], 0.0), writes=[b_wk])
    for k in range(2):
        P.add("sp", lambda e, k=k: e.dma_start(out=wk[:, k, :, 0:64],
                                               in_=w_uk_b[128 * k:128 * k + 128, :].rearrange("p (h d) -> p h d", h=8)),
              reads=[b_wkb], writes=[b_wk], dma=True, semkey="wk")
    P.add("sp", lambda e: e.dma_start(out=wv[:], in_=w_uv_b.rearrange("(k p) m -> p k m", p=128)), reads=[b_wvb], writes=[b_wv],
          dma=True, semkey="wv")
    P.add("sp", lambda e: e.dma_start(out=rc[:], in_=rope_c), writes=[b_rc], dma=True, semkey="rc")
    P.add("sp", lambda e: e.dma_start(out=rs[:], in_=rope_s), writes=[b_rs], dma=True, semkey="rs")
    P.add("sp", lambda e: e.dma_start(out=rotm[:], in_=rotm_d), writes=[b_rot], dma=True, semkey="rotm")
    P.add("sp", lambda e: e.dma_start(out=maskf[:], in_=maskd_d), writes=[b_mkf], dma=True, semkey="maskf")
    P.add("pool", lambda e: e.tensor_copy(out=maskd[:], in_=maskf[:]), reads=[b_mkf], writes=[b_mk])
    P.add("pool", lambda e: e.memset(sel65[:], 0.0), writes=[b_sel])
    P.add("pool", lambda e: e.memset(sel65[64:65, :], 1.0), writes=[b_sel])
    ident = sm[:, SL["ident"][0]:SL["ident"][0] + 128]

    rr = [0]

    def tbank():
        i = 2 + rr[0] % 6
        rr[0] += 1
        return banks[i], bank_bufs[i]

    raw = AR.alloc([96, 512], F32); sqh = AR.alloc([96, 512], BF16); lnv = AR.alloc([96, 512], F32)
    rstd = AR.alloc([96, 512], F32); qg = AR.alloc([96, 512], F32); t1 = AR.alloc([96, 512], F32); t2 = AR.alloc([96, 512], F32)
    kst = [AR.alloc([96, 512], BF16) for _ in range(2)]
    b_raw, b_sqh, b_lnv, b_rstd, b_qg, b_t1, b_t2 = [P.buf(n) for n in ("raw", "sqh", "lnv", "rstd", "qg", "t1", "t2")]
    b_kst = [P.buf("kst0"), P.buf("kst1")]

    def normrope(ps, b_ps, gname, out_ap, b_out, n, n0):
        P.add("act", lambda e: e.activation(out=raw[:, 0:n], in_=ps[0:96, 0:n], func=AF.Copy), reads=[b_ps], writes=[b_raw])
        P.add("pool", lambda e: e.tensor_tensor(out=sqh[:, 0:n], in0=raw[:, 0:n], in1=raw[:, 0:n], op=MUL), reads=[b_raw],
              writes=[b_sqh])
        p2, b_p2 = tbank()
        P.add("pe", lambda e: e.matmul(p2[0:96, 0:n], lhsT=ones_bf[0:96, 0:96], rhs=sqh[:, 0:n], start=True, stop=True),
              reads=[b_sqh, b_ones], writes=[b_p2])
        P.add("act", lambda e: e.activation(out=lnv[:, 0:n], in_=p2[0:96, 0:n], func=AF.Ln, scale=1.0 / QK_HEAD, bias=EPS),
              reads=[b_p2], writes=[b_lnv])
        P.add("act", lambda e: e.activation(out=rstd[:, 0:n], in_=lnv[:, 0:n], func=AF.Exp, scale=-0.5), reads=[b_lnv],
              writes=[b_rstd])
        P.add("dve", lambda e: e.scalar_tensor_tensor(out=qg[:, 0:n], in0=raw[:, 0:n], scalar=smc(gname)[0:96, :], in1=rstd[:, 0:n],
                                                      op0=MUL, op1=MUL), reads=[b_raw, b_rstd, b_sm], writes=[b_qg])
        p3, b_p3 = tbank()
        P.add("pe", lambda e: e.matmul(p3[0:96, 0:n], lhsT=rotm[:, :], rhs=qg[:, 0:n], start=True, stop=True),
              reads=[b_rot, b_qg], writes=[b_p3])
        P.add("dve", lambda e: e.tensor_tensor(out=t1[:, 0:n], in0=qg[:, 0:n], in1=rc[:, n0:n0 + n], op=MUL), reads=[b_qg, b_rc],
              writes=[b_t1])
        P.add("dve", lambda e: e.tensor_tensor(out=t2[:, 0:n], in0=p3[0:96, 0:n], in1=rs[:, n0:n0 + n], op=MUL),
              reads=[b_p3, b_rs], writes=[b_t2])
        P.add("pool", lambda e: e.tensor_tensor(out=out_ap, in0=t1[:, 0:n], in1=t2[:, 0:n], op=ADD), reads=[b_t1, b_t2],
              writes=[b_out])

    b_Kown = [P.buf(f"K_own{h}") for h in range(8)]
    b_Vown = [P.buf(f"V_own{h}") for h in range(8)]
    b_Kall = [P.buf(f"K_all{h}") for h in range(8)]
    b_Vall = [P.buf(f"V_all{h}") for h in range(8)]
    Vst = AR.alloc([128, 8, 16, 65], BF16)
    b_Vst = P.buf("Vst")
    P.add("pool", lambda e: e.memset(Vst[:, :, :, 64:65], 1.0), writes=[b_Vst])
    for blk in range(16):
        ps, b_ps = tbank()
        for k in range(2):
            P.add("pe", lambda e, ps=ps, k=k, blk=blk: e.matmul(ps[:, 0:512], lhsT=ckvn[:, k, 128 * blk:128 * blk + 128], rhs=wv[:, k, :],
                                                                 start=(k == 0), stop=(k == 1)), reads=[b_ckvn, b_wv], writes=[b_ps])
        P.add("act", lambda e, ps=ps, blk=blk: e.activation(out=Vst[:, :, blk, 0:64], in_=ps[:, 0:512].rearrange("p (h v) -> p h v", h=8),
                                                            func=AF.Copy), reads=[b_ps], writes=[b_Vst])
    for h in range(N_HEADS):
        P.add("sp", lambda e, h=h: e.dma_start(out=V_own[h], in_=Vst[:, h].rearrange("p b e -> p (b e)")), reads=[b_Vst],
              writes=[b_Vown[h]], dma=True, semkey=f"vst{h}")
        P.add("pool", lambda e, h=h: e.collective_compute("AllGather", ALU.bypass, replica_groups=RG, ins=[V_own[h].opt()],
                                                          outs=[V_all[h].opt()]),
              reads=[b_Vown[h]], writes=[b_Vall[h]], dma="cc", semkey=f"ccV{h}")
    kcount = 0
    for h in range(N_HEADS):
        for ci, (n0, n) in enumerate(chunks):
            ps, b_ps = tbank()
            for k in range(3):
                P.add("pe", lambda e, ps=ps, k=k, h=h, n0=n0, n=n: e.matmul(
                    ps[0:96, 0:n], lhsT=wq[:, k, 96 * h:96 * h + 96], rhs=cqn[:, k, n0:n0 + n], start=(k == 0), stop=(k == 2)),
                    reads=[b_wq, b_cqn], writes=[b_ps])
            normrope(ps, b_ps, "g_q", qT[:, h, n0:n0 + n], b_qT, n, n0)
            ps, b_ps = tbank()
            for k in range(2):
                P.add("pe", lambda e, ps=ps, k=k, h=h, n0=n0, n=n: e.matmul(
                    ps[0:96, 0:n], lhsT=wk[:, k, h, :], rhs=ckvn[:, k, n0:n0 + n], start=(k == 0), stop=False),
                    reads=[b_wk, b_ckvn], writes=[b_ps])
            P.add("pe", lambda e, ps=ps, n0=n0, n=n: e.matmul(
                ps[0:96, 0:n], lhsT=ident[64:96, 0:96], rhs=krK[64:96, n0:n0 + n], start=False, stop=True, tile_position=(64, 0)),
                reads=[b_sm, b_krK], writes=[b_ps])
            ks, b_ks = kst[kcount % 2], b_kst[kcount % 2]
            kcount += 1
            normrope(ps, b_ps, "g_k", ks[:, 0:n], b_ks, n, n0)
            npr = min(n0 + n, NPT) - n0
            if npr > 0:
                P.add("sp", lambda e, ks=ks, h=h, n0=n0, npr=npr: e.dma_start(out=K_own[h][:, n0:n0 + npr],
                                                                               in_=ks[:, 0:npr]),
                      reads=[b_ks], writes=[b_Kown[h]], dma=True, semkey=f"kst{(kcount - 1) % 2}")
            if npr < n:
                P.add("pool", lambda e, ks=ks, h=h, npr=npr, n=n: e.tensor_copy(out=kS[:, h, :], in_=ks[:, npr:n]), reads=[b_ks],
                      writes=[b_kS])
        P.add("pool", lambda e, h=h: e.collective_compute("AllGather", ALU.bypass, replica_groups=RG, ins=[K_own[h].opt()],
                                                          outs=[K_all[h].opt()]),
              reads=[b_Kown[h]], writes=[b_Kall[h]], dma="cc", semkey=f"ccK{h}")
    import os
    if os.environ.get("CUT") == "3":
        return {}
    AR.release(mC1)
    P.new_phase()
    Kh = [AR.alloc([96, 4, NPT], BF16) for _ in range(2)]
    Vh = [AR.alloc([128, 4, 16 * 65], BF16) for _ in range(2)]
    Vvis = AR.alloc([128, 4, 16 * 65], BF16)
    Vful = AR.alloc([128, 4, 16 * 65], BF16)
    b_Kh = [P.buf("Kh0"), P.buf("Kh1")]
    b_Vh = [P.buf("Vh0"), P.buf("Vh1")]
    b_Vvis, b_Vful = P.buf("Vvis"), P.buf("Vful")
    PT = [AR.alloc([128, 512], BF16) for _ in range(6)]
    b_PT = [P.buf(f"PT{i}") for i in range(6)]
    Osb = AR.alloc([65, 512], F32); rl = AR.alloc([64, 512], F32); ast = AR.alloc([64, 512], BF16)
    b_Osb, b_rl, b_ast = P.buf("Osb"), P.buf("rl"), P.buf("ast")

    def load_head(h):
        i = h % 2
        P.add("sp", lambda e: e.dma_start(out=Kh[i][:], in_=K_all[h].rearrange("(r d) t -> d r t", r=4)), reads=[b_Kall[h]], writes=[b_Kh[i]], dma=True,
              semkey=f"Kh{i}")
        P.add("sp", lambda e: e.dma_start(out=Vh[i][:], in_=V_all[h].rearrange("(r p) x -> p r x", r=4)), reads=[b_Vall[h]], writes=[b_Vh[i]], dma=True,
              semkey=f"Vh{i}")

    load_head(0)
    pcount = 0
    for h in range(N_HEADS):
        i = h % 2
        if h + 1 < N_HEADS:
            load_head(h + 1)
        for r in range(4):
            P.add("pool", lambda e, r=r, i=i: e.tensor_scalar(out=Vvis[:, r, :], in0=Vh[i][:, r, :], scalar1=smc("vis", r), scalar2=None,
                                                              op0=MUL), reads=[b_Vh[i], b_sm], writes=[b_Vvis])
            P.add("pool", lambda e, r=r, i=i: e.tensor_scalar(out=Vful[:, r, :], in0=Vh[i][:, r, :], scalar1=smc("full", r), scalar2=None,
                                                              op0=MUL), reads=[b_Vh[i], b_sm], writes=[b_Vful])
        for qc in range(4):
            O, b_O = banks[qc % 2], bank_bufs[qc % 2]
            first = [True]
            pending = []

            def flush(keep):
                while len(pending) > keep:
                    pending.pop(0)()
            for r in range(4):
                for kb in range(16):
                    S_, b_S = tbank()
                    P.add("pe", lambda e, S_=S_, r=r, kb=kb, i=i, h=h, qc=qc: e.matmul(
                        S_[:, 0:512], lhsT=Kh[i][:, r, 128 * kb:128 * kb + 128], rhs=qT[:, h, 512 * qc:512 * qc + 512],
                        start=True, stop=True), reads=[b_Kh[i], b_qT], writes=[b_S])
                    pt, b_pt = PT[pcount % 6], b_PT[pcount % 6]
                    pcount += 1
                    P.add("act", lambda e, S_=S_, pt=pt: e.activation(out=pt[:], in_=S_[:, 0:512], func=AF.Exp, scale=SCALE),
                          reads=[b_S], writes=[b_pt])

                    def pv(r=r, kb=kb, pt=pt, b_pt=b_pt, O=O, b_O=b_O, qc=qc, i=i):
                        d = kb - 4 * qc
                        segs = []
                        if d < 0:
                            segs.append((0, 512, Vvis, b_Vvis))
                        elif d > 3:
                            segs.append((0, 512, Vful, b_Vful))
                        else:
                            if d > 0:
                                segs.append((0, 128 * d, Vful, b_Vful))
                            P.add("dve", lambda e: e.tensor_tensor(out=pt[:, 128 * d:128 * d + 128], in0=pt[:, 128 * d:128 * d + 128],
                                                                   in1=maskd[:, r, :], op=MUL), reads=[b_pt, b_mk], writes=[b_pt])
                            segs.append((128 * d, 128 * d + 128, Vh[i], b_Vh[i]))
                            if d < 3:
                                segs.append((128 * d + 128, 512, Vvis, b_Vvis))
                        for (c0, c1, Vx, b_Vx) in segs:
                            st = first[0]
                            first[0] = False
                            P.add("pe", lambda e, c0=c0, c1=c1, Vx=Vx, st=st: e.matmul(
                                O[0:65, c0:c1], lhsT=Vx[:, r, 65 * kb:65 * kb + 65], rhs=pt[:, c0:c1], start=st, stop=False),
                                reads=[b_Vx, b_pt], writes=[b_O])
                    pending.append(pv)
                    flush(3)
            flush(0)
            P.add("act", lambda e, O=O: e.activation(out=Osb[:], in_=O[0:65, 0:512], func=AF.Copy), reads=[b_O], writes=[b_Osb])
            lb, b_lb = tbank()
            P.add("pe", lambda e, lb=lb: e.matmul(lb[0:64, 0:512], lhsT=sel65[:, :], rhs=Osb[:, :], start=True, stop=True),
                  reads=[b_sel, b_Osb], writes=[b_lb])
            P.add("dve", lambda e, lb=lb: e.reciprocal(out=rl[:], in_=lb[0:64, 0:512]), reads=[b_lb], writes=[b_rl])
            cols = slice(512 * qc, 512 * qc + 512)
            if h % 2 == 0:
                P.add("dve", lambda e, h=h, cols=cols: e.tensor_tensor(out=attT[0:64, h // 2, cols], in0=Osb[0:64, :], in1=rl[:], op=MUL),
                      reads=[b_Osb, b_rl], writes=[b_attT])
            else:
                P.add("dve", lambda e: e.tensor_tensor(out=ast[:], in0=Osb[0:64, :], in1=rl[:], op=MUL), reads=[b_Osb, b_rl],
                      writes=[b_ast])
                P.add("sp", lambda e, h=h, cols=cols: e.dma_start(out=attT[64:128, h // 2, cols], in_=ast[:]), reads=[b_ast],
                      writes=[b_attT], dma=True, semkey="ast")
    return dict(mC1=mC1, qT=qT, b_qT=b_qT, w_uk_b=w_uk_b, b_wkb=b_wkb, w_uv_b=w_uv_b, b_wvb=b_wvb, rotm_d=rotm_d)


def build_tail(P, AR, nc, din, dout, dint, stores, banks, bank_bufs, cast_w, xT, w_in_b, b_w_in_b, sm, b_sm, smc, ones_bf, b_ones,
               attT, b_attT, ysT, b_ys, chunks):
    MUL, ADD = ALU.mult, ALU.add
    names = [("w_oa", 512, D), ("w_os", 512, D), ("w_out", D, D), ("w_up", D, 2 * D_FF), ("w_down", D_FF, D),
             ("w_pg", D, D), ("w_pp", PLE, D)]
    W, bW = {"w_in": w_in_b}, {"w_in": b_w_in_b}
    for nm, r, c in names:
        src = din(nm, [r, c])
        W[nm] = dint(nm + "_b", [r, c], BF16)
        bW[nm] = cast_w(W[nm], src, r, c, "c_" + nm)
    pT_d = din("pT", [PLE, T])
    scT_d = din("scT", [128, 44, 16, 2])
    o_yT = dout("o_yT", [D, T])
    o_cvp = dout("o_cvp", [128, 44, 2])
    o_cvs = dout("o_cvs", [128, 44, 16, 2])
    cc_h_in = dint("cc_h_in", [128, 16], F32)
    cc_h_out = dint("cc_h_out", [512, 16], F32)

    AR.release(0)
    P.new_phase()
    HO = 2
    x1 = AR.alloc([128, KD, 514], F32)
    x1c4 = AR.alloc([128, KD, 290], F32)
    xn = AR.alloc([128, KD, 514], BF16)
    scr = AR.alloc([128, KD, 514], BF16)
    hT = AR.alloc([128, 22, 512], BF16)
    upx = [[AR.alloc([128, 514], F32) for _ in range(2)] for _ in range(2)]
    cav = [[AR.alloc([128, 512], F32) for _ in range(2)] for _ in range(2)]
    sgt = [AR.alloc([128, 512], F32) for _ in range(2)]
    lnv = AR.alloc([128, 514], F32)
    rstd = AR.alloc([128, 514], F32)
    pTc = AR.alloc([128, 2, 512], BF16)
    scT = AR.alloc([128, 44, 16, 2], F32)
    upS = [AR.alloc([128, 16, 6], F32) for _ in range(2)]
    cvp = AR.alloc([128, 44, 2], F32)
    cvs = AR.alloc([128, 44, 16, 2], F32)
    carry = AR.alloc([128, 44, 2], F32)
    Hg = AR.alloc([128, 4, 16], F32)
    hsend = AR.alloc([128, 16], F32)
    hrecv = AR.alloc([128, 16], F32)
    NSLAB = 4
    slabs = [AR.alloc([128, 4096], BF16) for _ in range(NSLAB)]
    b_x1, b_x1c4, b_xn, b_scr, b_hT, b_lnv, b_rstd, b_pTc, b_scT, b_cvp, b_cvs, b_carry, b_Hg, b_hsend, b_hrecv = [
        P.buf(n) for n in ("x1", "x1c4", "xn", "scr", "hT", "lnv", "rstd", "pTc", "scT", "cvp", "cvs", "carry", "Hg", "hsend", "hrecv")]
    b_upx = [[P.buf(f"upx{i}{j}") for j in range(2)] for i in range(2)]
    b_cav = [[P.buf(f"cav{i}{j}") for j in range(2)] for i in range(2)]
    b_sgt = [P.buf("sgt0"), P.buf("sgt1")]
    b_upS = [P.buf("upS0"), P.buf("upS1")]
    b_slab = [P.buf(f"slab{i}") for i in range(NSLAB)]
    P.add("sp", lambda e: e.dma_start(out=scT[:], in_=scT_d), writes=[b_scT], dma=True, semkey="scT")
    P.add("pool", lambda e: e.memset(carry[:], 0.0), writes=[b_carry])

    rr = [0]

    def tbank():
        i = rr[0] % 8
        rr[0] += 1
        return banks[i], bank_bufs[i]

    sl = [0]

    def load_slab(wname, kt, c0, width):
        i = sl[0] % NSLAB
        sl[0] += 1
        v = slabs[i][:, 0:kt * width].rearrange("p (k m) -> p k m", k=kt)
        src, bsrc = W[wname], bW[wname]
        P.add("sp", lambda e: e.dma_start(out=v, in_=src.rearrange("(k p) m -> p k m", p=128)[:, :, c0:c0 + width]),
              reads=[bsrc], writes=[b_slab[i]], dma=True, semkey=f"slab{i}")
        return v, b_slab[i]

    def mm_group(ps, b_ps, n, w, b_w, kt, wc0, rhs_fn, rd):
        for k in range(kt):
            P.add("pe", lambda e, k=k: e.matmul(ps[:, 0:n], lhsT=w[:, k, wc0:wc0 + 128], rhs=rhs_fn(k), start=(k == 0), stop=(k == kt - 1)),
                  reads=[b_w] + rd, writes=[b_ps])

    def norm_to_xn(src, b_src, gname, c_lo, c_hi):
        n = c_hi - c_lo
        P.add("pool", lambda e: e.tensor_tensor(out=scr[:, :, c_lo:c_hi], in0=src[:, :, c_lo:c_hi], in1=src[:, :, c_lo:c_hi], op=MUL),
              reads=[b_src], writes=[b_scr])
        ps, b_ps = tbank()
        for k in range(KD):
            P.add("pe", lambda e, k=k: e.matmul(ps[:, 0:n], lhsT=ones_bf[:], rhs=scr[:, k, c_lo:c_hi], start=(k == 0), stop=(k == KD - 1)),
                  reads=[b_scr, b_ones], writes=[b_ps])
        P.add("act", lambda e: e.activation(out=lnv[:, 0:n], in_=ps[:, 0:n], func=AF.Ln, scale=1.0 / D, bias=EPS), reads=[b_ps],
              writes=[b_lnv])
        P.add("act", lambda e: e.activation(out=rstd[:, 0:n], in_=lnv[:, 0:n], func=AF.Exp, scale=-0.5), reads=[b_lnv], writes=[b_rstd])
        for k in range(KD):
            P.add("dve", lambda e, k=k: e.scalar_tensor_tensor(out=xn[:, k, c_lo:c_hi], in0=src[:, k, c_lo:c_hi], scalar=smc(gname, k),
                                                               in1=rstd[:, 0:n], op0=MUL, op1=MUL),
                  reads=[b_src, b_rstd, b_sm], writes=[b_xn])

    xT_v = xT.rearrange("(k p) t -> p k t", p=128)

    def phase_D(xt, b_xt, n0, n):
        lo, hi = HO, HO + n
        P.add("sp", lambda e: e.dma_start(out=xt[:, :, lo:hi], in_=xT_v[:, :, n0:n0 + n]), writes=[b_xt], dma=True, semkey="x1ld")
        norm_to_xn(xt, b_xt, "g_mix", lo, hi)
        mixed = scr
        for q in range(2):
            wga, b_wga = load_slab("w_in", KD, OFF_GA + 512 * q, 512)
            wgs, b_wgs = load_slab("w_in", KD, OFF_GS + 512 * q, 512)
            woa, b_woa = load_slab("w_oa", 4, 512 * q, 512)
            wos, b_wos = load_slab("w_os", 4, 512 * q, 512)
            for mi in range(4):
                m = 4 * q + mi
                pga, b_pga = tbank()
                mm_group(pga, b_pga, n, wga, b_wga, KD, 128 * mi, lambda k: xn[:, k, lo:hi], [b_xn])
                pgs, b_pgs = tbank()
                mm_group(pgs, b_pgs, n, wgs, b_wgs, KD, 128 * mi, lambda k: xn[:, k, lo:hi], [b_xn])
                poa, b_poa = tbank()
                mm_group(poa, b_poa, n, woa, b_woa, 4, 128 * mi, lambda k: attT[:, k, n0:n0 + n], [b_attT])
                pos_, b_pos = tbank()
                mm_group(pos_, b_pos, n, wos, b_wos, 4, 128 * mi, lambda k: ysT[:, k, n0:n0 + n], [b_ys])
                P.add("act", lambda e, pga=pga: e.activation(out=sgt[0][:, 0:n], in_=pga[:, 0:n], func=AF.Sigmoid), reads=[b_pga],
                      writes=[b_sgt[0]])
                P.add("act", lambda e, pgs=pgs: e.activation(out=sgt[1][:, 0:n], in_=pgs[:, 0:n], func=AF.Sigmoid), reads=[b_pgs],
                      writes=[b_sgt[1]])
                P.add("dve", lambda e, poa=poa: e.tensor_tensor(out=cav[0][0][:, 0:n], in0=poa[:, 0:n], in1=sgt[0][:, 0:n], op=MUL),
                      reads=[b_poa, b_sgt[0]], writes=[b_cav[0][0]])
                P.add("dve", lambda e, pos_=pos_: e.tensor_tensor(out=cav[0][1][:, 0:n], in0=pos_[:, 0:n], in1=sgt[1][:, 0:n], op=MUL),
                      reads=[b_pos, b_sgt[1]], writes=[b_cav[0][1]])
                P.add("pool", lambda e, m=m: e.tensor_tensor(out=mixed[:, m, lo:hi], in0=cav[0][0][:, 0:n], in1=cav[0][1][:, 0:n], op=ADD),
                      reads=[b_cav[0][0], b_cav[0][1]], writes=[b_scr])
        for q in range(2):
            wo, b_wo = load_slab("w_out", KD, 512 * q, 512)
            for mi in range(4):
                m = 4 * q + mi
                ps, b_ps = tbank()
                mm_group(ps, b_ps, n, wo, b_wo, KD, 128 * mi, lambda k: mixed[:, k, lo:hi], [b_scr])
                P.add("dve", lambda e, ps=ps, m=m: e.tensor_tensor(out=xt[:, m, lo:hi], in0=ps[:, 0:n], in1=xt[:, m, lo:hi], op=ADD),
                      reads=[b_ps, b_xt], writes=[b_xt])

    ucount = [0]

    def conv3(ci_, src3, dst, b_src, b_dst, tile, nn, view=None):
        w0, w1, w2 = (smc("conv_w", tap * 44 + tile) for tap in range(3))
        P.add("act", lambda e: e.activation(out=dst, in_=src3(2), func=AF.Identity, scale=w2, bias=smc("conv_b", tile)),
              reads=[b_src, b_sm], writes=[b_dst])
        P.add("dve", lambda e: e.scalar_tensor_tensor(out=dst, in0=src3(1), scalar=w1, in1=dst, op0=MUL, op1=ADD),
              reads=[b_src, b_sm, b_dst], writes=[b_dst])
        P.add("dve", lambda e: e.scalar_tensor_tensor(out=dst, in0=src3(0), scalar=w0, in1=dst, op0=MUL, op1=ADD),
              reads=[b_src, b_sm, b_dst], writes=[b_dst])

    def phase_E(xt, b_xt, n0, n, first, last):
        lo, hi = HO, HO + n
        c_lo = 0 if first else HO
        N = hi - c_lo
        npr = min(n0 + n, NPT) - n0
        ns = n - npr
        norm_to_xn(xt, b_xt, "g_ffn", c_lo, hi)
        for q in range(6):
            npair = min(4, 22 - 4 * q)
            wa, b_wa = load_slab("w_up", KD, 512 * q, 128 * npair)
            wv_, b_wv = load_slab("w_up", KD, D_FF + 512 * q, 128 * npair)
            for pi_ in range(npair):
                p = 4 * q + pi_
                ub = ucount[0] % 2
                ucount[0] += 1
                for av, (w, b_w) in enumerate(((wa, b_wa), (wv_, b_wv))):
                    tile = p + 22 * av
                    u, b_u = upx[ub][av], b_upx[ub][av]
                    c, b_c = cav[ub][av], b_cav[ub][av]
                    ps, b_ps = tbank()
                    mm_group(ps, b_ps, N, w, b_w, KD, 128 * pi_, lambda k: xn[:, k, c_lo:hi], [b_xn])
                    P.add("act", lambda e, ps=ps, u=u: e.activation(out=u[:, c_lo:hi], in_=ps[:, 0:N], func=AF.Copy), reads=[b_ps],
                          writes=[b_u])
                    if not first:
                        P.add("pool", lambda e, u=u, tile=tile: e.tensor_copy(out=u[:, 0:2], in_=carry[:, tile, :]), reads=[b_carry],
                              writes=[b_u])
                    conv3(0, lambda k, u=u: u[:, k:k + npr], c[:, 0:npr], b_u, b_c, tile, npr)
                    P.add("pool", lambda e, u=u, tile=tile: e.tensor_copy(out=carry[:, tile, :], in_=u[:, npr:npr + 2]), reads=[b_u],
                          writes=[b_carry])
                    if last:
                        P.add("pool", lambda e, u=u, tile=tile: e.tensor_copy(out=cvp[:, tile, :], in_=u[:, npr:npr + 2]), reads=[b_u],
                              writes=[b_cvp])
                    if ns:
                        us, b_us = upS[av], b_upS[av]
                        P.add("pool", lambda e, us=us, tile=tile: e.tensor_copy(out=us[:, :, 0:2], in_=scT[:, tile, :, :]), reads=[b_scT],
                              writes=[b_us])
                        P.add("pool", lambda e, us=us, u=u: e.tensor_copy(
                            out=us[:, :, 2:6], in_=u[:, HO + npr:HO + n].rearrange("p (q s) -> p q s", s=4)), reads=[b_u], writes=[b_us])
                        conv3(0, lambda k, us=us: us[:, :, k:k + 4], c[:, npr:n].rearrange("p (q s) -> p q s", s=4), b_us, b_c, tile, ns)
                        P.add("pool", lambda e, us=us, tile=tile: e.tensor_copy(out=cvs[:, tile, :, :], in_=us[:, :, 4:6]), reads=[b_us],
                              writes=[b_cvs])
                ca, cv = cav[ub][0], cav[ub][1]
                P.add("act", lambda e, ca=ca: e.activation(out=ca[:, 0:n], in_=ca[:, 0:n], func=AF.Gelu_apprx_tanh), reads=[b_cav[ub][0]],
                      writes=[b_cav[ub][0]])
                P.add("dve", lambda e, ca=ca, cv=cv, p=p: e.tensor_tensor(out=hT[:, p, 0:n], in0=ca[:, 0:n], in1=cv[:, 0:n], op=MUL),
                      reads=[b_cav[ub][0], b_cav[ub][1]], writes=[b_hT])
        for m in range(KD):
            wd, b_wd = load_slab("w_down", 22, 128 * m, 128)
            ps, b_ps = tbank()
            mm_group(ps, b_ps, n, wd, b_wd, 22, 0, lambda k: hT[:, k, 0:n], [b_hT])
            P.add("dve", lambda e, ps=ps, m=m: e.tensor_tensor(out=xt[:, m, lo:hi], in0=ps[:, 0:n], in1=xt[:, m, lo:hi], op=ADD),
                  reads=[b_ps, b_xt], writes=[b_xt])
        norm_to_xn(xt, b_xt, "g_ple", lo, hi)
        P.add("pool", lambda e: e.dma_start(out=pTc[:, :, 0:n], in_=pT_d.rearrange("(k p) t -> p k t", p=128)[:, :, n0:n0 + n]),
              writes=[b_pTc], dma=True, semkey="pTc")
        for q in range(2):
            wg_, b_wg = load_slab("w_pg", KD, 512 * q, 512)
            wp_, b_wp = load_slab("w_pp", 2, 512 * q, 512)
            for mi in range(4):
                m = 4 * q + mi
                pg, b_pg = tbank()
                mm_group(pg, b_pg, n, wg_, b_wg, KD, 128 * mi, lambda k: xn[:, k, lo:hi], [b_xn])
                pp, b_pp = tbank()
                mm_group(pp, b_pp, n, wp_, b_wp, 2, 128 * mi, lambda k: pTc[:, k, 0:n], [b_pTc])
                P.add("act", lambda e, pg=pg: e.activation(out=sgt[0][:, 0:n], in_=pg[:, 0:n], func=AF.Sigmoid), reads=[b_pg],
                      writes=[b_sgt[0]])
                P.add("dve", lambda e, pp=pp: e.tensor_tensor(out=sgt[1][:, 0:n], in0=pp[:, 0:n], in1=sgt[0][:, 0:n], op=MUL),
                      reads=[b_pp, b_sgt[0]], writes=[b_sgt[1]])
                P.add("pool", lambda e, m=m: e.tensor_tensor(out=xt[:, m, lo:hi], in0=xt[:, m, lo:hi], in1=sgt[1][:, 0:n], op=ADD),
                      reads=[b_xt, b_sgt[1]], writes=[b_xt])
        stores.append(P.add("sp", lambda e: e.dma_start(out=o_yT.rearrange("(k p) t -> p k t", p=128)[:, :, n0:n0 + n], in_=xt[:, :, lo:hi]),
                            reads=[b_xt], dma=True, semkey="st_y"))

    n0_4, n_4 = chunks[4]
    phase_D(x1c4, b_x1c4, n0_4, n_4)
    lastp = HO + (NPT - n0_4)
    P.add("pool", lambda e: e.tensor_copy(out=hsend[:].rearrange("p (k c) -> p k c", c=2), in_=x1c4[:, :, lastp - 2:lastp]),
          reads=[b_x1c4], writes=[b_hsend])
    b_hin, b_hout = P.buf("cc_h_in"), P.buf("cc_h_out")
    P.add("sp", lambda e: e.dma_start(out=cc_h_in, in_=hsend[:]), reads=[b_hsend], writes=[b_hin], dma=True, semkey="hs1")
    P.add("pool", lambda e: e.collective_compute("AllGather", ALU.bypass, replica_groups=[[0, 1, 2, 3], [4, 5, 6, 7]],
                                                 ins=[cc_h_in.opt()], outs=[cc_h_out.opt()]),
          reads=[b_hin], writes=[b_hout], dma="cc", semkey="hs2")
    P.add("sp", lambda e: e.dma_start(out=Hg[:], in_=cc_h_out.rearrange("(r p) f -> p r f", p=128)), reads=[b_hout], writes=[b_Hg],
          dma=True, semkey="hs3")
    P.add("dve", lambda e: e.tensor_scalar(out=hrecv[:], in0=Hg[:, 0, :], scalar1=smc("hsel", 0), scalar2=None, op0=MUL),
          reads=[b_Hg, b_sm], writes=[b_hrecv])
    for r in range(1, 4):
        P.add("dve", lambda e, r=r: e.scalar_tensor_tensor(out=hrecv[:], in0=Hg[:, r, :], scalar=smc("hsel", r), in1=hrecv[:],
                                                           op0=MUL, op1=ADD), reads=[b_Hg, b_sm, b_hrecv], writes=[b_hrecv])
    for ci in range(4):
        n0, n = chunks[ci]
        if ci == 0:
            P.add("pool", lambda e: e.tensor_copy(out=x1[:, :, 0:2], in_=hrecv[:].rearrange("p (k c) -> p k c", c=2)),
                  reads=[b_hrecv], writes=[b_x1])
        phase_D(x1, b_x1, n0, n)
        phase_E(x1, b_x1, n0, n, ci == 0, False)
    phase_E(x1c4, b_x1c4, n0_4, n_4, False, True)
    stores.append(P.add("sp", lambda e: e.dma_start(out=o_cvp, in_=cvp[:]), reads=[b_cvp], dma=True, semkey="st_cvp"))
    stores.append(P.add("sp", lambda e: e.dma_start(out=o_cvs, in_=cvs[:]), reads=[b_cvs], dma=True, semkey="st_cvs"))


def build_sample_attn(P, AR, nc, din, dout, dint, stores, banks, bank_bufs, sm, b_sm, smc, ones_bf, b_ones, ckvn, b_ckvn, krK, b_krK,
                      qT, b_qT, attT, b_attT, n_pool, w_uk_b, b_wkb, w_uv_b, b_wvb, rope_c, rope_s, rotm_d, mC1):
    MUL, ADD = ALU.mult, ALU.add
    CW = KV_LORA + QK_ROPE
    cache = din("cache", [n_pool * PAGE, CW])
    ptab = din("ptab", [1, SEQ_PER_CORE * NPAGES], I32)
    w_ukT_d = din("w_ukT", [64, 8 * KV_LORA])
    ropeP_d = din("ropeP", [128, 2, NPAGES, 16])
    gk_rep_d = din("gk_rep", [128, 32])
    hselm_d = din("hselm", [128, 4, 32])
    cmask_d = din("cmask", [32, 4])

    AR.release(mC1)
    P.new_phase()
    NB_PG = 4
    pb = [AR.alloc([128, 4, CW], BF16) for _ in range(NB_PG)]
    b_pb = [P.buf(f"pb{i}") for i in range(NB_PG)]
    kt3 = [AR.alloc([128, 4, 96], BF16) for _ in range(2)]
    b_kt3 = [P.buf("kt3_0"), P.buf("kt3_1")]
    rt = [AR.alloc([128, 4, 16], F32) for _ in range(4)]
    b_rt = P.buf("rt")
    ropeP = AR.alloc([128, 2, NPAGES, 16], F32)
    gkr = AR.alloc([128, 32], F32)
    tabs = AR.alloc([128, 4, NPAGES, 16], F32)
    ptb = AR.alloc([128, SEQ_PER_CORE * NPAGES], I32)
    idx = AR.alloc([128, SEQ_PER_CORE * NPAGES], I32)
    iot = AR.alloc([128, 1], I32)
    wk_sb = AR.alloc([128, 2, 512], BF16)
    wv_sb = AR.alloc([128, 2, 512], BF16)
    wukT = AR.alloc([64, 8, KV_LORA], BF16)
    wukT_f = AR.alloc([64, 8 * KV_LORA], F32)
    hselm = AR.alloc([128, 4, 32], BF16)
    hselm_f = AR.alloc([128, 4, 32], F32)
    cmask = AR.alloc([32, 4], F32)
    qgk = AR.alloc([64, 8, NST], BF16)
    Qabs = AR.alloc([128, 2, SEQ_PER_CORE, 32], BF16)
    Qrope = AR.alloc([96, SEQ_PER_CORE, 32], BF16)
    cT_sb = [AR.alloc([128, 2, 512], BF16) for _ in range(2)]
    krT_sb = [AR.alloc([96, 512], BF16) for _ in range(2)]
    kn_sb = [AR.alloc([128, 512], BF16) for _ in range(4)]
    sq_sb = [AR.alloc([128, 512], BF16) for _ in range(4)]
    sqk = AR.alloc([32, 512], BF16)
    lnr = AR.alloc([32, 512], F32)
    rr_ = AR.alloc([32, 512], F32)
    sr = AR.alloc([32, 512], F32)
    Pm = AR.alloc([32, 512], BF16)
    PT_sb = [AR.alloc([128, 4, 32], BF16) for _ in range(2)]
    Lacc = AR.alloc([32, 20], F32)
    accs = AR.alloc([32, KV_LORA], F32)
    lsum = AR.alloc([32, 1], F32)
    olat = AR.alloc([32, KV_LORA], BF16)
    olT = AR.alloc([128, 2, SEQ_PER_CORE, 32], BF16)
    knew = AR.alloc([96, NST], BF16)
    kraw = AR.alloc([96, NST], BF16)
    kg32 = AR.alloc([96, NST], F32)
    kt1 = AR.alloc([96, NST], F32)
    kt2 = AR.alloc([96, NST], F32)
    rc_s = AR.alloc([96, NST], F32)
    rs_s = AR.alloc([96, NST], F32)
    rotm = AR.alloc([96, 96], F32)
    cnew = AR.alloc([4, KV_LORA], BF16)
    names = ["kt", "ropeP", "gkr", "tabs", "ptb", "idx", "iot", "wk", "wv", "wukT", "hselm", "cmask", "qgk", "Qabs", "Qrope",
             "sqk", "lnr", "rr", "sr", "Pm", "Lacc", "accs", "lsum", "olat", "olT", "knew", "kraw", "ktmp", "rcs", "rotm", "cnew"]
    B = {n: P.buf("s_" + n) for n in names}
    b_cT = [P.buf("cT0"), P.buf("cT1")]
    b_krT = [P.buf("krT0"), P.buf("krT1")]
    b_kn = [P.buf(f"kn{i}") for i in range(4)]
    b_sq = [P.buf(f"sq{i}") for i in range(4)]
    b_PT = [P.buf("PTs0"), P.buf("PTs1")]

    rrb = [0]

    def tbank():
        i = 1 + rrb[0] % 7
        rrb[0] += 1
        return banks[i], bank_bufs[i]
    ACC, b_ACC = banks[0], bank_bufs[0]

    ld = lambda out, in_, wr, key, rd=(): P.add("sp", lambda e: e.dma_start(out=out, in_=in_), reads=list(rd), writes=[wr], dma=True,
                                                semkey=key)
    ld(ropeP[:], ropeP_d, B["ropeP"], "s_ropeP")
    ld(gkr[:], gk_rep_d, B["gkr"], "s_gkr")
    ld(ptb[:], ptab.partition_broadcast(128), B["ptb"], "s_ptb")
    ld(wk_sb[:], w_uk_b.rearrange("(k p) m -> p k m", p=128), B["wk"], "s_wk", [b_wkb])
    ld(wv_sb[:], w_uv_b.rearrange("(k p) m -> p k m", p=128), B["wv"], "s_wv", [b_wvb])
    ld(wukT_f[:], w_ukT_d, B["wukT"], "s_wukT")
    ld(hselm_f[:], hselm_d, B["hselm"], "s_hselm")
    ld(cmask[:], cmask_d, B["cmask"], "s_cmask")
    ld(rc_s[:], rope_c[:, NPT:T], B["rcs"], "s_rcs")
    ld(rs_s[:], rope_s[:, NPT:T], B["rcs"], "s_rss")
    ld(rotm[:], rotm_d, B["rotm"], "s_rotm")
    P.add("pool", lambda e: e.tensor_copy(out=wukT[:].rearrange("p h l -> p (h l)"), in_=wukT_f[:]), reads=[B["wukT"]], writes=[B["wukT"]])
    P.add("pool", lambda e: e.tensor_copy(out=hselm[:], in_=hselm_f[:]), reads=[B["hselm"]], writes=[B["hselm"]])
    P.add("pool", lambda e: e.iota(iot[:], [[0, 1]], base=0, channel_multiplier=1), writes=[B["iot"]])
    P.add("dve", lambda e: e.tensor_scalar(out=idx[:], in0=ptb[:], scalar1=float(PAGE), scalar2=iot[:, 0:1], op0=MUL, op1=ADD),
          reads=[B["ptb"], B["iot"]], writes=[B["idx"]])
    g1 = gkr[:, 0:16].unsqueeze(1).broadcast_to([128, NPAGES, 16])
    g2 = gkr[:, 16:32].unsqueeze(1).broadcast_to([128, NPAGES, 16])
    tt_ = lambda o, a, b_, op: P.add("pool", lambda e: e.tensor_tensor(out=o, in0=a, in1=b_, op=op), reads=[B["ropeP"], B["gkr"], B["tabs"]],
                                     writes=[B["tabs"]])
    tt_(tabs[:, 0], ropeP[:, 0], g1, MUL)
    tt_(tabs[:, 1], ropeP[:, 0], g2, MUL)
    tt_(tabs[:, 2], ropeP[:, 1], g2, MUL)
    P.add("pool", lambda e: e.tensor_single_scalar(out=tabs[:, 2], in_=tabs[:, 2], scalar=-1.0, op=MUL), reads=[B["tabs"]],
          writes=[B["tabs"]])
    tt_(tabs[:, 3], ropeP[:, 1], g1, MUL)
    for i in range(2):
        P.add("pool", lambda e, i=i: e.memset(kt3[i][:], 0.0), writes=[b_kt3[i]])

    P.add("dve", lambda e: e.tensor_scalar(out=qgk[:], in0=qT[0:64, :, NPT:T], scalar1=smc("g_k")[0:64, :], scalar2=None, op0=MUL),
          reads=[b_qT, b_sm], writes=[B["qgk"]])
    for h in range(N_HEADS):
        for kt in range(2):
            ps, b_ps = tbank()
            P.add("pe", lambda e, ps=ps, h=h, kt=kt: e.matmul(ps[:, 0:NST], lhsT=wukT[:, h, 128 * kt:128 * kt + 128], rhs=qgk[:, h, :],
                                                              start=True, stop=True), reads=[B["wukT"], B["qgk"]], writes=[b_ps])
            P.add("act", lambda e, ps=ps, h=h, kt=kt: e.activation(out=Qabs[:, kt, :, 4 * h:4 * h + 4],
                                                                   in_=ps[:, 0:NST].rearrange("p (q t) -> p q t", t=4), func=AF.Copy),
                  reads=[b_ps], writes=[B["Qabs"]])
        P.add("pool", lambda e, h=h: e.tensor_copy(out=Qrope[64:96, :, 4 * h:4 * h + 4],
                                                   in_=qT[64:96, h, NPT:T].rearrange("p (q t) -> p q t", t=4)),
              reads=[b_qT], writes=[B["Qrope"]])
    P.add("act", lambda e: e.activation(out=kraw[64:96, :], in_=krK[64:96, NPT:T], func=AF.Copy), reads=[b_krK], writes=[B["kraw"]])
    P.add("dve", lambda e: e.tensor_scalar(out=kg32[64:96, :], in0=krK[64:96, NPT:T], scalar1=smc("g_k")[64:96, :], scalar2=None, op0=MUL),
          reads=[b_krK, b_sm], writes=[B["ktmp"]])
    ps, b_ps = tbank()
    P.add("pe", lambda e, ps=ps: e.matmul(ps[0:96, 0:NST], lhsT=rotm[64:96, 0:96], rhs=kg32[64:96, :], start=True, stop=True,
                                          tile_position=(64, 0)), reads=[B["rotm"], B["ktmp"]], writes=[b_ps])
    P.add("dve", lambda e: e.tensor_tensor(out=kt1[64:96, :], in0=kg32[64:96, :], in1=rc_s[64:96, :], op=MUL), reads=[B["ktmp"], B["rcs"]],
          writes=[B["ktmp"]])
    P.add("dve", lambda e, ps=ps: e.tensor_tensor(out=kt2[64:96, :], in0=ps[64:96, 0:NST], in1=rs_s[64:96, :], op=MUL),
          reads=[b_ps, B["rcs"]], writes=[B["ktmp"]])
    P.add("dve", lambda e: e.tensor_tensor(out=knew[64:96, :], in0=kt1[64:96, :], in1=kt2[64:96, :], op=ADD), reads=[B["ktmp"]],
          writes=[B["knew"]])

    idb = AR.alloc([128, 128], BF16)
    b_idb = P.buf("idb")
    P.add("pool", lambda e: e.tensor_copy(out=idb[:], in_=sm[:, SL["ident"][0]:SL["ident"][0] + 128]), reads=[b_sm], writes=[b_idb])
    sqk2 = [sqk, AR.alloc([32, 512], BF16)]
    lnr2 = [lnr, AR.alloc([32, 512], F32)]
    rr2 = [rr_, AR.alloc([32, 512], F32)]
    sr2 = [sr, AR.alloc([32, 512], F32)]
    Pm2 = [Pm, AR.alloc([32, 512], BF16)]
    Lacc2 = [Lacc, AR.alloc([32, 20], F32)]
    Bq = [{n: P.buf(f"s2_{n}{i}") for n in ("sqk", "lnr", "rr", "sr", "Pm")} for i in range(2)]
    b_Lacc2 = [P.buf("Lacc0"), P.buf("Lacc1")]
    cnt = [0]
    gcnt = [0]

    def tbank2():
        i = 2 + rrb[0] % 6
        rrb[0] += 1
        return banks[i], bank_bufs[i]

    def chunk(q, col, npos, cT, b_cTs, kraw_ap, b_kraw, krop_ap, b_krop, crows, first, mask):
        n = npos
        ACC, b_ACC = banks[q % 2], bank_bufs[q % 2]
        Lq, b_Lq = Lacc2[q % 2], b_Lacc2[q % 2]
        ci = gcnt[0] % 2
        gcnt[0] += 1
        sqk_, lnr_, rr__, sr_, Pm_ = sqk2[ci], lnr2[ci], rr2[ci], sr2[ci], Pm2[ci]
        Bc = Bq[ci]
        pss_l = []
        for m in range(4):
            ps, b_ps = tbank2()
            for kt in range(2):
                P.add("pe", lambda e, ps=ps, m=m, kt=kt: e.matmul(ps[:, 0:n], lhsT=wk_sb[:, kt, 128 * m:128 * m + 128], rhs=cT[:, kt, 0:n],
                                                                  start=(kt == 0), stop=(kt == 1)), reads=[B["wk"]] + b_cTs, writes=[b_ps])
            pss_l.append((ps, b_ps))
        for m in range(4):
            ps, b_ps = pss_l[m]
            if m < 2:
                P.add("act", lambda e, ps=ps, m=m: e.activation(out=kn_sb[m][:, 0:n], in_=ps[:, 0:n], func=AF.Copy), reads=[b_ps],
                      writes=[b_kn[m]])
            else:
                P.add("dve", lambda e, ps=ps, m=m: e.tensor_copy(out=kn_sb[m][:, 0:n], in_=ps[:, 0:n]), reads=[b_ps], writes=[b_kn[m]])
            P.add("dve", lambda e, m=m: e.tensor_tensor(out=sq_sb[m][:, 0:n], in0=kn_sb[m][:, 0:n], in1=kn_sb[m][:, 0:n], op=MUL),
                  reads=[b_kn[m]], writes=[b_sq[m]])
        P.add("pool", lambda e: e.tensor_tensor(out=sqk_[:, 0:n], in0=kraw_ap, in1=kraw_ap, op=MUL), reads=b_kraw, writes=[Bc["sqk"]])
        yield
        pss, b_pss = tbank2()
        for m in range(4):
            P.add("pe", lambda e, m=m: e.matmul(pss[0:32, 0:n], lhsT=hselm[:, m, :], rhs=sq_sb[m][:, 0:n], start=(m == 0), stop=False),
                  reads=[B["hselm"], b_sq[m]], writes=[b_pss])
        P.add("pe", lambda e: e.matmul(pss[0:32, 0:n], lhsT=ones_bf[0:32, 0:32], rhs=sqk_[:, 0:n], start=False, stop=True),
              reads=[b_ones, Bc["sqk"]], writes=[b_pss])
        psc, b_psc = tbank2()
        for kt in range(2):
            P.add("pe", lambda e, kt=kt: e.matmul(psc[0:32, 0:n], lhsT=Qabs[:, kt, q, :], rhs=cT[:, kt, 0:n], start=(kt == 0), stop=False),
                  reads=[B["Qabs"]] + b_cTs, writes=[b_psc])
        P.add("pe", lambda e: e.matmul(psc[0:32, 0:n], lhsT=Qrope[64:96, q, :], rhs=krop_ap, start=False, stop=True, tile_position=(64, 0)),
              reads=[B["Qrope"]] + b_krop, writes=[b_psc])
        P.add("act", lambda e: e.activation(out=lnr_[:, 0:n], in_=pss[0:32, 0:n], func=AF.Ln, scale=1.0 / QK_HEAD, bias=EPS),
              reads=[b_pss], writes=[Bc["lnr"]])
        P.add("act", lambda e: e.activation(out=rr__[:, 0:n], in_=lnr_[:, 0:n], func=AF.Exp, scale=-0.5), reads=[Bc["lnr"]],
              writes=[Bc["rr"]])
        P.add("dve", lambda e: e.tensor_tensor(out=sr_[:, 0:n], in0=psc[0:32, 0:n], in1=rr__[:, 0:n], op=MUL), reads=[b_psc, Bc["rr"]],
              writes=[Bc["sr"]])
        if mask:
            P.add("act", lambda e: e.activation(out=sr_[:, 0:n], in_=sr_[:, 0:n], func=AF.Exp, scale=SCALE), reads=[Bc["sr"]],
                  writes=[Bc["sr"]])
            P.add("dve", lambda e: e.tensor_tensor(out=sr_[:, 0:n], in0=sr_[:, 0:n], in1=cmask[:, 0:n], op=MUL), reads=[Bc["sr"], B["cmask"]],
                  writes=[Bc["sr"]])
            P.add("dve", lambda e: e.tensor_copy(out=Pm_[:, 0:n], in_=sr_[:, 0:n]), reads=[Bc["sr"]], writes=[Bc["Pm"]])
            P.add("dve", lambda e: e.reduce_sum(out=Lq[:, col:col + 1], in_=sr_[:, 0:n], axis=AX.X), reads=[Bc["sr"]], writes=[b_Lq])
        else:
            P.add("act", lambda e: e.activation(out=Pm_[:, 0:n], in_=sr_[:, 0:n], func=AF.Exp, scale=SCALE, accum_out=Lq[:, col:col + 1]),
                  reads=[Bc["sr"]], writes=[Bc["Pm"], b_Lq])
        yield
        pT_, b_pT = tbank2()
        pTb = pT_[:].bitcast(BF16)
        nblk = len(crows)
        for bi, (cap, b_cap, c0, c1) in enumerate(crows):
            P.add("pe", lambda e, bi=bi, c0=c0, c1=c1: e.transpose(pTb[0:c1 - c0, 32 * bi:32 * bi + 32], Pm_[:, c0:c1], idb[0:32, 0:32]),
                  reads=[Bc["Pm"], b_idb], writes=[b_pT])
        pi_ = cnt[0] % 2
        cnt[0] += 1
        rows = crows[0][3] - crows[0][2]
        P.add("act", lambda e, pi_=pi_: e.activation(out=PT_sb[pi_][0:rows, 0:nblk, :],
                                                     in_=pTb[0:rows, 0:32 * nblk].rearrange("p (b c) -> p b c", c=32), func=AF.Copy),
              reads=[b_pT], writes=[b_PT[pi_]])
        yield
        for bi, (cap, b_cap, c0, c1) in enumerate(crows):
            P.add("pe", lambda e, bi=bi, cap=cap, c0=c0, c1=c1, pi_=pi_, st=(first and bi == 0): e.matmul(
                ACC[0:32, 0:KV_LORA], lhsT=PT_sb[pi_][0:c1 - c0, bi, :], rhs=cap, start=st, stop=False),
                reads=[b_PT[pi_]] + b_cap, writes=[b_ACC])

    pgc = [0]

    def page_chunk(q, g):
        bi_ = pgc[0] % NB_PG
        ki = pgc[0] % 2
        ci_ = pgc[0] % 2
        pgc[0] += 1
        pbt, b_pbt = pb[bi_], b_pb[bi_]
        for pg in range(4):
            j = q * NPAGES + 4 * g + pg
            P.add("pool", lambda e, pbt=pbt, pg=pg, j=j: e.indirect_dma_start(
                out=pbt[:, pg, :], out_offset=None, in_=cache, in_offset=bass.IndirectOffsetOnAxis(ap=idx[:, j:j + 1], axis=0)),
                reads=[B["idx"]], writes=[b_pbt], dma=True, semkey=f"pb{bi_}")
        k3, b_k3 = kt3[ki], b_kt3[ki]
        kr1, kr2 = pbt[:, :, KV_LORA:KV_LORA + 16], pbt[:, :, KV_LORA + 16:KV_LORA + 32]
        pgs = slice(4 * g, 4 * g + 4)
        pl = lambda fn, rd, wr: P.add("pool", fn, reads=rd, writes=wr)
        pl(lambda e: e.tensor_copy(out=k3[:, :, 0:32], in_=pbt[:, :, KV_LORA:CW]), [b_pbt], [b_k3])
        pl(lambda e: e.tensor_tensor(out=rt[0][:], in0=kr1, in1=tabs[:, 0, pgs, :], op=MUL), [b_pbt, B["tabs"]], [b_rt])
        pl(lambda e: e.tensor_tensor(out=rt[1][:], in0=kr2, in1=tabs[:, 2, pgs, :], op=MUL), [b_pbt, B["tabs"]], [b_rt])
        pl(lambda e: e.tensor_tensor(out=k3[:, :, 64:80], in0=rt[0][:], in1=rt[1][:], op=ADD), [b_rt], [b_k3])
        pl(lambda e: e.tensor_tensor(out=rt[2][:], in0=kr2, in1=tabs[:, 1, pgs, :], op=MUL), [b_pbt, B["tabs"]], [b_rt])
        pl(lambda e: e.tensor_tensor(out=rt[3][:], in0=kr1, in1=tabs[:, 3, pgs, :], op=MUL), [b_pbt, B["tabs"]], [b_rt])
        pl(lambda e: e.tensor_tensor(out=k3[:, :, 80:96], in0=rt[2][:], in1=rt[3][:], op=ADD), [b_rt], [b_k3])
        yield
        psT, b_psT = tbank2()
        psTb = psT[:].bitcast(BF16).rearrange("p (k n) -> p k n", k=2)
        psK, b_psK = tbank2()
        psKb = psK[:].bitcast(BF16)
        for pg in range(4):
            for kt in range(2):
                P.add("pe", lambda e, pg=pg, kt=kt: e.transpose(psTb[:, kt, 128 * pg:128 * pg + 128], pbt[:, pg, 128 * kt:128 * kt + 128],
                                                                idb[:, :]), reads=[b_pbt, b_idb], writes=[b_psT])
            P.add("pe", lambda e, pg=pg: e.transpose(psKb[0:96, 128 * pg:128 * pg + 128], k3[:, pg, :], idb[:, :]), reads=[b_k3, b_idb],
                  writes=[b_psK])
        P.add("dve", lambda e: e.tensor_copy(out=cT_sb[ci_][:], in_=psTb), reads=[b_psT], writes=[b_cT[ci_]])
        P.add("act", lambda e: e.activation(out=krT_sb[ci_][:], in_=psKb[0:96, 0:512], func=AF.Copy), reads=[b_psK], writes=[b_krT[ci_]])
        yield
        crows = [(pbt[:, pg, 0:KV_LORA], [b_pbt], 128 * pg, 128 * pg + 128) for pg in range(4)]
        yield from chunk(q, g, 512, cT_sb[ci_], [b_cT[ci_]], krT_sb[ci_][0:32, 0:512], [b_krT[ci_]], krT_sb[ci_][64:96, 0:512],
                         [b_krT[ci_]], crows, g == 0, False)

    def run_pipelined(gens, step=2):
        active = []
        it = iter(gens)
        more = True
        while more or active:
            if more:
                try:
                    active.append(next(it))
                except StopIteration:
                    more = False
            for _ in range(step):
                for gg in list(active):
                    try:
                        next(gg)
                    except StopIteration:
                        active.remove(gg)

    for q in range(SEQ_PER_CORE):
        run_pipelined([page_chunk(q, g) for g in range(NPAGES // 4)])
        ACC, b_ACC = banks[q % 2], bank_bufs[q % 2]
        Lq, b_Lq = Lacc2[q % 2], b_Lacc2[q % 2]
        c0 = NPT + 4 * q
        psn, b_psn = tbank2()
        psnb = psn[:].bitcast(BF16)
        for kt in range(2):
            P.add("pe", lambda e, kt=kt, c0=c0, psnb=psnb: e.transpose(psnb[0:4, 128 * kt:128 * kt + 128], ckvn[:, kt, c0:c0 + 4], idb[:, :]),
                  reads=[b_ckvn, b_idb], writes=[b_psn])
        P.add("act", lambda e, psnb=psnb: e.activation(out=cnew[:], in_=psnb[0:4, 0:KV_LORA], func=AF.Copy), reads=[b_psn], writes=[B["cnew"]])
        for _ in chunk(q, 16, 4, ckvn[:, :, c0:c0 + 4], [b_ckvn], kraw[64:96, 4 * q:4 * q + 4], [B["kraw"]], knew[64:96, 4 * q:4 * q + 4],
                       [B["knew"]], [(cnew[:, :], [B["cnew"]], 0, 4)], False, True):
            pass
        P.add("act", lambda e, ACC=ACC: e.activation(out=accs[:], in_=ACC[0:32, 0:KV_LORA], func=AF.Copy), reads=[b_ACC], writes=[B["accs"]])
        P.add("dve", lambda e, Lq=Lq: e.reduce_sum(out=lsum[:], in_=Lq[:, 0:17], axis=AX.X), reads=[b_Lq], writes=[B["lsum"]])
        P.add("dve", lambda e: e.reciprocal(out=lsum[:], in_=lsum[:]), reads=[B["lsum"]], writes=[B["lsum"]])
        P.add("dve", lambda e: e.tensor_scalar(out=olat[:], in0=accs[:], scalar1=lsum[:, 0:1], scalar2=None, op0=MUL),
              reads=[B["accs"], B["lsum"]], writes=[B["olat"]])
        pso, b_pso = tbank2()
        psob = pso[:].bitcast(BF16)
        for kt in range(2):
            P.add("pe", lambda e, kt=kt, psob=psob: e.transpose(psob[:, 32 * kt:32 * kt + 32], olat[:, 128 * kt:128 * kt + 128],
                                                                idb[0:32, 0:32]), reads=[B["olat"], b_idb], writes=[b_pso])
        P.add("act", lambda e, q=q, psob=psob: e.activation(out=olT[:, :, q, :], in_=psob[:, 0:64].rearrange("p (k c) -> p k c", k=2),
                                                            func=AF.Copy), reads=[b_pso], writes=[B["olT"]])
    for hp in range(4):
        ps, b_ps = tbank()
        for hh in range(2):
            h = 2 * hp + hh
            for kt in range(2):
                P.add("pe", lambda e, ps=ps, hh=hh, h=h, kt=kt: e.matmul(
                    ps[64 * hh:64 * hh + 64, 0:NST], lhsT=wv_sb[:, kt, 64 * h:64 * h + 64], rhs=olT[:, kt, :, 4 * h:4 * h + 4],
                    start=(kt == 0), stop=(kt == 1), tile_position=(0, 64 * hh)), reads=[B["wv"], B["olT"]], writes=[b_ps])
        P.add("act", lambda e, ps=ps, hp=hp: e.activation(out=attT[:, hp, NPT:T], in_=ps[:, 0:NST], func=AF.Copy), reads=[b_ps],
              writes=[b_attT])


def build(stage=99, n_pool=10240, dbg=False):
    nc = bass.Bass("TRN2", target_bir_lowering=False)
    P = Prog(nc)
    ins_, outs_ = {}, {}

    def din(name, shape, dt=F32):
        ins_[name] = nc.dram_tensor(name, list(shape), dt, kind="ExternalInput").ap()
        return ins_[name]

    def dout(name, shape, dt=F32):
        outs_[name] = nc.dram_tensor(name, list(shape), dt, kind="ExternalOutput").ap()
        return outs_[name]

    def dint(name, shape, dt):
        return nc.dram_tensor(name, list(shape), dt).ap()

    xT = din("xT", [D, T])
    small = din("small", [128, SL["_n"]])
    rope_c = din("rope_c", [96, T])
    rope_s = din("rope_s", [96, T])
    w_in = din("w_in", [D, IN_COLS])
    o_ckvT = dout("o_ckvT", [KV_LORA, T])
    o_krT = dout("o_krT", [QK_ROPE, T])

    w_in_b = dint("w_in_b", [D, IN_COLS], BF16)

    stores = []
    pool_q = "pool"

    def cast_w(dst, src, rows, cols, key):
        a = 1
        while cols // a > 2048 or cols % a:
            a += 1
        s2 = src.rearrange("k (a m) -> (k a) m", a=a) if a > 1 else src
        d2 = dst.rearrange("k (a m) -> (k a) m", a=a) if a > 1 else dst
        b = P.buf(key)
        P.add(pool_q, lambda e: e.dma_start(out=d2, in_=s2), writes=[b], dma=True, semkey=key)
        return b

    b_w_in_b = cast_w(w_in_b, w_in, D, IN_COLS, "c_w_in")
    w_glu = din("w_glu", [SSM_W, 2 * SSM_W])
    w_glu_b = dint("w_glu_b", [SSM_W, 2 * SSM_W], BF16)
    b_w_glu_b = cast_w(w_glu_b, w_glu, SSM_W, 2 * SSM_W, "c_w_glu")

    ones_bf = P.sbuf("ones_bf", [128, 128], BF16)
    b_ones = P.buf("ones")
    P.add("pool", lambda e: e.memset(ones_bf[:], 1.0), writes=[b_ones])
    sm = P.sbuf("sm", [128, SL["_n"]], F32)
    b_sm = P.buf("sm")
    P.add("sp", lambda e: e.dma_start(out=sm[:], in_=small), writes=[b_sm], dma=True, semkey="sm")

    def smc(name, i=0):
        o = SL[name][0] + i
        return sm[:, o:o + 1]

    NB = 8
    banks = [P.psum(f"ps{i}", [128, 512], F32) for i in range(NB)]
    bank_bufs = [P.buf(f"ps{i}") for i in range(NB)]
    bank_rr = [0]

    def next_bank():
        i = bank_rr[0] % NB
        bank_rr[0] += 1
        return banks[i], bank_bufs[i]

    cqn = P.sbuf("cqn", [128, 3, T], BF16)
    ckvn = P.sbuf("ckvn", [128, 2, T], BF16)
    krK = P.sbuf("krK", [96, T], F32)
    uT = P.sbuf("uT", [128, 4, T], BF16)
    b_cqn, b_ckvn, b_krT, b_uT = P.buf("cqn"), P.buf("ckvn"), P.buf("krT"), P.buf("uT")

    chunks = _token_chunks()
    AR = Arena(P, "arena", 142 * 1024)

    NA = OFF_GA
    wA = AR.alloc([128, KD, NA], BF16)
    b_wA = P.buf("wA")
    P.add("sp", lambda e: e.dma_start(out=wA[:], in_=w_in_b.rearrange("(k p) m -> p k m", p=128)[:, :, 0:NA]),
          reads=[b_w_in_b], writes=[b_wA], dma=True, semkey="wA")

    xc = [AR.alloc([128, KD, 512], F32) for i in range(2)]
    b_xc = [P.buf(f"xc{i}") for i in range(2)]
    sq = AR.alloc([128, KD, 512], BF16)
    b_sq = P.buf("sq")
    lnv = AR.alloc([128, 512], F32)
    b_lnv = P.buf("lnv")
    rstd = AR.alloc([128, 512], F32)
    b_rstd = P.buf("rstd")
    xn = AR.alloc([128, KD, 512], BF16)
    b_xn = P.buf("xn")
    cqf = AR.alloc([128, 3, 512], F32)
    b_cqf = P.buf("cqf")
    ckvf = AR.alloc([128, 2, 512], F32)
    b_ckvf = P.buf("ckvf")
    ckvo = AR.alloc([128, 2, 512], F32)
    b_ckvo = P.buf("ckvo")

    xT_v = xT.rearrange("(k p) t -> p k t", p=128)

    def rms_rstd(src, b_src, nk, n, nfeat, dst=rstd, b_dst=b_rstd):
        P.add("dve", lambda e: e.tensor_tensor(out=sq[:, 0:nk, 0:n], in0=src[:, 0:nk, 0:n], in1=src[:, 0:nk, 0:n],
                                               op=ALU.mult), reads=[b_src], writes=[b_sq])
        ps, b_ps = next_bank()
        for k in range(nk):
            P.add("pe", lambda e, k=k: e.matmul(ps[:, 0:n], lhsT=ones_bf[:], rhs=sq[:, k, 0:n], start=(k == 0),
                                                stop=(k == nk - 1)), reads=[b_sq, b_ones], writes=[b_ps])
        P.add("act", lambda e: e.activation(out=lnv[:, 0:n], in_=ps[:, 0:n], func=AF.Ln, scale=1.0 / nfeat, bias=EPS),
              reads=[b_ps], writes=[b_lnv])
        P.add("act", lambda e: e.activation(out=dst[:, 0:n], in_=lnv[:, 0:n], func=AF.Exp, scale=-0.5),
              reads=[b_lnv], writes=[b_dst])

    for ci, (n0, n) in enumerate(chunks):
        xb, b_x = xc[ci % 2], b_xc[ci % 2]
        P.add("sp", lambda e, xb=xb, n0=n0, n=n: e.dma_start(out=xb[:, :, 0:n], in_=xT_v[:, :, n0:n0 + n]),
              writes=[b_x], dma=True, semkey=f"xc{ci % 2}")
        rms_rstd(xb, b_x, KD, n, D)
        for k in range(KD):
            P.add("dve", lambda e, k=k, xb=xb, n=n: e.scalar_tensor_tensor(
                out=xn[:, k, 0:n], in0=xb[:, k, 0:n], scalar=smc("g_mix", k), in1=rstd[:, 0:n],
                op0=ALU.mult, op1=ALU.mult), reads=[b_x, b_rstd, b_sm], writes=[b_xn])
        groups = [("cq", i, OFF_CKV * 0 + 128 * i, 128) for i in range(3)] + \
                 [("ckv", i, OFF_CKV + 128 * i, 128) for i in range(2)] + \
                 [("kr", 0, OFF_KR, 32)] + [("u", i, OFF_U + 128 * i, 128) for i in range(4)]
        for kind, i, c0, m in groups:
            ps, b_ps = next_bank()
            for k in range(KD):
                if kind == "kr":
                    P.add("pe", lambda e, k=k, c0=c0, m=m, ps=ps, n=n: e.matmul(
                        ps[64:96, 0:n], lhsT=wA[:, k, c0:c0 + m], rhs=xn[:, k, 0:n], start=(k == 0), stop=(k == KD - 1),
                        tile_position=(0, 64)), reads=[b_wA, b_xn], writes=[b_ps])
                    continue
                P.add("pe", lambda e, k=k, c0=c0, m=m, ps=ps, n=n: e.matmul(
                    ps[0:m, 0:n], lhsT=wA[:, k, c0:c0 + m], rhs=xn[:, k, 0:n], start=(k == 0), stop=(k == KD - 1)),
                    reads=[b_wA, b_xn], writes=[b_ps])
            if kind == "cq":
                P.add("act", lambda e, i=i, ps=ps, n=n: e.activation(out=cqf[:, i, 0:n], in_=ps[:, 0:n], func=AF.Copy),
                      reads=[b_ps], writes=[b_cqf])
            elif kind == "ckv":
                P.add("act", lambda e, i=i, ps=ps, n=n: e.activation(out=ckvf[:, i, 0:n], in_=ps[:, 0:n], func=AF.Copy),
                      reads=[b_ps], writes=[b_ckvf])
            elif kind == "kr":
                P.add("act", lambda e, ps=ps, n=n, n0=n0: e.activation(out=krK[64:96, n0:n0 + n], in_=ps[64:96, 0:n], func=AF.Copy),
                      reads=[b_ps], writes=[b_krT])
            else:
                P.add("act", lambda e, i=i, ps=ps, n=n, n0=n0: e.activation(out=uT[:, i, n0:n0 + n], in_=ps[:, 0:n], func=AF.Copy),
                      reads=[b_ps], writes=[b_uT])
        rms_rstd(cqf, b_cqf, 3, n, Q_LORA)
        for i in range(3):
            P.add("dve", lambda e, i=i, n=n, n0=n0: e.scalar_tensor_tensor(
                out=cqn[:, i, n0:n0 + n], in0=cqf[:, i, 0:n], scalar=smc("g_cq", i), in1=rstd[:, 0:n],
                op0=ALU.mult, op1=ALU.mult), reads=[b_cqf, b_rstd, b_sm], writes=[b_cqn])
        rms_rstd(ckvf, b_ckvf, 2, n, KV_LORA)
        for i in range(2):
            P.add("dve", lambda e, i=i, n=n: e.scalar_tensor_tensor(
                out=ckvo[:, i, 0:n], in0=ckvf[:, i, 0:n], scalar=smc("g_ckv", i), in1=rstd[:, 0:n],
                op0=ALU.mult, op1=ALU.mult), reads=[b_ckvf, b_rstd, b_sm], writes=[b_ckvo])
        P.add("pool", lambda e, n=n, n0=n0: e.tensor_copy(out=ckvn[:, :, n0:n0 + n], in_=ckvo[:, :, 0:n]),
              reads=[b_ckvo], writes=[b_ckvn])
        stores.append(P.add("sp", lambda e, n=n, n0=n0: e.dma_start(
            out=o_ckvT.rearrange("(k p) t -> p k t", p=128)[:, :, n0:n0 + n], in_=ckvo[:, :, 0:n]),
            reads=[b_ckvo], dma=True, semkey="st_ckv"))
    stores.append(P.add("sp", lambda e: e.dma_start(out=o_krT, in_=krK[64:96, :]), reads=[b_krT], dma=True, semkey="st_kr"))


    if stage >= 2:
        ssm = build_ssm(P, AR, nc, din, dout, dint, stores, next_bank, uT, b_uT, smc, b_sm, sm, w_glu_b, b_w_glu_b, chunks)
        if dbg:
            o_dbg_ys = dout("o_dbg_ys", [128, 4, T], BF16)
            stores.append(P.add("sp", lambda e: e.dma_start(out=o_dbg_ys, in_=uT[:]), reads=[b_uT], dma=True, semkey="dbg_ys"))


    if stage >= 3:
        attT = P.sbuf("attT", [128, 4, T], BF16)
        b_attT = P.buf("attT")
        kS = P.sbuf("kS", [96, 8, NST], BF16)
        b_kS = P.buf("kS")
        att = build_attn(P, AR, nc, din, dout, dint, stores, banks, bank_bufs, cast_w, cqn, b_cqn, ckvn, b_ckvn, krK, b_krT,
                         sm, b_sm, smc, ones_bf, b_ones, rope_c, rope_s, attT, b_attT, chunks, kS, b_kS)
        if stage >= 5:
            build_sample_attn(P, AR, nc, din, dout, dint, stores, banks, bank_bufs, sm, b_sm, smc, ones_bf, b_ones, ckvn, b_ckvn,
                              krK, b_krT, att["qT"], att["b_qT"], attT, b_attT, n_pool, att["w_uk_b"], att["b_wkb"], att["w_uv_b"],
                              att["b_wvb"], rope_c, rope_s, att["rotm_d"], att["mC1"])
        if dbg:
            o_dbg_att = dout("o_dbg_att", [128, 4, T], BF16)
            stores.append(P.add("sp", lambda e: e.dma_start(out=o_dbg_att, in_=attT[:]), reads=[b_attT], dma=True, semkey="dbg_att"))


    if stage >= 4:
        build_tail(P, AR, nc, din, dout, dint, stores, banks, bank_bufs, cast_w, xT, w_in_b, b_w_in_b, sm, b_sm, smc, ones_bf, b_ones,
                   attT, b_attT, ssm["ysT"], ssm["b_ys"], chunks)

    P.add("sp", lambda e: None, after=stores)
    P.finalize()
    return nc, ins_, outs_, P


def _rope_tables(pos):
    inv_freq = np.power(np.float32(10000.0), -np.arange(0, QK_ROPE, 2, dtype=np.float32) / np.float32(QK_ROPE)).astype(np.float32)
    ang = pos.astype(np.float32)[:, None] * inv_freq[None, :]
    return np.cos(ang).astype(np.float32), np.sin(ang).astype(np.float32)


def _prep_core(c, inp):
    b, j = c // 4, c % 4
    m = {}
    xp = inp["x_prompt"][b, NPT * j:NPT * (j + 1)]
    xs = inp["x_sample"][SEQ_PER_CORE * c:SEQ_PER_CORE * (c + 1)].reshape(NST, D)
    m["xT"] = np.ascontiguousarray(np.concatenate([xp, xs], 0).T)
    pos = np.concatenate([NPT * j + np.arange(NPT), np.tile(PAST + np.arange(4), SEQ_PER_CORE)])
    cs, sn = _rope_tables(pos)
    rc = np.ones((96, T), np.float32)
    rs = np.zeros((96, T), np.float32)
    rc[64:80] = cs.T
    rc[80:96] = cs.T
    rs[64:80] = sn.T
    rs[80:96] = sn.T
    m["rope_c"], m["rope_s"] = rc, rs
    sm = np.zeros((128, SL["_n"]), np.float32)

    def put(name, arr):
        o, n = SL[name]
        sm[:arr.shape[0], o:o + arr.shape[1]] = arr
    put("g_mix", inp["g_mix"][0].reshape(8, 128).T)
    put("g_cq", inp["g_cq"][0].reshape(3, 128).T)
    put("g_ckv", inp["g_ckv"][0].reshape(2, 128).T)
    put("g_ffn", inp["g_ffn"][0].reshape(8, 128).T)
    put("g_ple", inp["g_ple"][0].reshape(8, 128).T)
    cw = inp["conv_w"][0].reshape(3, 44, 128)
    put("conv_w", cw.transpose(2, 0, 1).reshape(128, 132))
    put("conv_b", inp["conv_b"][0].reshape(44, 128).T)
    put("g_q", inp["g_q"][0].reshape(96, 1))
    put("g_k", inp["g_k"][0].reshape(96, 1))
    put("d_skip", inp["d_skip"][0].reshape(4, 128).T)
    put("vis", np.tile((np.arange(4) <= j).astype(np.float32)[None], (128, 1)))
    put("full", np.tile((np.arange(4) < j).astype(np.float32)[None], (128, 1)))
    put("ident", np.eye(128, dtype=np.float32))
    put("hsel", np.tile((np.arange(4) == j - 1).astype(np.float32)[None], (128, 1)))
    m["small"] = sm
    m["w_in"] = inp["w_in"][0]
    m["w_glu"] = inp["w_glu"][0]
    m["w_uq"] = inp["w_uq"][0].reshape(Q_LORA, 768)
    m["w_oa"], m["w_os"], m["w_out"] = inp["w_oa"][0], inp["w_os"][0], inp["w_out"][0]
    m["w_up"], m["w_down"] = inp["w_up"][0], inp["w_down"][0]
    m["w_pg"], m["w_pp"] = inp["w_ple_gate"][0], inp["w_ple_proj"][0]
    pp_ = inp["p_prompt"][0, b, NPT * j:NPT * (j + 1)]
    ps_ = inp["p_sample"][0, SEQ_PER_CORE * c:SEQ_PER_CORE * (c + 1)].reshape(NST, PLE)
    m["pT"] = np.ascontiguousarray(np.concatenate([pp_, ps_], 0).T)
    sc = inp["state_conv"][0, SEQ_PER_CORE * c:SEQ_PER_CORE * (c + 1)]
    m["scT"] = np.ascontiguousarray(sc.reshape(16, 2, 44, 128).transpose(3, 2, 0, 1))
    m["w_uk"] = inp["w_uk"][0].reshape(KV_LORA, 512)
    m["w_uv"] = inp["w_uv"][0].reshape(KV_LORA, 512)
    tri = (np.arange(128)[:, None] <= np.arange(128)[None, :]).astype(np.float32)
    md = np.zeros((128, 4, 128), np.float32)
    for r in range(4):
        vis, full = float(r <= j), float(r < j)
        md[:, r, :] = full + (vis - full) * tri
    m["maskd"] = md
    m["ptab"] = np.ascontiguousarray(inp["page_table"][SEQ_PER_CORE * c:SEQ_PER_CORE * (c + 1)].reshape(1, -1).astype(np.int32))
    m["w_ukT"] = np.ascontiguousarray(inp["w_uk"][0].transpose(2, 1, 0).reshape(64, 8 * KV_LORA))
    cs_p, sn_p = _rope_tables(np.arange(PAST))
    rp = np.stack([cs_p.reshape(NPAGES, 128, 16), sn_p.reshape(NPAGES, 128, 16)], 0)
    m["ropeP"] = np.ascontiguousarray(rp.transpose(2, 0, 1, 3))
    m["gk_rep"] = np.tile(inp["g_k"][0, 64:96][None], (128, 1)).astype(np.float32)
    hs_ = np.zeros((128, 4, 32), np.float32)
    for mm in range(4):
        for hh in range(2):
            hs_[64 * hh:64 * hh + 64, mm, 4 * (2 * mm + hh):4 * (2 * mm + hh) + 4] = 1.0
    m["hselm"] = hs_
    cm = np.zeros((32, 4), np.float32)
    for h_ in range(8):
        for t_ in range(4):
            cm[4 * h_ + t_, :t_ + 1] = 1.0
    m["cmask"] = cm
    rot = np.zeros((96, 96), np.float32)
    for i in range(16):
        rot[80 + i, 64 + i] = -1.0
        rot[64 + i, 80 + i] = 1.0
    m["rotm"] = rot
    m["ssm_s"], m["ssm_r"] = _ssm_packs(c, inp)
    return m


def _ssm_packs(c, inp):
    j = c % 4
    a_re, a_im, logdt = inp["a_re"][0], inp["a_im"][0], inp["log_dt"][0]
    b_re, b_im, c_re, c_im = inp["b_re"][0], inp["b_im"][0], inp["c_re"][0], inp["c_im"][0]

    def st(a):
        return a.reshape(16, 2, 64).transpose(1, 2, 0).reshape(128, 16)
    ss = np.zeros((128, SSL["_n"]), np.float32)

    def put(lay, arr_, name, arr):
        o, n = lay[name]
        arr_[:, o:o + n] = arr.reshape(128, n)
    put(SSL, ss, "a_re", st(a_re))
    put(SSL, ss, "a_im", st(a_im))
    put(SSL, ss, "logdt", st(np.repeat(logdt[:, None], 64, 1)))
    for nm, cc in (("c_re", c_re), ("c_im", c_im)):
        c4 = cc.reshape(16, 2, 16, 64)
        pad = np.zeros((2, 64, 16, 2, 16), np.float32)
        for g2 in range(2):
            pad[g2, :, :, g2, :] = c4[:, g2].transpose(2, 0, 1)
        put(SSL, ss, nm, pad)
    for nm, bb in (("b_re", b_re), ("b_im", b_im)):
        b4 = bb.reshape(16, 2, 64, 16)
        pad = np.zeros((2, 64, 16, 2, 16), np.float32)
        for g2 in range(2):
            pad[g2, :, :, g2, :] = b4[:, g2].transpose(1, 0, 2)
        put(SSL, ss, nm, pad)
    for nm, key in (("h0_re", "state_ssm_re"), ("h0_im", "state_ssm_im")):
        h = inp[key][0, SEQ_PER_CORE * c:SEQ_PER_CORE * (c + 1)]
        h4 = h.reshape(16, 16, 2, 64)
        put(SSL, ss, nm, h4.transpose(2, 3, 1, 0))
    m = np.zeros((128, 12), np.float32)
    for i in range(4):
        n = j - 1 - i
        if 0 <= n <= 2:
            m[:, 3 * i + n] = 1.0
    put(SSL, ss, "msk", m)
    blk = np.zeros((128, 4), np.float32)
    for k4 in range(4):
        blk[32 * k4:32 * k4 + 32, k4] = 1.0
    put(SSL, ss, "blk", blk)
    sr = np.zeros((128, SRL["_n"]), np.float32)

    def rowrep(a):
        a4 = a.reshape(4, 4, 2, 64)
        out = np.zeros((4, 2, 16, 4, 64), np.float32)
        out[:] = a4.transpose(1, 2, 0, 3)[:, :, None, :, :]
        return out
    put(SRL, sr, "a_re", rowrep(a_re))
    put(SRL, sr, "a_im", rowrep(a_im))
    put(SRL, sr, "logdt", rowrep(np.repeat(logdt[:, None], 64, 1)))
    for nm, bb in (("b_re", b_re), ("b_im", b_im)):
        b5 = bb.reshape(4, 4, 2, 64, 16)
        pad = np.zeros((4, 2, 16, 4, 2, 64), np.float32)
        for g2 in range(2):
            pad[:, g2, :, :, g2, :] = b5[:, :, g2].transpose(1, 3, 0, 2)
        put(SRL, sr, nm, pad)
    put(SRL, sr, "ident", np.eye(128, dtype=np.float32))
    return ss, sr


_CACHE = {}


def kernel(**inputs):
    inp = {k: np.asarray(v) for k, v in inputs.items()}
    if "nc" not in _CACHE:
        _CACHE["nc"] = build()
    nc, ins_, outs_, P = _CACHE["nc"]
    in_maps = []
    cache = None
    if "cache" in ins_:
        cache = np.concatenate([inp["cache_ckv"][0], inp["cache_kr"][0]], axis=-1).reshape(-1, KV_LORA + QK_ROPE)
    for c in range(8):
        m = _prep_core(c, inp)
        if cache is not None:
            m["cache"] = cache
        in_maps.append({k: np.ascontiguousarray(m[k]) for k in ins_})
    res = run_bass_kernel_spmd(nc, in_maps, core_ids=list(range(8)))
    R = res.results
    f32 = np.float32
    ckv_p = np.zeros((1, 2, 8192, KV_LORA), f32)
    kr_p = np.zeros((1, 2, 8192, QK_ROPE), f32)
    ckv_s = np.zeros((1, 128, 4, KV_LORA), f32)
    kr_s = np.zeros((1, 128, 4, QK_ROPE), f32)
    for c in range(8):
        b, j = c // 4, c % 4
        ck = R[c]["o_ckvT"].T
        kr = R[c]["o_krT"].T
        ckv_p[0, b, NPT * j:NPT * (j + 1)] = ck[:NPT]
        kr_p[0, b, NPT * j:NPT * (j + 1)] = kr[:NPT]
        ckv_s[0, 16 * c:16 * c + 16] = ck[NPT:].reshape(16, 4, KV_LORA)
        kr_s[0, 16 * c:16 * c + 16] = kr[NPT:].reshape(16, 4, QK_ROPE)
    yp = np.zeros((2, 8192, D), f32)
    ys = np.zeros((128, 4, D), f32)
    cv_p = np.zeros((1, 2, 2, 2 * D_FF), f32)
    cv_s = np.zeros((1, 128, 2, 2 * D_FF), f32)
    if "o_yT" in R[0]:
        for c in range(8):
            b, j = c // 4, c % 4
            y = R[c]["o_yT"].T
            yp[b, NPT * j:NPT * (j + 1)] = y[:NPT]
            ys[16 * c:16 * c + 16] = y[NPT:].reshape(16, 4, D)
            cs_ = R[c]["o_cvs"]
            cv_s[0, 16 * c:16 * c + 16] = cs_.transpose(2, 3, 1, 0).reshape(16, 2, 2 * D_FF)
            if j == 3:
                cv_p[0, b] = R[c]["o_cvp"].transpose(2, 1, 0).reshape(2, 2 * D_FF)
    z = lambda *s: np.zeros(s, f32)
    sre_p, sim_p, sre_s, sim_s = z(1, 2, 32, 64), z(1, 2, 32, 64), z(1, 128, 32, 64), z(1, 128, 32, 64)
    if "o_hp" in R[0]:
        for c in range(8):
            b, j = c // 4, c % 4
            hs_ = R[c]["o_hs"].reshape(2, 64, 2, 16, 16)
            hs_ = hs_.transpose(2, 4, 3, 0, 1).reshape(2, 16, 32, 64)
            sre_s[0, 16 * c:16 * c + 16] = hs_[0]
            sim_s[0, 16 * c:16 * c + 16] = hs_[1]
            if j == 3:
                hp_ = R[c]["o_hp"].reshape(2, 64, 2, 16).transpose(2, 3, 0, 1).reshape(2, 32, 64)
                sre_p[0, b] = hp_[0]
                sim_p[0, b] = hp_[1]
    return (yp, ys, ckv_p, kr_p, ckv_s, kr_s, sre_p, sim_p, sre_s, sim_s, cv_p, cv_s)
```

```python
import contextlib
import numpy as np
import concourse.bass as bass
import concourse.mybir as mybir
from concourse.bass_utils import run_bass_kernel_spmd

F32 = mybir.dt.float32
BF16 = mybir.dt.bfloat16
I32 = mybir.dt.int32
AF = mybir.ActivationFunctionType
ALU = mybir.AluOpType
AX = mybir.AxisListType

D = 1024
NPT = 2048
NST = 64
T = NPT + NST
KD = D // 128
N_HEADS = 8
QK_NOPE, QK_ROPE, QK_HEAD, V_HEAD = 64, 32, 96, 64
Q_LORA, KV_LORA = 384, 256
SSM_W, GROUP, N_GROUPS, STATE = 512, 16, 32, 64
D_FF = 2816
PLE = 256
EPS = 1e-6
OFF_CKV = Q_LORA
OFF_KR = OFF_CKV + KV_LORA
OFF_U = OFF_KR + QK_ROPE
OFF_GA = OFF_U + SSM_W
OFF_GS = OFF_GA + D
IN_COLS = OFF_GS + D
SCALE = QK_HEAD ** -0.5
PAST = 8192
PAGE = 128
NPAGES = 64
SEQ_PER_CORE = 16


class Buf:
    __slots__ = ("name", "last_w", "readers")

    def __init__(self, name, fence=()):
        self.name = name
        self.last_w = None
        self.readers = list(fence)


class Op:
    __slots__ = ("eng", "fn", "deps", "dma", "sem", "value", "marked", "idx")

    def __init__(self, eng, fn, dma):
        self.eng = eng
        self.fn = fn
        self.dma = dma
        self.deps = ()
        self.sem = None
        self.value = 0
        self.marked = False
        self.idx = 0


class Prog:
    ENGS = ("pe", "act", "dve", "pool", "sp")

    def __init__(self, nc):
        self.nc = nc
        self.ops = {e: [] for e in self.ENGS}
        self.stack = contextlib.ExitStack()
        self.dma_sems = {}
        self.nbuf = 0
        self.fence = []
        self.live = []

    def sbuf(self, name, shape, dtype):
        return self.stack.enter_context(self.nc.sbuf_tensor(name, list(shape), dtype))

    def psum(self, name, shape, dtype=F32):
        return self.stack.enter_context(self.nc.psum_tensor(name, list(shape), dtype))

    def sem(self, name):
        return self.stack.enter_context(self.nc.semaphore(name))

    def buf(self, name=None):
        self.nbuf += 1
        b = Buf(name or f"b{self.nbuf}", self.fence)
        self.live.append(b)
        return b

    def new_phase(self):
        f = []
        for b in self.live:
            if b.last_w is not None:
                f.append(b.last_w)
            f.extend(b.readers)
        self.fence = list(dict.fromkeys(f))[-64:] if False else list(dict.fromkeys(f))
        self.live = []

    def add(self, eng, fn, reads=(), writes=(), dma=False, semkey=None, after=()):
        op = Op(eng, fn, dma)
        deps = set(after)
        for b in reads:
            if b.last_w is not None:
                deps.add(b.last_w)
        for b in writes:
            if b.last_w is not None:
                deps.add(b.last_w)
            deps.update(b.readers)
        op.deps = tuple(deps)
        for b in reads:
            b.readers.append(op)
        for b in writes:
            b.last_w = op
            b.readers = []
        op.idx = len(self.ops[eng])
        self.ops[eng].append(op)
        if dma:
            key = semkey if semkey is not None else id(op)
            if key not in self.dma_sems:
                self.dma_sems[key] = [self.sem(f"dq{len(self.dma_sems)}"), 0]
            ent = self.dma_sems[key]
            ent[1] += (1 if dma == "cc" else 16)
            op.sem = ent[0]
            op.value = ent[1]
            op.marked = True
        return op

    def finalize(self):
        nc = self.nc
        esem = {e: self.sem(f"eng_{e}") for e in self.ENGS}
        for e in self.ENGS:
            for op in self.ops[e]:
                for d in op.deps:
                    if d.dma:
                        continue
                    if d.eng != e:
                        d.marked = True
                    elif e != "pe" and (op.idx - d.idx) <= 2:
                        d.marked = True
        for e in self.ENGS:
            c = 0
            for op in self.ops[e]:
                if op.dma:
                    continue
                if op.marked:
                    c += 1
                    op.sem = esem[e]
                    op.value = c
        self.stats = {e: len(self.ops[e]) for e in self.ENGS}

        def emit(ename, eng):
            waited = {}
            for op in self.ops[ename]:
                need = {}
                for d in op.deps:
                    if not d.marked:
                        continue
                    if (not d.dma) and d.eng == ename and (ename == "pe" or (op.idx - d.idx) > 2):
                        continue
                    k = id(d.sem)
                    if waited.get(k, 0) >= d.value:
                        continue
                    if k not in need or need[k][1] < d.value:
                        need[k] = (d.sem, d.value)
                for k, (s, v) in need.items():
                    eng.wait_ge(s, v)
                    waited[k] = v
                ins = op.fn(eng)
                if op.marked and ins is not None:
                    if op.dma == "cc":
                        ins.then_inc(op.sem, 1)
                    elif op.dma:
                        ins.then_inc(op.sem, 16)
                    else:
                        ins.then_inc(op.sem, 1)

        with nc.Block() as block:
            @block.tensor
            def _(e):
                emit("pe", e)

            @block.scalar
            def _(e):
                emit("act", e)

            @block.vector
            def _(e):
                emit("dve", e)

            @block.gpsimd
            def _(e):
                emit("pool", e)

            @block.sync
            def _(e):
                emit("sp", e)
        self.stack.close()


class Arena:
    def __init__(self, P, name, nbytes):
        self.t = P.sbuf(name, [128, nbytes // 4], F32)
        self.off = 0
        self.cap = nbytes
        self.peak = 0

    def alloc(self, shape, dtype=F32):
        esz = 2 if dtype == BF16 else 4
        n = int(np.prod(shape[1:]))
        nb = (n * esz + 31) // 32 * 32
        assert self.off + nb <= self.cap, ("arena overflow", self.off, nb, self.cap)
        v = self.t[0:shape[0], self.off // 4:(self.off + nb) // 4]
        self.off += nb
        self.peak = max(self.peak, self.off)
        if dtype != F32:
            v = v.bitcast(dtype)
        v = v[:, 0:n]
        if len(shape) > 2:
            names = "abcde"[:len(shape) - 1]
            pat = "p (" + " ".join(names) + ") -> p " + " ".join(names)
            v = v.rearrange(pat, **{c: int(d) for c, d in zip(names[1:], shape[2:])})
        return v

    def mark(self):
        return self.off

    def release(self, m):
        self.off = m


def _small_layout():
    lay = {}
    off = 0

    def put(name, n):
        nonlocal off
        lay[name] = (off, n)
        off += n
    put("g_mix", 8)
    put("g_cq", 3)
    put("g_ckv", 2)
    put("g_ffn", 8)
    put("g_ple", 8)
    put("conv_w", 3 * 44)
    put("conv_b", 44)
    put("g_q", 1)
    put("g_k", 1)
    put("d_skip", 4)
    put("vis", 4)
    put("full", 4)
    put("ident", 128)
    put("hsel", 4)
    lay["_n"] = off
    return lay


SL = _small_layout()


def _token_chunks():
    return [(0, 510), (510, 512), (1022, 512), (1534, 290), (1824, 288)]


def _pack_layout(items):
    lay, off = {}, 0
    for name, n in items:
        lay[name] = (off, n)
        off += n
    lay["_n"] = off
    return lay


SSL = _pack_layout([("a_re", 16), ("a_im", 16), ("logdt", 16), ("c_re", 512), ("c_im", 512), ("b_re", 512),
                    ("b_im", 512), ("h0_re", 256), ("h0_im", 256), ("msk", 12), ("blk", 4)])
SRL = _pack_layout([("a_re", 256), ("a_im", 256), ("logdt", 256), ("b_re", 512), ("b_im", 512), ("ident", 128)])
PWS = [1, 2, 3, 4, 8, 12, 16]
PWR = [1, 2, 3]
TWO_PI = 6.283185307179586


def build_ssm(P, AR, nc, din, dout, dint, stores, next_bank, uT, b_uT, smc, b_sm, sm, w_glu_b, b_w_glu_b, chunks):
    LOOP_ENG = "pool"
    ss_d = din("ssm_s", [128, SSL["_n"]])
    sr_d = din("ssm_r", [128, SRL["_n"]])
    o_hp = dout("o_hp", [128, 2, 16])
    o_hs = dout("o_hs", [128, 2, 16, 16])
    cc_e_in = dint("cc_e_in", [128, 32], F32)
    cc_e_out = dint("cc_e_out", [512, 32], F32)

    AR.release(0)
    P.new_phase()
    sst = AR.alloc([128, SSL["_n"]], F32)
    Wc = AR.alloc([128, 16, 4, 2, 32], BF16)
    Ktab = AR.alloc([128, 4, 4, 128], BF16)
    A4w = AR.alloc([128, 4, 4, 2, 128], BF16)
    nlim = AR.alloc([128, len(PWS), 16], F32)
    LS_pre = True
    b_sst, b_srt = P.buf("sst"), P.buf("srt")
    P.add("sp", lambda e: e.dma_start(out=sst[:], in_=ss_d), writes=[b_sst], dma=True, semkey="sst")

    def S(name, a=None):
        o, n = SSL[name]
        v = sst[:, o:o + n]
        return v if a is None else v.rearrange("p (a b) -> p a b", a=a)

    def R(name, a=None):
        o, n = SRL[name]
        v = srt[:, o:o + n]
        return v if a is None else v.rearrange("p (a b) -> p a b", a=a)

    def tt(eng, out, a, b, op, rd, wr):
        return P.add(eng, lambda e: e.tensor_tensor(out=out, in0=a, in1=b, op=op), reads=rd, writes=wr)

    def tss(eng, out, a, scalar, op, rd, wr):
        return P.add(eng, lambda e: e.tensor_single_scalar(out=out, in_=a, scalar=scalar, op=op), reads=rd, writes=wr)

    def stt(eng, out, a, scalar, b, op0, op1, rd, wr):
        return P.add("dve", lambda e: e.scalar_tensor_tensor(out=out, in0=a, scalar=scalar, in1=b, op0=op0, op1=op1),
                     reads=rd, writes=wr)

    def act(out, in_, func, rd, wr, scale=1.0, bias=0.0):
        return P.add("act", lambda e: e.activation(out=out, in_=in_, func=func, scale=scale, bias=bias), reads=rd, writes=wr)

    def cp(eng, out, in_, rd, wr):
        return P.add(eng, lambda e: e.tensor_copy(out=out, in_=in_), reads=rd, writes=wr)

    MUL, ADD, SUB = ALU.mult, ALU.add, ALU.subtract

    def lam_pow(pfx, a_re, a_im, logdt, Fd, powers, b_src, eng):
        npw = len(powers)
        bt = P.buf(pfx + "_t")
        mk = lambda nm, sh, dt_=F32: AR.alloc(sh, dt_)
        dt = mk("dt", [128, Fd]); dre = mk("dre", [128, Fd]); dim = mk("dim", [128, Fd])
        ang = mk("ang", [128, npw, Fd]); angc = mk("angc", [128, npw, Fd]); mag = mk("mag", [128, npw, Fd])
        ki = mk("ki", [128, npw, Fd], I32); kf = mk("kf", [128, npw, Fd])
        sn = mk("sn", [128, npw, Fd]); cs = mk("cs", [128, npw, Fd])
        lre = mk("lre", [128, npw, Fd]); lim = mk("lim", [128, npw, Fd])
        act(dt[:], logdt, AF.Exp, [b_src], [bt])
        tt(eng, dre[:], dt[:], a_re, MUL, [bt, b_src], [bt])
        tt(eng, dim[:], dt[:], a_im, MUL, [bt, b_src], [bt])
        for i, n in enumerate(powers):
            tss(eng, ang[:, i, :], dim[:], n / TWO_PI, MUL, [bt], [bt])
            act(mag[:, i, :], dre[:], AF.Exp, [bt], [bt], scale=float(n))
        tss(eng, angc[:], ang[:], 0.25, ADD, [bt], [bt])
        for src, dst in ((ang, sn), (angc, cs)):
            cp("dve", ki[:], src[:], [bt], [bt])
            cp("dve", kf[:], ki[:], [bt], [bt])
            tt(eng, kf[:], src[:], kf[:], SUB, [bt], [bt])
            act(dst[:], kf[:], AF.Sin, [bt], [bt], scale=6.28318)
        tt(eng, lre[:], mag[:], cs[:], MUL, [bt], [bt])
        tt(eng, lim[:], mag[:], sn[:], MUL, [bt], [bt])
        den = mk("den", [128, Fd]); t1 = mk("t1", [128, Fd]); t2 = mk("t2", [128, Fd]); nr = mk("nr", [128, Fd])
        fre = mk("fre", [128, Fd]); fim = mk("fim", [128, Fd])
        tt(eng, den[:], a_re, a_re, MUL, [b_src], [bt])
        tt(eng, t1[:], a_im, a_im, MUL, [b_src], [bt])
        tt(eng, den[:], den[:], t1[:], ADD, [bt], [bt])
        P.add("dve", lambda e: e.reciprocal(out=den[:], in_=den[:]), reads=[bt], writes=[bt])
        tss(eng, nr[:], lre[:, 0, :], -1.0, ADD, [bt], [bt])
        tt(eng, t1[:], nr[:], a_re, MUL, [bt, b_src], [bt])
        tt(eng, t2[:], lim[:, 0, :], a_im, MUL, [bt, b_src], [bt])
        tt(eng, t1[:], t1[:], t2[:], ADD, [bt], [bt])
        tt(eng, fre[:], t1[:], den[:], MUL, [bt], [bt])
        tt(eng, t1[:], lim[:, 0, :], a_re, MUL, [bt, b_src], [bt])
        tt(eng, t2[:], nr[:], a_im, MUL, [bt, b_src], [bt])
        tt(eng, t1[:], t1[:], t2[:], SUB, [bt], [bt])
        tt(eng, fim[:], t1[:], den[:], MUL, [bt], [bt])
        return dict(lre=lre, lim=lim, fre=fre, fim=fim, b=bt)

    def cmul(eng, ore, oim, are, aim, bre, bim, t1, t2, rd, wr):
        tt(eng, t1, are, bre, MUL, rd, wr)
        tt(eng, t2, aim, bim, MUL, rd, wr)
        tt(eng, ore, t1, t2, SUB, rd, wr)
        tt(eng, t1, are, bim, MUL, rd, wr)
        tt(eng, t2, aim, bre, MUL, rd, wr)
        tt(eng, oim, t1, t2, ADD, rd, wr)

    ENG = "pool"
    LS = lam_pow("ls", S("a_re"), S("a_im"), S("logdt"), 16, PWS, b_sst, ENG)
    mB0 = AR.mark()
    srt = AR.alloc([128, SRL["_n"]], F32)
    P.add("sp", lambda e: e.dma_start(out=srt[:], in_=sr_d), writes=[b_srt], dma=True, semkey="srt")
    LR = lam_pow("lr", R("a_re"), R("a_im"), R("logdt"), 256, PWR, b_srt, ENG)
    bS, bR = LS["b"], LR["b"]
    pi = {n: i for i, n in enumerate(PWS)}

    def bc(ap2, n):
        return ap2.unsqueeze(2).broadcast_to([128, 16, n])

    bbs_re = AR.alloc([128, 16, 32], F32); bbs_im = AR.alloc([128, 16, 32], F32)
    x_re = AR.alloc([128, 16, 32], F32); x_im = AR.alloc([128, 16, 32], F32)
    u1 = AR.alloc([128, 16, 32], F32); u2 = AR.alloc([128, 16, 32], F32)
    negc_im = AR.alloc([128, 16, 32], F32)
    cmul(ENG, bbs_re[:], bbs_im[:], bc(LS["fre"][:], 32), bc(LS["fim"][:], 32), S("b_re", 16), S("b_im", 16),
         u1[:], u2[:], [bS, b_sst], [bS])
    tss(ENG, negc_im[:], S("c_im", 16), -1.0, MUL, [b_sst], [bS])

    b_Wc = P.buf("Wc")
    Wc_v = Wc[:].rearrange("p (kk k4) s c n -> p kk k4 s c n", k4=4)
    u1_v = u1[:].rearrange("p (kk k4) n -> p kk k4 n", k4=4)
    u2_v = u2[:].rearrange("p (kk k4) n -> p kk k4 n", k4=4)
    for s in range(4):
        lr_, li_ = LS["lre"][:, pi[s + 1], :], LS["lim"][:, pi[s + 1], :]
        tt(ENG, u1[:], S("c_re", 16), bc(lr_, 32), MUL, [b_sst, bS], [bS])
        tt(ENG, u2[:], S("c_im", 16), bc(li_, 32), MUL, [b_sst, bS], [bS])
        for k4 in range(4):
            tt(ENG, Wc_v[:, :, k4, s, 0, :], u1_v[:, :, k4, :], u2_v[:, :, k4, :], SUB, [bS], [bS, b_Wc])
        tt(ENG, u1[:], S("c_re", 16), bc(li_, 32), MUL, [b_sst, bS], [bS])
        tt(ENG, u2[:], S("c_im", 16), bc(lr_, 32), MUL, [b_sst, bS], [bS])
        for k4 in range(4):
            stt(ENG, Wc_v[:, :, k4, s, 1, :], u1_v[:, :, k4, :], -1.0, u2_v[:, :, k4, :], MUL, SUB,
                [bS], [bS, b_Wc])

    b_Kt = P.buf("Ktab")
    Kc = AR.alloc([128, 4, 32], F32)
    Kf = AR.alloc([128, 4, 128], F32)
    b_Kc, b_Kf = P.buf("Kc"), P.buf("Kf")
    xs_re = AR.alloc([128, 4, 16, 32], F32); xs_im = AR.alloc([128, 4, 16, 32], F32)
    b_xs = P.buf("xs")
    cp(ENG, xs_re[:, 0], bbs_re[:], [bS], [b_xs])
    cp(ENG, xs_im[:, 0], bbs_im[:], [bS], [b_xs])
    for tau in range(1, 4):
        cmul(ENG, xs_re[:, tau], xs_im[:, tau], bc(LS["lre"][:, pi[tau], :], 32), bc(LS["lim"][:, pi[tau], :], 32),
             bbs_re[:], bbs_im[:], u1[:], u2[:], [bS], [bS, b_xs])
    for kk in range(4):
        ps, b_ps = next_bank()
        for k4 in range(4):
            k = 4 * kk + k4
            for tau in range(4):
                o = ps[32 * k4:32 * k4 + 32, tau * 32:tau * 32 + 32]
                P.add("pe", lambda e, o=o, k=k, tau=tau, k4=k4: e.matmul(
                    o, lhsT=xs_re[:, tau, k, :], rhs=S("c_re", 16)[:, k, :], start=True, stop=False,
                    tile_position=(0, 32 * k4)), reads=[b_xs, b_sst], writes=[b_ps])
                P.add("pe", lambda e, o=o, k=k, tau=tau, k4=k4: e.matmul(
                    o, lhsT=xs_im[:, tau, k, :], rhs=negc_im[:, k, :], start=False, stop=True,
                    tile_position=(0, 32 * k4)), reads=[b_xs, bS], writes=[b_ps])
        cp("dve", Kc[:], ps[:, 0:128].rearrange("p (t n) -> p t n", t=4), [b_ps], [b_Kc])
        for k4 in range(4):
            tss("dve", Kf[:, :, 32 * k4:32 * k4 + 32], Kc[:], S("blk")[:, k4:k4 + 1], MUL, [b_Kc, b_sst], [b_Kf])
        stt("dve", Kf[:, 0, :], R("ident"), smc("d_skip", kk), Kf[:, 0, :], MUL, ADD, [b_srt, b_sm, b_Kf], [b_Kf])
        cp("dve", Ktab[:, kk], Kf[:], [b_Kf], [b_Kt])

    b_A4 = P.buf("A4w")
    bbr_re = AR.alloc([128, 4, 2, 64], F32); bbr_im = AR.alloc([128, 4, 2, 64], F32)
    r1 = AR.alloc([128, 4, 2, 64], F32); r2 = AR.alloc([128, 4, 2, 64], F32)
    bcr = lambda t: t.rearrange("p (kk q) -> p kk q", kk=4).unsqueeze(2).broadcast_to([128, 4, 2, 64])
    v4 = lambda t: t.rearrange("p (kk g q) -> p kk g q", kk=4, g=2)
    cmul(ENG, bbr_re[:], bbr_im[:], bcr(LR["fre"][:]), bcr(LR["fim"][:]), v4(R("b_re")), v4(R("b_im")), r1[:], r2[:],
         [bR, b_srt], [bR])
    a4v = lambda s_, c_: A4w[:, :, s_, c_, :].rearrange("p kk (g q) -> p kk g q", g=2)
    cp(ENG, a4v(3, 0), bbr_re[:], [bR], [b_A4])
    cp(ENG, a4v(3, 1), bbr_im[:], [bR], [b_A4])
    pir = {n: i for i, n in enumerate(PWR)}
    for s in range(3):
        n = 3 - s
        cmul(ENG, a4v(s, 0), a4v(s, 1), bcr(LR["lre"][:, pir[n], :]), bcr(LR["lim"][:, pir[n], :]),
             bbr_re[:], bbr_im[:], r1[:], r2[:], [bR], [bR, b_A4])

    tss(ENG, nlim[:], LS["lim"][:], -1.0, MUL, [bS], [bS])
    Lre = lambda n, k: LS["lre"][:, pi[n], k:k + 1]
    Lim = lambda n, k: LS["lim"][:, pi[n], k:k + 1]
    nLim = lambda n, k: nlim[:, pi[n], k:k + 1]

    AR.release(mB0)
    P.new_phase()
    uP = [uT[:, kk, 0:NPT].rearrange("p (c e) -> p e c", e=16) for kk in range(4)]
    uS = [uT[:, kk, NPT:T].rearrange("p (q s) -> p s q", s=4) for kk in range(4)]

    S16 = [AR.alloc([128, 16, 128], F32) for c in range(2)]
    b_S16 = P.buf("S16")
    H16 = [AR.alloc([128, 16, 129], F32) for c in range(2)]
    b_H = [P.buf("H16re"), P.buf("H16im")]
    pr = [AR.alloc([128, 2, 128], F32) for i in range(2)]
    b_pr = [P.buf("pr0"), P.buf("pr1")]

    def s4_matmuls(k, n_c, usrc, nj):
        kk, k4 = divmod(k, 4)
        out = []
        for comp in range(2):
            ps, b_ps = next_bank()
            for j in range(nj):
                for s in range(4):
                    rhs = usrc[kk][32 * k4:32 * k4 + 32, 4 * j + s, :]
                    P.add("pe", lambda e, ps=ps, j=j, s=s, rhs=rhs, comp=comp, kk=kk, k4=k4: e.matmul(
                        ps[:, j * n_c:(j + 1) * n_c], lhsT=A4w[32 * k4:32 * k4 + 32, kk, s, comp, :], rhs=rhs,
                        start=(s == 0), stop=(s == 3), tile_position=(32 * k4, 0)),
                        reads=[b_A4, b_uT], writes=[b_ps])
            out.append((ps, b_ps))
        return out

    def prefix_step(k, j, src, b_src, dst, b_dst, Sre, Sim, b_sre, b_sim, n_c, o_re=None, o_im=None, b_o=()):
        ore = dst[:, 0, 0:n_c] if o_re is None else o_re
        oim = dst[:, 1, 0:n_c] if o_im is None else o_im
        wr = [b_dst] + list(b_o)
        stt("dve", dst[:, 0, 0:n_c], src[:, 0, 0:n_c], Lre(4, k), Sre, MUL, ADD, [b_src, bS, b_sre], [b_dst])
        stt("dve", ore, src[:, 1, 0:n_c], nLim(4, k), dst[:, 0, 0:n_c], MUL, ADD, [b_src, bS, b_dst], wr)
        stt("dve", dst[:, 1, 0:n_c], src[:, 0, 0:n_c], Lim(4, k), Sim, MUL, ADD, [b_src, bS, b_sim], [b_dst])
        stt("dve", oim, src[:, 1, 0:n_c], Lre(4, k), dst[:, 1, 0:n_c], MUL, ADD, [b_src, bS, b_dst], wr)

    for k in range(16):
        (pre, b_pre), (pim, b_pim) = s4_matmuls(k, 128, uP, 4)
        act(pr[0][:, 0, :], pre[:, 0:128], AF.Copy, [b_pre], [b_pr[0]])
        act(pr[0][:, 1, :], pim[:, 0:128], AF.Copy, [b_pim], [b_pr[0]])
        cur = 0
        for j in range(1, 4):
            last = (j == 3)
            prefix_step(k, j, pr[cur], b_pr[cur], pr[1 - cur], b_pr[1 - cur], pre[:, j * 128:(j + 1) * 128],
                        pim[:, j * 128:(j + 1) * 128], b_pre, b_pim, 128,
                        o_re=S16[0][:, k, :] if last else None, o_im=S16[1][:, k, :] if last else None,
                        b_o=[b_S16] if last else ())
            cur = 1 - cur

    lt = [AR.alloc([128, 16], F32) for i in range(6)]
    b_lt = [P.buf(f"lt{i}") for i in range(6)]
    L16re, L16im = LS["lre"][:, pi[16], :], LS["lim"][:, pi[16], :]

    def run_loop():
        for c in range(128):
            hre, him = H16[0][:, :, c], H16[1][:, :, c]
            tt(LOOP_ENG, lt[0][:], L16re, hre, MUL, [bS, b_H[0]], [b_lt[0]])
            tt(LOOP_ENG, lt[1][:], L16im, him, MUL, [bS, b_H[1]], [b_lt[1]])
            tt(LOOP_ENG, lt[3][:], L16re, him, MUL, [bS, b_H[1]], [b_lt[3]])
            tt(LOOP_ENG, lt[4][:], L16im, hre, MUL, [bS, b_H[0]], [b_lt[4]])
            tt(LOOP_ENG, lt[2][:], lt[0][:], lt[1][:], SUB, [b_lt[0], b_lt[1]], [b_lt[2]])
            tt(LOOP_ENG, lt[5][:], lt[3][:], lt[4][:], ADD, [b_lt[3], b_lt[4]], [b_lt[5]])
            tt(LOOP_ENG, H16[0][:, :, c + 1], lt[2][:], S16[0][:, :, c], ADD, [b_lt[2], b_S16], [b_H[0]])
            tt(LOOP_ENG, H16[1][:, :, c + 1], lt[5][:], S16[1][:, :, c], ADD, [b_lt[5], b_S16], [b_H[1]])
            yield

    P.add(LOOP_ENG, lambda e: e.memset(H16[0][:, :, 0], 0.0), writes=[b_H[0]])
    P.add(LOOP_ENG, lambda e: e.memset(H16[1][:, :, 0], 0.0), writes=[b_H[1]])
    for _ in run_loop():
        pass
    Eloc = AR.alloc([128, 2, 16], F32)
    b_E = P.buf("Eloc")
    cp(LOOP_ENG, Eloc[:, 0, :], H16[0][:, :, 128], [b_H[0]], [b_E])
    cp(LOOP_ENG, Eloc[:, 1, :], H16[1][:, :, 128], [b_H[1]], [b_E])
    b_cci, b_cco = P.buf("cc_e_in"), P.buf("cc_e_out")
    P.add("sp", lambda e: e.dma_start(out=cc_e_in, in_=Eloc[:].rearrange("p a b -> p (a b)")), reads=[b_E], writes=[b_cci],
          dma=True, semkey="cce1")
    P.add("pool", lambda e: e.collective_compute("AllGather", ALU.bypass, replica_groups=[[0, 1, 2, 3], [4, 5, 6, 7]],
                                                 ins=[cc_e_in.opt()], outs=[cc_e_out.opt()]),
          reads=[b_cci], writes=[b_cco], dma="cc", semkey="cce2")
    Eg = AR.alloc([128, 4, 2, 16], F32)
    b_Eg = P.buf("Eg")
    P.add("sp", lambda e: e.dma_start(out=Eg[:].rearrange("p r a b -> p r (a b)"),
                                      in_=cc_e_out.rearrange("(r p) f -> p r f", p=128)),
          reads=[b_cco], writes=[b_Eg], dma=True, semkey="cce3")
    Lp = AR.alloc([128, 3, 2, 16], F32)
    b_Lp = P.buf("Lp")
    sq_a = AR.alloc([128, 2, 16], F32); sq_b = AR.alloc([128, 2, 16], F32)
    q1 = AR.alloc([128, 16], F32); q2 = AR.alloc([128, 16], F32)
    b_q = P.buf("sqtmp")
    CE = "dve"
    cp(CE, sq_a[:, 0, :], L16re, [bS], [b_q])
    cp(CE, sq_a[:, 1, :], L16im, [bS], [b_q])
    src_, dst_ = sq_a, sq_b
    for it in range(7):
        o = Lp[:, 0] if it == 6 else dst_
        tt(CE, q1[:], src_[:, 0, :], src_[:, 0, :], MUL, [b_q], [b_q])
        tt(CE, q2[:], src_[:, 1, :], src_[:, 1, :], MUL, [b_q], [b_q])
        tt(CE, o[:, 0, :], q1[:], q2[:], SUB, [b_q], [b_q, b_Lp])
        stt(CE, o[:, 1, :], src_[:, 0, :], 2.0, src_[:, 1, :], MUL, MUL, [b_q], [b_q, b_Lp])
        src_, dst_ = dst_, src_
    cmul(CE, Lp[:, 1, 0, :], Lp[:, 1, 1, :], Lp[:, 0, 0, :], Lp[:, 0, 1, :], Lp[:, 0, 0, :], Lp[:, 0, 1, :], q1[:], q2[:],
         [b_Lp, b_q], [b_Lp, b_q])
    cmul(CE, Lp[:, 2, 0, :], Lp[:, 2, 1, :], Lp[:, 1, 0, :], Lp[:, 1, 1, :], Lp[:, 0, 0, :], Lp[:, 0, 1, :], q1[:], q2[:],
         [b_Lp, b_q], [b_Lp, b_q])
    hin = AR.alloc([128, 2, 16], F32)
    cf = AR.alloc([128, 2, 16], F32)
    b_hin, b_cf = P.buf("hin"), P.buf("cf")
    P.add(CE, lambda e: e.memset(hin[:], 0.0), writes=[b_hin])
    msk = S("msk")
    for i in range(4):
        m0, m1, m2 = (msk[:, 3 * i + n:3 * i + n + 1] for n in range(3))
        tss(CE, cf[:, 0, :], Lp[:, 0, 0, :], m1, MUL, [b_Lp, b_sst], [b_cf])
        stt(CE, cf[:, 0, :], Lp[:, 1, 0, :], m2, cf[:, 0, :], MUL, ADD, [b_Lp, b_sst, b_cf], [b_cf])
        tss(CE, cf[:, 0, :], cf[:, 0, :], m0, ADD, [b_cf, b_sst], [b_cf])
        tss(CE, cf[:, 1, :], Lp[:, 0, 1, :], m1, MUL, [b_Lp, b_sst], [b_cf])
        stt(CE, cf[:, 1, :], Lp[:, 1, 1, :], m2, cf[:, 1, :], MUL, ADD, [b_Lp, b_sst, b_cf], [b_cf])
        ere, eim = Eg[:, i, 0, :], Eg[:, i, 1, :]
        tt(CE, q1[:], cf[:, 0, :], ere, MUL, [b_cf, b_Eg], [b_q])
        tt(CE, hin[:, 0, :], hin[:, 0, :], q1[:], ADD, [b_q, b_hin], [b_hin])
        tt(CE, q1[:], cf[:, 1, :], eim, MUL, [b_cf, b_Eg], [b_q])
        tt(CE, hin[:, 0, :], hin[:, 0, :], q1[:], SUB, [b_q, b_hin], [b_hin])
        tt(CE, q1[:], cf[:, 0, :], eim, MUL, [b_cf, b_Eg], [b_q])
        tt(CE, hin[:, 1, :], hin[:, 1, :], q1[:], ADD, [b_q, b_hin], [b_hin])
        tt(CE, q1[:], cf[:, 1, :], ere, MUL, [b_cf, b_Eg], [b_q])
        tt(CE, hin[:, 1, :], hin[:, 1, :], q1[:], ADD, [b_q, b_hin], [b_hin])
    cp(LOOP_ENG, H16[0][:, :, 0], hin[:, 0, :], [b_hin], [b_H[0]])
    cp(LOOP_ENG, H16[1][:, :, 0], hin[:, 1, :], [b_hin], [b_H[1]])
    for _ in run_loop():
        pass
    hp = AR.alloc([128, 2, 16], F32)
    b_hp = P.buf("hp")
    cp(LOOP_ENG, hp[:, 0, :], H16[0][:, :, 128], [b_H[0]], [b_hp])
    cp(LOOP_ENG, hp[:, 1, :], H16[1][:, :, 128], [b_H[1]], [b_hp])
    stores.append(P.add("sp", lambda e: e.dma_start(out=o_hp, in_=hp[:]), reads=[b_hp], dma=True, semkey="st_hp"))

    yT = AR.alloc([128, 4, T], BF16)
    b_yT = P.buf("yT")
    H4 = AR.alloc([128, 4, 4, 2, 128], BF16)
    b_H4 = P.buf("H4")
    h0b = AR.alloc([128, 2, 16, 16], BF16)
    b_h0b = P.buf("h0b")
    cp("dve", h0b[:, 0], S("h0_re", 16), [b_sst], [b_h0b])
    cp("dve", h0b[:, 1], S("h0_im", 16), [b_sst], [b_h0b])
    hs = AR.alloc([128, 2, 16, 16], F32)
    b_hs = P.buf("hs")
    htmp = AR.alloc([128, 2, 128], F32)
    b_htmp = P.buf("htmp")
    yP = [yT[:, kk, 0:NPT].rearrange("p (c j s) -> p j s c", j=4, s=4) for kk in range(4)]
    yS = [yT[:, kk, NPT:T].rearrange("p (q s) -> p s q", s=4) for kk in range(4)]

    def out_stage(kk, j, n_c, Hsrc, b_hsrc, usrc, dst):
        ps, b_ps = next_bank()
        for s_lo in range(4):
            o = ps[:, s_lo * n_c:(s_lo + 1) * n_c]
            nmm = 8 + s_lo + 1
            idx = 0
            for k4 in range(4):
                k = 4 * kk + k4
                for comp in range(2):
                    o4 = ps[32 * k4:32 * k4 + 32, s_lo * n_c:(s_lo + 1) * n_c]
                    P.add("pe", lambda e, o4=o4, k=k, s_lo=s_lo, comp=comp, k4=k4, idx=idx, nmm=nmm: e.matmul(
                        o4, lhsT=Wc[:, k, s_lo, comp, :], rhs=Hsrc(k, k4, comp), start=(comp == 0), stop=False,
                        tile_position=(0, 32 * k4)),
                        reads=[b_Wc, b_hsrc], writes=[b_ps])
                    idx += 1
            for tau in range(s_lo + 1):
                rhs = usrc[kk][:, 4 * j + s_lo - tau, :]
                P.add("pe", lambda e, o=o, tau=tau, rhs=rhs, idx=idx, nmm=nmm, kk=kk: e.matmul(
                    o, lhsT=Ktab[:, kk, tau, :], rhs=rhs, start=False, stop=(idx == nmm - 1)),
                    reads=[b_Kt, b_uT], writes=[b_ps])
                idx += 1
        act(dst, ps[:, 0:4 * n_c].rearrange("p (s c) -> p s c", s=4), AF.Gelu_apprx_tanh, [b_ps], [b_yT])

    for kk in range(4):
        for k4 in range(4):
            k = 4 * kk + k4
            (pre, b_pre), (pim, b_pim) = s4_matmuls(k, 128, uP, 3)
            cp("dve", H4[:, k4, 0, 0, :], H16[0][:, k, 0:128], [b_H[0]], [b_H4])
            cp("dve", H4[:, k4, 0, 1, :], H16[1][:, k, 0:128], [b_H[1]], [b_H4])
            act(pr[0][:, 0, :], pre[:, 0:128], AF.Copy, [b_pre], [b_pr[0]])
            act(pr[0][:, 1, :], pim[:, 0:128], AF.Copy, [b_pim], [b_pr[0]])
            cur = 0
            for j in range(1, 4):
                n = 4 * j
                stt("dve", htmp[:, 0, :], H16[0][:, k, 0:128], Lre(n, k), pr[cur][:, 0, :], MUL, ADD,
                    [b_H[0], bS, b_pr[cur]], [b_htmp])
                stt("dve", H4[:, k4, j, 0, :], H16[1][:, k, 0:128], nLim(n, k), htmp[:, 0, :], MUL, ADD,
                    [b_H[1], bS, b_htmp], [b_H4])
                stt("dve", htmp[:, 1, :], H16[0][:, k, 0:128], Lim(n, k), pr[cur][:, 1, :], MUL, ADD,
                    [b_H[0], bS, b_pr[cur]], [b_htmp])
                stt("dve", H4[:, k4, j, 1, :], H16[1][:, k, 0:128], Lre(n, k), htmp[:, 1, :], MUL, ADD,
                    [b_H[1], bS, b_htmp], [b_H4])
                if j < 3:
                    prefix_step(k, j, pr[cur], b_pr[cur], pr[1 - cur], b_pr[1 - cur], pre[:, j * 128:(j + 1) * 128],
                                pim[:, j * 128:(j + 1) * 128], b_pre, b_pim, 128)
                    cur = 1 - cur
            (sre, b_sre), (sim, b_sim) = s4_matmuls(k, 16, uS, 1)
            h0r, h0i = S("h0_re", 16)[:, k, :], S("h0_im", 16)[:, k, :]
            stt("dve", hs[:, 0, k, :], h0r, Lre(4, k), sre[:, 0:16], MUL, ADD, [b_sst, bS, b_sre], [b_hs])
            stt("dve", hs[:, 0, k, :], h0i, nLim(4, k), hs[:, 0, k, :], MUL, ADD, [b_sst, bS, b_hs], [b_hs])
            stt("dve", hs[:, 1, k, :], h0r, Lim(4, k), sim[:, 0:16], MUL, ADD, [b_sst, bS, b_sim], [b_hs])
            stt("dve", hs[:, 1, k, :], h0i, Lre(4, k), hs[:, 1, k, :], MUL, ADD, [b_sst, bS, b_hs], [b_hs])
        for j in range(4):
            out_stage(kk, j, 128, lambda k, k4, comp, j=j: H4[:, k4, j, comp, :], b_H4, uP, yP[kk][:, j])
        out_stage(kk, 0, 16, lambda k, k4, comp: h0b[:, comp, k, :], b_h0b, uS, yS[kk])
    stores.append(P.add("sp", lambda e: e.dma_start(out=o_hs, in_=hs[:]), reads=[b_hs], dma=True, semkey="st_hs"))

    wg = AR.alloc([128, 4, 1024], BF16)
    b_wg = P.buf("wg")
    P.add("sp", lambda e: e.dma_start(out=wg[:], in_=w_glu_b.rearrange("(k p) m -> p k m", p=128)), reads=[b_w_glu_b],
          writes=[b_wg], dma=True, semkey="wg")
    ysT, b_ys = uT, b_uT
    sg = AR.alloc([128, 512], F32)
    b_sg = P.buf("sg")
    for (n0, n) in chunks:
        for m in range(4):
            pv, b_pv = next_bank()
            pg, b_pg = next_bank()
            for k in range(4):
                P.add("pe", lambda e, pv=pv, k=k, m=m, n0=n0, n=n: e.matmul(
                    pv[:, 0:n], lhsT=wg[:, k, 128 * m:128 * m + 128], rhs=yT[:, k, n0:n0 + n], start=(k == 0), stop=(k == 3)),
                    reads=[b_wg, b_yT], writes=[b_pv])
            for k in range(4):
                P.add("pe", lambda e, pg=pg, k=k, m=m, n0=n0, n=n: e.matmul(
                    pg[:, 0:n], lhsT=wg[:, k, 512 + 128 * m:512 + 128 * m + 128], rhs=yT[:, k, n0:n0 + n], start=(k == 0),
                    stop=(k == 3)), reads=[b_wg, b_yT], writes=[b_pg])
            act(sg[:, 0:n], pg[:, 0:n], AF.Sigmoid, [b_pg], [b_sg])
            tt("dve", ysT[:, m, n0:n0 + n], pv[:, 0:n], sg[:, 0:n], MUL, [b_pv, b_sg], [b_ys])
    return dict(ysT=ysT, b_ys=b_ys)


def build_attn(P, AR, nc, din, dout, dint, stores, banks, bank_bufs, cast_w, cqn, b_cqn, ckvn, b_ckvn, krK, b_krK,
               sm, b_sm, smc, ones_bf, b_ones, rope_c, rope_s, attT, b_attT, chunks, kS, b_kS):
    MUL, ADD = ALU.mult, ALU.add
    w_uq = din("w_uq", [Q_LORA, N_HEADS * QK_HEAD])
    w_uk = din("w_uk", [KV_LORA, N_HEADS * QK_NOPE])
    w_uv = din("w_uv", [KV_LORA, N_HEADS * V_HEAD])
    maskd_d = din("maskd", [128, 4, 128])
    rotm_d = din("rotm", [96, 96])
    w_uq_b = dint("w_uq_b", [Q_LORA, N_HEADS * QK_HEAD], BF16)
    w_uk_b = dint("w_uk_b", [KV_LORA, N_HEADS * QK_NOPE], BF16)
    w_uv_b = dint("w_uv_b", [KV_LORA, N_HEADS * V_HEAD], BF16)
    b_wqb = cast_w(w_uq_b, w_uq, Q_LORA, 768, "c_wuq")
    b_wkb = cast_w(w_uk_b, w_uk, KV_LORA, 512, "c_wuk")
    b_wvb = cast_w(w_uv_b, w_uv, KV_LORA, 512, "c_wuv")
    K_own = [dint(f"K_own{h}", [QK_HEAD, NPT], BF16) for h in range(N_HEADS)]
    K_all = [dint(f"K_all{h}", [4 * QK_HEAD, NPT], BF16) for h in range(N_HEADS)]
    V_own = [dint(f"V_own{h}", [128, 16 * 65], BF16) for h in range(N_HEADS)]
    V_all = [dint(f"V_all{h}", [512, 16 * 65], BF16) for h in range(N_HEADS)]
    RG = [[0, 1, 2, 3], [4, 5, 6, 7]]

    AR.release(0)
    P.new_phase()
    qT = AR.alloc([96, 8, T], BF16)
    maskd = AR.alloc([128, 4, 128], BF16)
    sel65 = AR.alloc([65, 64], F32)
    mC1 = AR.mark()
    wq = AR.alloc([128, 3, 768], BF16)
    wk = AR.alloc([128, 2, 8, 96], BF16)
    wv = AR.alloc([128, 2, 512], BF16)
    rc = AR.alloc([96, T], F32)
    rs = AR.alloc([96, T], F32)
    rotm = AR.alloc([96, 96], F32)
    maskf = AR.alloc([128, 4, 128], F32)
    b_wq, b_wk, b_wv, b_rc, b_rs, b_rot, b_mk, b_mkf, b_sel, b_qT = [P.buf(n) for n in
        ("wq", "wk", "wv", "rc", "rs", "rotm", "maskd", "maskf", "sel65", "qT")]
    P.add("sp", lambda e: e.dma_start(out=wq[:], in_=w_uq_b.rearrange("(k p) m -> p k m", p=128)), reads=[b_wqb], writes=[b_wq],
          dma=True, semkey="wq")
    P.add("pool", lambda e: e.memset(wk[:], 0.0), writes=[b_wk])
    for k in range(2):
        P.add("sp", lambda e, k=k: e.dma_start(out=wk[:, k, :, 0:64],
                                               in_=w_uk_b[128 * k:128 * k + 128, :].rearrange("p (h d) -> p h d", h=8)),
              reads=[b_wkb], writes=[b_wk], dma=True, semkey="wk")
    P.add("sp", lambda e: e.dma_start(out=wv[:], in_=w_uv_b.rearrange("(k p) m -> p k m", p=128)), reads=[b_wvb], writes=[b_wv],
          dma=True, semkey="wv")
    P.add("sp", lambda e: e.dma_start(out=rc[:], in_=rope_c), writes=[b_rc], dma=True, semkey="rc")
    P.add("sp", lambda e: e.dma_start(out=rs[:], in_=rope_s), writes=[b_rs], dma=True, semkey="rs")
    P.add("sp", lambda e: e.dma_start(out=rotm[:], in_=rotm_d), writes=[b_rot], dma=True, semkey="rotm")
    P.add("sp", lambda e: e.dma_start(out=maskf[:], in_=maskd_d), writes=[b_mkf], dma=True, semkey="maskf")
    P.add("pool", lambda e: e.tensor_copy(out=maskd[:], in_=maskf[:]), reads=[b_mkf], writes=[b_mk])
    P.add("pool", lambda e: e.memset(sel65[:], 0.0), writes=[b_sel])
    P.add("pool", lambda e: e.memset(sel65[64:65, :], 1.0), writes=[b_sel])
    ident = sm[:, SL["ident"][0]:SL["ident"][0] + 128]

    rr = [0]

    def tbank():
        i = 2 + rr[0] % 6
        rr[0] += 1
        return banks[i], bank_bufs[i]

    raw = AR.alloc([96, 512], F32); sqh = AR.alloc([96, 512], BF16); lnv = AR.alloc([96, 512], F32)
    rstd = AR.alloc([96, 512], F32); qg = AR.alloc([96, 512], F32); t1 = AR.alloc([96, 512], F32); t2 = AR.alloc([96, 512], F32)
    kst = [AR.alloc([96, 512], BF16) for _ in range(2)]
    b_raw, b_sqh, b_lnv, b_rstd, b_qg, b_t1, b_t2 = [P.buf(n) for n in ("raw", "sqh", "lnv", "rstd", "qg", "t1", "t2")]
    b_kst = [P.buf("kst0"), P.buf("kst1")]

    def normrope(ps, b_ps, gname, out_ap, b_out, n, n0):
        P.add("act", lambda e: e.activation(out=raw[:, 0:n], in_=ps[0:96, 0:n], func=AF.Copy), reads=[b_ps], writes=[b_raw])
        P.add("pool", lambda e: e.tensor_tensor(out=sqh[:, 0:n], in0=raw[:, 0:n], in1=raw[:, 0:n], op=MUL), reads=[b_raw],
              writes=[b_sqh])
        p2, b_p2 = tbank()
        P.add("pe", lambda e: e.matmul(p2[0:96, 0:n], lhsT=ones_bf[0:96, 0:96], rhs=sqh[:, 0:n], start=True, stop=True),
              reads=[b_sqh, b_ones], writes=[b_p2])
        P.add("act", lambda e: e.activation(out=lnv[:, 0:n], in_=p2[0:96, 0:n], func=AF.Ln, scale=1.0 / QK_HEAD, bias=EPS),
              reads=[b_p2], writes=[b_lnv])
        P.add("act", lambda e: e.activation(out=rstd[:, 0:n], in_=lnv[:, 0:n], func=AF.Exp, scale=-0.5), reads=[b_lnv],
              writes=[b_rstd])
        P.add("dve", lambda e: e.scalar_tensor_tensor(out=qg[:, 0:n], in0=raw[:, 0:n], scalar=smc(gname)[0:96, :], in1=rstd[:, 0:n],
                                                      op0=MUL, op1=MUL), reads=[b_raw, b_rstd, b_sm], writes=[b_qg])
        p3, b_p3 = tbank()
        P.add("pe", lambda e: e.matmul(p3[0:96, 0:n], lhsT=rotm[:, :], rhs=qg[:, 0:n], start=True, stop=True),
              reads=[b_rot, b_qg], writes=[b_p3])
        P.add("dve", lambda e: e.tensor_tensor(out=t1[:, 0:n], in0=qg[:, 0:n], in1=rc[:, n0:n0 + n], op=MUL), reads=[b_qg, b_rc],
              writes=[b_t1])
        P.add("dve", lambda e: e.tensor_tensor(out=t2[:, 0:n], in0=p3[0:96, 0:n], in1=rs[:, n0:n0 + n], op=MUL),
              reads=[b_p3, b_rs], writes=[b_t2])
        P.add("pool", lambda e: e.tensor_tensor(out=out_ap, in0=t1[:, 0:n], in1=t2[:, 0:n], op=ADD), reads=[b_t1, b_t2],
              writes=[b_out])

    b_Kown = [P.buf(f"K_own{h}") for h in range(8)]
    b_Vown = [P.buf(f"V_own{h}") for h in range(8)]
    b_Kall = [P.buf(f"K_all{h}") for h in range(8)]
    b_Vall = [P.buf(f"V_all{h}") for h in range(8)]
    Vst = AR.alloc([128, 8, 16, 65], BF16)
    b_Vst = P.buf("Vst")
    P.add("pool", lambda e: e.memset(Vst[:, :, :, 64:65], 1.0), writes=[b_Vst])
    for blk in range(16):
        ps, b_ps = tbank()
        for k in range(2):
            P.add("pe", lambda e, ps=ps, k=k, blk=blk: e.matmul(ps[:, 0:512], lhsT=ckvn[:, k, 128 * blk:128 * blk + 128], rhs=wv[:, k, :],
                                                                 start=(k == 0), stop=(k == 1)), reads=[b_ckvn, b_wv], writes=[b_ps])
        P.add("act", lambda e, ps=ps, blk=blk: e.activation(out=Vst[:, :, blk, 0:64], in_=ps[:, 0:512].rearrange("p (h v) -> p h v", h=8),
                                                            func=AF.Copy), reads=[b_ps], writes=[b_Vst])
    for h in range(N_HEADS):
        P.add("sp", lambda e, h=h: e.dma_start(out=V_own[h], in_=Vst[:, h].rearrange("p b e -> p (b e)")), reads=[b_Vst],
              writes=[b_Vown[h]], dma=True, semkey=f"vst{h}")
        P.add("pool", lambda e, h=h: e.collective_compute("AllGather", ALU.bypass, replica_groups=RG, ins=[V_own[h].opt()],
                                                          outs=[V_all[h].opt()]),
              reads=[b_Vown[h]], writes=[b_Vall[h]], dma="cc", semkey=f"ccV{h}")
    kcount = 0
    for h in range(N_HEADS):
        for ci, (n0, n) in enumerate(chunks):
            ps, b_ps = tbank()
            for k in range(3):
                P.add("pe", lambda e, ps=ps, k=k, h=h, n0=n0, n=n: e.matmul(
                    ps[0:96, 0:n], lhsT=wq[:, k, 96 * h:96 * h + 96], rhs=cqn[:, k, n0:n0 + n], start=(k == 0), stop=(k == 2)),
                    reads=[b_wq, b_cqn], writes=[b_ps])
            normrope(ps, b_ps, "g_q", qT[:, h, n0:n0 + n], b_qT, n, n0)
            ps, b_ps = tbank()
            for k in range(2):
                P.add("pe", lambda e, ps=ps, k=k, h=h, n0=n0, n=n: e.matmul(
                    ps[0:96, 0:n], lhsT=wk[:, k, h, :], rhs=ckvn[:, k, n0:n0 + n], start=(k == 0), stop=False),
                    reads=[b_wk, b_ckvn], writes=[b_ps])
            P.add("pe", lambda e, ps=ps, n0=n0, n=n: e.matmul(
                ps[0:96, 0:n], lhsT=ident[64:96, 0:96], rhs=krK[64:96, n0:n0 + n], start=False, stop=True, tile_position=(64, 0)),
                reads=[b_sm, b_krK], writes=[b_ps])
            ks, b_ks = kst[kcount % 2], b_kst[kcount % 2]
            kcount += 1
            normrope(ps, b_ps, "g_k", ks[:, 0:n], b_ks, n, n0)
            npr = min(n0 + n, NPT) - n0
            if npr > 0:
                P.add("sp", lambda e, ks=ks, h=h, n0=n0, npr=npr: e.dma_start(out=K_own[h][:, n0:n0 + npr],
                                                                               in_=ks[:, 0:npr]),
                      reads=[b_ks], writes=[b_Kown[h]], dma=True, semkey=f"kst{(kcount - 1) % 2}")
            if npr < n:
                P.add("pool", lambda e, ks=ks, h=h, npr=npr, n=n: e.tensor_copy(out=kS[:, h, :], in_=ks[:, npr:n]), reads=[b_ks],
                      writes=[b_kS])
        P.add("pool", lambda e, h=h: e.collective_compute("AllGather", ALU.bypass, replica_groups=RG, ins=[K_own[h].opt()],
                                                          outs=[K_all[h].opt()]),
              reads=[b_Kown[h]], writes=[b_Kall[h]], dma="cc", semkey=f"ccK{h}")
    import os
    if os.environ.get("CUT") == "3":
        return {}
    AR.release(mC1)
    P.new_phase()
    Kh = [AR.alloc([96, 4, NPT], BF16) for _ in range(2)]
    Vh = [AR.alloc([128, 4, 16 * 65], BF16) for _ in range(2)]
    Vvis = AR.alloc([128, 4, 16 * 65], BF16)
    Vful = AR.alloc([128, 4, 16 * 65], BF16)
    b_Kh = [P.buf("Kh0"), P.buf("Kh1")]
    b_Vh = [P.buf("Vh0"), P.buf("Vh1")]
    b_Vvis, b_Vful = P.buf("Vvis"), P.buf("Vful")
    PT = [AR.alloc([128, 512], BF16) for _ in range(6)]
    b_PT = [P.buf(f"PT{i}") for i in range(6)]
    Osb = AR.alloc([65, 512], F32); rl = AR.alloc([64, 512], F32); ast = AR.alloc([64, 512], BF16)
    b_Osb, b_rl, b_ast = P.buf("Osb"), P.buf("rl"), P.buf("ast")

    def load_head(h):
        i = h % 2
        P.add("sp", lambda e: e.dma_start(out=Kh[i][:], in_=K_all[h].rearrange("(r d) t -> d r t", r=4)), reads=[b_Kall[h]], writes=[b_Kh[i]], dma=True,
              semkey=f"Kh{i}")
        P.add("sp", lambda e: e.dma_start(out=Vh[i][:], in_=V_all[h].rearrange("(r p) x -> p r x", r=4)), reads=[b_Vall[h]], writes=[b_Vh[i]], dma=True,
              semkey=f"Vh{i}")

    load_head(0)
    pcount = 0
    for h in range(N_HEADS):
        i = h % 2
        if h + 1 < N_HEADS:
            load_head(h + 1)
        for r in range(4):
            P.add("pool", lambda e, r=r, i=i: e.tensor_scalar(out=Vvis[:, r, :], in0=Vh[i][:, r, :], scalar1=smc("vis", r), scalar2=None,
                                                              op0=MUL), reads=[b_Vh[i], b_sm], writes=[b_Vvis])
            P.add("pool", lambda e, r=r, i=i: e.tensor_scalar(out=Vful[:, r, :], in0=Vh[i][:, r, :], scalar1=smc("full", r), scalar2=None,
                                                              op0=MUL), reads=[b_Vh[i], b_sm], writes=[b_Vful])
        for qc in range(4):
            O, b_O = banks[qc % 2], bank_bufs[qc % 2]
            first = [True]
            pending = []

            def flush(keep):
                while len(pending) > keep:
                    pending.pop(0)()
            for r in range(4):
                for kb in range(16):
                    S_, b_S = tbank()
                    P.add("pe", lambda e, S_=S_, r=r, kb=kb, i=i, h=h, qc=qc: e.matmul(
                        S_[:, 0:512], lhsT=Kh[i][:, r, 128 * kb:128 * kb + 128], rhs=qT[:, h, 512 * qc:512 * qc + 512],
                        start=True, stop=True), reads=[b_Kh[i], b_qT], writes=[b_S])
                    pt, b_pt = PT[pcount % 6], b_PT[pcount % 6]
                    pcount += 1
                    P.add("act", lambda e, S_=S_, pt=pt: e.activation(out=pt[:], in_=S_[:, 0:512], func=AF.Exp, scale=SCALE),
                          reads=[b_S], writes=[b_pt])

                    def pv(r=r, kb=kb, pt=pt, b_pt=b_pt, O=O, b_O=b_O, qc=qc, i=i):
                        d = kb - 4 * qc
                        segs = []
                        if d < 0:
                            segs.append((0, 512, Vvis, b_Vvis))
                        elif d > 3:
                            segs.append((0, 512, Vful, b_Vful))
                        else:
                            if d > 0:
                                segs.append((0, 128 * d, Vful, b_Vful))
                            P.add("dve", lambda e: e.tensor_tensor(out=pt[:, 128 * d:128 * d + 128], in0=pt[:, 128 * d:128 * d + 128],
                                                                   in1=maskd[:, r, :], op=MUL), reads=[b_pt, b_mk], writes=[b_pt])
                            segs.append((128 * d, 128 * d + 128, Vh[i], b_Vh[i]))
                            if d < 3:
                                segs.append((128 * d + 128, 512, Vvis, b_Vvis))
                        for (c0, c1, Vx, b_Vx) in segs:
                            st = first[0]
                            first[0] = False
                            P.add("pe", lambda e, c0=c0, c1=c1, Vx=Vx, st=st: e.matmul(
                                O[0:65, c0:c1], lhsT=Vx[:, r, 65 * kb:65 * kb + 65], rhs=pt[:, c0:c1], start=st, stop=False),
                                reads=[b_Vx, b_pt], writes=[b_O])
                    pending.append(pv)
                    flush(3)
            flush(0)
            P.add("act", lambda e, O=O: e.activation(out=Osb[:], in_=O[0:65, 0:512], func=AF.Copy), reads=[b_O], writes=[b_Osb])
            lb, b_lb = tbank()
            P.add("pe", lambda e, lb=lb: e.matmul(lb[0:64, 0:512], lhsT=sel65[:, :], rhs=Osb[:, :], start=True, stop=True),
                  reads=[b_sel, b_Osb], writes=[b_lb])
            P.add("dve", lambda e, lb=lb: e.reciprocal(out=rl[:], in_=lb[0:64, 0:512]), reads=[b_lb], writes=[b_rl])
            cols = slice(512 * qc, 512 * qc + 512)
            if h % 2 == 0:
                P.add("dve", lambda e, h=h, cols=cols: e.tensor_tensor(out=attT[0:64, h // 2, cols], in0=Osb[0:64, :], in1=rl[:], op=MUL),
                      reads=[b_Osb, b_rl], writes=[b_attT])
            else:
                P.add("dve", lambda e: e.tensor_tensor(out=ast[:], in0=Osb[0:64, :], in1=rl[:], op=MUL), reads=[b_Osb, b_rl],
                      writes=[b_ast])
                P.add("sp", lambda e, h=h, cols=cols: e.dma_start(out=attT[64:128, h // 2, cols], in_=ast[:]), reads=[b_ast],
                      writes=[b_attT], dma=True, semkey="ast")
    return dict(mC1=mC1, qT=qT, b_qT=b_qT, w_uk_b=w_uk_b, b_wkb=b_wkb, w_uv_b=w_uv_b, b_wvb=b_wvb, rotm_d=rotm_d)


def build_tail(P, AR, nc, din, dout, dint, stores, banks, bank_bufs, cast_w, xT, w_in_b, b_w_in_b, sm, b_sm, smc, ones_bf, b_ones,
               attT, b_attT, ysT, b_ys, chunks):
    MUL, ADD = ALU.mult, ALU.add
    names = [("w_oa", 512, D), ("w_os", 512, D), ("w_out", D, D), ("w_up", D, 2 * D_FF), ("w_down", D_FF, D),
             ("w_pg", D, D), ("w_pp", PLE, D)]
    W, bW = {"w_in": w_in_b}, {"w_in": b_w_in_b}
    for nm, r, c in names:
        src = din(nm, [r, c])
        W[nm] = dint(nm + "_b", [r, c], BF16)
        bW[nm] = cast_w(W[nm], src, r, c, "c_" + nm)
    pT_d = din("pT", [PLE, T])
    scT_d = din("scT", [128, 44, 16, 2])
    o_yT = dout("o_yT", [D, T])
    o_cvp = dout("o_cvp", [128, 44, 2])
    o_cvs = dout("o_cvs", [128, 44, 16, 2])
    cc_h_in = dint("cc_h_in", [128, 16], F32)
    cc_h_out = dint("cc_h_out", [512, 16], F32)

    AR.release(0)
    P.new_phase()
    HO = 2
    x1 = AR.alloc([128, KD, 514], F32)
    x1c4 = AR.alloc([128, KD, 290], F32)
    xn = AR.alloc([128, KD, 514], BF16)
    scr = AR.alloc([128, KD, 514], BF16)
    hT = AR.alloc([128, 22, 512], BF16)
    upx = [[AR.alloc([128, 514], F32) for _ in range(2)] for _ in range(2)]
    cav = [[AR.alloc([128, 512], F32) for _ in range(2)] for _ in range(2)]
    sgt = [AR.alloc([128, 512], F32) for _ in range(2)]
    lnv = AR.alloc([128, 514], F32)
    rstd = AR.alloc([128, 514], F32)
    pTc = AR.alloc([128, 2, 512], BF16)
    scT = AR.alloc([128, 44, 16, 2], F32)
    upS = [AR.alloc([128, 16, 6], F32) for _ in range(2)]
    cvp = AR.alloc([128, 44, 2], F32)
    cvs = AR.alloc([128, 44, 16, 2], F32)
    carry = AR.alloc([128, 44, 2], F32)
    Hg = AR.alloc([128, 4, 16], F32)
    hsend = AR.alloc([128, 16], F32)
    hrecv = AR.alloc([128, 16], F32)
    NSLAB = 4
    slabs = [AR.alloc([128, 4096], BF16) for _ in range(NSLAB)]
    b_x1, b_x1c4, b_xn, b_scr, b_hT, b_lnv, b_rstd, b_pTc, b_scT, b_cvp, b_cvs, b_carry, b_Hg, b_hsend, b_hrecv = [
        P.buf(n) for n in ("x1", "x1c4", "xn", "scr", "hT", "lnv", "rstd", "pTc", "scT", "cvp", "cvs", "carry", "Hg", "hsend", "hrecv")]
    b_upx = [[P.buf(f"upx{i}{j}") for j in range(2)] for i in range(2)]
    b_cav = [[P.buf(f"cav{i}{j}") for j in range(2)] for i in range(2)]
    b_sgt = [P.buf("sgt0"), P.buf("sgt1")]
    b_upS = [P.buf("upS0"), P.buf("upS1")]
    b_slab = [P.buf(f"slab{i}") for i in range(NSLAB)]
    P.add("sp", lambda e: e.dma_start(out=scT[:], in_=scT_d), writes=[b_scT], dma=True, semkey="scT")
    P.add("pool", lambda e: e.memset(carry[:], 0.0), writes=[b_carry])

    rr = [0]

    def tbank():
        i = rr[0] % 8
        rr[0] += 1
        return banks[i], bank_bufs[i]

    sl = [0]

    def load_slab(wname, kt, c0, width):
        i = sl[0] % NSLAB
        sl[0] += 1
        v = slabs[i][:, 0:kt * width].rearrange("p (k m) -> p k m", k=kt)
        src, bsrc = W[wname], bW[wname]
        P.add("sp", lambda e: e.dma_start(out=v, in_=src.rearrange("(k p) m -> p k m", p=128)[:, :, c0:c0 + width]),
              reads=[bsrc], writes=[b_slab[i]], dma=True, semkey=f"slab{i}")
        return v, b_slab[i]

    def mm_group(ps, b_ps, n, w, b_w, kt, wc0, rhs_fn, rd):
        for k in range(kt):
            P.add("pe", lambda e, k=k: e.matmul(ps[:, 0:n], lhsT=w[:, k, wc0:wc0 + 128], rhs=rhs_fn(k), start=(k == 0), stop=(k == kt - 1)),
                  reads=[b_w] + rd, writes=[b_ps])

    def norm_to_xn(src, b_src, gname, c_lo, c_hi):
        n = c_hi - c_lo
        P.add("pool", lambda e: e.tensor_tensor(out=scr[:, :, c_lo:c_hi], in0=src[:, :, c_lo:c_hi], in1=src[:, :, c_lo:c_hi], op=MUL),
              reads=[b_src], writes=[b_scr])
        ps, b_ps = tbank()
        for k in range(KD):
            P.add("pe", lambda e, k=k: e.matmul(ps[:, 0:n], lhsT=ones_bf[:], rhs=scr[:, k, c_lo:c_hi], start=(k == 0), stop=(k == KD - 1)),
                  reads=[b_scr, b_ones], writes=[b_ps])
        P.add("act", lambda e: e.activation(out=lnv[:, 0:n], in_=ps[:, 0:n], func=AF.Ln, scale=1.0 / D, bias=EPS), reads=[b_ps],
              writes=[b_lnv])
        P.add("act", lambda e: e.activation(out=rstd[:, 0:n], in_=lnv[:, 0:n], func=AF.Exp, scale=-0.5), reads=[b_lnv], writes=[b_rstd])
        for k in range(KD):
            P.add("dve", lambda e, k=k: e.scalar_tensor_tensor(out=xn[:, k, c_lo:c_hi], in0=src[:, k, c_lo:c_hi], scalar=smc(gname, k),
                                                               in1=rstd[:, 0:n], op0=MUL, op1=MUL),
                  reads=[b_src, b_rstd, b_sm], writes=[b_xn])

    xT_v = xT.rearrange("(k p) t -> p k t", p=128)

    def phase_D(xt, b_xt, n0, n):
        lo, hi = HO, HO + n
        P.add("sp", lambda e: e.dma_start(out=xt[:, :, lo:hi], in_=xT_v[:, :, n0:n0 + n]), writes=[b_xt], dma=True, semkey="x1ld")
        norm_to_xn(xt, b_xt, "g_mix", lo, hi)
        mixed = scr
        for q in range(2):
            wga, b_wga = load_slab("w_in", KD, OFF_GA + 512 * q, 512)
            wgs, b_wgs = load_slab("w_in", KD, OFF_GS + 512 * q, 512)
            woa, b_woa = load_slab("w_oa", 4, 512 * q, 512)
            wos, b_wos = load_slab("w_os", 4, 512 * q, 512)
            for mi in range(4):
                m = 4 * q + mi
                pga, b_pga = tbank()
                mm_group(pga, b_pga, n, wga, b_wga, KD, 128 * mi, lambda k: xn[:, k, lo:hi], [b_xn])
                pgs, b_pgs = tbank()
                mm_group(pgs, b_pgs, n, wgs, b_wgs, KD, 128 * mi, lambda k: xn[:, k, lo:hi], [b_xn])
                poa, b_poa = tbank()
                mm_group(poa, b_poa, n, woa, b_woa, 4, 128 * mi, lambda k: attT[:, k, n0:n0 + n], [b_attT])
                pos_, b_pos = tbank()
                mm_group(pos_, b_pos, n, wos, b_wos, 4, 128 * mi, lambda k: ysT[:, k, n0:n0 + n], [b_ys])
                P.add("act", lambda e, pga=pga: e.activation(out=sgt[0][:, 0:n], in_=pga[:, 0:n], func=AF.Sigmoid), reads=[b_pga],
                      writes=[b_sgt[0]])
                P.add("act", lambda e, pgs=pgs: e.activation(out=sgt[1][:, 0:n], in_=pgs[:, 0:n], func=AF.Sigmoid), reads=[b_pgs],
                      writes=[b_sgt[1]])
                P.add("dve", lambda e, poa=poa: e.tensor_tensor(out=cav[0][0][:, 0:n], in0=poa[:, 0:n], in1=sgt[0][:, 0:n], op=MUL),
                      reads=[b_poa, b_sgt[0]], writes=[b_cav[0][0]])
                P.add("dve", lambda e, pos_=pos_: e.tensor_tensor(out=cav[0][1][:, 0:n], in0=pos_[:, 0:n], in1=sgt[1][:, 0:n], op=MUL),
                      reads=[b_pos, b_sgt[1]], writes=[b_cav[0][1]])
                P.add("pool", lambda e, m=m: e.tensor_tensor(out=mixed[:, m, lo:hi], in0=cav[0][0][:, 0:n], in1=cav[0][1][:, 0:n], op=ADD),
                      reads=[b_cav[0][0], b_cav[0][1]], writes=[b_scr])
        for q in range(2):
            wo, b_wo = load_slab("w_out", KD, 512 * q, 512)
            for mi in range(4):
                m = 4 * q + mi
                ps, b_ps = tbank()
                mm_group(ps, b_ps, n, wo, b_wo, KD, 128 * mi, lambda k: mixed[:, k, lo:hi], [b_scr])
                P.add("dve", lambda e, ps=ps, m=m: e.tensor_tensor(out=xt[:, m, lo:hi], in0=ps[:, 0:n], in1=xt[:, m, lo:hi], op=ADD),
                      reads=[b_ps, b_xt], writes=[b_xt])

    ucount = [0]

    def conv3(ci_, src3, dst, b_src, b_dst, tile, nn, view=None):
        w0, w1, w2 = (smc("conv_w", tap * 44 + tile) for tap in range(3))
        P.add("act", lambda e: e.activation(out=dst, in_=src3(2), func=AF.Identity, scale=w2, bias=smc("conv_b", tile)),
              reads=[b_src, b_sm], writes=[b_dst])
        P.add("dve", lambda e: e.scalar_tensor_tensor(out=dst, in0=src3(1), scalar=w1, in1=dst, op0=MUL, op1=ADD),
              reads=[b_src, b_sm, b_dst], writes=[b_dst])
        P.add("dve", lambda e: e.scalar_tensor_tensor(out=dst, in0=src3(0), scalar=w0, in1=dst, op0=MUL, op1=ADD),
              reads=[b_src, b_sm, b_dst], writes=[b_dst])

    def phase_E(xt, b_xt, n0, n, first, last):
        lo, hi = HO, HO + n
        c_lo = 0 if first else HO
        N = hi - c_lo
        npr = min(n0 + n, NPT) - n0
        ns = n - npr
        norm_to_xn(xt, b_xt, "g_ffn", c_lo, hi)
        for q in range(6):
            npair = min(4, 22 - 4 * q)
            wa, b_wa = load_slab("w_up", KD, 512 * q, 128 * npair)
            wv_, b_wv = load_slab("w_up", KD, D_FF + 512 * q, 128 * npair)
            for pi_ in range(npair):
                p = 4 * q + pi_
                ub = ucount[0] % 2
                ucount[0] += 1
                for av, (w, b_w) in enumerate(((wa, b_wa), (wv_, b_wv))):
                    tile = p + 22 * av
                    u, b_u = upx[ub][av], b_upx[ub][av]
                    c, b_c = cav[ub][av], b_cav[ub][av]
                    ps, b_ps = tbank()
                    mm_group(ps, b_ps, N, w, b_w, KD, 128 * pi_, lambda k: xn[:, k, c_lo:hi], [b_xn])
                    P.add("act", lambda e, ps=ps, u=u: e.activation(out=u[:, c_lo:hi], in_=ps[:, 0:N], func=AF.Copy), reads=[b_ps],
                          writes=[b_u])
                    if not first:
                        P.add("pool", lambda e, u=u, tile=tile: e.tensor_copy(out=u[:, 0:2], in_=carry[:, tile, :]), reads=[b_carry],
                              writes=[b_u])
                    conv3(0, lambda k, u=u: u[:, k:k + npr], c[:, 0:npr], b_u, b_c, tile, npr)
                    P.add("pool", lambda e, u=u, tile=tile: e.tensor_copy(out=carry[:, tile, :], in_=u[:, npr:npr + 2]), reads=[b_u],
                          writes=[b_carry])
                    if last:
                        P.add("pool", lambda e, u=u, tile=tile: e.tensor_copy(out=cvp[:, tile, :], in_=u[:, npr:npr + 2]), reads=[b_u],
                              writes=[b_cvp])
                    if ns:
                        us, b_us = upS[av], b_upS[av]
                        P.add("pool", lambda e, us=us, tile=tile: e.tensor_copy(out=us[:, :, 0:2], in_=scT[:, tile, :, :]), reads=[b_scT],
                              writes=[b_us])
                        P.add("pool", lambda e, us=us, u=u: e.tensor_copy(
                            out=us[:, :, 2:6], in_=u[:, HO + npr:HO + n].rearrange("p (q s) -> p q s", s=4)), reads=[b_u], writes=[b_us])
                        conv3(0, lambda k, us=us: us[:, :, k:k + 4], c[:, npr:n].rearrange("p (q s) -> p q s", s=4), b_us, b_c, tile, ns)
                        P.add("pool", lambda e, us=us, tile=tile: e.tensor_copy(out=cvs[:, tile, :, :], in_=us[:, :, 4:6]), reads=[b_us],
                              writes=[b_cvs])
                ca, cv = cav[ub][0], cav[ub][1]
                P.add("act", lambda e, ca=ca: e.activation(out=ca[:, 0:n], in_=ca[:, 0:n], func=AF.Gelu_apprx_tanh), reads=[b_cav[ub][0]],
                      writes=[b_cav[ub][0]])
                P.add("dve", lambda e, ca=ca, cv=cv, p=p: e.tensor_tensor(out=hT[:, p, 0:n], in0=ca[:, 0:n], in1=cv[:, 0:n], op=MUL),
                      reads=[b_cav[ub][0], b_cav[ub][1]], writes=[b_hT])
        for m in range(KD):
            wd, b_wd = load_slab("w_down", 22, 128 * m, 128)
            ps, b_ps = tbank()
            mm_group(ps, b_ps, n, wd, b_wd, 22, 0, lambda k: hT[:, k, 0:n], [b_hT])
            P.add("dve", lambda e, ps=ps, m=m: e.tensor_tensor(out=xt[:, m, lo:hi], in0=ps[:, 0:n], in1=xt[:, m, lo:hi], op=ADD),
                  reads=[b_ps, b_xt], writes=[b_xt])
        norm_to_xn(xt, b_xt, "g_ple", lo, hi)
        P.add("pool", lambda e: e.dma_start(out=pTc[:, :, 0:n], in_=pT_d.rearrange("(k p) t -> p k t", p=128)[:, :, n0:n0 + n]),
              writes=[b_pTc], dma=True, semkey="pTc")
        for q in range(2):
            wg_, b_wg = load_slab("w_pg", KD, 512 * q, 512)
            wp_, b_wp = load_slab("w_pp", 2, 512 * q, 512)
            for mi in range(4):
                m = 4 * q + mi
                pg, b_pg = tbank()
                mm_group(pg, b_pg, n, wg_, b_wg, KD, 128 * mi, lambda k: xn[:, k, lo:hi], [b_xn])
                pp, b_pp = tbank()
                mm_group(pp, b_pp, n, wp_, b_wp, 2, 128 * mi, lambda k: pTc[:, k, 0:n], [b_pTc])
                P.add("act", lambda e, pg=pg: e.activation(out=sgt[0][:, 0:n], in_=pg[:, 0:n], func=AF.Sigmoid), reads=[b_pg],
                      writes=[b_sgt[0]])
                P.add("dve", lambda e, pp=pp: e.tensor_tensor(out=sgt[1][:, 0:n], in0=pp[:, 0:n], in1=sgt[0][:, 0:n], op=MUL),
                      reads=[b_pp, b_sgt[0]], writes=[b_sgt[1]])
                P.add("pool", lambda e, m=m: e.tensor_tensor(out=xt[:, m, lo:hi], in0=xt[:, m, lo:hi], in1=sgt[1][:, 0:n], op=ADD),
                      reads=[b_xt, b_sgt[1]], writes=[b_xt])
        stores.append(P.add("sp", lambda e: e.dma_start(out=o_yT.rearrange("(k p) t -> p k t", p=128)[:, :, n0:n0 + n], in_=xt[:, :, lo:hi]),
                            reads=[b_xt], dma=True, semkey="st_y"))

    n0_4, n_4 = chunks[4]
    phase_D(x1c4, b_x1c4, n0_4, n_4)
    lastp = HO + (NPT - n0_4)
    P.add("pool", lambda e: e.tensor_copy(out=hsend[:].rearrange("p (k c) -> p k c", c=2), in_=x1c4[:, :, lastp - 2:lastp]),
          reads=[b_x1c4], writes=[b_hsend])
    b_hin, b_hout = P.buf("cc_h_in"), P.buf("cc_h_out")
    P.add("sp", lambda e: e.dma_start(out=cc_h_in, in_=hsend[:]), reads=[b_hsend], writes=[b_hin], dma=True, semkey="hs1")
    P.add("pool", lambda e: e.collective_compute("AllGather", ALU.bypass, replica_groups=[[0, 1, 2, 3], [4, 5, 6, 7]],
                                                 ins=[cc_h_in.opt()], outs=[cc_h_out.opt()]),
          reads=[b_hin], writes=[b_hout], dma="cc", semkey="hs2")
    P.add("sp", lambda e: e.dma_start(out=Hg[:], in_=cc_h_out.rearrange("(r p) f -> p r f", p=128)), reads=[b_hout], writes=[b_Hg],
          dma=True, semkey="hs3")
    P.add("dve", lambda e: e.tensor_scalar(out=hrecv[:], in0=Hg[:, 0, :], scalar1=smc("hsel", 0), scalar2=None, op0=MUL),
          reads=[b_Hg, b_sm], writes=[b_hrecv])
    for r in range(1, 4):
        P.add("dve", lambda e, r=r: e.scalar_tensor_tensor(out=hrecv[:], in0=Hg[:, r, :], scalar=smc("hsel", r), in1=hrecv[:],
                                                           op0=MUL, op1=ADD), reads=[b_Hg, b_sm, b_hrecv], writes=[b_hrecv])
    for ci in range(4):
        n0, n = chunks[ci]
        if ci == 0:
            P.add("pool", lambda e: e.tensor_copy(out=x1[:, :, 0:2], in_=hrecv[:].rearrange("p (k c) -> p k c", c=2)),
                  reads=[b_hrecv], writes=[b_x1])
        phase_D(x1, b_x1, n0, n)
        phase_E(x1, b_x1, n0, n, ci == 0, False)
    phase_E(x1c4, b_x1c4, n0_4, n_4, False, True)
    stores.append(P.add("sp", lambda e: e.dma_start(out=o_cvp, in_=cvp[:]), reads=[b_cvp], dma=True, semkey="st_cvp"))
    stores.append(P.add("sp", lambda e: e.dma_start(out=o_cvs, in_=cvs[:]), reads=[b_cvs], dma=True, semkey="st_cvs"))


def build_sample_attn(P, AR, nc, din, dout, dint, stores, banks, bank_bufs, sm, b_sm, smc, ones_bf, b_ones, ckvn, b_ckvn, krK, b_krK,
                      qT, b_qT, attT, b_attT, n_pool, w_uk_b, b_wkb, w_uv_b, b_wvb, rope_c, rope_s, rotm_d, mC1):
    MUL, ADD = ALU.mult, ALU.add
    CW = KV_LORA + QK_ROPE
    cache = din("cache", [n_pool * 32, 4 * CW])
    ptab = din("ptab", [128, SEQ_PER_CORE * 16], I32)
    p32c_d = din("p32c", [128, 1], I32)
    w_ukT_d = din("w_ukT", [64, 8 * KV_LORA])
    ropeP_d = din("ropeP", [128, 2, NPAGES, 16])
    gk_rep_d = din("gk_rep", [128, 32])
    hselm_d = din("hselm", [128, 4, 32])
    cmask_d = din("cmask", [32, 4])

    AR.release(mC1)
    P.new_phase()
    NB_PG = 5
    pb = [AR.alloc([128, 4, CW], BF16) for _ in range(NB_PG)]
    b_pb = [P.buf(f"pb{i}") for i in range(NB_PG)]
    kt3 = [AR.alloc([128, 4, 96], BF16) for _ in range(2)]
    b_kt3 = [P.buf("kt3_0"), P.buf("kt3_1")]
    rt = [AR.alloc([128, 4, 16], F32) for _ in range(4)]
    b_rt = P.buf("rt")
    ropeP = AR.alloc([128, 2, NPAGES, 16], F32)
    gkr = AR.alloc([128, 32], F32)
    tabs = AR.alloc([128, 4, NPAGES, 16], F32)
    ptb = AR.alloc([128, SEQ_PER_CORE * 16], I32)
    idx = AR.alloc([128, SEQ_PER_CORE * 16], I32)
    iot = AR.alloc([128, 1], I32)
    wk_sb = AR.alloc([128, 2, 512], BF16)
    wv_sb = AR.alloc([128, 2, 512], BF16)
    wukT = AR.alloc([64, 8, KV_LORA], BF16)
    wukT_f = AR.alloc([64, 8 * KV_LORA], F32)
    hselm = AR.alloc([128, 4, 32], BF16)
    hselm_f = AR.alloc([128, 4, 32], F32)
    cmask = AR.alloc([32, 4], F32)
    qgk = AR.alloc([64, 8, NST], BF16)
    Qabs = AR.alloc([128, 2, SEQ_PER_CORE, 32], BF16)
    Qrope = AR.alloc([96, SEQ_PER_CORE, 32], BF16)
    cT_sb = [AR.alloc([128, 2, 512], BF16) for _ in range(2)]
    krT_sb = [AR.alloc([96, 512], BF16) for _ in range(2)]
    kn_sb = [AR.alloc([128, 512], BF16) for _ in range(4)]
    sq_sb = [AR.alloc([128, 512], BF16) for _ in range(4)]
    sqk = AR.alloc([32, 512], BF16)
    lnr = AR.alloc([32, 512], F32)
    rr_ = AR.alloc([32, 512], F32)
    sr = AR.alloc([32, 512], F32)
    Pm = AR.alloc([32, 512], BF16)
    PT_sb = [AR.alloc([128, 4, 32], BF16) for _ in range(2)]
    Lacc = AR.alloc([32, 20], F32)
    accs = AR.alloc([32, KV_LORA], F32)
    lsum = AR.alloc([32, 1], F32)
    olat = AR.alloc([32, KV_LORA], BF16)
    olT = AR.alloc([128, 2, SEQ_PER_CORE, 32], BF16)
    knew = AR.alloc([96, NST], BF16)
    kraw = AR.alloc([96, NST], BF16)
    kg32 = AR.alloc([96, NST], F32)
    kt1 = AR.alloc([96, NST], F32)
    kt2 = AR.alloc([96, NST], F32)
    rc_s = AR.alloc([96, NST], F32)
    rs_s = AR.alloc([96, NST], F32)
    rotm = AR.alloc([96, 96], F32)
    cnew = AR.alloc([4, KV_LORA], BF16)
    names = ["kt", "ropeP", "gkr", "tabs", "ptb", "idx", "iot", "wk", "wv", "wukT", "hselm", "cmask", "qgk", "Qabs", "Qrope",
             "sqk", "lnr", "rr", "sr", "Pm", "Lacc", "accs", "lsum", "olat", "olT", "knew", "kraw", "ktmp", "rcs", "rotm", "cnew"]
    B = {n: P.buf("s_" + n) for n in names}
    b_cT = [P.buf("cT0"), P.buf("cT1")]
    b_krT = [P.buf("krT0"), P.buf("krT1")]
    b_kn = [P.buf(f"kn{i}") for i in range(4)]
    b_sq = [P.buf(f"sq{i}") for i in range(4)]
    b_PT = [P.buf("PTs0"), P.buf("PTs1")]

    rrb = [0]

    def tbank():
        i = 1 + rrb[0] % 7
        rrb[0] += 1
        return banks[i], bank_bufs[i]
    ACC, b_ACC = banks[0], bank_bufs[0]

    ld = lambda out, in_, wr, key, rd=(): P.add("sp", lambda e: e.dma_start(out=out, in_=in_), reads=list(rd), writes=[wr], dma=True,
                                                semkey=key)
    ld(ropeP[:], ropeP_d, B["ropeP"], "s_ropeP")
    ld(gkr[:], gk_rep_d, B["gkr"], "s_gkr")
    ld(ptb[:], ptab, B["ptb"], "s_ptb")
    ld(iot[:], p32c_d, B["iot"], "s_iot")
    ld(wk_sb[:], w_uk_b.rearrange("(k p) m -> p k m", p=128), B["wk"], "s_wk", [b_wkb])
    ld(wv_sb[:], w_uv_b.rearrange("(k p) m -> p k m", p=128), B["wv"], "s_wv", [b_wvb])
    ld(wukT_f[:], w_ukT_d, B["wukT"], "s_wukT")
    ld(hselm_f[:], hselm_d, B["hselm"], "s_hselm")
    ld(cmask[:], cmask_d, B["cmask"], "s_cmask")
    ld(rc_s[:], rope_c[:, NPT:T], B["rcs"], "s_rcs")
    ld(rs_s[:], rope_s[:, NPT:T], B["rcs"], "s_rss")
    ld(rotm[:], rotm_d, B["rotm"], "s_rotm")
    P.add("pool", lambda e: e.tensor_copy(out=wukT[:].rearrange("p h l -> p (h l)"), in_=wukT_f[:]), reads=[B["wukT"]], writes=[B["wukT"]])
    P.add("pool", lambda e: e.tensor_copy(out=hselm[:], in_=hselm_f[:]), reads=[B["hselm"]], writes=[B["hselm"]])
    P.add("dve", lambda e: e.tensor_scalar(out=idx[:], in0=ptb[:], scalar1=32.0, scalar2=iot[:, 0:1], op0=MUL, op1=ADD),
          reads=[B["ptb"], B["iot"]], writes=[B["idx"]])
    g1 = gkr[:, 0:16].unsqueeze(1).broadcast_to([128, NPAGES, 16])
    g2 = gkr[:, 16:32].unsqueeze(1).broadcast_to([128, NPAGES, 16])
    tt_ = lambda o, a, b_, op: P.add("pool", lambda e: e.tensor_tensor(out=o, in0=a, in1=b_, op=op), reads=[B["ropeP"], B["gkr"], B["tabs"]],
                                     writes=[B["tabs"]])
    tt_(tabs[:, 0], ropeP[:, 0], g1, MUL)
    tt_(tabs[:, 1], ropeP[:, 0], g2, MUL)
    tt_(tabs[:, 2], ropeP[:, 1], g2, MUL)
    P.add("pool", lambda e: e.tensor_single_scalar(out=tabs[:, 2], in_=tabs[:, 2], scalar=-1.0, op=MUL), reads=[B["tabs"]],
          writes=[B["tabs"]])
    tt_(tabs[:, 3], ropeP[:, 1], g1, MUL)
    for i in range(2):
        P.add("pool", lambda e, i=i: e.memset(kt3[i][:], 0.0), writes=[b_kt3[i]])

    P.add("dve", lambda e: e.tensor_scalar(out=qgk[:], in0=qT[0:64, :, NPT:T], scalar1=smc("g_k")[0:64, :], scalar2=None, op0=MUL),
          reads=[b_qT, b_sm], writes=[B["qgk"]])
    for h in range(N_HEADS):
        for kt in range(2):
            ps, b_ps = tbank()
            P.add("pe", lambda e, ps=ps, h=h, kt=kt: e.matmul(ps[:, 0:NST], lhsT=wukT[:, h, 128 * kt:128 * kt + 128], rhs=qgk[:, h, :],
                                                              start=True, stop=True), reads=[B["wukT"], B["qgk"]], writes=[b_ps])
            P.add("act", lambda e, ps=ps, h=h, kt=kt: e.activation(out=Qabs[:, kt, :, 4 * h:4 * h + 4],
                                                                   in_=ps[:, 0:NST].rearrange("p (q t) -> p q t", t=4), func=AF.Copy),
                  reads=[b_ps], writes=[B["Qabs"]])
        P.add("pool", lambda e, h=h: e.tensor_copy(out=Qrope[64:96, :, 4 * h:4 * h + 4],
                                                   in_=qT[64:96, h, NPT:T].rearrange("p (q t) -> p q t", t=4)),
              reads=[b_qT], writes=[B["Qrope"]])
    P.add("act", lambda e: e.activation(out=kraw[64:96, :], in_=krK[64:96, NPT:T], func=AF.Copy), reads=[b_krK], writes=[B["kraw"]])
    P.add("dve", lambda e: e.tensor_scalar(out=kg32[64:96, :], in0=krK[64:96, NPT:T], scalar1=smc("g_k")[64:96, :], scalar2=None, op0=MUL),
          reads=[b_krK, b_sm], writes=[B["ktmp"]])
    ps, b_ps = tbank()
    P.add("pe", lambda e, ps=ps: e.matmul(ps[0:96, 0:NST], lhsT=rotm[64:96, 0:96], rhs=kg32[64:96, :], start=True, stop=True,
                                          tile_position=(64, 0)), reads=[B["rotm"], B["ktmp"]], writes=[b_ps])
    P.add("dve", lambda e: e.tensor_tensor(out=kt1[64:96, :], in0=kg32[64:96, :], in1=rc_s[64:96, :], op=MUL), reads=[B["ktmp"], B["rcs"]],
          writes=[B["ktmp"]])
    P.add("dve", lambda e, ps=ps: e.tensor_tensor(out=kt2[64:96, :], in0=ps[64:96, 0:NST], in1=rs_s[64:96, :], op=MUL),
          reads=[b_ps, B["rcs"]], writes=[B["ktmp"]])
    P.add("dve", lambda e: e.tensor_tensor(out=knew[64:96, :], in0=kt1[64:96, :], in1=kt2[64:96, :], op=ADD), reads=[B["ktmp"]],
          writes=[B["knew"]])

    idb = AR.alloc([128, 128], BF16)
    b_idb = P.buf("idb")
    P.add("pool", lambda e: e.tensor_copy(out=idb[:], in_=sm[:, SL["ident"][0]:SL["ident"][0] + 128]), reads=[b_sm], writes=[b_idb])
    sqk2 = [sqk, AR.alloc([32, 512], BF16)]
    lnr2 = [lnr, AR.alloc([32, 512], F32)]
    rr2 = [rr_, AR.alloc([32, 512], F32)]
    sr2 = [sr, AR.alloc([32, 512], F32)]
    Pm2 = [Pm, AR.alloc([32, 512], BF16)]
    Lacc2 = [Lacc, AR.alloc([32, 20], F32)]
    Bq = [{n: P.buf(f"s2_{n}{i}") for n in ("sqk", "lnr", "rr", "sr", "Pm")} for i in range(2)]
    b_Lacc2 = [P.buf("Lacc0"), P.buf("Lacc1")]
    cnt = [0]
    gcnt = [0]

    def tbank2():
        i = 2 + rrb[0] % 6
        rrb[0] += 1
        return banks[i], bank_bufs[i]

    def chunk(q, col, npos, cT, b_cTs, kraw_ap, b_kraw, krop_ap, b_krop, crows, first, mask):
        n = npos
        ACC, b_ACC = banks[q % 2], bank_bufs[q % 2]
        Lq, b_Lq = Lacc2[q % 2], b_Lacc2[q % 2]
        ci = gcnt[0] % 2
        gcnt[0] += 1
        sqk_, lnr_, rr__, sr_, Pm_ = sqk2[ci], lnr2[ci], rr2[ci], sr2[ci], Pm2[ci]
        Bc = Bq[ci]
        pss_l = []
        for m in range(4):
            ps, b_ps = tbank2()
            for kt in range(2):
                P.add("pe", lambda e, ps=ps, m=m, kt=kt: e.matmul(ps[:, 0:n], lhsT=wk_sb[:, kt, 128 * m:128 * m + 128], rhs=cT[:, kt, 0:n],
                                                                  start=(kt == 0), stop=(kt == 1)), reads=[B["wk"]] + b_cTs, writes=[b_ps])
            pss_l.append((ps, b_ps))
        for m in range(4):
            ps, b_ps = pss_l[m]
            if m < 2:
                P.add("act", lambda e, ps=ps, m=m: e.activation(out=kn_sb[m][:, 0:n], in_=ps[:, 0:n], func=AF.Copy), reads=[b_ps],
                      writes=[b_kn[m]])
            else:
                P.add("dve", lambda e, ps=ps, m=m: e.tensor_copy(out=kn_sb[m][:, 0:n], in_=ps[:, 0:n]), reads=[b_ps], writes=[b_kn[m]])
            P.add("dve", lambda e, m=m: e.tensor_tensor(out=sq_sb[m][:, 0:n], in0=kn_sb[m][:, 0:n], in1=kn_sb[m][:, 0:n], op=MUL),
                  reads=[b_kn[m]], writes=[b_sq[m]])
        P.add("pool", lambda e: e.tensor_tensor(out=sqk_[:, 0:n], in0=kraw_ap, in1=kraw_ap, op=MUL), reads=b_kraw, writes=[Bc["sqk"]])
        yield
        pss, b_pss = tbank2()
        for m in range(4):
            P.add("pe", lambda e, m=m: e.matmul(pss[0:32, 0:n], lhsT=hselm[:, m, :], rhs=sq_sb[m][:, 0:n], start=(m == 0), stop=False),
                  reads=[B["hselm"], b_sq[m]], writes=[b_pss])
        P.add("pe", lambda e: e.matmul(pss[0:32, 0:n], lhsT=ones_bf[0:32, 0:32], rhs=sqk_[:, 0:n], start=False, stop=True),
              reads=[b_ones, Bc["sqk"]], writes=[b_pss])
        psc, b_psc = tbank2()
        for kt in range(2):
            P.add("pe", lambda e, kt=kt: e.matmul(psc[0:32, 0:n], lhsT=Qabs[:, kt, q, :], rhs=cT[:, kt, 0:n], start=(kt == 0), stop=False),
                  reads=[B["Qabs"]] + b_cTs, writes=[b_psc])
        P.add("pe", lambda e: e.matmul(psc[0:32, 0:n], lhsT=Qrope[64:96, q, :], rhs=krop_ap, start=False, stop=True, tile_position=(64, 0)),
              reads=[B["Qrope"]] + b_krop, writes=[b_psc])
        P.add("act", lambda e: e.activation(out=lnr_[:, 0:n], in_=pss[0:32, 0:n], func=AF.Ln, scale=1.0 / QK_HEAD, bias=EPS),
              reads=[b_pss], writes=[Bc["lnr"]])
        P.add("act", lambda e: e.activation(out=rr__[:, 0:n], in_=lnr_[:, 0:n], func=AF.Exp, scale=-0.5), reads=[Bc["lnr"]],
              writes=[Bc["rr"]])
        P.add("dve", lambda e: e.tensor_tensor(out=sr_[:, 0:n], in0=psc[0:32, 0:n], in1=rr__[:, 0:n], op=MUL), reads=[b_psc, Bc["rr"]],
              writes=[Bc["sr"]])
        if mask:
            P.add("act", lambda e: e.activation(out=sr_[:, 0:n], in_=sr_[:, 0:n], func=AF.Exp, scale=SCALE), reads=[Bc["sr"]],
                  writes=[Bc["sr"]])
            P.add("dve", lambda e: e.tensor_tensor(out=sr_[:, 0:n], in0=sr_[:, 0:n], in1=cmask[:, 0:n], op=MUL), reads=[Bc["sr"], B["cmask"]],
                  writes=[Bc["sr"]])
            P.add("dve", lambda e: e.tensor_copy(out=Pm_[:, 0:n], in_=sr_[:, 0:n]), reads=[Bc["sr"]], writes=[Bc["Pm"]])
            P.add("dve", lambda e: e.reduce_sum(out=Lq[:, col:col + 1], in_=sr_[:, 0:n], axis=AX.X), reads=[Bc["sr"]], writes=[b_Lq])
        else:
            P.add("act", lambda e: e.activation(out=Pm_[:, 0:n], in_=sr_[:, 0:n], func=AF.Exp, scale=SCALE, accum_out=Lq[:, col:col + 1]),
                  reads=[Bc["sr"]], writes=[Bc["Pm"], b_Lq])
        yield
        pT_, b_pT = tbank2()
        pTb = pT_[:].bitcast(BF16)
        nblk = len(crows)
        for bi, (cap, b_cap, c0, c1) in enumerate(crows):
            P.add("pe", lambda e, bi=bi, c0=c0, c1=c1: e.transpose(pTb[0:c1 - c0, 32 * bi:32 * bi + 32], Pm_[:, c0:c1], idb[0:32, 0:32]),
                  reads=[Bc["Pm"], b_idb], writes=[b_pT])
        pi_ = cnt[0] % 2
        cnt[0] += 1
        rows = crows[0][3] - crows[0][2]
        P.add("act", lambda e, pi_=pi_: e.activation(out=PT_sb[pi_][0:rows, 0:nblk, :],
                                                     in_=pTb[0:rows, 0:32 * nblk].rearrange("p (b c) -> p b c", c=32), func=AF.Copy),
              reads=[b_pT], writes=[b_PT[pi_]])
        yield
        for bi, (cap, b_cap, c0, c1) in enumerate(crows):
            P.add("pe", lambda e, bi=bi, cap=cap, c0=c0, c1=c1, pi_=pi_, st=(first and bi == 0): e.matmul(
                ACC[0:32, 0:KV_LORA], lhsT=PT_sb[pi_][0:c1 - c0, bi, :], rhs=cap, start=st, stop=False),
                reads=[b_PT[pi_]] + b_cap, writes=[b_ACC])

    pgc = [0]
    kcn = [0]

    def page_chunk(q, g):
        bi_ = pgc[0] % NB_PG
        ki = pgc[0] % 2
        ci_ = pgc[0] % 2
        pgc[0] += 1
        pbt, b_pbt = pb[bi_], b_pb[bi_]
        ch = q * 16 + g
        P.add("pool", lambda e: e.indirect_dma_start(
            out=pbt[:].rearrange("p a c -> p (a c)"), out_offset=None, in_=cache,
            in_offset=bass.IndirectOffsetOnAxis(ap=idx[:, ch:ch + 1], axis=0)),
            reads=[B["idx"]], writes=[b_pbt], dma=True, semkey=f"pb{bi_}")
        yield
        yield
        yield
        yield
        ki = kcn[0] % 2
        kcn[0] += 1
        k3, b_k3 = kt3[ki], b_kt3[ki]
        kr1, kr2 = pbt[:, :, KV_LORA:KV_LORA + 16], pbt[:, :, KV_LORA + 16:KV_LORA + 32]
        pgs = slice(4 * g, 4 * g + 4)
        pl = lambda fn, rd, wr: P.add("pool", fn, reads=rd, writes=wr)
        pl(lambda e: e.tensor_copy(out=k3[:, :, 0:32], in_=pbt[:, :, KV_LORA:CW]), [b_pbt], [b_k3])
        pl(lambda e: e.tensor_tensor(out=rt[0][:], in0=kr1, in1=tabs[:, 0, pgs, :], op=MUL), [b_pbt, B["tabs"]], [b_rt])
        pl(lambda e: e.tensor_tensor(out=rt[1][:], in0=kr2, in1=tabs[:, 2, pgs, :], op=MUL), [b_pbt, B["tabs"]], [b_rt])
        pl(lambda e: e.tensor_tensor(out=k3[:, :, 64:80], in0=rt[0][:], in1=rt[1][:], op=ADD), [b_rt], [b_k3])
        pl(lambda e: e.tensor_tensor(out=rt[2][:], in0=kr2, in1=tabs[:, 1, pgs, :], op=MUL), [b_pbt, B["tabs"]], [b_rt])
        pl(lambda e: e.tensor_tensor(out=rt[3][:], in0=kr1, in1=tabs[:, 3, pgs, :], op=MUL), [b_pbt, B["tabs"]], [b_rt])
        pl(lambda e: e.tensor_tensor(out=k3[:, :, 80:96], in0=rt[2][:], in1=rt[3][:], op=ADD), [b_rt], [b_k3])
        yield
        psT, b_psT = tbank2()
        psTb = psT[:].bitcast(BF16).rearrange("p (k n) -> p k n", k=2)
        psK, b_psK = tbank2()
        psKb = psK[:].bitcast(BF16)
        for pg in range(4):
            for kt in range(2):
                P.add("pe", lambda e, pg=pg, kt=kt: e.transpose(psTb[:, kt, 128 * pg:128 * pg + 128], pbt[:, pg, 128 * kt:128 * kt + 128],
                                                                idb[:, :]), reads=[b_pbt, b_idb], writes=[b_psT])
            P.add("pe", lambda e, pg=pg: e.transpose(psKb[0:96, 128 * pg:128 * pg + 128], k3[:, pg, :], idb[:, :]), reads=[b_k3, b_idb],
                  writes=[b_psK])
        P.add("dve", lambda e: e.tensor_copy(out=cT_sb[ci_][:], in_=psTb), reads=[b_psT], writes=[b_cT[ci_]])
        P.add("act", lambda e: e.activation(out=krT_sb[ci_][:], in_=psKb[0:96, 0:512], func=AF.Copy), reads=[b_psK], writes=[b_krT[ci_]])
        yield
        crows = [(pbt[:, pg, 0:KV_LORA], [b_pbt], 128 * pg, 128 * pg + 128) for pg in range(4)]
        yield from chunk(q, g, 512, cT_sb[ci_], [b_cT[ci_]], krT_sb[ci_][0:32, 0:512], [b_krT[ci_]], krT_sb[ci_][64:96, 0:512],
                         [b_krT[ci_]], crows, g == 0, False)

    def run_pipelined(gens, step=2):
        active = []
        it = iter(gens)
        more = True
        while more or active:
            if more:
                try:
                    active.append(next(it))
                except StopIteration:
                    more = False
            for _ in range(step):
                for gg in list(active):
                    try:
                        next(gg)
                    except StopIteration:
                        active.remove(gg)

    for q in range(SEQ_PER_CORE):
        run_pipelined([page_chunk(q, g) for g in range(NPAGES // 4)])
        ACC, b_ACC = banks[q % 2], bank_bufs[q % 2]
        Lq, b_Lq = Lacc2[q % 2], b_Lacc2[q % 2]
        c0 = NPT + 4 * q
        psn, b_psn = tbank2()
        psnb = psn[:].bitcast(BF16)
        for kt in range(2):
            P.add("pe", lambda e, kt=kt, c0=c0, psnb=psnb: e.transpose(psnb[0:4, 128 * kt:128 * kt + 128], ckvn[:, kt, c0:c0 + 4], idb[:, :]),
                  reads=[b_ckvn, b_idb], writes=[b_psn])
        P.add("act", lambda e, psnb=psnb: e.activation(out=cnew[:], in_=psnb[0:4, 0:KV_LORA], func=AF.Copy), reads=[b_psn], writes=[B["cnew"]])
        for _ in chunk(q, 16, 4, ckvn[:, :, c0:c0 + 4], [b_ckvn], kraw[64:96, 4 * q:4 * q + 4], [B["kraw"]], knew[64:96, 4 * q:4 * q + 4],
                       [B["knew"]], [(cnew[:, :], [B["cnew"]], 0, 4)], False, True):
            pass
        P.add("act", lambda e, ACC=ACC: e.activation(out=accs[:], in_=ACC[0:32, 0:KV_LORA], func=AF.Copy), reads=[b_ACC], writes=[B["accs"]])
        P.add("dve", lambda e, Lq=Lq: e.reduce_sum(out=lsum[:], in_=Lq[:, 0:17], axis=AX.X), reads=[b_Lq], writes=[B["lsum"]])
        P.add("dve", lambda e: e.reciprocal(out=lsum[:], in_=lsum[:]), reads=[B["lsum"]], writes=[B["lsum"]])
        P.add("dve", lambda e: e.tensor_scalar(out=olat[:], in0=accs[:], scalar1=lsum[:, 0:1], scalar2=None, op0=MUL),
              reads=[B["accs"], B["lsum"]], writes=[B["olat"]])
        pso, b_pso = tbank2()
        psob = pso[:].bitcast(BF16)
        for kt in range(2):
            P.add("pe", lambda e, kt=kt, psob=psob: e.transpose(psob[:, 32 * kt:32 * kt + 32], olat[:, 128 * kt:128 * kt + 128],
                                                                idb[0:32, 0:32]), reads=[B["olat"], b_idb], writes=[b_pso])
        P.add("act", lambda e, q=q, psob=psob: e.activation(out=olT[:, :, q, :], in_=psob[:, 0:64].rearrange("p (k c) -> p k c", k=2),
                                                            func=AF.Copy), reads=[b_pso], writes=[B["olT"]])
    for hp in range(4):
        ps, b_ps = tbank()
        for hh in range(2):
            h = 2 * hp + hh
            for kt in range(2):
                P.add("pe", lambda e, ps=ps, hh=hh, h=h, kt=kt: e.matmul(
                    ps[64 * hh:64 * hh + 64, 0:NST], lhsT=wv_sb[:, kt, 64 * h:64 * h + 64], rhs=olT[:, kt, :, 4 * h:4 * h + 4],
                    start=(kt == 0), stop=(kt == 1), tile_position=(0, 64 * hh)), reads=[B["wv"], B["olT"]], writes=[b_ps])
        P.add("act", lambda e, ps=ps, hp=hp: e.activation(out=attT[:, hp, NPT:T], in_=ps[:, 0:NST], func=AF.Copy), reads=[b_ps],
              writes=[b_attT])


def build(stage=99, n_pool=10240, dbg=False):
    nc = bass.Bass("TRN2", target_bir_lowering=False)
    P = Prog(nc)
    ins_, outs_ = {}, {}

    def din(name, shape, dt=F32):
        ins_[name] = nc.dram_tensor(name, list(shape), dt, kind="ExternalInput").ap()
        return ins_[name]

    def dout(name, shape, dt=F32):
        outs_[name] = nc.dram_tensor(name, list(shape), dt, kind="ExternalOutput").ap()
        return outs_[name]

    def dint(name, shape, dt):
        return nc.dram_tensor(name, list(shape), dt).ap()

    xT = din("xT", [D, T])
    small = din("small", [128, SL["_n"]])
    rope_c = din("rope_c", [96, T])
    rope_s = din("rope_s", [96, T])
    w_in = din("w_in", [D, IN_COLS])
    o_ckvT = dout("o_ckvT", [KV_LORA, T])
    o_krT = dout("o_krT", [QK_ROPE, T])

    w_in_b = dint("w_in_b", [D, IN_COLS], BF16)

    stores = []
    pool_q = "pool"

    def cast_w(dst, src, rows, cols, key):
        a = 1
        while cols // a > 2048 or cols % a:
            a += 1
        s2 = src.rearrange("k (a m) -> (k a) m", a=a) if a > 1 else src
        d2 = dst.rearrange("k (a m) -> (k a) m", a=a) if a > 1 else dst
        b = P.buf(key)
        P.add(pool_q, lambda e: e.dma_start(out=d2, in_=s2), writes=[b], dma=True, semkey=key)
        return b

    b_w_in_b = cast_w(w_in_b, w_in, D, IN_COLS, "c_w_in")
    w_glu = din("w_glu", [SSM_W, 2 * SSM_W])
    w_glu_b = dint("w_glu_b", [SSM_W, 2 * SSM_W], BF16)
    b_w_glu_b = cast_w(w_glu_b, w_glu, SSM_W, 2 * SSM_W, "c_w_glu")

    ones_bf = P.sbuf("ones_bf", [128, 128], BF16)
    b_ones = P.buf("ones")
    P.add("pool", lambda e: e.memset(ones_bf[:], 1.0), writes=[b_ones])
    sm = P.sbuf("sm", [128, SL["_n"]], F32)
    b_sm = P.buf("sm")
    P.add("sp", lambda e: e.dma_start(out=sm[:], in_=small), writes=[b_sm], dma=True, semkey="sm")

    def smc(name, i=0):
        o = SL[name][0] + i
        return sm[:, o:o + 1]

    NB = 8
    banks = [P.psum(f"ps{i}", [128, 512], F32) for i in range(NB)]
    bank_bufs = [P.buf(f"ps{i}") for i in range(NB)]
    bank_rr = [0]

    def next_bank():
        i = bank_rr[0] % NB
        bank_rr[0] += 1
        return banks[i], bank_bufs[i]

    cqn = P.sbuf("cqn", [128, 3, T], BF16)
    ckvn = P.sbuf("ckvn", [128, 2, T], BF16)
    krK = P.sbuf("krK", [96, T], F32)
    uT = P.sbuf("uT", [128, 4, T], BF16)
    b_cqn, b_ckvn, b_krT, b_uT = P.buf("cqn"), P.buf("ckvn"), P.buf("krT"), P.buf("uT")

    chunks = _token_chunks()
    AR = Arena(P, "arena", 142 * 1024)

    NA = OFF_GA
    wA = AR.alloc([128, KD, NA], BF16)
    b_wA = P.buf("wA")
    P.add("sp", lambda e: e.dma_start(out=wA[:], in_=w_in_b.rearrange("(k p) m -> p k m", p=128)[:, :, 0:NA]),
          reads=[b_w_in_b], writes=[b_wA], dma=True, semkey="wA")

    xc = [AR.alloc([128, KD, 512], F32) for i in range(2)]
    b_xc = [P.buf(f"xc{i}") for i in range(2)]
    sq = AR.alloc([128, KD, 512], BF16)
    b_sq = P.buf("sq")
    lnv = AR.alloc([128, 512], F32)
    b_lnv = P.buf("lnv")
    rstd = AR.alloc([128, 512], F32)
    b_rstd = P.buf("rstd")
    xn = AR.alloc([128, KD, 512], BF16)
    b_xn = P.buf("xn")
    cqf = AR.alloc([128, 3, 512], F32)
    b_cqf = P.buf("cqf")
    ckvf = AR.alloc([128, 2, 512], F32)
    b_ckvf = P.buf("ckvf")
    ckvo = AR.alloc([128, 2, 512], F32)
    b_ckvo = P.buf("ckvo")

    xT_v = xT.rearrange("(k p) t -> p k t", p=128)

    def rms_rstd(src, b_src, nk, n, nfeat, dst=rstd, b_dst=b_rstd):
        P.add("dve", lambda e: e.tensor_tensor(out=sq[:, 0:nk, 0:n], in0=src[:, 0:nk, 0:n], in1=src[:, 0:nk, 0:n],
                                               op=ALU.mult), reads=[b_src], writes=[b_sq])
        ps, b_ps = next_bank()
        for k in range(nk):
            P.add("pe", lambda e, k=k: e.matmul(ps[:, 0:n], lhsT=ones_bf[:], rhs=sq[:, k, 0:n], start=(k == 0),
                                                stop=(k == nk - 1)), reads=[b_sq, b_ones], writes=[b_ps])
        P.add("act", lambda e: e.activation(out=lnv[:, 0:n], in_=ps[:, 0:n], func=AF.Ln, scale=1.0 / nfeat, bias=EPS),
              reads=[b_ps], writes=[b_lnv])
        P.add("act", lambda e: e.activation(out=dst[:, 0:n], in_=lnv[:, 0:n], func=AF.Exp, scale=-0.5),
              reads=[b_lnv], writes=[b_dst])

    for ci, (n0, n) in enumerate(chunks):
        xb, b_x = xc[ci % 2], b_xc[ci % 2]
        P.add("sp", lambda e, xb=xb, n0=n0, n=n: e.dma_start(out=xb[:, :, 0:n], in_=xT_v[:, :, n0:n0 + n]),
              writes=[b_x], dma=True, semkey=f"xc{ci % 2}")
        rms_rstd(xb, b_x, KD, n, D)
        for k in range(KD):
            P.add("dve", lambda e, k=k, xb=xb, n=n: e.scalar_tensor_tensor(
                out=xn[:, k, 0:n], in0=xb[:, k, 0:n], scalar=smc("g_mix", k), in1=rstd[:, 0:n],
                op0=ALU.mult, op1=ALU.mult), reads=[b_x, b_rstd, b_sm], writes=[b_xn])
        groups = [("cq", i, OFF_CKV * 0 + 128 * i, 128) for i in range(3)] + \
                 [("ckv", i, OFF_CKV + 128 * i, 128) for i in range(2)] + \
                 [("kr", 0, OFF_KR, 32)] + [("u", i, OFF_U + 128 * i, 128) for i in range(4)]
        for kind, i, c0, m in groups:
            ps, b_ps = next_bank()
            for k in range(KD):
                if kind == "kr":
                    P.add("pe", lambda e, k=k, c0=c0, m=m, ps=ps, n=n: e.matmul(
                        ps[64:96, 0:n], lhsT=wA[:, k, c0:c0 + m], rhs=xn[:, k, 0:n], start=(k == 0), stop=(k == KD - 1),
                        tile_position=(0, 64)), reads=[b_wA, b_xn], writes=[b_ps])
                    continue
                P.add("pe", lambda e, k=k, c0=c0, m=m, ps=ps, n=n: e.matmul(
                    ps[0:m, 0:n], lhsT=wA[:, k, c0:c0 + m], rhs=xn[:, k, 0:n], start=(k == 0), stop=(k == KD - 1)),
                    reads=[b_wA, b_xn], writes=[b_ps])
            if kind == "cq":
                P.add("act", lambda e, i=i, ps=ps, n=n: e.activation(out=cqf[:, i, 0:n], in_=ps[:, 0:n], func=AF.Copy),
                      reads=[b_ps], writes=[b_cqf])
            elif kind == "ckv":
                P.add("act", lambda e, i=i, ps=ps, n=n: e.activation(out=ckvf[:, i, 0:n], in_=ps[:, 0:n], func=AF.Copy),
                      reads=[b_ps], writes=[b_ckvf])
            elif kind == "kr":
                P.add("act", lambda e, ps=ps, n=n, n0=n0: e.activation(out=krK[64:96, n0:n0 + n], in_=ps[64:96, 0:n], func=AF.Copy),
                      reads=[b_ps], writes=[b_krT])
            else:
                P.add("act", lambda e, i=i, ps=ps, n=n, n0=n0: e.activation(out=uT[:, i, n0:n0 + n], in_=ps[:, 0:n], func=AF.Copy),
                      reads=[b_ps], writes=[b_uT])
        rms_rstd(cqf, b_cqf, 3, n, Q_LORA)
        for i in range(3):
            P.add("dve", lambda e, i=i, n=n, n0=n0: e.scalar_tensor_tensor(
                out=cqn[:, i, n0:n0 + n], in0=cqf[:, i, 0:n], scalar=smc("g_cq", i), in1=rstd[:, 0:n],
                op0=ALU.mult, op1=ALU.mult), reads=[b_cqf, b_rstd, b_sm], writes=[b_cqn])
        rms_rstd(ckvf, b_ckvf, 2, n, KV_LORA)
        for i in range(2):
            P.add("dve", lambda e, i=i, n=n: e.scalar_tensor_tensor(
                out=ckvo[:, i, 0:n], in0=ckvf[:, i, 0:n], scalar=smc("g_ckv", i), in1=rstd[:, 0:n],
                op0=ALU.mult, op1=ALU.mult), reads=[b_ckvf, b_rstd, b_sm], writes=[b_ckvo])
        P.add("pool", lambda e, n=n, n0=n0: e.tensor_copy(out=ckvn[:, :, n0:n0 + n], in_=ckvo[:, :, 0:n]),
              reads=[b_ckvo], writes=[b_ckvn])
        stores.append(P.add("sp", lambda e, n=n, n0=n0: e.dma_start(
            out=o_ckvT.rearrange("(k p) t -> p k t", p=128)[:, :, n0:n0 + n], in_=ckvo[:, :, 0:n]),
            reads=[b_ckvo], dma=True, semkey="st_ckv"))
    stores.append(P.add("sp", lambda e: e.dma_start(out=o_krT, in_=krK[64:96, :]), reads=[b_krT], dma=True, semkey="st_kr"))


    if stage >= 2:
        ssm = build_ssm(P, AR, nc, din, dout, dint, stores, next_bank, uT, b_uT, smc, b_sm, sm, w_glu_b, b_w_glu_b, chunks)
        if dbg:
            o_dbg_ys = dout("o_dbg_ys", [128, 4, T], BF16)
            stores.append(P.add("sp", lambda e: e.dma_start(out=o_dbg_ys, in_=uT[:]), reads=[b_uT], dma=True, semkey="dbg_ys"))


    if stage >= 3:
        attT = P.sbuf("attT", [128, 4, T], BF16)
        b_attT = P.buf("attT")
        kS = P.sbuf("kS", [96, 8, NST], BF16)
        b_kS = P.buf("kS")
        att = build_attn(P, AR, nc, din, dout, dint, stores, banks, bank_bufs, cast_w, cqn, b_cqn, ckvn, b_ckvn, krK, b_krT,
                         sm, b_sm, smc, ones_bf, b_ones, rope_c, rope_s, attT, b_attT, chunks, kS, b_kS)
        if stage >= 5:
            build_sample_attn(P, AR, nc, din, dout, dint, stores, banks, bank_bufs, sm, b_sm, smc, ones_bf, b_ones, ckvn, b_ckvn,
                              krK, b_krT, att["qT"], att["b_qT"], attT, b_attT, n_pool, att["w_uk_b"], att["b_wkb"], att["w_uv_b"],
                              att["b_wvb"], rope_c, rope_s, att["rotm_d"], att["mC1"])
        if dbg:
            o_dbg_att = dout("o_dbg_att", [128, 4, T], BF16)
            stores.append(P.add("sp", lambda e: e.dma_start(out=o_dbg_att, in_=attT[:]), reads=[b_attT], dma=True, semkey="dbg_att"))


    if stage >= 4:
        build_tail(P, AR, nc, din, dout, dint, stores, banks, bank_bufs, cast_w, xT, w_in_b, b_w_in_b, sm, b_sm, smc, ones_bf, b_ones,
                   attT, b_attT, ssm["ysT"], ssm["b_ys"], chunks)

    P.add("sp", lambda e: None, after=stores)
    P.finalize()
    return nc, ins_, outs_, P


def _rope_tables(pos):
    inv_freq = np.power(np.float32(10000.0), -np.arange(0, QK_ROPE, 2, dtype=np.float32) / np.float32(QK_ROPE)).astype(np.float32)
    ang = pos.astype(np.float32)[:, None] * inv_freq[None, :]
    return np.cos(ang).astype(np.float32), np.sin(ang).astype(np.float32)


def _prep_core(c, inp):
    b, j = c // 4, c % 4
    m = {}
    xp = inp["x_prompt"][b, NPT * j:NPT * (j + 1)]
    xs = inp["x_sample"][SEQ_PER_CORE * c:SEQ_PER_CORE * (c + 1)].reshape(NST, D)
    m["xT"] = np.ascontiguousarray(np.concatenate([xp, xs], 0).T)
    pos = np.concatenate([NPT * j + np.arange(NPT), np.tile(PAST + np.arange(4), SEQ_PER_CORE)])
    cs, sn = _rope_tables(pos)
    rc = np.ones((96, T), np.float32)
    rs = np.zeros((96, T), np.float32)
    rc[64:80] = cs.T
    rc[80:96] = cs.T
    rs[64:80] = sn.T
    rs[80:96] = sn.T
    m["rope_c"], m["rope_s"] = rc, rs
    sm = np.zeros((128, SL["_n"]), np.float32)

    def put(name, arr):
        o, n = SL[name]
        sm[:arr.shape[0], o:o + arr.shape[1]] = arr
    put("g_mix", inp["g_mix"][0].reshape(8, 128).T)
    put("g_cq", inp["g_cq"][0].reshape(3, 128).T)
    put("g_ckv", inp["g_ckv"][0].reshape(2, 128).T)
    put("g_ffn", inp["g_ffn"][0].reshape(8, 128).T)
    put("g_ple", inp["g_ple"][0].reshape(8, 128).T)
    cw = inp["conv_w"][0].reshape(3, 44, 128)
    put("conv_w", cw.transpose(2, 0, 1).reshape(128, 132))
    put("conv_b", inp["conv_b"][0].reshape(44, 128).T)
    put("g_q", inp["g_q"][0].reshape(96, 1))
    put("g_k", inp["g_k"][0].reshape(96, 1))
    put("d_skip", inp["d_skip"][0].reshape(4, 128).T)
    put("vis", np.tile((np.arange(4) <= j).astype(np.float32)[None], (128, 1)))
    put("full", np.tile((np.arange(4) < j).astype(np.float32)[None], (128, 1)))
    put("ident", np.eye(128, dtype=np.float32))
    put("hsel", np.tile((np.arange(4) == j - 1).astype(np.float32)[None], (128, 1)))
    m["small"] = sm
    m["w_in"] = inp["w_in"][0]
    m["w_glu"] = inp["w_glu"][0]
    m["w_uq"] = inp["w_uq"][0].reshape(Q_LORA, 768)
    m["w_oa"], m["w_os"], m["w_out"] = inp["w_oa"][0], inp["w_os"][0], inp["w_out"][0]
    m["w_up"], m["w_down"] = inp["w_up"][0], inp["w_down"][0]
    m["w_pg"], m["w_pp"] = inp["w_ple_gate"][0], inp["w_ple_proj"][0]
    pp_ = inp["p_prompt"][0, b, NPT * j:NPT * (j + 1)]
    ps_ = inp["p_sample"][0, SEQ_PER_CORE * c:SEQ_PER_CORE * (c + 1)].reshape(NST, PLE)
    m["pT"] = np.ascontiguousarray(np.concatenate([pp_, ps_], 0).T)
    sc = inp["state_conv"][0, SEQ_PER_CORE * c:SEQ_PER_CORE * (c + 1)]
    m["scT"] = np.ascontiguousarray(sc.reshape(16, 2, 44, 128).transpose(3, 2, 0, 1))
    m["w_uk"] = inp["w_uk"][0].reshape(KV_LORA, 512)
    m["w_uv"] = inp["w_uv"][0].reshape(KV_LORA, 512)
    tri = (np.arange(128)[:, None] <= np.arange(128)[None, :]).astype(np.float32)
    md = np.zeros((128, 4, 128), np.float32)
    for r in range(4):
        vis, full = float(r <= j), float(r < j)
        md[:, r, :] = full + (vis - full) * tri
    m["maskd"] = md
    pt_ = inp["page_table"][SEQ_PER_CORE * c:SEQ_PER_CORE * (c + 1)].astype(np.int32).reshape(16, 16, 4)
    m["ptab"] = np.ascontiguousarray(np.repeat(pt_.transpose(2, 0, 1).reshape(4, 256), 32, axis=0))
    m["p32c"] = (np.arange(128, dtype=np.int32) % 32).reshape(128, 1)
    m["w_ukT"] = np.ascontiguousarray(inp["w_uk"][0].transpose(2, 1, 0).reshape(64, 8 * KV_LORA))
    cs_p, sn_p = _rope_tables(np.arange(PAST))
    rp = np.stack([cs_p, sn_p], 0).reshape(2, 16, 4, 32, 4, 16)
    m["ropeP"] = np.ascontiguousarray(rp.transpose(2, 3, 0, 1, 4, 5).reshape(128, 2, NPAGES, 16))
    m["gk_rep"] = np.tile(inp["g_k"][0, 64:96][None], (128, 1)).astype(np.float32)
    hs_ = np.zeros((128, 4, 32), np.float32)
    for mm in range(4):
        for hh in range(2):
            hs_[64 * hh:64 * hh + 64, mm, 4 * (2 * mm + hh):4 * (2 * mm + hh) + 4] = 1.0
    m["hselm"] = hs_
    cm = np.zeros((32, 4), np.float32)
    for h_ in range(8):
        for t_ in range(4):
            cm[4 * h_ + t_, :t_ + 1] = 1.0
    m["cmask"] = cm
    rot = np.zeros((96, 96), np.float32)
    for i in range(16):
        rot[80 + i, 64 + i] = -1.0
        rot[64 + i, 80 + i] = 1.0
    m["rotm"] = rot
    m["ssm_s"], m["ssm_r"] = _ssm_packs(c, inp)
    return m


def _ssm_packs(c, inp):
    j = c % 4
    a_re, a_im, logdt = inp["a_re"][0], inp["a_im"][0], inp["log_dt"][0]
    b_re, b_im, c_re, c_im = inp["b_re"][0], inp["b_im"][0], inp["c_re"][0], inp["c_im"][0]

    def st(a):
        return a.reshape(16, 2, 64).transpose(1, 2, 0).reshape(128, 16)
    ss = np.zeros((128, SSL["_n"]), np.float32)

    def put(lay, arr_, name, arr):
        o, n = lay[name]
        arr_[:, o:o + n] = arr.reshape(128, n)
    put(SSL, ss, "a_re", st(a_re))
    put(SSL, ss, "a_im", st(a_im))
    put(SSL, ss, "logdt", st(np.repeat(logdt[:, None], 64, 1)))
    for nm, cc in (("c_re", c_re), ("c_im", c_im)):
        c4 = cc.reshape(16, 2, 16, 64)
        pad = np.zeros((2, 64, 16, 2, 16), np.float32)
        for g2 in range(2):
            pad[g2, :, :, g2, :] = c4[:, g2].transpose(2, 0, 1)
        put(SSL, ss, nm, pad)
    for nm, bb in (("b_re", b_re), ("b_im", b_im)):
        b4 = bb.reshape(16, 2, 64, 16)
        pad = np.zeros((2, 64, 16, 2, 16), np.float32)
        for g2 in range(2):
            pad[g2, :, :, g2, :] = b4[:, g2].transpose(1, 0, 2)
        put(SSL, ss, nm, pad)
    for nm, key in (("h0_re", "state_ssm_re"), ("h0_im", "state_ssm_im")):
        h = inp[key][0, SEQ_PER_CORE * c:SEQ_PER_CORE * (c + 1)]
        h4 = h.reshape(16, 16, 2, 64)
        put(SSL, ss, nm, h4.transpose(2, 3, 1, 0))
    m = np.zeros((128, 12), np.float32)
    for i in range(4):
        n = j - 1 - i
        if 0 <= n <= 2:
            m[:, 3 * i + n] = 1.0
    put(SSL, ss, "msk", m)
    blk = np.zeros((128, 4), np.float32)
    for k4 in range(4):
        blk[32 * k4:32 * k4 + 32, k4] = 1.0
    put(SSL, ss, "blk", blk)
    sr = np.zeros((128, SRL["_n"]), np.float32)

    def rowrep(a):
        a4 = a.reshape(4, 4, 2, 64)
        out = np.zeros((4, 2, 16, 4, 64), np.float32)
        out[:] = a4.transpose(1, 2, 0, 3)[:, :, None, :, :]
        return out
    put(SRL, sr, "a_re", rowrep(a_re))
    put(SRL, sr, "a_im", rowrep(a_im))
    put(SRL, sr, "logdt", rowrep(np.repeat(logdt[:, None], 64, 1)))
    for nm, bb in (("b_re", b_re), ("b_im", b_im)):
        b5 = bb.reshape(4, 4, 2, 64, 16)
        pad = np.zeros((4, 2, 16, 4, 2, 64), np.float32)
        for g2 in range(2):
            pad[:, g2, :, :, g2, :] = b5[:, :, g2].transpose(1, 3, 0, 2)
        put(SRL, sr, nm, pad)
    put(SRL, sr, "ident", np.eye(128, dtype=np.float32))
    return ss, sr


_CACHE = {}


def kernel(**inputs):
    inp = {k: np.asarray(v) for k, v in inputs.items()}
    if "nc" not in _CACHE:
        _CACHE["nc"] = build()
    nc, ins_, outs_, P = _CACHE["nc"]
    in_maps = []
    cache = None
    if "cache" in ins_:
        cache = np.concatenate([inp["cache_ckv"][0], inp["cache_kr"][0]], axis=-1).reshape(-1, 4 * (KV_LORA + QK_ROPE))
    for c in range(8):
        m = _prep_core(c, inp)
        if cache is not None:
            m["cache"] = cache
        in_maps.append({k: np.ascontiguousarray(m[k]) for k in ins_})
    res = run_bass_kernel_spmd(nc, in_maps, core_ids=list(range(8)))
    R = res.results
    f32 = np.float32
    ckv_p = np.zeros((1, 2, 8192, KV_LORA), f32)
    kr_p = np.zeros((1, 2, 8192, QK_ROPE), f32)
    ckv_s = np.zeros((1, 128, 4, KV_LORA), f32)
    kr_s = np.zeros((1, 128, 4, QK_ROPE), f32)
    for c in range(8):
        b, j = c // 4, c % 4
        ck = R[c]["o_ckvT"].T
        kr = R[c]["o_krT"].T
        ckv_p[0, b, NPT * j:NPT * (j + 1)] = ck[:NPT]
        kr_p[0, b, NPT * j:NPT * (j + 1)] = kr[:NPT]
        ckv_s[0, 16 * c:16 * c + 16] = ck[NPT:].reshape(16, 4, KV_LORA)
        kr_s[0, 16 * c:16 * c + 16] = kr[NPT:].reshape(16, 4, QK_ROPE)
    yp = np.zeros((2, 8192, D), f32)
    ys = np.zeros((128, 4, D), f32)
    cv_p = np.zeros((1, 2, 2, 2 * D_FF), f32)
    cv_s = np.zeros((1, 128, 2, 2 * D_FF), f32)
    if "o_yT" in R[0]:
        for c in range(8):
            b, j = c // 4, c % 4
            y = R[c]["o_yT"].T
            yp[b, NPT * j:NPT * (j + 1)] = y[:NPT]
            ys[16 * c:16 * c + 16] = y[NPT:].reshape(16, 4, D)
            cs_ = R[c]["o_cvs"]
            cv_s[0, 16 * c:16 * c + 16] = cs_.transpose(2, 3, 1, 0).reshape(16, 2, 2 * D_FF)
            if j == 3:
                cv_p[0, b] = R[c]["o_cvp"].transpose(2, 1, 0).reshape(2, 2 * D_FF)
    z = lambda *s: np.zeros(s, f32)
    sre_p, sim_p, sre_s, sim_s = z(1, 2, 32, 64), z(1, 2, 32, 64), z(1, 128, 32, 64), z(1, 128, 32, 64)
    if "o_hp" in R[0]:
        for c in range(8):
            b, j = c // 4, c % 4
            hs_ = R[c]["o_hs"].reshape(2, 64, 2, 16, 16)
            hs_ = hs_.transpose(2, 4, 3, 0, 1).reshape(2, 16, 32, 64)
            sre_s[0, 16 * c:16 * c + 16] = hs_[0]
            sim_s[0, 16 * c:16 * c + 16] = hs_[1]
            if j == 3:
                hp_ = R[c]["o_hp"].reshape(2, 64, 2, 16).transpose(2, 3, 0, 1).reshape(2, 32, 64)
                sre_p[0, b] = hp_[0]
                sim_p[0, b] = hp_[1]
    return (yp, ys, ckv_p, kr_p, ckv_s, kr_s, sre_p, sim_p, sre_s, sim_s, cv_p, cv_s)
```

```python
import contextlib
import numpy as np
import concourse.bass as bass
import concourse.mybir as mybir
from concourse.bass_utils import run_bass_kernel_spmd

F32 = mybir.dt.float32
BF16 = mybir.dt.bfloat16
I32 = mybir.dt.int32
AF = mybir.ActivationFunctionType
ALU = mybir.AluOpType
AX = mybir.AxisListType

D = 1024
NPT = 2048
NST = 64
T = NPT + NST
KD = D // 128
N_HEADS = 8
QK_NOPE, QK_ROPE, QK_HEAD, V_HEAD = 64, 32, 96, 64
Q_LORA, KV_LORA = 384, 256
SSM_W, GROUP, N_GROUPS, STATE = 512, 16, 32, 64
D_FF = 2816
PLE = 256
EPS = 1e-6
OFF_CKV = Q_LORA
OFF_KR = OFF_CKV + KV_LORA
OFF_U = OFF_KR + QK_ROPE
OFF_GA = OFF_U + SSM_W
OFF_GS = OFF_GA + D
IN_COLS = OFF_GS + D
SCALE = QK_HEAD ** -0.5
PAST = 8192
PAGE = 128
NPAGES = 64
SEQ_PER_CORE = 16


class Buf:
    __slots__ = ("name", "last_w", "readers")

    def __init__(self, name, fence=()):
        self.name = name
        self.last_w = None
        self.readers = list(fence)


class Op:
    __slots__ = ("eng", "fn", "deps", "dma", "sem", "value", "marked", "idx")

    def __init__(self, eng, fn, dma):
        self.eng = eng
        self.fn = fn
        self.dma = dma
        self.deps = ()
        self.sem = None
        self.value = 0
        self.marked = False
        self.idx = 0


class Prog:
    ENGS = ("pe", "act", "dve", "pool", "sp")

    def __init__(self, nc):
        self.nc = nc
        self.ops = {e: [] for e in self.ENGS}
        self.stack = contextlib.ExitStack()
        self.dma_sems = {}
        self.nbuf = 0
        self.fence = []
        self.live = []

    def sbuf(self, name, shape, dtype):
        return self.stack.enter_context(self.nc.sbuf_tensor(name, list(shape), dtype))

    def psum(self, name, shape, dtype=F32):
        return self.stack.enter_context(self.nc.psum_tensor(name, list(shape), dtype))

    def sem(self, name):
        return self.stack.enter_context(self.nc.semaphore(name))

    def buf(self, name=None):
        self.nbuf += 1
        b = Buf(name or f"b{self.nbuf}", self.fence)
        self.live.append(b)
        return b

    def new_phase(self):
        f = []
        for b in self.live:
            if b.last_w is not None:
                f.append(b.last_w)
            f.extend(b.readers)
        self.fence = list(dict.fromkeys(f))[-64:] if False else list(dict.fromkeys(f))
        self.live = []

    def add(self, eng, fn, reads=(), writes=(), dma=False, semkey=None, after=()):
        op = Op(eng, fn, dma)
        deps = set(after)
        for b in reads:
            if b.last_w is not None:
                deps.add(b.last_w)
        for b in writes:
            if b.last_w is not None:
                deps.add(b.last_w)
            deps.update(b.readers)
        op.deps = tuple(deps)
        for b in reads:
            b.readers.append(op)
        for b in writes:
            b.last_w = op
            b.readers = []
        op.idx = len(self.ops[eng])
        self.ops[eng].append(op)
        if dma:
            key = semkey if semkey is not None else id(op)
            if key not in self.dma_sems:
                self.dma_sems[key] = [self.sem(f"dq{len(self.dma_sems)}"), 0]
            ent = self.dma_sems[key]
            ent[1] += (1 if dma == "cc" else 16)
            op.sem = ent[0]
            op.value = ent[1]
            op.marked = True
        return op

    def finalize(self):
        nc = self.nc
        esem = {e: self.sem(f"eng_{e}") for e in self.ENGS}
        for e in self.ENGS:
            for op in self.ops[e]:
                for d in op.deps:
                    if d.dma:
                        continue
                    if d.eng != e:
                        d.marked = True
                    elif e != "pe" and (op.idx - d.idx) <= 2:
                        d.marked = True
        for e in self.ENGS:
            c = 0
            for op in self.ops[e]:
                if op.dma:
                    continue
                if op.marked:
                    c += 1
                    op.sem = esem[e]
                    op.value = c
        self.stats = {e: len(self.ops[e]) for e in self.ENGS}

        def emit(ename, eng):
            waited = {}
            for op in self.ops[ename]:
                need = {}
                for d in op.deps:
                    if not d.marked:
                        continue
                    if (not d.dma) and d.eng == ename and (ename == "pe" or (op.idx - d.idx) > 2):
                        continue
                    k = id(d.sem)
                    if waited.get(k, 0) >= d.value:
                        continue
                    if k not in need or need[k][1] < d.value:
                        need[k] = (d.sem, d.value)
                for k, (s, v) in need.items():
                    eng.wait_ge(s, v)
                    waited[k] = v
                ins = op.fn(eng)
                if op.marked and ins is not None:
                    if op.dma == "cc":
                        ins.then_inc(op.sem, 1)
                    elif op.dma:
                        ins.then_inc(op.sem, 16)
                    else:
                        ins.then_inc(op.sem, 1)

        with nc.Block() as block:
            @block.tensor
            def _(e):
                emit("pe", e)

            @block.scalar
            def _(e):
                emit("act", e)

            @block.vector
            def _(e):
                emit("dve", e)

            @block.gpsimd
            def _(e):
                emit("pool", e)

            @block.sync
            def _(e):
                emit("sp", e)
        self.stack.close()


class Arena:
    def __init__(self, P, name, nbytes):
        self.t = P.sbuf(name, [128, nbytes // 4], F32)
        self.off = 0
        self.cap = nbytes
        self.peak = 0

    def alloc(self, shape, dtype=F32):
        esz = 2 if dtype == BF16 else 4
        n = int(np.prod(shape[1:]))
        nb = (n * esz + 31) // 32 * 32
        assert self.off + nb <= self.cap, ("arena overflow", self.off, nb, self.cap)
        v = self.t[0:shape[0], self.off // 4:(self.off + nb) // 4]
        self.off += nb
        self.peak = max(self.peak, self.off)
        if dtype != F32:
            v = v.bitcast(dtype)
        v = v[:, 0:n]
        if len(shape) > 2:
            names = "abcde"[:len(shape) - 1]
            pat = "p (" + " ".join(names) + ") -> p " + " ".join(names)
            v = v.rearrange(pat, **{c: int(d) for c, d in zip(names[1:], shape[2:])})
        return v

    def mark(self):
        return self.off

    def release(self, m):
        self.off = m


def _small_layout():
    lay = {}
    off = 0

    def put(name, n):
        nonlocal off
        lay[name] = (off, n)
        off += n
    put("g_mix", 8)
    put("g_cq", 3)
    put("g_ckv", 2)
    put("g_ffn", 8)
    put("g_ple", 8)
    put("conv_w", 3 * 44)
    put("conv_b", 44)
    put("g_q", 1)
    put("g_k", 1)
    put("d_skip", 4)
    put("vis", 4)
    put("full", 4)
    put("ident", 128)
    put("hsel", 4)
    lay["_n"] = off
    return lay


SL = _small_layout()


def _token_chunks():
    return [(0, 510), (510, 512), (1022, 512), (1534, 290), (1824, 288)]


def _pack_layout(items):
    lay, off = {}, 0
    for name, n in items:
        lay[name] = (off, n)
        off += n
    lay["_n"] = off
    return lay


SSL = _pack_layout([("a_re", 16), ("a_im", 16), ("logdt", 16), ("c_re", 512), ("c_im", 512), ("b_re", 512),
                    ("b_im", 512), ("h0_re", 256), ("h0_im", 256), ("msk", 12), ("blk", 4)])
SRL = _pack_layout([("a_re", 256), ("a_im", 256), ("logdt", 256), ("b_re", 512), ("b_im", 512), ("ident", 128)])
PWS = [1, 2, 3, 4, 8, 12, 16]
PWR = [1, 2, 3]
TWO_PI = 6.283185307179586


def build_ssm(P, AR, nc, din, dout, dint, stores, next_bank, uT, b_uT, smc, b_sm, sm, w_glu_b, b_w_glu_b, chunks):
    LOOP_ENG = "pool"
    ss_d = din("ssm_s", [128, SSL["_n"]])
    sr_d = din("ssm_r", [128, SRL["_n"]])
    o_hp = dout("o_hp", [128, 2, 16])
    o_hs = dout("o_hs", [128, 2, 16, 16])
    cc_e_in = dint("cc_e_in", [128, 32], F32)
    cc_e_out = dint("cc_e_out", [512, 32], F32)

    AR.release(0)
    P.new_phase()
    sst = AR.alloc([128, SSL["_n"]], F32)
    Wc = AR.alloc([128, 16, 4, 2, 32], BF16)
    Ktab = AR.alloc([128, 4, 4, 128], BF16)
    A4w = AR.alloc([128, 4, 4, 2, 128], BF16)
    nlim = AR.alloc([128, len(PWS), 16], F32)
    LS_pre = True
    b_sst, b_srt = P.buf("sst"), P.buf("srt")
    P.add("sp", lambda e: e.dma_start(out=sst[:], in_=ss_d), writes=[b_sst], dma=True, semkey="sst")

    def S(name, a=None):
        o, n = SSL[name]
        v = sst[:, o:o + n]
        return v if a is None else v.rearrange("p (a b) -> p a b", a=a)

    def R(name, a=None):
        o, n = SRL[name]
        v = srt[:, o:o + n]
        return v if a is None else v.rearrange("p (a b) -> p a b", a=a)

    def tt(eng, out, a, b, op, rd, wr):
        return P.add(eng, lambda e: e.tensor_tensor(out=out, in0=a, in1=b, op=op), reads=rd, writes=wr)

    def tss(eng, out, a, scalar, op, rd, wr):
        return P.add(eng, lambda e: e.tensor_single_scalar(out=out, in_=a, scalar=scalar, op=op), reads=rd, writes=wr)

    def stt(eng, out, a, scalar, b, op0, op1, rd, wr):
        return P.add("dve", lambda e: e.scalar_tensor_tensor(out=out, in0=a, scalar=scalar, in1=b, op0=op0, op1=op1),
                     reads=rd, writes=wr)

    def act(out, in_, func, rd, wr, scale=1.0, bias=0.0):
        return P.add("act", lambda e: e.activation(out=out, in_=in_, func=func, scale=scale, bias=bias), reads=rd, writes=wr)

    def cp(eng, out, in_, rd, wr):
        return P.add(eng, lambda e: e.tensor_copy(out=out, in_=in_), reads=rd, writes=wr)

    MUL, ADD, SUB = ALU.mult, ALU.add, ALU.subtract

    def lam_pow(pfx, a_re, a_im, logdt, Fd, powers, b_src, eng):
        npw = len(powers)
        bt = P.buf(pfx + "_t")
        mk = lambda nm, sh, dt_=F32: AR.alloc(sh, dt_)
        dt = mk("dt", [128, Fd]); dre = mk("dre", [128, Fd]); dim = mk("dim", [128, Fd])
        ang = mk("ang", [128, npw, Fd]); angc = mk("angc", [128, npw, Fd]); mag = mk("mag", [128, npw, Fd])
        ki = mk("ki", [128, npw, Fd], I32); kf = mk("kf", [128, npw, Fd])
        sn = mk("sn", [128, npw, Fd]); cs = mk("cs", [128, npw, Fd])
        lre = mk("lre", [128, npw, Fd]); lim = mk("lim", [128, npw, Fd])
        act(dt[:], logdt, AF.Exp, [b_src], [bt])
        tt(eng, dre[:], dt[:], a_re, MUL, [bt, b_src], [bt])
        tt(eng, dim[:], dt[:], a_im, MUL, [bt, b_src], [bt])
        for i, n in enumerate(powers):
            tss(eng, ang[:, i, :], dim[:], n / TWO_PI, MUL, [bt], [bt])
            act(mag[:, i, :], dre[:], AF.Exp, [bt], [bt], scale=float(n))
        tss(eng, angc[:], ang[:], 0.25, ADD, [bt], [bt])
        for src, dst in ((ang, sn), (angc, cs)):
            cp("dve", ki[:], src[:], [bt], [bt])
            cp("dve", kf[:], ki[:], [bt], [bt])
            tt(eng, kf[:], src[:], kf[:], SUB, [bt], [bt])
            act(dst[:], kf[:], AF.Sin, [bt], [bt], scale=6.28318)
        tt(eng, lre[:], mag[:], cs[:], MUL, [bt], [bt])
        tt(eng, lim[:], mag[:], sn[:], MUL, [bt], [bt])
        den = mk("den", [128, Fd]); t1 = mk("t1", [128, Fd]); t2 = mk("t2", [128, Fd]); nr = mk("nr", [128, Fd])
        fre = mk("fre", [128, Fd]); fim = mk("fim", [128, Fd])
        tt(eng, den[:], a_re, a_re, MUL, [b_src], [bt])
        tt(eng, t1[:], a_im, a_im, MUL, [b_src], [bt])
        tt(eng, den[:], den[:], t1[:], ADD, [bt], [bt])
        P.add("dve", lambda e: e.reciprocal(out=den[:], in_=den[:]), reads=[bt], writes=[bt])
        tss(eng, nr[:], lre[:, 0, :], -1.0, ADD, [bt], [bt])
        tt(eng, t1[:], nr[:], a_re, MUL, [bt, b_src], [bt])
        tt(eng, t2[:], lim[:, 0, :], a_im, MUL, [bt, b_src], [bt])
        tt(eng, t1[:], t1[:], t2[:], ADD, [bt], [bt])
        tt(eng, fre[:], t1[:], den[:], MUL, [bt], [bt])
        tt(eng, t1[:], lim[:, 0, :], a_re, MUL, [bt, b_src], [bt])
        tt(eng, t2[:], nr[:], a_im, MUL, [bt, b_src], [bt])
        tt(eng, t1[:], t1[:], t2[:], SUB, [bt], [bt])
        tt(eng, fim[:], t1[:], den[:], MUL, [bt], [bt])
        return dict(lre=lre, lim=lim, fre=fre, fim=fim, b=bt)

    def cmul(eng, ore, oim, are, aim, bre, bim, t1, t2, rd, wr):
        tt(eng, t1, are, bre, MUL, rd, wr)
        tt(eng, t2, aim, bim, MUL, rd, wr)
        tt(eng, ore, t1, t2, SUB, rd, wr)
        tt(eng, t1, are, bim, MUL, rd, wr)
        tt(eng, t2, aim, bre, MUL, rd, wr)
        tt(eng, oim, t1, t2, ADD, rd, wr)

    ENG = "dve"
    LS = lam_pow("ls", S("a_re"), S("a_im"), S("logdt"), 16, PWS, b_sst, ENG)
    mB0 = AR.mark()
    srt = AR.alloc([128, SRL["_n"]], F32)
    P.add("sp", lambda e: e.dma_start(out=srt[:], in_=sr_d), writes=[b_srt], dma=True, semkey="srt")
    LR = lam_pow("lr", R("a_re"), R("a_im"), R("logdt"), 256, PWR, b_srt, ENG)
    bS, bR = LS["b"], LR["b"]
    pi = {n: i for i, n in enumerate(PWS)}

    def bc(ap2, n):
        return ap2.unsqueeze(2).broadcast_to([128, 16, n])

    bbs_re = AR.alloc([128, 16, 32], F32); bbs_im = AR.alloc([128, 16, 32], F32)
    x_re = AR.alloc([128, 16, 32], F32); x_im = AR.alloc([128, 16, 32], F32)
    u1 = AR.alloc([128, 16, 32], F32); u2 = AR.alloc([128, 16, 32], F32)
    negc_im = AR.alloc([128, 16, 32], F32)
    cmul(ENG, bbs_re[:], bbs_im[:], bc(LS["fre"][:], 32), bc(LS["fim"][:], 32), S("b_re", 16), S("b_im", 16),
         u1[:], u2[:], [bS, b_sst], [bS])
    tss(ENG, negc_im[:], S("c_im", 16), -1.0, MUL, [b_sst], [bS])

    b_Wc = P.buf("Wc")
    Wc_v = Wc[:].rearrange("p (kk k4) s c n -> p kk k4 s c n", k4=4)
    u1_v = u1[:].rearrange("p (kk k4) n -> p kk k4 n", k4=4)
    u2_v = u2[:].rearrange("p (kk k4) n -> p kk k4 n", k4=4)
    for s in range(4):
        lr_, li_ = LS["lre"][:, pi[s + 1], :], LS["lim"][:, pi[s + 1], :]
        tt(ENG, u1[:], S("c_re", 16), bc(lr_, 32), MUL, [b_sst, bS], [bS])
        tt(ENG, u2[:], S("c_im", 16), bc(li_, 32), MUL, [b_sst, bS], [bS])
        for k4 in range(4):
            tt(ENG, Wc_v[:, :, k4, s, 0, :], u1_v[:, :, k4, :], u2_v[:, :, k4, :], SUB, [bS], [bS, b_Wc])
        tt(ENG, u1[:], S("c_re", 16), bc(li_, 32), MUL, [b_sst, bS], [bS])
        tt(ENG, u2[:], S("c_im", 16), bc(lr_, 32), MUL, [b_sst, bS], [bS])
        for k4 in range(4):
            stt(ENG, Wc_v[:, :, k4, s, 1, :], u1_v[:, :, k4, :], -1.0, u2_v[:, :, k4, :], MUL, SUB,
                [bS], [bS, b_Wc])

    b_Kt = P.buf("Ktab")
    Kc = AR.alloc([128, 4, 32], F32)
    Kf = AR.alloc([128, 4, 128], F32)
    b_Kc, b_Kf = P.buf("Kc"), P.buf("Kf")
    xs_re = AR.alloc([128, 4, 16, 32], F32); xs_im = AR.alloc([128, 4, 16, 32], F32)
    b_xs = P.buf("xs")
    cp(ENG, xs_re[:, 0], bbs_re[:], [bS], [b_xs])
    cp(ENG, xs_im[:, 0], bbs_im[:], [bS], [b_xs])
    for tau in range(1, 4):
        cmul(ENG, xs_re[:, tau], xs_im[:, tau], bc(LS["lre"][:, pi[tau], :], 32), bc(LS["lim"][:, pi[tau], :], 32),
             bbs_re[:], bbs_im[:], u1[:], u2[:], [bS], [bS, b_xs])
    for kk in range(4):
        ps, b_ps = next_bank()
        for k4 in range(4):
            k = 4 * kk + k4
            for tau in range(4):
                o = ps[32 * k4:32 * k4 + 32, tau * 32:tau * 32 + 32]
                P.add("pe", lambda e, o=o, k=k, tau=tau, k4=k4: e.matmul(
                    o, lhsT=xs_re[:, tau, k, :], rhs=S("c_re", 16)[:, k, :], start=True, stop=False,
                    tile_position=(0, 32 * k4)), reads=[b_xs, b_sst], writes=[b_ps])
                P.add("pe", lambda e, o=o, k=k, tau=tau, k4=k4: e.matmul(
                    o, lhsT=xs_im[:, tau, k, :], rhs=negc_im[:, k, :], start=False, stop=True,
                    tile_position=(0, 32 * k4)), reads=[b_xs, bS], writes=[b_ps])
        cp("dve", Kc[:], ps[:, 0:128].rearrange("p (t n) -> p t n", t=4), [b_ps], [b_Kc])
        for k4 in range(4):
            tss("dve", Kf[:, :, 32 * k4:32 * k4 + 32], Kc[:], S("blk")[:, k4:k4 + 1], MUL, [b_Kc, b_sst], [b_Kf])
        stt("dve", Kf[:, 0, :], R("ident"), smc("d_skip", kk), Kf[:, 0, :], MUL, ADD, [b_srt, b_sm, b_Kf], [b_Kf])
        cp("dve", Ktab[:, kk], Kf[:], [b_Kf], [b_Kt])

    b_A4 = P.buf("A4w")
    bbr_re = AR.alloc([128, 4, 2, 64], F32); bbr_im = AR.alloc([128, 4, 2, 64], F32)
    r1 = AR.alloc([128, 4, 2, 64], F32); r2 = AR.alloc([128, 4, 2, 64], F32)
    bcr = lambda t: t.rearrange("p (kk q) -> p kk q", kk=4).unsqueeze(2).broadcast_to([128, 4, 2, 64])
    v4 = lambda t: t.rearrange("p (kk g q) -> p kk g q", kk=4, g=2)
    cmul(ENG, bbr_re[:], bbr_im[:], bcr(LR["fre"][:]), bcr(LR["fim"][:]), v4(R("b_re")), v4(R("b_im")), r1[:], r2[:],
         [bR, b_srt], [bR])
    a4v = lambda s_, c_: A4w[:, :, s_, c_, :].rearrange("p kk (g q) -> p kk g q", g=2)
    cp(ENG, a4v(3, 0), bbr_re[:], [bR], [b_A4])
    cp(ENG, a4v(3, 1), bbr_im[:], [bR], [b_A4])
    pir = {n: i for i, n in enumerate(PWR)}
    for s in range(3):
        n = 3 - s
        cmul(ENG, a4v(s, 0), a4v(s, 1), bcr(LR["lre"][:, pir[n], :]), bcr(LR["lim"][:, pir[n], :]),
             bbr_re[:], bbr_im[:], r1[:], r2[:], [bR], [bR, b_A4])

    tss(ENG, nlim[:], LS["lim"][:], -1.0, MUL, [bS], [bS])
    Lre = lambda n, k: LS["lre"][:, pi[n], k:k + 1]
    Lim = lambda n, k: LS["lim"][:, pi[n], k:k + 1]
    nLim = lambda n, k: nlim[:, pi[n], k:k + 1]

    AR.release(mB0)
    P.new_phase()
    uP = [uT[:, kk, 0:NPT].rearrange("p (c e) -> p e c", e=16) for kk in range(4)]
    uS = [uT[:, kk, NPT:T].rearrange("p (q s) -> p s q", s=4) for kk in range(4)]

    S16 = [AR.alloc([128, 16, 128], F32) for c in range(2)]
    b_S16 = P.buf("S16")
    H16 = [AR.alloc([128, 16, 129], F32) for c in range(2)]
    b_H = [P.buf("H16re"), P.buf("H16im")]
    pr = [AR.alloc([128, 2, 128], F32) for i in range(2)]
    b_pr = [P.buf("pr0"), P.buf("pr1")]

    def s4_matmuls(k, n_c, usrc, nj):
        kk, k4 = divmod(k, 4)
        out = []
        for comp in range(2):
            ps, b_ps = next_bank()
            for j in range(nj):
                for s in range(4):
                    rhs = usrc[kk][32 * k4:32 * k4 + 32, 4 * j + s, :]
                    P.add("pe", lambda e, ps=ps, j=j, s=s, rhs=rhs, comp=comp, kk=kk, k4=k4: e.matmul(
                        ps[:, j * n_c:(j + 1) * n_c], lhsT=A4w[32 * k4:32 * k4 + 32, kk, s, comp, :], rhs=rhs,
                        start=(s == 0), stop=(s == 3), tile_position=(32 * k4, 0)),
                        reads=[b_A4, b_uT], writes=[b_ps])
            out.append((ps, b_ps))
        return out

    def prefix_step(k, j, src, b_src, dst, b_dst, Sre, Sim, b_sre, b_sim, n_c, o_re=None, o_im=None, b_o=()):
        ore = dst[:, 0, 0:n_c] if o_re is None else o_re
        oim = dst[:, 1, 0:n_c] if o_im is None else o_im
        wr = [b_dst] + list(b_o)
        stt("dve", dst[:, 0, 0:n_c], src[:, 0, 0:n_c], Lre(4, k), Sre, MUL, ADD, [b_src, bS, b_sre], [b_dst])
        stt("dve", ore, src[:, 1, 0:n_c], nLim(4, k), dst[:, 0, 0:n_c], MUL, ADD, [b_src, bS, b_dst], wr)
        stt("dve", dst[:, 1, 0:n_c], src[:, 0, 0:n_c], Lim(4, k), Sim, MUL, ADD, [b_src, bS, b_sim], [b_dst])
        stt("dve", oim, src[:, 1, 0:n_c], Lre(4, k), dst[:, 1, 0:n_c], MUL, ADD, [b_src, bS, b_dst], wr)

    for k in range(16):
        (pre, b_pre), (pim, b_pim) = s4_matmuls(k, 128, uP, 4)
        act(pr[0][:, 0, :], pre[:, 0:128], AF.Copy, [b_pre], [b_pr[0]])
        act(pr[0][:, 1, :], pim[:, 0:128], AF.Copy, [b_pim], [b_pr[0]])
        cur = 0
        for j in range(1, 4):
            last = (j == 3)
            prefix_step(k, j, pr[cur], b_pr[cur], pr[1 - cur], b_pr[1 - cur], pre[:, j * 128:(j + 1) * 128],
                        pim[:, j * 128:(j + 1) * 128], b_pre, b_pim, 128,
                        o_re=S16[0][:, k, :] if last else None, o_im=S16[1][:, k, :] if last else None,
                        b_o=[b_S16] if last else ())
            cur = 1 - cur

    lt = [AR.alloc([128, 16], F32) for i in range(6)]
    b_lt = [P.buf(f"lt{i}") for i in range(6)]
    L16re, L16im = LS["lre"][:, pi[16], :], LS["lim"][:, pi[16], :]

    def run_loop():
        for c in range(128):
            hre, him = H16[0][:, :, c], H16[1][:, :, c]
            tt(LOOP_ENG, lt[0][:], L16re, hre, MUL, [bS, b_H[0]], [b_lt[0]])
            tt(LOOP_ENG, lt[1][:], L16im, him, MUL, [bS, b_H[1]], [b_lt[1]])
            tt(LOOP_ENG, lt[3][:], L16re, him, MUL, [bS, b_H[1]], [b_lt[3]])
            tt(LOOP_ENG, lt[4][:], L16im, hre, MUL, [bS, b_H[0]], [b_lt[4]])
            tt(LOOP_ENG, lt[2][:], lt[0][:], lt[1][:], SUB, [b_lt[0], b_lt[1]], [b_lt[2]])
            tt(LOOP_ENG, lt[5][:], lt[3][:], lt[4][:], ADD, [b_lt[3], b_lt[4]], [b_lt[5]])
            tt(LOOP_ENG, H16[0][:, :, c + 1], lt[2][:], S16[0][:, :, c], ADD, [b_lt[2], b_S16], [b_H[0]])
            tt(LOOP_ENG, H16[1][:, :, c + 1], lt[5][:], S16[1][:, :, c], ADD, [b_lt[5], b_S16], [b_H[1]])
            yield

    P.add(LOOP_ENG, lambda e: e.memset(H16[0][:, :, 0], 0.0), writes=[b_H[0]])
    P.add(LOOP_ENG, lambda e: e.memset(H16[1][:, :, 0], 0.0), writes=[b_H[1]])
    for _ in run_loop():
        pass
    Eloc = AR.alloc([128, 2, 16], F32)
    b_E = P.buf("Eloc")
    cp(LOOP_ENG, Eloc[:, 0, :], H16[0][:, :, 128], [b_H[0]], [b_E])
    cp(LOOP_ENG, Eloc[:, 1, :], H16[1][:, :, 128], [b_H[1]], [b_E])
    b_cci, b_cco = P.buf("cc_e_in"), P.buf("cc_e_out")
    P.add("sp", lambda e: e.dma_start(out=cc_e_in, in_=Eloc[:].rearrange("p a b -> p (a b)")), reads=[b_E], writes=[b_cci],
          dma=True, semkey="cce1")
    P.add("pool", lambda e: e.collective_compute("AllGather", ALU.bypass, replica_groups=[[0, 1, 2, 3], [4, 5, 6, 7]],
                                                 ins=[cc_e_in.opt()], outs=[cc_e_out.opt()]),
          reads=[b_cci], writes=[b_cco], dma="cc", semkey="cce2")
    Eg = AR.alloc([128, 4, 2, 16], F32)
    b_Eg = P.buf("Eg")
    P.add("sp", lambda e: e.dma_start(out=Eg[:].rearrange("p r a b -> p r (a b)"),
                                      in_=cc_e_out.rearrange("(r p) f -> p r f", p=128)),
          reads=[b_cco], writes=[b_Eg], dma=True, semkey="cce3")
    Lp = AR.alloc([128, 3, 2, 16], F32)
    b_Lp = P.buf("Lp")
    sq_a = AR.alloc([128, 2, 16], F32); sq_b = AR.alloc([128, 2, 16], F32)
    q1 = AR.alloc([128, 16], F32); q2 = AR.alloc([128, 16], F32)
    b_q = P.buf("sqtmp")
    CE = "dve"
    cp(CE, sq_a[:, 0, :], L16re, [bS], [b_q])
    cp(CE, sq_a[:, 1, :], L16im, [bS], [b_q])
    src_, dst_ = sq_a, sq_b
    for it in range(7):
        o = Lp[:, 0] if it == 6 else dst_
        tt(CE, q1[:], src_[:, 0, :], src_[:, 0, :], MUL, [b_q], [b_q])
        tt(CE, q2[:], src_[:, 1, :], src_[:, 1, :], MUL, [b_q], [b_q])
        tt(CE, o[:, 0, :], q1[:], q2[:], SUB, [b_q], [b_q, b_Lp])
        stt(CE, o[:, 1, :], src_[:, 0, :], 2.0, src_[:, 1, :], MUL, MUL, [b_q], [b_q, b_Lp])
        src_, dst_ = dst_, src_
    cmul(CE, Lp[:, 1, 0, :], Lp[:, 1, 1, :], Lp[:, 0, 0, :], Lp[:, 0, 1, :], Lp[:, 0, 0, :], Lp[:, 0, 1, :], q1[:], q2[:],
         [b_Lp, b_q], [b_Lp, b_q])
    cmul(CE, Lp[:, 2, 0, :], Lp[:, 2, 1, :], Lp[:, 1, 0, :], Lp[:, 1, 1, :], Lp[:, 0, 0, :], Lp[:, 0, 1, :], q1[:], q2[:],
         [b_Lp, b_q], [b_Lp, b_q])
    hin = AR.alloc([128, 2, 16], F32)
    cf = AR.alloc([128, 2, 16], F32)
    b_hin, b_cf = P.buf("hin"), P.buf("cf")
    P.add(CE, lambda e: e.memset(hin[:], 0.0), writes=[b_hin])
    msk = S("msk")
    for i in range(4):
        m0, m1, m2 = (msk[:, 3 * i + n:3 * i + n + 1] for n in range(3))
        tss(CE, cf[:, 0, :], Lp[:, 0, 0, :], m1, MUL, [b_Lp, b_sst], [b_cf])
        stt(CE, cf[:, 0, :], Lp[:, 1, 0, :], m2, cf[:, 0, :], MUL, ADD, [b_Lp, b_sst, b_cf], [b_cf])
        tss(CE, cf[:, 0, :], cf[:, 0, :], m0, ADD, [b_cf, b_sst], [b_cf])
        tss(CE, cf[:, 1, :], Lp[:, 0, 1, :], m1, MUL, [b_Lp, b_sst], [b_cf])
        stt(CE, cf[:, 1, :], Lp[:, 1, 1, :], m2, cf[:, 1, :], MUL, ADD, [b_Lp, b_sst, b_cf], [b_cf])
        ere, eim = Eg[:, i, 0, :], Eg[:, i, 1, :]
        tt(CE, q1[:], cf[:, 0, :], ere, MUL, [b_cf, b_Eg], [b_q])
        tt(CE, hin[:, 0, :], hin[:, 0, :], q1[:], ADD, [b_q, b_hin], [b_hin])
        tt(CE, q1[:], cf[:, 1, :], eim, MUL, [b_cf, b_Eg], [b_q])
        tt(CE, hin[:, 0, :], hin[:, 0, :], q1[:], SUB, [b_q, b_hin], [b_hin])
        tt(CE, q1[:], cf[:, 0, :], eim, MUL, [b_cf, b_Eg], [b_q])
        tt(CE, hin[:, 1, :], hin[:, 1, :], q1[:], ADD, [b_q, b_hin], [b_hin])
        tt(CE, q1[:], cf[:, 1, :], ere, MUL, [b_cf, b_Eg], [b_q])
        tt(CE, hin[:, 1, :], hin[:, 1, :], q1[:], ADD, [b_q, b_hin], [b_hin])
    cp(LOOP_ENG, H16[0][:, :, 0], hin[:, 0, :], [b_hin], [b_H[0]])
    cp(LOOP_ENG, H16[1][:, :, 0], hin[:, 1, :], [b_hin], [b_H[1]])
    for _ in run_loop():
        pass
    hp = AR.alloc([128, 2, 16], F32)
    b_hp = P.buf("hp")
    cp(LOOP_ENG, hp[:, 0, :], H16[0][:, :, 128], [b_H[0]], [b_hp])
    cp(LOOP_ENG, hp[:, 1, :], H16[1][:, :, 128], [b_H[1]], [b_hp])
    stores.append(P.add("sp", lambda e: e.dma_start(out=o_hp, in_=hp[:]), reads=[b_hp], dma=True, semkey="st_hp"))

    yT = AR.alloc([128, 4, T], BF16)
    b_yT = P.buf("yT")
    H4 = AR.alloc([128, 4, 4, 2, 128], BF16)
    b_H4 = P.buf("H4")
    h0b = AR.alloc([128, 2, 16, 16], BF16)
    b_h0b = P.buf("h0b")
    cp("dve", h0b[:, 0], S("h0_re", 16), [b_sst], [b_h0b])
    cp("dve", h0b[:, 1], S("h0_im", 16), [b_sst], [b_h0b])
    hs = AR.alloc([128, 2, 16, 16], F32)
    b_hs = P.buf("hs")
    htmp = AR.alloc([128, 2, 128], F32)
    b_htmp = P.buf("htmp")
    yP = [yT[:, kk, 0:NPT].rearrange("p (c j s) -> p j s c", j=4, s=4) for kk in range(4)]
    yS = [yT[:, kk, NPT:T].rearrange("p (q s) -> p s q", s=4) for kk in range(4)]

    def out_stage(kk, j, n_c, Hsrc, b_hsrc, usrc, dst):
        ps, b_ps = next_bank()
        for s_lo in range(4):
            o = ps[:, s_lo * n_c:(s_lo + 1) * n_c]
            nmm = 8 + s_lo + 1
            idx = 0
            for k4 in range(4):
                k = 4 * kk + k4
                for comp in range(2):
                    o4 = ps[32 * k4:32 * k4 + 32, s_lo * n_c:(s_lo + 1) * n_c]
                    P.add("pe", lambda e, o4=o4, k=k, s_lo=s_lo, comp=comp, k4=k4, idx=idx, nmm=nmm: e.matmul(
                        o4, lhsT=Wc[:, k, s_lo, comp, :], rhs=Hsrc(k, k4, comp), start=(comp == 0), stop=False,
                        tile_position=(0, 32 * k4)),
                        reads=[b_Wc, b_hsrc], writes=[b_ps])
                    idx += 1
            for tau in range(s_lo + 1):
                rhs = usrc[kk][:, 4 * j + s_lo - tau, :]
                P.add("pe", lambda e, o=o, tau=tau, rhs=rhs, idx=idx, nmm=nmm, kk=kk: e.matmul(
                    o, lhsT=Ktab[:, kk, tau, :], rhs=rhs, start=False, stop=(idx == nmm - 1)),
                    reads=[b_Kt, b_uT], writes=[b_ps])
                idx += 1
        act(dst, ps[:, 0:4 * n_c].rearrange("p (s c) -> p s c", s=4), AF.Gelu_apprx_tanh, [b_ps], [b_yT])

    for kk in range(4):
        for k4 in range(4):
            k = 4 * kk + k4
            (pre, b_pre), (pim, b_pim) = s4_matmuls(k, 128, uP, 3)
            cp("dve", H4[:, k4, 0, 0, :], H16[0][:, k, 0:128], [b_H[0]], [b_H4])
            cp("dve", H4[:, k4, 0, 1, :], H16[1][:, k, 0:128], [b_H[1]], [b_H4])
            act(pr[0][:, 0, :], pre[:, 0:128], AF.Copy, [b_pre], [b_pr[0]])
            act(pr[0][:, 1, :], pim[:, 0:128], AF.Copy, [b_pim], [b_pr[0]])
            cur = 0
            for j in range(1, 4):
                n = 4 * j
                stt("dve", htmp[:, 0, :], H16[0][:, k, 0:128], Lre(n, k), pr[cur][:, 0, :], MUL, ADD,
                    [b_H[0], bS, b_pr[cur]], [b_htmp])
                stt("dve", H4[:, k4, j, 0, :], H16[1][:, k, 0:128], nLim(n, k), htmp[:, 0, :], MUL, ADD,
                    [b_H[1], bS, b_htmp], [b_H4])
                stt("dve", htmp[:, 1, :], H16[0][:, k, 0:128], Lim(n, k), pr[cur][:, 1, :], MUL, ADD,
                    [b_H[0], bS, b_pr[cur]], [b_htmp])
                stt("dve", H4[:, k4, j, 1, :], H16[1][:, k, 0:128], Lre(n, k), htmp[:, 1, :], MUL, ADD,
                    [b_H[1], bS, b_htmp], [b_H4])
                if j < 3:
                    prefix_step(k, j, pr[cur], b_pr[cur], pr[1 - cur], b_pr[1 - cur], pre[:, j * 128:(j + 1) * 128],
                                pim[:, j * 128:(j + 1) * 128], b_pre, b_pim, 128)
                    cur = 1 - cur
            (sre, b_sre), (sim, b_sim) = s4_matmuls(k, 16, uS, 1)
            h0r, h0i = S("h0_re", 16)[:, k, :], S("h0_im", 16)[:, k, :]
            stt("dve", hs[:, 0, k, :], h0r, Lre(4, k), sre[:, 0:16], MUL, ADD, [b_sst, bS, b_sre], [b_hs])
            stt("dve", hs[:, 0, k, :], h0i, nLim(4, k), hs[:, 0, k, :], MUL, ADD, [b_sst, bS, b_hs], [b_hs])
            stt("dve", hs[:, 1, k, :], h0r, Lim(4, k), sim[:, 0:16], MUL, ADD, [b_sst, bS, b_sim], [b_hs])
            stt("dve", hs[:, 1, k, :], h0i, Lre(4, k), hs[:, 1, k, :], MUL, ADD, [b_sst, bS, b_hs], [b_hs])
        for j in range(4):
            out_stage(kk, j, 128, lambda k, k4, comp, j=j: H4[:, k4, j, comp, :], b_H4, uP, yP[kk][:, j])
        out_stage(kk, 0, 16, lambda k, k4, comp: h0b[:, comp, k, :], b_h0b, uS, yS[kk])
    stores.append(P.add("sp", lambda e: e.dma_start(out=o_hs, in_=hs[:]), reads=[b_hs], dma=True, semkey="st_hs"))

    wg = AR.alloc([128, 4, 1024], BF16)
    b_wg = P.buf("wg")
    P.add("sp", lambda e: e.dma_start(out=wg[:], in_=w_glu_b.rearrange("(k p) m -> p k m", p=128)), reads=[b_w_glu_b],
          writes=[b_wg], dma=True, semkey="wg")
    ysT, b_ys = uT, b_uT
    sg = AR.alloc([128, 512], F32)
    b_sg = P.buf("sg")
    for (n0, n) in chunks:
        for m in range(4):
            pv, b_pv = next_bank()
            pg, b_pg = next_bank()
            for k in range(4):
                P.add("pe", lambda e, pv=pv, k=k, m=m, n0=n0, n=n: e.matmul(
                    pv[:, 0:n], lhsT=wg[:, k, 128 * m:128 * m + 128], rhs=yT[:, k, n0:n0 + n], start=(k == 0), stop=(k == 3)),
                    reads=[b_wg, b_yT], writes=[b_pv])
            for k in range(4):
                P.add("pe", lambda e, pg=pg, k=k, m=m, n0=n0, n=n: e.matmul(
                    pg[:, 0:n], lhsT=wg[:, k, 512 + 128 * m:512 + 128 * m + 128], rhs=yT[:, k, n0:n0 + n], start=(k == 0),
                    stop=(k == 3)), reads=[b_wg, b_yT], writes=[b_pg])
            act(sg[:, 0:n], pg[:, 0:n], AF.Sigmoid, [b_pg], [b_sg])
            tt("dve", ysT[:, m, n0:n0 + n], pv[:, 0:n], sg[:, 0:n], MUL, [b_pv, b_sg], [b_ys])
    return dict(ysT=ysT, b_ys=b_ys)


def build_attn(P, AR, nc, din, dout, dint, stores, banks, bank_bufs, cast_w, cqn, b_cqn, ckvn, b_ckvn, krK, b_krK,
               sm, b_sm, smc, ones_bf, b_ones, rope_c, rope_s, attT, b_attT, chunks, kS, b_kS):
    MUL, ADD = ALU.mult, ALU.add
    w_uq = din("w_uq", [Q_LORA, N_HEADS * QK_HEAD])
    w_uk = din("w_uk", [KV_LORA, N_HEADS * QK_NOPE])
    w_uv = din("w_uv", [KV_LORA, N_HEADS * V_HEAD])
    maskd_d = din("maskd", [128, 4, 128])
    rotm_d = din("rotm", [96, 96])
    w_uq_b = dint("w_uq_b", [Q_LORA, N_HEADS * QK_HEAD], BF16)
    w_uk_b = dint("w_uk_b", [KV_LORA, N_HEADS * QK_NOPE], BF16)
    w_uv_b = dint("w_uv_b", [KV_LORA, N_HEADS * V_HEAD], BF16)
    b_wqb = cast_w(w_uq_b, w_uq, Q_LORA, 768, "c_wuq")
    b_wkb = cast_w(w_uk_b, w_uk, KV_LORA, 512, "c_wuk")
    b_wvb = cast_w(w_uv_b, w_uv, KV_LORA, 512, "c_wuv")
    K_own = [dint(f"K_own{h}", [QK_HEAD, NPT], BF16) for h in range(N_HEADS)]
    K_all = [dint(f"K_all{h}", [4 * QK_HEAD, NPT], BF16) for h in range(N_HEADS)]
    V_own = [dint(f"V_own{h}", [128, 16 * 65], BF16) for h in range(N_HEADS)]
    V_all = [dint(f"V_all{h}", [512, 16 * 65], BF16) for h in range(N_HEADS)]
    RG = [[0, 1, 2, 3], [4, 5, 6, 7]]

    AR.release(0)
    P.new_phase()
    qT = AR.alloc([96, 8, T], BF16)
    maskd = AR.alloc([128, 4, 128], BF16)
    sel65 = AR.alloc([65, 64], F32)
    mC1 = AR.mark()
    wq = AR.alloc([128, 3, 768], BF16)
    wk = AR.alloc([128, 2, 8, 96], BF16)
    wv = AR.alloc([128, 2, 512], BF16)
    rc = AR.alloc([96, T], F32)
    rs = AR.alloc([96, T], F32)
    rotm = AR.alloc([96, 96], F32)
    maskf = AR.alloc([128, 4, 128], F32)
    b_wq, b_wk, b_wv, b_rc, b_rs, b_rot, b_mk, b_mkf, b_sel, b_qT = [P.buf(n) for n in
        ("wq", "wk", "wv", "rc", "rs", "rotm", "maskd", "maskf", "sel65", "qT")]
    P.add("sp", lambda e: e.dma_start(out=wq[:], in_=w_uq_b.rearrange("(k p) m -> p k m", p=128)), reads=[b_wqb], writes=[b_wq],
          dma=True, semkey="wq")
    P.add("pool", lambda e: e.memset(wk[:], 0.0), writes=[b_wk])
    for k in range(2):
        P.add("sp", lambda e, k=k: e.dma_start(out=wk[:, k, :, 0:64],
                                               in_=w_uk_b[128 * k:128 * k + 128, :].rearrange("p (h d) -> p h d", h=8)),
              reads=[b_wkb], writes=[b_wk], dma=True, semkey="wk")
    P.add("sp", lambda e: e.dma_start(out=wv[:], in_=w_uv_b.rearrange("(k p) m -> p k m", p=128)), reads=[b_wvb], writes=[b_wv],
          dma=True, semkey="wv")
    P.add("sp", lambda e: e.dma_start(out=rc[:], in_=rope_c), writes=[b_rc], dma=True, semkey="rc")
    P.add("sp", lambda e: e.dma_start(out=rs[:], in_=rope_s), writes=[b_rs], dma=True, semkey="rs")
    P.add("sp", lambda e: e.dma_start(out=rotm[:], in_=rotm_d), writes=[b_rot], dma=True, semkey="rotm")
    P.add("sp", lambda e: e.dma_start(out=maskf[:], in_=maskd_d), writes=[b_mkf], dma=True, semkey="maskf")
    P.add("pool", lambda e: e.tensor_copy(out=maskd[:], in_=maskf[:]), reads=[b_mkf], writes=[b_mk])
    P.add("pool", lambda e: e.memset(sel65[:], 0.0), writes=[b_sel])
    P.add("pool", lambda e: e.memset(sel65[64:65, :], 1.0), writes=[b_sel])
    ident = sm[:, SL["ident"][0]:SL["ident"][0] + 128]

    rr = [0]

    def tbank():
        i = 2 + rr[0] % 6
        rr[0] += 1
        return banks[i], bank_bufs[i]

    NT_ = 2
    tmpl = [dict(raw=AR.alloc([96, 512], F32), sqh=AR.alloc([96, 512], BF16), lnv=AR.alloc([96, 512], F32),
                 rstd=AR.alloc([96, 512], F32), qg=AR.alloc([96, 512], F32), t1=AR.alloc([96, 512], F32),
                 t2=AR.alloc([96, 512], F32)) for _ in range(NT_)]
    tmpb = [{k: P.buf(f"nr_{k}{i}") for k in ("raw", "sqh", "lnv", "rstd", "qg", "t1", "t2")} for i in range(NT_)]
    kst = [AR.alloc([96, 512], BF16) for _ in range(2)]
    b_kst = [P.buf("kst0"), P.buf("kst1")]
    nrc = [0]

    def normrope(ps, b_ps, gname, out_ap, b_out, n, n0):
        ti = nrc[0] % NT_
        nrc[0] += 1
        t_, b_ = tmpl[ti], tmpb[ti]
        raw, sqh, lnv, rstd, qg, t1, t2 = (t_[k] for k in ("raw", "sqh", "lnv", "rstd", "qg", "t1", "t2"))
        P.add("act", lambda e: e.activation(out=raw[:, 0:n], in_=ps[0:96, 0:n], func=AF.Copy), reads=[b_ps], writes=[b_["raw"]])
        P.add("dve", lambda e: e.tensor_tensor(out=sqh[:, 0:n], in0=raw[:, 0:n], in1=raw[:, 0:n], op=MUL), reads=[b_["raw"]],
              writes=[b_["sqh"]])
        p2, b_p2 = tbank()
        P.add("pe", lambda e: e.matmul(p2[0:96, 0:n], lhsT=ones_bf[0:96, 0:96], rhs=sqh[:, 0:n], start=True, stop=True),
              reads=[b_["sqh"], b_ones], writes=[b_p2])
        P.add("act", lambda e: e.activation(out=lnv[:, 0:n], in_=p2[0:96, 0:n], func=AF.Ln, scale=1.0 / QK_HEAD, bias=EPS),
              reads=[b_p2], writes=[b_["lnv"]])
        P.add("act", lambda e: e.activation(out=rstd[:, 0:n], in_=lnv[:, 0:n], func=AF.Exp, scale=-0.5), reads=[b_["lnv"]],
              writes=[b_["rstd"]])
        P.add("dve", lambda e: e.scalar_tensor_tensor(out=qg[:, 0:n], in0=raw[:, 0:n], scalar=smc(gname)[0:96, :], in1=rstd[:, 0:n],
                                                      op0=MUL, op1=MUL), reads=[b_["raw"], b_["rstd"], b_sm], writes=[b_["qg"]])
        p3, b_p3 = tbank()
        P.add("pe", lambda e: e.matmul(p3[0:96, 0:n], lhsT=rotm[:, :], rhs=qg[:, 0:n], start=True, stop=True),
              reads=[b_rot, b_["qg"]], writes=[b_p3])
        P.add("dve", lambda e: e.tensor_tensor(out=t1[:, 0:n], in0=qg[:, 0:n], in1=rc[:, n0:n0 + n], op=MUL), reads=[b_["qg"], b_rc],
              writes=[b_["t1"]])
        P.add("dve", lambda e: e.tensor_tensor(out=t2[:, 0:n], in0=p3[0:96, 0:n], in1=rs[:, n0:n0 + n], op=MUL),
              reads=[b_p3, b_rs], writes=[b_["t2"]])
        P.add("dve", lambda e: e.tensor_tensor(out=out_ap, in0=t1[:, 0:n], in1=t2[:, 0:n], op=ADD), reads=[b_["t1"], b_["t2"]],
              writes=[b_out])

    b_Kown = [P.buf(f"K_own{h}") for h in range(8)]
    b_Vown = [P.buf(f"V_own{h}") for h in range(8)]
    b_Kall = [P.buf(f"K_all{h}") for h in range(8)]
    b_Vall = [P.buf(f"V_all{h}") for h in range(8)]
    Vst = AR.alloc([128, 8, 16, 65], BF16)
    b_Vst = P.buf("Vst")
    P.add("pool", lambda e: e.memset(Vst[:, :, :, 64:65], 1.0), writes=[b_Vst])
    for blk in range(16):
        ps, b_ps = tbank()
        for k in range(2):
            P.add("pe", lambda e, ps=ps, k=k, blk=blk: e.matmul(ps[:, 0:512], lhsT=ckvn[:, k, 128 * blk:128 * blk + 128], rhs=wv[:, k, :],
                                                                 start=(k == 0), stop=(k == 1)), reads=[b_ckvn, b_wv], writes=[b_ps])
        P.add("act", lambda e, ps=ps, blk=blk: e.activation(out=Vst[:, :, blk, 0:64], in_=ps[:, 0:512].rearrange("p (h v) -> p h v", h=8),
                                                            func=AF.Copy), reads=[b_ps], writes=[b_Vst])
    for h in range(N_HEADS):
        P.add("sp", lambda e, h=h: e.dma_start(out=V_own[h], in_=Vst[:, h].rearrange("p b e -> p (b e)")), reads=[b_Vst],
              writes=[b_Vown[h]], dma=True, semkey=f"vst{h}")
        P.add("pool", lambda e, h=h: e.collective_compute("AllGather", ALU.bypass, replica_groups=RG, ins=[V_own[h].opt()],
                                                          outs=[V_all[h].opt()]),
              reads=[b_Vown[h]], writes=[b_Vall[h]], dma="cc", semkey=f"ccV{h}")
    kcount = 0
    for h in range(N_HEADS):
        for ci, (n0, n) in enumerate(chunks):
            ps, b_ps = tbank()
            for k in range(3):
                P.add("pe", lambda e, ps=ps, k=k, h=h, n0=n0, n=n: e.matmul(
                    ps[0:96, 0:n], lhsT=wq[:, k, 96 * h:96 * h + 96], rhs=cqn[:, k, n0:n0 + n], start=(k == 0), stop=(k == 2)),
                    reads=[b_wq, b_cqn], writes=[b_ps])
            normrope(ps, b_ps, "g_q", qT[:, h, n0:n0 + n], b_qT, n, n0)
            ps, b_ps = tbank()
            for k in range(2):
                P.add("pe", lambda e, ps=ps, k=k, h=h, n0=n0, n=n: e.matmul(
                    ps[0:96, 0:n], lhsT=wk[:, k, h, :], rhs=ckvn[:, k, n0:n0 + n], start=(k == 0), stop=False),
                    reads=[b_wk, b_ckvn], writes=[b_ps])
            P.add("pe", lambda e, ps=ps, n0=n0, n=n: e.matmul(
                ps[0:96, 0:n], lhsT=ident[64:96, 0:96], rhs=krK[64:96, n0:n0 + n], start=False, stop=True, tile_position=(64, 0)),
                reads=[b_sm, b_krK], writes=[b_ps])
            ks, b_ks = kst[kcount % 2], b_kst[kcount % 2]
            kcount += 1
            normrope(ps, b_ps, "g_k", ks[:, 0:n], b_ks, n, n0)
            npr = min(n0 + n, NPT) - n0
            if npr > 0:
                P.add("sp", lambda e, ks=ks, h=h, n0=n0, npr=npr: e.dma_start(out=K_own[h][:, n0:n0 + npr],
                                                                               in_=ks[:, 0:npr]),
                      reads=[b_ks], writes=[b_Kown[h]], dma=True, semkey=f"kst{(kcount - 1) % 2}")
            if npr < n:
                P.add("pool", lambda e, ks=ks, h=h, npr=npr, n=n: e.tensor_copy(out=kS[:, h, :], in_=ks[:, npr:n]), reads=[b_ks],
                      writes=[b_kS])
        P.add("pool", lambda e, h=h: e.collective_compute("AllGather", ALU.bypass, replica_groups=RG, ins=[K_own[h].opt()],
                                                          outs=[K_all[h].opt()]),
              reads=[b_Kown[h]], writes=[b_Kall[h]], dma="cc", semkey=f"ccK{h}")
    import os
    if os.environ.get("CUT") == "3":
        return {}
    AR.release(mC1)
    P.new_phase()
    Kh = [AR.alloc([96, 4, NPT], BF16) for _ in range(2)]
    Vh = [AR.alloc([128, 4, 16 * 65], BF16) for _ in range(2)]
    Vvis = AR.alloc([128, 4, 16 * 65], BF16)
    Vful = AR.alloc([128, 4, 16 * 65], BF16)
    b_Kh = [P.buf("Kh0"), P.buf("Kh1")]
    b_Vh = [P.buf("Vh0"), P.buf("Vh1")]
    b_Vvis, b_Vful = P.buf("Vvis"), P.buf("Vful")
    PT = [AR.alloc([128, 512], BF16) for _ in range(6)]
    b_PT = [P.buf(f"PT{i}") for i in range(6)]
    Osb = AR.alloc([65, 512], F32); rl = AR.alloc([64, 512], F32); ast = AR.alloc([64, 512], BF16)
    b_Osb, b_rl, b_ast = P.buf("Osb"), P.buf("rl"), P.buf("ast")

    def load_head(h):
        i = h % 2
        P.add("sp", lambda e: e.dma_start(out=Kh[i][:], in_=K_all[h].rearrange("(r d) t -> d r t", r=4)), reads=[b_Kall[h]], writes=[b_Kh[i]], dma=True,
              semkey=f"Kh{i}")
        P.add("sp", lambda e: e.dma_start(out=Vh[i][:], in_=V_all[h].rearrange("(r p) x -> p r x", r=4)), reads=[b_Vall[h]], writes=[b_Vh[i]], dma=True,
              semkey=f"Vh{i}")

    load_head(0)
    pcount = 0
    for h in range(N_HEADS):
        i = h % 2
        if h + 1 < N_HEADS:
            load_head(h + 1)
        for r in range(4):
            P.add("dve", lambda e, r=r, i=i: e.tensor_scalar(out=Vvis[:, r, :], in0=Vh[i][:, r, :], scalar1=smc("vis", r), scalar2=None,
                                                              op0=MUL), reads=[b_Vh[i], b_sm], writes=[b_Vvis])
            P.add("dve", lambda e, r=r, i=i: e.tensor_scalar(out=Vful[:, r, :], in0=Vh[i][:, r, :], scalar1=smc("full", r), scalar2=None,
                                                              op0=MUL), reads=[b_Vh[i], b_sm], writes=[b_Vful])
        for qc in range(4):
            O, b_O = banks[qc % 2], bank_bufs[qc % 2]
            first = [True]
            pending = []

            def flush(keep):
                while len(pending) > keep:
                    pending.pop(0)()
            for r in range(4):
                for kb in range(16):
                    S_, b_S = tbank()
                    P.add("pe", lambda e, S_=S_, r=r, kb=kb, i=i, h=h, qc=qc: e.matmul(
                        S_[:, 0:512], lhsT=Kh[i][:, r, 128 * kb:128 * kb + 128], rhs=qT[:, h, 512 * qc:512 * qc + 512],
                        start=True, stop=True), reads=[b_Kh[i], b_qT], writes=[b_S])
                    pt, b_pt = PT[pcount % 6], b_PT[pcount % 6]
                    pcount += 1
                    P.add("act", lambda e, S_=S_, pt=pt: e.activation(out=pt[:], in_=S_[:, 0:512], func=AF.Exp, scale=SCALE),
                          reads=[b_S], writes=[b_pt])

                    def pv(r=r, kb=kb, pt=pt, b_pt=b_pt, O=O, b_O=b_O, qc=qc, i=i):
                        d = kb - 4 * qc
                        segs = []
                        if d < 0:
                            segs.append((0, 512, Vvis, b_Vvis))
                        elif d > 3:
                            segs.append((0, 512, Vful, b_Vful))
                        else:
                            if d > 0:
                                segs.append((0, 128 * d, Vful, b_Vful))
                            P.add("dve", lambda e: e.tensor_tensor(out=pt[:, 128 * d:128 * d + 128], in0=pt[:, 128 * d:128 * d + 128],
                                                                   in1=maskd[:, r, :], op=MUL), reads=[b_pt, b_mk], writes=[b_pt])
                            segs.append((128 * d, 128 * d + 128, Vh[i], b_Vh[i]))
                            if d < 3:
                                segs.append((128 * d + 128, 512, Vvis, b_Vvis))
                        for (c0, c1, Vx, b_Vx) in segs:
                            st = first[0]
                            first[0] = False
                            P.add("pe", lambda e, c0=c0, c1=c1, Vx=Vx, st=st: e.matmul(
                                O[0:65, c0:c1], lhsT=Vx[:, r, 65 * kb:65 * kb + 65], rhs=pt[:, c0:c1], start=st, stop=False),
                                reads=[b_Vx, b_pt], writes=[b_O])
                    pending.append(pv)
                    flush(3)
            flush(0)
            P.add("act", lambda e, O=O: e.activation(out=Osb[:], in_=O[0:65, 0:512], func=AF.Copy), reads=[b_O], writes=[b_Osb])
            lb, b_lb = tbank()
            P.add("pe", lambda e, lb=lb: e.matmul(lb[0:64, 0:512], lhsT=sel65[:, :], rhs=Osb[:, :], start=True, stop=True),
                  reads=[b_sel, b_Osb], writes=[b_lb])
            P.add("dve", lambda e, lb=lb: e.reciprocal(out=rl[:], in_=lb[0:64, 0:512]), reads=[b_lb], writes=[b_rl])
            cols = slice(512 * qc, 512 * qc + 512)
            if h % 2 == 0:
                P.add("dve", lambda e, h=h, cols=cols: e.tensor_tensor(out=attT[0:64, h // 2, cols], in0=Osb[0:64, :], in1=rl[:], op=MUL),
                      reads=[b_Osb, b_rl], writes=[b_attT])
            else:
                P.add("dve", lambda e: e.tensor_tensor(out=ast[:], in0=Osb[0:64, :], in1=rl[:], op=MUL), reads=[b_Osb, b_rl],
                      writes=[b_ast])
                P.add("sp", lambda e, h=h, cols=cols: e.dma_start(out=attT[64:128, h // 2, cols], in_=ast[:]), reads=[b_ast],
                      writes=[b_attT], dma=True, semkey="ast")
    return dict(mC1=mC1, qT=qT, b_qT=b_qT, w_uk_b=w_uk_b, b_wkb=b_wkb, w_uv_b=w_uv_b, b_wvb=b_wvb, rotm_d=rotm_d)


def build_tail(P, AR, nc, din, dout, dint, stores, banks, bank_bufs, cast_w, xT, w_in_b, b_w_in_b, sm, b_sm, smc, ones_bf, b_ones,
               attT, b_attT, ysT, b_ys, chunks):
    MUL, ADD = ALU.mult, ALU.add
    names = [("w_oa", 512, D), ("w_os", 512, D), ("w_out", D, D), ("w_up", D, 2 * D_FF), ("w_down", D_FF, D),
             ("w_pg", D, D), ("w_pp", PLE, D)]
    W, bW = {"w_in": w_in_b}, {"w_in": b_w_in_b}
    for nm, r, c in names:
        src = din(nm, [r, c])
        W[nm] = dint(nm + "_b", [r, c], BF16)
        bW[nm] = cast_w(W[nm], src, r, c, "c_" + nm)
    pT_d = din("pT", [PLE, T])
    scT_d = din("scT", [128, 44, 16, 2])
    o_yT = dout("o_yT", [D, T])
    o_cvp = dout("o_cvp", [128, 44, 2])
    o_cvs = dout("o_cvs", [128, 44, 16, 2])
    cc_h_in = dint("cc_h_in", [128, 16], F32)
    cc_h_out = dint("cc_h_out", [512, 16], F32)

    AR.release(0)
    P.new_phase()
    HO = 2
    x1 = AR.alloc([128, KD, 514], F32)
    x1c4 = AR.alloc([128, KD, 290], F32)
    xn = AR.alloc([128, KD, 514], BF16)
    scr = AR.alloc([128, KD, 514], BF16)
    hT = AR.alloc([128, 22, 512], BF16)
    upx = [[AR.alloc([128, 514], F32) for _ in range(2)] for _ in range(2)]
    cav = [[AR.alloc([128, 512], F32) for _ in range(2)] for _ in range(2)]
    sgt = [AR.alloc([128, 512], F32) for _ in range(2)]
    lnv = AR.alloc([128, 514], F32)
    rstd = AR.alloc([128, 514], F32)
    pTc = AR.alloc([128, 2, 512], BF16)
    scT = AR.alloc([128, 44, 16, 2], F32)
    upS = [AR.alloc([128, 16, 6], F32) for _ in range(2)]
    cvp = AR.alloc([128, 44, 2], F32)
    cvs = AR.alloc([128, 44, 16, 2], F32)
    carry = AR.alloc([128, 44, 2], F32)
    Hg = AR.alloc([128, 4, 16], F32)
    hsend = AR.alloc([128, 16], F32)
    hrecv = AR.alloc([128, 16], F32)
    NSLAB = 4
    slabs = [AR.alloc([128, 4096], BF16) for _ in range(NSLAB)]
    b_x1, b_x1c4, b_xn, b_scr, b_hT, b_lnv, b_rstd, b_pTc, b_scT, b_cvp, b_cvs, b_carry, b_Hg, b_hsend, b_hrecv = [
        P.buf(n) for n in ("x1", "x1c4", "xn", "scr", "hT", "lnv", "rstd", "pTc", "scT", "cvp", "cvs", "carry", "Hg", "hsend", "hrecv")]
    b_upx = [[P.buf(f"upx{i}{j}") for j in range(2)] for i in range(2)]
    b_cav = [[P.buf(f"cav{i}{j}") for j in range(2)] for i in range(2)]
    b_sgt = [P.buf("sgt0"), P.buf("sgt1")]
    b_upS = [P.buf("upS0"), P.buf("upS1")]
    b_slab = [P.buf(f"slab{i}") for i in range(NSLAB)]
    P.add("sp", lambda e: e.dma_start(out=scT[:], in_=scT_d), writes=[b_scT], dma=True, semkey="scT")
    P.add("pool", lambda e: e.memset(carry[:], 0.0), writes=[b_carry])

    rr = [0]

    def tbank():
        i = rr[0] % 8
        rr[0] += 1
        return banks[i], bank_bufs[i]

    sl = [0]

    def load_slab(wname, kt, c0, width):
        i = sl[0] % NSLAB
        sl[0] += 1
        v = slabs[i][:, 0:kt * width].rearrange("p (k m) -> p k m", k=kt)
        src, bsrc = W[wname], bW[wname]
        P.add("sp", lambda e: e.dma_start(out=v, in_=src.rearrange("(k p) m -> p k m", p=128)[:, :, c0:c0 + width]),
              reads=[bsrc], writes=[b_slab[i]], dma=True, semkey=f"slab{i}")
        return v, b_slab[i]

    def mm_group(ps, b_ps, n, w, b_w, kt, wc0, rhs_fn, rd):
        for k in range(kt):
            P.add("pe", lambda e, k=k: e.matmul(ps[:, 0:n], lhsT=w[:, k, wc0:wc0 + 128], rhs=rhs_fn(k), start=(k == 0), stop=(k == kt - 1)),
                  reads=[b_w] + rd, writes=[b_ps])

    def norm_to_xn(src, b_src, gname, c_lo, c_hi):
        n = c_hi - c_lo
        P.add("pool", lambda e: e.tensor_tensor(out=scr[:, :, c_lo:c_hi], in0=src[:, :, c_lo:c_hi], in1=src[:, :, c_lo:c_hi], op=MUL),
              reads=[b_src], writes=[b_scr])
        ps, b_ps = tbank()
        for k in range(KD):
            P.add("pe", lambda e, k=k: e.matmul(ps[:, 0:n], lhsT=ones_bf[:], rhs=scr[:, k, c_lo:c_hi], start=(k == 0), stop=(k == KD - 1)),
                  reads=[b_scr, b_ones], writes=[b_ps])
        P.add("act", lambda e: e.activation(out=lnv[:, 0:n], in_=ps[:, 0:n], func=AF.Ln, scale=1.0 / D, bias=EPS), reads=[b_ps],
              writes=[b_lnv])
        P.add("act", lambda e: e.activation(out=rstd[:, 0:n], in_=lnv[:, 0:n], func=AF.Exp, scale=-0.5), reads=[b_lnv], writes=[b_rstd])
        for k in range(KD):
            P.add("dve", lambda e, k=k: e.scalar_tensor_tensor(out=xn[:, k, c_lo:c_hi], in0=src[:, k, c_lo:c_hi], scalar=smc(gname, k),
                                                               in1=rstd[:, 0:n], op0=MUL, op1=MUL),
                  reads=[b_src, b_rstd, b_sm], writes=[b_xn])

    xT_v = xT.rearrange("(k p) t -> p k t", p=128)

    def phase_D(xt, b_xt, n0, n):
        lo, hi = HO, HO + n
        P.add("sp", lambda e: e.dma_start(out=xt[:, :, lo:hi], in_=xT_v[:, :, n0:n0 + n]), writes=[b_xt], dma=True, semkey="x1ld")
        norm_to_xn(xt, b_xt, "g_mix", lo, hi)
        mixed = scr
        for q in range(2):
            wga, b_wga = load_slab("w_in", KD, OFF_GA + 512 * q, 512)
            wgs, b_wgs = load_slab("w_in", KD, OFF_GS + 512 * q, 512)
            woa, b_woa = load_slab("w_oa", 4, 512 * q, 512)
            wos, b_wos = load_slab("w_os", 4, 512 * q, 512)
            for mi in range(4):
                m = 4 * q + mi
                pga, b_pga = tbank()
                mm_group(pga, b_pga, n, wga, b_wga, KD, 128 * mi, lambda k: xn[:, k, lo:hi], [b_xn])
                pgs, b_pgs = tbank()
                mm_group(pgs, b_pgs, n, wgs, b_wgs, KD, 128 * mi, lambda k: xn[:, k, lo:hi], [b_xn])
                poa, b_poa = tbank()
                mm_group(poa, b_poa, n, woa, b_woa, 4, 128 * mi, lambda k: attT[:, k, n0:n0 + n], [b_attT])
                pos_, b_pos = tbank()
                mm_group(pos_, b_pos, n, wos, b_wos, 4, 128 * mi, lambda k: ysT[:, k, n0:n0 + n], [b_ys])
                P.add("act", lambda e, pga=pga: e.activation(out=sgt[0][:, 0:n], in_=pga[:, 0:n], func=AF.Sigmoid), reads=[b_pga],
                      writes=[b_sgt[0]])
                P.add("act", lambda e, pgs=pgs: e.activation(out=sgt[1][:, 0:n], in_=pgs[:, 0:n], func=AF.Sigmoid), reads=[b_pgs],
                      writes=[b_sgt[1]])
                P.add("dve", lambda e, poa=poa: e.tensor_tensor(out=cav[0][0][:, 0:n], in0=poa[:, 0:n], in1=sgt[0][:, 0:n], op=MUL),
                      reads=[b_poa, b_sgt[0]], writes=[b_cav[0][0]])
                P.add("dve", lambda e, pos_=pos_: e.tensor_tensor(out=cav[0][1][:, 0:n], in0=pos_[:, 0:n], in1=sgt[1][:, 0:n], op=MUL),
                      reads=[b_pos, b_sgt[1]], writes=[b_cav[0][1]])
                P.add("pool", lambda e, m=m: e.tensor_tensor(out=mixed[:, m, lo:hi], in0=cav[0][0][:, 0:n], in1=cav[0][1][:, 0:n], op=ADD),
                      reads=[b_cav[0][0], b_cav[0][1]], writes=[b_scr])
        for q in range(2):
            wo, b_wo = load_slab("w_out", KD, 512 * q, 512)
            for mi in range(4):
                m = 4 * q + mi
                ps, b_ps = tbank()
                mm_group(ps, b_ps, n, wo, b_wo, KD, 128 * mi, lambda k: mixed[:, k, lo:hi], [b_scr])
                P.add("dve", lambda e, ps=ps, m=m: e.tensor_tensor(out=xt[:, m, lo:hi], in0=ps[:, 0:n], in1=xt[:, m, lo:hi], op=ADD),
                      reads=[b_ps, b_xt], writes=[b_xt])

    ucount = [0]

    def conv3(ci_, src3, dst, b_src, b_dst, tile, nn, view=None):
        w0, w1, w2 = (smc("conv_w", tap * 44 + tile) for tap in range(3))
        P.add("act", lambda e: e.activation(out=dst, in_=src3(2), func=AF.Identity, scale=w2, bias=smc("conv_b", tile)),
              reads=[b_src, b_sm], writes=[b_dst])
        P.add("dve", lambda e: e.scalar_tensor_tensor(out=dst, in0=src3(1), scalar=w1, in1=dst, op0=MUL, op1=ADD),
              reads=[b_src, b_sm, b_dst], writes=[b_dst])
        P.add("dve", lambda e: e.scalar_tensor_tensor(out=dst, in0=src3(0), scalar=w0, in1=dst, op0=MUL, op1=ADD),
              reads=[b_src, b_sm, b_dst], writes=[b_dst])

    def phase_E(xt, b_xt, n0, n, first, last):
        lo, hi = HO, HO + n
        c_lo = 0 if first else HO
        N = hi - c_lo
        npr = min(n0 + n, NPT) - n0
        ns = n - npr
        norm_to_xn(xt, b_xt, "g_ffn", c_lo, hi)
        for q in range(6):
            npair = min(4, 22 - 4 * q)
            wa, b_wa = load_slab("w_up", KD, 512 * q, 128 * npair)
            wv_, b_wv = load_slab("w_up", KD, D_FF + 512 * q, 128 * npair)
            for pi_ in range(npair):
                p = 4 * q + pi_
                ub = ucount[0] % 2
                ucount[0] += 1
                for av, (w, b_w) in enumerate(((wa, b_wa), (wv_, b_wv))):
                    tile = p + 22 * av
                    u, b_u = upx[ub][av], b_upx[ub][av]
                    c, b_c = cav[ub][av], b_cav[ub][av]
                    ps, b_ps = tbank()
                    mm_group(ps, b_ps, N, w, b_w, KD, 128 * pi_, lambda k: xn[:, k, c_lo:hi], [b_xn])
                    P.add("act", lambda e, ps=ps, u=u: e.activation(out=u[:, c_lo:hi], in_=ps[:, 0:N], func=AF.Copy), reads=[b_ps],
                          writes=[b_u])
                    if not first:
                        P.add("pool", lambda e, u=u, tile=tile: e.tensor_copy(out=u[:, 0:2], in_=carry[:, tile, :]), reads=[b_carry],
                              writes=[b_u])
                    conv3(0, lambda k, u=u: u[:, k:k + npr], c[:, 0:npr], b_u, b_c, tile, npr)
                    P.add("pool", lambda e, u=u, tile=tile: e.tensor_copy(out=carry[:, tile, :], in_=u[:, npr:npr + 2]), reads=[b_u],
                          writes=[b_carry])
                    if last:
                        P.add("pool", lambda e, u=u, tile=tile: e.tensor_copy(out=cvp[:, tile, :], in_=u[:, npr:npr + 2]), reads=[b_u],
                              writes=[b_cvp])
                    if ns:
                        us, b_us = upS[av], b_upS[av]
                        P.add("pool", lambda e, us=us, tile=tile: e.tensor_copy(out=us[:, :, 0:2], in_=scT[:, tile, :, :]), reads=[b_scT],
                              writes=[b_us])
                        P.add("pool", lambda e, us=us, u=u: e.tensor_copy(
                            out=us[:, :, 2:6], in_=u[:, HO + npr:HO + n].rearrange("p (q s) -> p q s", s=4)), reads=[b_u], writes=[b_us])
                        conv3(0, lambda k, us=us: us[:, :, k:k + 4], c[:, npr:n].rearrange("p (q s) -> p q s", s=4), b_us, b_c, tile, ns)
                        P.add("pool", lambda e, us=us, tile=tile: e.tensor_copy(out=cvs[:, tile, :, :], in_=us[:, :, 4:6]), reads=[b_us],
                              writes=[b_cvs])
                ca, cv = cav[ub][0], cav[ub][1]
                P.add("act", lambda e, ca=ca: e.activation(out=ca[:, 0:n], in_=ca[:, 0:n], func=AF.Gelu_apprx_tanh), reads=[b_cav[ub][0]],
                      writes=[b_cav[ub][0]])
                P.add("dve", lambda e, ca=ca, cv=cv, p=p: e.tensor_tensor(out=hT[:, p, 0:n], in0=ca[:, 0:n], in1=cv[:, 0:n], op=MUL),
                      reads=[b_cav[ub][0], b_cav[ub][1]], writes=[b_hT])
        for m in range(KD):
            wd, b_wd = load_slab("w_down", 22, 128 * m, 128)
            ps, b_ps = tbank()
            mm_group(ps, b_ps, n, wd, b_wd, 22, 0, lambda k: hT[:, k, 0:n], [b_hT])
            P.add("dve", lambda e, ps=ps, m=m: e.tensor_tensor(out=xt[:, m, lo:hi], in0=ps[:, 0:n], in1=xt[:, m, lo:hi], op=ADD),
                  reads=[b_ps, b_xt], writes=[b_xt])
        norm_to_xn(xt, b_xt, "g_ple", lo, hi)
        P.add("pool", lambda e: e.dma_start(out=pTc[:, :, 0:n], in_=pT_d.rearrange("(k p) t -> p k t", p=128)[:, :, n0:n0 + n]),
              writes=[b_pTc], dma=True, semkey="pTc")
        for q in range(2):
            wg_, b_wg = load_slab("w_pg", KD, 512 * q, 512)
            wp_, b_wp = load_slab("w_pp", 2, 512 * q, 512)
            for mi in range(4):
                m = 4 * q + mi
                pg, b_pg = tbank()
                mm_group(pg, b_pg, n, wg_, b_wg, KD, 128 * mi, lambda k: xn[:, k, lo:hi], [b_xn])
                pp, b_pp = tbank()
                mm_group(pp, b_pp, n, wp_, b_wp, 2, 128 * mi, lambda k: pTc[:, k, 0:n], [b_pTc])
                P.add("act", lambda e, pg=pg: e.activation(out=sgt[0][:, 0:n], in_=pg[:, 0:n], func=AF.Sigmoid), reads=[b_pg],
                      writes=[b_sgt[0]])
                P.add("dve", lambda e, pp=pp: e.tensor_tensor(out=sgt[1][:, 0:n], in0=pp[:, 0:n], in1=sgt[0][:, 0:n], op=MUL),
                      reads=[b_pp, b_sgt[0]], writes=[b_sgt[1]])
                P.add("pool", lambda e, m=m: e.tensor_tensor(out=xt[:, m, lo:hi], in0=xt[:, m, lo:hi], in1=sgt[1][:, 0:n], op=ADD),
                      reads=[b_xt, b_sgt[1]], writes=[b_xt])
        stores.append(P.add("sp", lambda e: e.dma_start(out=o_yT.rearrange("(k p) t -> p k t", p=128)[:, :, n0:n0 + n], in_=xt[:, :, lo:hi]),
                            reads=[b_xt], dma=True, semkey="st_y"))

    n0_4, n_4 = chunks[4]
    phase_D(x1c4, b_x1c4, n0_4, n_4)
    lastp = HO + (NPT - n0_4)
    P.add("pool", lambda e: e.tensor_copy(out=hsend[:].rearrange("p (k c) -> p k c", c=2), in_=x1c4[:, :, lastp - 2:lastp]),
          reads=[b_x1c4], writes=[b_hsend])
    b_hin, b_hout = P.buf("cc_h_in"), P.buf("cc_h_out")
    P.add("sp", lambda e: e.dma_start(out=cc_h_in, in_=hsend[:]), reads=[b_hsend], writes=[b_hin], dma=True, semkey="hs1")
    P.add("pool", lambda e: e.collective_compute("AllGather", ALU.bypass, replica_groups=[[0, 1, 2, 3], [4, 5, 6, 7]],
                                                 ins=[cc_h_in.opt()], outs=[cc_h_out.opt()]),
          reads=[b_hin], writes=[b_hout], dma="cc", semkey="hs2")
    P.add("sp", lambda e: e.dma_start(out=Hg[:], in_=cc_h_out.rearrange("(r p) f -> p r f", p=128)), reads=[b_hout], writes=[b_Hg],
          dma=True, semkey="hs3")
    P.add("dve", lambda e: e.tensor_scalar(out=hrecv[:], in0=Hg[:, 0, :], scalar1=smc("hsel", 0), scalar2=None, op0=MUL),
          reads=[b_Hg, b_sm], writes=[b_hrecv])
    for r in range(1, 4):
        P.add("dve", lambda e, r=r: e.scalar_tensor_tensor(out=hrecv[:], in0=Hg[:, r, :], scalar=smc("hsel", r), in1=hrecv[:],
                                                           op0=MUL, op1=ADD), reads=[b_Hg, b_sm, b_hrecv], writes=[b_hrecv])
    for ci in range(4):
        n0, n = chunks[ci]
        if ci == 0:
            P.add("pool", lambda e: e.tensor_copy(out=x1[:, :, 0:2], in_=hrecv[:].rearrange("p (k c) -> p k c", c=2)),
                  reads=[b_hrecv], writes=[b_x1])
        phase_D(x1, b_x1, n0, n)
        phase_E(x1, b_x1, n0, n, ci == 0, False)
    phase_E(x1c4, b_x1c4, n0_4, n_4, False, True)
    stores.append(P.add("sp", lambda e: e.dma_start(out=o_cvp, in_=cvp[:]), reads=[b_cvp], dma=True, semkey="st_cvp"))
    stores.append(P.add("sp", lambda e: e.dma_start(out=o_cvs, in_=cvs[:]), reads=[b_cvs], dma=True, semkey="st_cvs"))


def build_sample_attn(P, AR, nc, din, dout, dint, stores, banks, bank_bufs, sm, b_sm, smc, ones_bf, b_ones, ckvn, b_ckvn, krK, b_krK,
                      qT, b_qT, attT, b_attT, n_pool, w_uk_b, b_wkb, w_uv_b, b_wvb, rope_c, rope_s, rotm_d, mC1):
    MUL, ADD = ALU.mult, ALU.add
    CW = KV_LORA + QK_ROPE
    cache = din("cache", [n_pool * 32, 4 * CW])
    ptab = din("ptab", [128, SEQ_PER_CORE * 16], I32)
    p32c_d = din("p32c", [128, 1], I32)
    w_ukT_d = din("w_ukT", [64, 8 * KV_LORA])
    ropeP_d = din("ropeP", [128, 2, NPAGES, 16])
    gk_rep_d = din("gk_rep", [128, 32])
    hselm_d = din("hselm", [128, 4, 32])
    cmask_d = din("cmask", [32, 4])

    AR.release(mC1)
    P.new_phase()
    NB_PG = 5
    pb = [AR.alloc([128, 4, CW], BF16) for _ in range(NB_PG)]
    b_pb = [P.buf(f"pb{i}") for i in range(NB_PG)]
    kt3 = [AR.alloc([128, 4, 96], BF16) for _ in range(2)]
    b_kt3 = [P.buf("kt3_0"), P.buf("kt3_1")]
    rt = [AR.alloc([128, 4, 16], F32) for _ in range(4)]
    b_rt = P.buf("rt")
    ropeP = AR.alloc([128, 2, NPAGES, 16], F32)
    gkr = AR.alloc([128, 32], F32)
    tabs = AR.alloc([128, 4, NPAGES, 16], F32)
    ptb = AR.alloc([128, SEQ_PER_CORE * 16], I32)
    idx = AR.alloc([128, SEQ_PER_CORE * 16], I32)
    iot = AR.alloc([128, 1], I32)
    wk_sb = AR.alloc([128, 2, 512], BF16)
    wv_sb = AR.alloc([128, 2, 512], BF16)
    wukT = AR.alloc([64, 8, KV_LORA], BF16)
    wukT_f = AR.alloc([64, 8 * KV_LORA], F32)
    hselm = AR.alloc([128, 4, 32], BF16)
    hselm_f = AR.alloc([128, 4, 32], F32)
    cmask = AR.alloc([32, 4], F32)
    qgk = AR.alloc([64, 8, NST], BF16)
    Qabs = AR.alloc([128, 2, SEQ_PER_CORE, 32], BF16)
    Qrope = AR.alloc([96, SEQ_PER_CORE, 32], BF16)
    cT_sb = [AR.alloc([128, 2, 512], BF16) for _ in range(2)]
    krT_sb = [AR.alloc([96, 512], BF16) for _ in range(2)]
    kn_sb = [AR.alloc([128, 512], BF16) for _ in range(4)]
    sq_sb = [AR.alloc([128, 512], BF16) for _ in range(4)]
    sqk = AR.alloc([32, 512], BF16)
    lnr = AR.alloc([32, 512], F32)
    rr_ = AR.alloc([32, 512], F32)
    sr = AR.alloc([32, 512], F32)
    Pm = AR.alloc([32, 512], BF16)
    PT_sb = [AR.alloc([128, 4, 32], BF16) for _ in range(2)]
    Lacc = AR.alloc([32, 20], F32)
    accs = AR.alloc([32, KV_LORA], F32)
    lsum = AR.alloc([32, 1], F32)
    olat = AR.alloc([32, KV_LORA], BF16)
    olT = AR.alloc([128, 2, SEQ_PER_CORE, 32], BF16)
    knew = AR.alloc([96, NST], BF16)
    kraw = AR.alloc([96, NST], BF16)
    kg32 = AR.alloc([96, NST], F32)
    kt1 = AR.alloc([96, NST], F32)
    kt2 = AR.alloc([96, NST], F32)
    rc_s = AR.alloc([96, NST], F32)
    rs_s = AR.alloc([96, NST], F32)
    rotm = AR.alloc([96, 96], F32)
    cnew = AR.alloc([4, KV_LORA], BF16)
    names = ["kt", "ropeP", "gkr", "tabs", "ptb", "idx", "iot", "wk", "wv", "wukT", "hselm", "cmask", "qgk", "Qabs", "Qrope",
             "sqk", "lnr", "rr", "sr", "Pm", "Lacc", "accs", "lsum", "olat", "olT", "knew", "kraw", "ktmp", "rcs", "rotm", "cnew"]
    B = {n: P.buf("s_" + n) for n in names}
    b_cT = [P.buf("cT0"), P.buf("cT1")]
    b_krT = [P.buf("krT0"), P.buf("krT1")]
    b_kn = [P.buf(f"kn{i}") for i in range(4)]
    b_sq = [P.buf(f"sq{i}") for i in range(4)]
    b_PT = [P.buf("PTs0"), P.buf("PTs1")]

    rrb = [0]

    def tbank():
        i = 1 + rrb[0] % 7
        rrb[0] += 1
        return banks[i], bank_bufs[i]
    ACC, b_ACC = banks[0], bank_bufs[0]

    ld = lambda out, in_, wr, key, rd=(): P.add("sp", lambda e: e.dma_start(out=out, in_=in_), reads=list(rd), writes=[wr], dma=True,
                                                semkey=key)
    ld(ropeP[:], ropeP_d, B["ropeP"], "s_ropeP")
    ld(gkr[:], gk_rep_d, B["gkr"], "s_gkr")
    ld(ptb[:], ptab, B["ptb"], "s_ptb")
    ld(iot[:], p32c_d, B["iot"], "s_iot")
    ld(wk_sb[:], w_uk_b.rearrange("(k p) m -> p k m", p=128), B["wk"], "s_wk", [b_wkb])
    ld(wv_sb[:], w_uv_b.rearrange("(k p) m -> p k m", p=128), B["wv"], "s_wv", [b_wvb])
    ld(wukT_f[:], w_ukT_d, B["wukT"], "s_wukT")
    ld(hselm_f[:], hselm_d, B["hselm"], "s_hselm")
    ld(cmask[:], cmask_d, B["cmask"], "s_cmask")
    ld(rc_s[:], rope_c[:, NPT:T], B["rcs"], "s_rcs")
    ld(rs_s[:], rope_s[:, NPT:T], B["rcs"], "s_rss")
    ld(rotm[:], rotm_d, B["rotm"], "s_rotm")
    P.add("pool", lambda e: e.tensor_copy(out=wukT[:].rearrange("p h l -> p (h l)"), in_=wukT_f[:]), reads=[B["wukT"]], writes=[B["wukT"]])
    P.add("pool", lambda e: e.tensor_copy(out=hselm[:], in_=hselm_f[:]), reads=[B["hselm"]], writes=[B["hselm"]])
    P.add("dve", lambda e: e.tensor_scalar(out=idx[:], in0=ptb[:], scalar1=32.0, scalar2=iot[:, 0:1], op0=MUL, op1=ADD),
          reads=[B["ptb"], B["iot"]], writes=[B["idx"]])
    g1 = gkr[:, 0:16].unsqueeze(1).broadcast_to([128, NPAGES, 16])
    g2 = gkr[:, 16:32].unsqueeze(1).broadcast_to([128, NPAGES, 16])
    tt_ = lambda o, a, b_, op: P.add("pool", lambda e: e.tensor_tensor(out=o, in0=a, in1=b_, op=op), reads=[B["ropeP"], B["gkr"], B["tabs"]],
                                     writes=[B["tabs"]])
    tt_(tabs[:, 0], ropeP[:, 0], g1, MUL)
    tt_(tabs[:, 1], ropeP[:, 0], g2, MUL)
    tt_(tabs[:, 2], ropeP[:, 1], g2, MUL)
    P.add("pool", lambda e: e.tensor_single_scalar(out=tabs[:, 2], in_=tabs[:, 2], scalar=-1.0, op=MUL), reads=[B["tabs"]],
          writes=[B["tabs"]])
    tt_(tabs[:, 3], ropeP[:, 1], g1, MUL)
    for i in range(2):
        P.add("pool", lambda e, i=i: e.memset(kt3[i][:], 0.0), writes=[b_kt3[i]])

    P.add("dve", lambda e: e.tensor_scalar(out=qgk[:], in0=qT[0:64, :, NPT:T], scalar1=smc("g_k")[0:64, :], scalar2=None, op0=MUL),
          reads=[b_qT, b_sm], writes=[B["qgk"]])
    for h in range(N_HEADS):
        for kt in range(2):
            ps, b_ps = tbank()
            P.add("pe", lambda e, ps=ps, h=h, kt=kt: e.matmul(ps[:, 0:NST], lhsT=wukT[:, h, 128 * kt:128 * kt + 128], rhs=qgk[:, h, :],
                                                              start=True, stop=True), reads=[B["wukT"], B["qgk"]], writes=[b_ps])
            P.add("act", lambda e, ps=ps, h=h, kt=kt: e.activation(out=Qabs[:, kt, :, 4 * h:4 * h + 4],
                                                                   in_=ps[:, 0:NST].rearrange("p (q t) -> p q t", t=4), func=AF.Copy),
                  reads=[b_ps], writes=[B["Qabs"]])
        P.add("pool", lambda e, h=h: e.tensor_copy(out=Qrope[64:96, :, 4 * h:4 * h + 4],
                                                   in_=qT[64:96, h, NPT:T].rearrange("p (q t) -> p q t", t=4)),
              reads=[b_qT], writes=[B["Qrope"]])
    P.add("act", lambda e: e.activation(out=kraw[64:96, :], in_=krK[64:96, NPT:T], func=AF.Copy), reads=[b_krK], writes=[B["kraw"]])
    P.add("dve", lambda e: e.tensor_scalar(out=kg32[64:96, :], in0=krK[64:96, NPT:T], scalar1=smc("g_k")[64:96, :], scalar2=None, op0=MUL),
          reads=[b_krK, b_sm], writes=[B["ktmp"]])
    ps, b_ps = tbank()
    P.add("pe", lambda e, ps=ps: e.matmul(ps[0:96, 0:NST], lhsT=rotm[64:96, 0:96], rhs=kg32[64:96, :], start=True, stop=True,
                                          tile_position=(64, 0)), reads=[B["rotm"], B["ktmp"]], writes=[b_ps])
    P.add("dve", lambda e: e.tensor_tensor(out=kt1[64:96, :], in0=kg32[64:96, :], in1=rc_s[64:96, :], op=MUL), reads=[B["ktmp"], B["rcs"]],
          writes=[B["ktmp"]])
    P.add("dve", lambda e, ps=ps: e.tensor_tensor(out=kt2[64:96, :], in0=ps[64:96, 0:NST], in1=rs_s[64:96, :], op=MUL),
          reads=[b_ps, B["rcs"]], writes=[B["ktmp"]])
    P.add("dve", lambda e: e.tensor_tensor(out=knew[64:96, :], in0=kt1[64:96, :], in1=kt2[64:96, :], op=ADD), reads=[B["ktmp"]],
          writes=[B["knew"]])

    idb = AR.alloc([128, 128], BF16)
    b_idb = P.buf("idb")
    P.add("pool", lambda e: e.tensor_copy(out=idb[:], in_=sm[:, SL["ident"][0]:SL["ident"][0] + 128]), reads=[b_sm], writes=[b_idb])
    sqk2 = [sqk, AR.alloc([32, 512], BF16)]
    lnr2 = [lnr, AR.alloc([32, 512], F32)]
    rr2 = [rr_, AR.alloc([32, 512], F32)]
    sr2 = [sr, AR.alloc([32, 512], F32)]
    Pm2 = [Pm, AR.alloc([32, 512], BF16)]
    Lacc2 = [Lacc, AR.alloc([32, 20], F32)]
    Bq = [{n: P.buf(f"s2_{n}{i}") for n in ("sqk", "lnr", "rr", "sr", "Pm")} for i in range(2)]
    b_Lacc2 = [P.buf("Lacc0"), P.buf("Lacc1")]
    cnt = [0]
    gcnt = [0]

    def tbank2():
        i = 2 + rrb[0] % 6
        rrb[0] += 1
        return banks[i], bank_bufs[i]

    def chunk(q, col, npos, cT, b_cTs, kraw_ap, b_kraw, krop_ap, b_krop, crows, first, mask):
        n = npos
        ACC, b_ACC = banks[q % 2], bank_bufs[q % 2]
        Lq, b_Lq = Lacc2[q % 2], b_Lacc2[q % 2]
        ci = gcnt[0] % 2
        gcnt[0] += 1
        sqk_, lnr_, rr__, sr_, Pm_ = sqk2[ci], lnr2[ci], rr2[ci], sr2[ci], Pm2[ci]
        Bc = Bq[ci]
        pss_l = []
        for m in range(4):
            ps, b_ps = tbank2()
            for kt in range(2):
                P.add("pe", lambda e, ps=ps, m=m, kt=kt: e.matmul(ps[:, 0:n], lhsT=wk_sb[:, kt, 128 * m:128 * m + 128], rhs=cT[:, kt, 0:n],
                                                                  start=(kt == 0), stop=(kt == 1)), reads=[B["wk"]] + b_cTs, writes=[b_ps])
            pss_l.append((ps, b_ps))
        for m in range(4):
            ps, b_ps = pss_l[m]
            if m < 2:
                P.add("act", lambda e, ps=ps, m=m: e.activation(out=kn_sb[m][:, 0:n], in_=ps[:, 0:n], func=AF.Copy), reads=[b_ps],
                      writes=[b_kn[m]])
            else:
                P.add("dve", lambda e, ps=ps, m=m: e.tensor_copy(out=kn_sb[m][:, 0:n], in_=ps[:, 0:n]), reads=[b_ps], writes=[b_kn[m]])
            P.add("dve", lambda e, m=m: e.tensor_tensor(out=sq_sb[m][:, 0:n], in0=kn_sb[m][:, 0:n], in1=kn_sb[m][:, 0:n], op=MUL),
                  reads=[b_kn[m]], writes=[b_sq[m]])
        P.add("pool", lambda e: e.tensor_tensor(out=sqk_[:, 0:n], in0=kraw_ap, in1=kraw_ap, op=MUL), reads=b_kraw, writes=[Bc["sqk"]])
        yield
        pss, b_pss = tbank2()
        for m in range(4):
            P.add("pe", lambda e, m=m: e.matmul(pss[0:32, 0:n], lhsT=hselm[:, m, :], rhs=sq_sb[m][:, 0:n], start=(m == 0), stop=False),
                  reads=[B["hselm"], b_sq[m]], writes=[b_pss])
        P.add("pe", lambda e: e.matmul(pss[0:32, 0:n], lhsT=ones_bf[0:32, 0:32], rhs=sqk_[:, 0:n], start=False, stop=True),
              reads=[b_ones, Bc["sqk"]], writes=[b_pss])
        psc, b_psc = tbank2()
        for kt in range(2):
            P.add("pe", lambda e, kt=kt: e.matmul(psc[0:32, 0:n], lhsT=Qabs[:, kt, q, :], rhs=cT[:, kt, 0:n], start=(kt == 0), stop=False),
                  reads=[B["Qabs"]] + b_cTs, writes=[b_psc])
        P.add("pe", lambda e: e.matmul(psc[0:32, 0:n], lhsT=Qrope[64:96, q, :], rhs=krop_ap, start=False, stop=True, tile_position=(64, 0)),
              reads=[B["Qrope"]] + b_krop, writes=[b_psc])
        P.add("act", lambda e: e.activation(out=lnr_[:, 0:n], in_=pss[0:32, 0:n], func=AF.Ln, scale=1.0 / QK_HEAD, bias=EPS),
              reads=[b_pss], writes=[Bc["lnr"]])
        P.add("act", lambda e: e.activation(out=rr__[:, 0:n], in_=lnr_[:, 0:n], func=AF.Exp, scale=-0.5), reads=[Bc["lnr"]],
              writes=[Bc["rr"]])
        P.add("dve", lambda e: e.tensor_tensor(out=sr_[:, 0:n], in0=psc[0:32, 0:n], in1=rr__[:, 0:n], op=MUL), reads=[b_psc, Bc["rr"]],
              writes=[Bc["sr"]])
        if mask:
            P.add("act", lambda e: e.activation(out=sr_[:, 0:n], in_=sr_[:, 0:n], func=AF.Exp, scale=SCALE), reads=[Bc["sr"]],
                  writes=[Bc["sr"]])
            P.add("dve", lambda e: e.tensor_tensor(out=sr_[:, 0:n], in0=sr_[:, 0:n], in1=cmask[:, 0:n], op=MUL), reads=[Bc["sr"], B["cmask"]],
                  writes=[Bc["sr"]])
            P.add("dve", lambda e: e.tensor_copy(out=Pm_[:, 0:n], in_=sr_[:, 0:n]), reads=[Bc["sr"]], writes=[Bc["Pm"]])
            P.add("dve", lambda e: e.reduce_sum(out=Lq[:, col:col + 1], in_=sr_[:, 0:n], axis=AX.X), reads=[Bc["sr"]], writes=[b_Lq])
        else:
            P.add("act", lambda e: e.activation(out=Pm_[:, 0:n], in_=sr_[:, 0:n], func=AF.Exp, scale=SCALE, accum_out=Lq[:, col:col + 1]),
                  reads=[Bc["sr"]], writes=[Bc["Pm"], b_Lq])
        yield
        pT_, b_pT = tbank2()
        pTb = pT_[:].bitcast(BF16)
        nblk = len(crows)
        for bi, (cap, b_cap, c0, c1) in enumerate(crows):
            P.add("pe", lambda e, bi=bi, c0=c0, c1=c1: e.transpose(pTb[0:c1 - c0, 32 * bi:32 * bi + 32], Pm_[:, c0:c1], idb[0:32, 0:32]),
                  reads=[Bc["Pm"], b_idb], writes=[b_pT])
        pi_ = cnt[0] % 2
        cnt[0] += 1
        rows = crows[0][3] - crows[0][2]
        P.add("act", lambda e, pi_=pi_: e.activation(out=PT_sb[pi_][0:rows, 0:nblk, :],
                                                     in_=pTb[0:rows, 0:32 * nblk].rearrange("p (b c) -> p b c", c=32), func=AF.Copy),
              reads=[b_pT], writes=[b_PT[pi_]])
        yield
        for bi, (cap, b_cap, c0, c1) in enumerate(crows):
            P.add("pe", lambda e, bi=bi, cap=cap, c0=c0, c1=c1, pi_=pi_, st=(first and bi == 0): e.matmul(
                ACC[0:32, 0:KV_LORA], lhsT=PT_sb[pi_][0:c1 - c0, bi, :], rhs=cap, start=st, stop=False),
                reads=[b_PT[pi_]] + b_cap, writes=[b_ACC])

    pgc = [0]
    kcn = [0]

    def page_chunk(q, g):
        bi_ = pgc[0] % NB_PG
        ki = pgc[0] % 2
        ci_ = pgc[0] % 2
        pgc[0] += 1
        pbt, b_pbt = pb[bi_], b_pb[bi_]
        ch = q * 16 + g
        P.add("pool", lambda e: e.indirect_dma_start(
            out=pbt[:].rearrange("p a c -> p (a c)"), out_offset=None, in_=cache,
            in_offset=bass.IndirectOffsetOnAxis(ap=idx[:, ch:ch + 1], axis=0)),
            reads=[B["idx"]], writes=[b_pbt], dma=True, semkey=f"pb{bi_}")
        yield
        yield
        yield
        yield
        ki = kcn[0] % 2
        kcn[0] += 1
        k3, b_k3 = kt3[ki], b_kt3[ki]
        kr1, kr2 = pbt[:, :, KV_LORA:KV_LORA + 16], pbt[:, :, KV_LORA + 16:KV_LORA + 32]
        pgs = slice(4 * g, 4 * g + 4)
        pl = lambda fn, rd, wr: P.add("pool", fn, reads=rd, writes=wr)
        pl(lambda e: e.tensor_copy(out=k3[:, :, 0:32], in_=pbt[:, :, KV_LORA:CW]), [b_pbt], [b_k3])
        pl(lambda e: e.tensor_tensor(out=rt[0][:], in0=kr1, in1=tabs[:, 0, pgs, :], op=MUL), [b_pbt, B["tabs"]], [b_rt])
        pl(lambda e: e.tensor_tensor(out=rt[1][:], in0=kr2, in1=tabs[:, 2, pgs, :], op=MUL), [b_pbt, B["tabs"]], [b_rt])
        pl(lambda e: e.tensor_tensor(out=k3[:, :, 64:80], in0=rt[0][:], in1=rt[1][:], op=ADD), [b_rt], [b_k3])
        pl(lambda e: e.tensor_tensor(out=rt[2][:], in0=kr2, in1=tabs[:, 1, pgs, :], op=MUL), [b_pbt, B["tabs"]], [b_rt])
        pl(lambda e: e.tensor_tensor(out=rt[3][:], in0=kr1, in1=tabs[:, 3, pgs, :], op=MUL), [b_pbt, B["tabs"]], [b_rt])
        pl(lambda e: e.tensor_tensor(out=k3[:, :, 80:96], in0=rt[2][:], in1=rt[3][:], op=ADD), [b_rt], [b_k3])
        yield
        psT, b_psT = tbank2()
        psTb = psT[:].bitcast(BF16).rearrange("p (k n) -> p k n", k=2)
        psK, b_psK = tbank2()
        psKb = psK[:].bitcast(BF16)
        for pg in range(4):
            for kt in range(2):
                P.add("pe", lambda e, pg=pg, kt=kt: e.transpose(psTb[:, kt, 128 * pg:128 * pg + 128], pbt[:, pg, 128 * kt:128 * kt + 128],
                                                                idb[:, :]), reads=[b_pbt, b_idb], writes=[b_psT])
            P.add("pe", lambda e, pg=pg: e.transpose(psKb[0:96, 128 * pg:128 * pg + 128], k3[:, pg, :], idb[:, :]), reads=[b_k3, b_idb],
                  writes=[b_psK])
        P.add("dve", lambda e: e.tensor_copy(out=cT_sb[ci_][:], in_=psTb), reads=[b_psT], writes=[b_cT[ci_]])
        P.add("act", lambda e: e.activation(out=krT_sb[ci_][:], in_=psKb[0:96, 0:512], func=AF.Copy), reads=[b_psK], writes=[b_krT[ci_]])
        yield
        crows = [(pbt[:, pg, 0:KV_LORA], [b_pbt], 128 * pg, 128 * pg + 128) for pg in range(4)]
        yield from chunk(q, g, 512, cT_sb[ci_], [b_cT[ci_]], krT_sb[ci_][0:32, 0:512], [b_krT[ci_]], krT_sb[ci_][64:96, 0:512],
                         [b_krT[ci_]], crows, g == 0, False)

    def run_pipelined(gens, step=2):
        active = []
        it = iter(gens)
        more = True
        while more or active:
            if more:
                try:
                    active.append(next(it))
                except StopIteration:
                    more = False
            for _ in range(step):
                for gg in list(active):
                    try:
                        next(gg)
                    except StopIteration:
                        active.remove(gg)

    for q in range(SEQ_PER_CORE):
        run_pipelined([page_chunk(q, g) for g in range(NPAGES // 4)])
        ACC, b_ACC = banks[q % 2], bank_bufs[q % 2]
        Lq, b_Lq = Lacc2[q % 2], b_Lacc2[q % 2]
        c0 = NPT + 4 * q
        psn, b_psn = tbank2()
        psnb = psn[:].bitcast(BF16)
        for kt in range(2):
            P.add("pe", lambda e, kt=kt, c0=c0, psnb=psnb: e.transpose(psnb[0:4, 128 * kt:128 * kt + 128], ckvn[:, kt, c0:c0 + 4], idb[:, :]),
                  reads=[b_ckvn, b_idb], writes=[b_psn])
        P.add("act", lambda e, psnb=psnb: e.activation(out=cnew[:], in_=psnb[0:4, 0:KV_LORA], func=AF.Copy), reads=[b_psn], writes=[B["cnew"]])
        for _ in chunk(q, 16, 4, ckvn[:, :, c0:c0 + 4], [b_ckvn], kraw[64:96, 4 * q:4 * q + 4], [B["kraw"]], knew[64:96, 4 * q:4 * q + 4],
                       [B["knew"]], [(cnew[:, :], [B["cnew"]], 0, 4)], False, True):
            pass
        P.add("act", lambda e, ACC=ACC: e.activation(out=accs[:], in_=ACC[0:32, 0:KV_LORA], func=AF.Copy), reads=[b_ACC], writes=[B["accs"]])
        P.add("dve", lambda e, Lq=Lq: e.reduce_sum(out=lsum[:], in_=Lq[:, 0:17], axis=AX.X), reads=[b_Lq], writes=[B["lsum"]])
        P.add("dve", lambda e: e.reciprocal(out=lsum[:], in_=lsum[:]), reads=[B["lsum"]], writes=[B["lsum"]])
        P.add("dve", lambda e: e.tensor_scalar(out=olat[:], in0=accs[:], scalar1=lsum[:, 0:1], scalar2=None, op0=MUL),
              reads=[B["accs"], B["lsum"]], writes=[B["olat"]])
        pso, b_pso = tbank2()
        psob = pso[:].bitcast(BF16)
        for kt in range(2):
            P.add("pe", lambda e, kt=kt, psob=psob: e.transpose(psob[:, 32 * kt:32 * kt + 32], olat[:, 128 * kt:128 * kt + 128],
                                                                idb[0:32, 0:32]), reads=[B["olat"], b_idb], writes=[b_pso])
        P.add("act", lambda e, q=q, psob=psob: e.activation(out=olT[:, :, q, :], in_=psob[:, 0:64].rearrange("p (k c) -> p k c", k=2),
                                                            func=AF.Copy), reads=[b_pso], writes=[B["olT"]])
    for hp in range(4):
        ps, b_ps = tbank()
        for hh in range(2):
            h = 2 * hp + hh
            for kt in range(2):
                P.add("pe", lambda e, ps=ps, hh=hh, h=h, kt=kt: e.matmul(
                    ps[64 * hh:64 * hh + 64, 0:NST], lhsT=wv_sb[:, kt, 64 * h:64 * h + 64], rhs=olT[:, kt, :, 4 * h:4 * h + 4],
                    start=(kt == 0), stop=(kt == 1), tile_position=(0, 64 * hh)), reads=[B["wv"], B["olT"]], writes=[b_ps])
        P.add("act", lambda e, ps=ps, hp=hp: e.activation(out=attT[:, hp, NPT:T], in_=ps[:, 0:NST], func=AF.Copy), reads=[b_ps],
              writes=[b_attT])


def build(stage=99, n_pool=10240, dbg=False):
    nc = bass.Bass("TRN2", target_bir_lowering=False)
    P = Prog(nc)
    ins_, outs_ = {}, {}

    def din(name, shape, dt=F32):
        ins_[name] = nc.dram_tensor(name, list(shape), dt, kind="ExternalInput").ap()
        return ins_[name]

    def dout(name, shape, dt=F32):
        outs_[name] = nc.dram_tensor(name, list(shape), dt, kind="ExternalOutput").ap()
        return outs_[name]

    def dint(name, shape, dt):
        return nc.dram_tensor(name, list(shape), dt).ap()

    xT = din("xT", [D, T])
    small = din("small", [128, SL["_n"]])
    rope_c = din("rope_c", [96, T])
    rope_s = din("rope_s", [96, T])
    w_in = din("w_in", [D, IN_COLS])
    o_ckvT = dout("o_ckvT", [KV_LORA, T])
    o_krT = dout("o_krT", [QK_ROPE, T])

    w_in_b = dint("w_in_b", [D, IN_COLS], BF16)

    stores = []
    pool_q = "pool"

    def cast_w(dst, src, rows, cols, key):
        a = 1
        while cols // a > 2048 or cols % a:
            a += 1
        s2 = src.rearrange("k (a m) -> (k a) m", a=a) if a > 1 else src
        d2 = dst.rearrange("k (a m) -> (k a) m", a=a) if a > 1 else dst
        b = P.buf(key)
        P.add(pool_q, lambda e: e.dma_start(out=d2, in_=s2), writes=[b], dma=True, semkey=key)
        return b

    b_w_in_b = cast_w(w_in_b, w_in, D, IN_COLS, "c_w_in")
    w_glu = din("w_glu", [SSM_W, 2 * SSM_W])
    w_glu_b = dint("w_glu_b", [SSM_W, 2 * SSM_W], BF16)
    b_w_glu_b = cast_w(w_glu_b, w_glu, SSM_W, 2 * SSM_W, "c_w_glu")

    ones_bf = P.sbuf("ones_bf", [128, 128], BF16)
    b_ones = P.buf("ones")
    P.add("pool", lambda e: e.memset(ones_bf[:], 1.0), writes=[b_ones])
    sm = P.sbuf("sm", [128, SL["_n"]], F32)
    b_sm = P.buf("sm")
    P.add("sp", lambda e: e.dma_start(out=sm[:], in_=small), writes=[b_sm], dma=True, semkey="sm")

    def smc(name, i=0):
        o = SL[name][0] + i
        return sm[:, o:o + 1]

    NB = 8
    banks = [P.psum(f"ps{i}", [128, 512], F32) for i in range(NB)]
    bank_bufs = [P.buf(f"ps{i}") for i in range(NB)]
    bank_rr = [0]

    def next_bank():
        i = bank_rr[0] % NB
        bank_rr[0] += 1
        return banks[i], bank_bufs[i]

    cqn = P.sbuf("cqn", [128, 3, T], BF16)
    ckvn = P.sbuf("ckvn", [128, 2, T], BF16)
    krK = P.sbuf("krK", [96, T], F32)
    uT = P.sbuf("uT", [128, 4, T], BF16)
    b_cqn, b_ckvn, b_krT, b_uT = P.buf("cqn"), P.buf("ckvn"), P.buf("krT"), P.buf("uT")

    chunks = _token_chunks()
    AR = Arena(P, "arena", 142 * 1024)

    NA = OFF_GA
    wA = AR.alloc([128, KD, NA], BF16)
    b_wA = P.buf("wA")
    P.add("sp", lambda e: e.dma_start(out=wA[:], in_=w_in_b.rearrange("(k p) m -> p k m", p=128)[:, :, 0:NA]),
          reads=[b_w_in_b], writes=[b_wA], dma=True, semkey="wA")

    xc = [AR.alloc([128, KD, 512], F32) for i in range(2)]
    b_xc = [P.buf(f"xc{i}") for i in range(2)]
    sq = AR.alloc([128, KD, 512], BF16)
    b_sq = P.buf("sq")
    lnv = AR.alloc([128, 512], F32)
    b_lnv = P.buf("lnv")
    rstd = AR.alloc([128, 512], F32)
    b_rstd = P.buf("rstd")
    xn = AR.alloc([128, KD, 512], BF16)
    b_xn = P.buf("xn")
    cqf = AR.alloc([128, 3, 512], F32)
    b_cqf = P.buf("cqf")
    ckvf = AR.alloc([128, 2, 512], F32)
    b_ckvf = P.buf("ckvf")
    ckvo = AR.alloc([128, 2, 512], F32)
    b_ckvo = P.buf("ckvo")

    xT_v = xT.rearrange("(k p) t -> p k t", p=128)

    def rms_rstd(src, b_src, nk, n, nfeat, dst=rstd, b_dst=b_rstd):
        P.add("dve", lambda e: e.tensor_tensor(out=sq[:, 0:nk, 0:n], in0=src[:, 0:nk, 0:n], in1=src[:, 0:nk, 0:n],
                                               op=ALU.mult), reads=[b_src], writes=[b_sq])
        ps, b_ps = next_bank()
        for k in range(nk):
            P.add("pe", lambda e, k=k: e.matmul(ps[:, 0:n], lhsT=ones_bf[:], rhs=sq[:, k, 0:n], start=(k == 0),
                                                stop=(k == nk - 1)), reads=[b_sq, b_ones], writes=[b_ps])
        P.add("act", lambda e: e.activation(out=lnv[:, 0:n], in_=ps[:, 0:n], func=AF.Ln, scale=1.0 / nfeat, bias=EPS),
              reads=[b_ps], writes=[b_lnv])
        P.add("act", lambda e: e.activation(out=dst[:, 0:n], in_=lnv[:, 0:n], func=AF.Exp, scale=-0.5),
              reads=[b_lnv], writes=[b_dst])

    for ci, (n0, n) in enumerate(chunks):
        xb, b_x = xc[ci % 2], b_xc[ci % 2]
        P.add("sp", lambda e, xb=xb, n0=n0, n=n: e.dma_start(out=xb[:, :, 0:n], in_=xT_v[:, :, n0:n0 + n]),
              writes=[b_x], dma=True, semkey=f"xc{ci % 2}")
        rms_rstd(xb, b_x, KD, n, D)
        for k in range(KD):
            P.add("dve", lambda e, k=k, xb=xb, n=n: e.scalar_tensor_tensor(
                out=xn[:, k, 0:n], in0=xb[:, k, 0:n], scalar=smc("g_mix", k), in1=rstd[:, 0:n],
                op0=ALU.mult, op1=ALU.mult), reads=[b_x, b_rstd, b_sm], writes=[b_xn])
        groups = [("cq", i, OFF_CKV * 0 + 128 * i, 128) for i in range(3)] + \
                 [("ckv", i, OFF_CKV + 128 * i, 128) for i in range(2)] + \
                 [("kr", 0, OFF_KR, 32)] + [("u", i, OFF_U + 128 * i, 128) for i in range(4)]
        for kind, i, c0, m in groups:
            ps, b_ps = next_bank()
            for k in range(KD):
                if kind == "kr":
                    P.add("pe", lambda e, k=k, c0=c0, m=m, ps=ps, n=n: e.matmul(
                        ps[64:96, 0:n], lhsT=wA[:, k, c0:c0 + m], rhs=xn[:, k, 0:n], start=(k == 0), stop=(k == KD - 1),
                        tile_position=(0, 64)), reads=[b_wA, b_xn], writes=[b_ps])
                    continue
                P.add("pe", lambda e, k=k, c0=c0, m=m, ps=ps, n=n: e.matmul(
                    ps[0:m, 0:n], lhsT=wA[:, k, c0:c0 + m], rhs=xn[:, k, 0:n], start=(k == 0), stop=(k == KD - 1)),
                    reads=[b_wA, b_xn], writes=[b_ps])
            if kind == "cq":
                P.add("act", lambda e, i=i, ps=ps, n=n: e.activation(out=cqf[:, i, 0:n], in_=ps[:, 0:n], func=AF.Copy),
                      reads=[b_ps], writes=[b_cqf])
            elif kind == "ckv":
                P.add("act", lambda e, i=i, ps=ps, n=n: e.activation(out=ckvf[:, i, 0:n], in_=ps[:, 0:n], func=AF.Copy),
                      reads=[b_ps], writes=[b_ckvf])
            elif kind == "kr":
                P.add("act", lambda e, ps=ps, n=n, n0=n0: e.activation(out=krK[64:96, n0:n0 + n], in_=ps[64:96, 0:n], func=AF.Copy),
                      reads=[b_ps], writes=[b_krT])
            else:
                P.add("act", lambda e, i=i, ps=ps, n=n, n0=n0: e.activation(out=uT[:, i, n0:n0 + n], in_=ps[:, 0:n], func=AF.Copy),
                      reads=[b_ps], writes=[b_uT])
        rms_rstd(cqf, b_cqf, 3, n, Q_LORA)
        for i in range(3):
            P.add("dve", lambda e, i=i, n=n, n0=n0: e.scalar_tensor_tensor(
                out=cqn[:, i, n0:n0 + n], in0=cqf[:, i, 0:n], scalar=smc("g_cq", i), in1=rstd[:, 0:n],
                op0=ALU.mult, op1=ALU.mult), reads=[b_cqf, b_rstd, b_sm], writes=[b_cqn])
        rms_rstd(ckvf, b_ckvf, 2, n, KV_LORA)
        for i in range(2):
            P.add("dve", lambda e, i=i, n=n: e.scalar_tensor_tensor(
                out=ckvo[:, i, 0:n], in0=ckvf[:, i, 0:n], scalar=smc("g_ckv", i), in1=rstd[:, 0:n],
                op0=ALU.mult, op1=ALU.mult), reads=[b_ckvf, b_rstd, b_sm], writes=[b_ckvo])
        P.add("pool", lambda e, n=n, n0=n0: e.tensor_copy(out=ckvn[:, :, n0:n0 + n], in_=ckvo[:, :, 0:n]),
              reads=[b_ckvo], writes=[b_ckvn])
        stores.append(P.add("sp", lambda e, n=n, n0=n0: e.dma_start(
            out=o_ckvT.rearrange("(k p) t -> p k t", p=128)[:, :, n0:n0 + n], in_=ckvo[:, :, 0:n]),
            reads=[b_ckvo], dma=True, semkey="st_ckv"))
    stores.append(P.add("sp", lambda e: e.dma_start(out=o_krT, in_=krK[64:96, :]), reads=[b_krT], dma=True, semkey="st_kr"))


    if stage >= 2:
        ssm = build_ssm(P, AR, nc, din, dout, dint, stores, next_bank, uT, b_uT, smc, b_sm, sm, w_glu_b, b_w_glu_b, chunks)
        if dbg:
            o_dbg_ys = dout("o_dbg_ys", [128, 4, T], BF16)
            stores.append(P.add("sp", lambda e: e.dma_start(out=o_dbg_ys, in_=uT[:]), reads=[b_uT], dma=True, semkey="dbg_ys"))


    if stage >= 3:
        attT = P.sbuf("attT", [128, 4, T], BF16)
        b_attT = P.buf("attT")
        kS = P.sbuf("kS", [96, 8, NST], BF16)
        b_kS = P.buf("kS")
        att = build_attn(P, AR, nc, din, dout, dint, stores, banks, bank_bufs, cast_w, cqn, b_cqn, ckvn, b_ckvn, krK, b_krT,
                         sm, b_sm, smc, ones_bf, b_ones, rope_c, rope_s, attT, b_attT, chunks, kS, b_kS)
        if stage >= 5:
            build_sample_attn(P, AR, nc, din, dout, dint, stores, banks, bank_bufs, sm, b_sm, smc, ones_bf, b_ones, ckvn, b_ckvn,
                              krK, b_krT, att["qT"], att["b_qT"], attT, b_attT, n_pool, att["w_uk_b"], att["b_wkb"], att["w_uv_b"],
                              att["b_wvb"], rope_c, rope_s, att["rotm_d"], att["mC1"])
        if dbg:
            o_dbg_att = dout("o_dbg_att", [128, 4, T], BF16)
            stores.append(P.add("sp", lambda e: e.dma_start(out=o_dbg_att, in_=attT[:]), reads=[b_attT], dma=True, semkey="dbg_att"))


    if stage >= 4:
        build_tail(P, AR, nc, din, dout, dint, stores, banks, bank_bufs, cast_w, xT, w_in_b, b_w_in_b, sm, b_sm, smc, ones_bf, b_ones,
                   attT, b_attT, ssm["ysT"], ssm["b_ys"], chunks)

    P.add("sp", lambda e: None, after=stores)
    P.finalize()
    return nc, ins_, outs_, P


def _rope_tables(pos):
    inv_freq = np.power(np.float32(10000.0), -np.arange(0, QK_ROPE, 2, dtype=np.float32) / np.float32(QK_ROPE)).astype(np.float32)
    ang = pos.astype(np.float32)[:, None] * inv_freq[None, :]
    return np.cos(ang).astype(np.float32), np.sin(ang).astype(np.float32)


def _prep_core(c, inp):
    b, j = c // 4, c % 4
    m = {}
    xp = inp["x_prompt"][b, NPT * j:NPT * (j + 1)]
    xs = inp["x_sample"][SEQ_PER_CORE * c:SEQ_PER_CORE * (c + 1)].reshape(NST, D)
    m["xT"] = np.ascontiguousarray(np.concatenate([xp, xs], 0).T)
    pos = np.concatenate([NPT * j + np.arange(NPT), np.tile(PAST + np.arange(4), SEQ_PER_CORE)])
    cs, sn = _rope_tables(pos)
    rc = np.ones((96, T), np.float32)
    rs = np.zeros((96, T), np.float32)
    rc[64:80] = cs.T
    rc[80:96] = cs.T
    rs[64:80] = sn.T
    rs[80:96] = sn.T
    m["rope_c"], m["rope_s"] = rc, rs
    sm = np.zeros((128, SL["_n"]), np.float32)

    def put(name, arr):
        o, n = SL[name]
        sm[:arr.shape[0], o:o + arr.shape[1]] = arr
    put("g_mix", inp["g_mix"][0].reshape(8, 128).T)
    put("g_cq", inp["g_cq"][0].reshape(3, 128).T)
    put("g_ckv", inp["g_ckv"][0].reshape(2, 128).T)
    put("g_ffn", inp["g_ffn"][0].reshape(8, 128).T)
    put("g_ple", inp["g_ple"][0].reshape(8, 128).T)
    cw = inp["conv_w"][0].reshape(3, 44, 128)
    put("conv_w", cw.transpose(2, 0, 1).reshape(128, 132))
    put("conv_b", inp["conv_b"][0].reshape(44, 128).T)
    put("g_q", inp["g_q"][0].reshape(96, 1))
    put("g_k", inp["g_k"][0].reshape(96, 1))
    put("d_skip", inp["d_skip"][0].reshape(4, 128).T)
    put("vis", np.tile((np.arange(4) <= j).astype(np.float32)[None], (128, 1)))
    put("full", np.tile((np.arange(4) < j).astype(np.float32)[None], (128, 1)))
    put("ident", np.eye(128, dtype=np.float32))
    put("hsel", np.tile((np.arange(4) == j - 1).astype(np.float32)[None], (128, 1)))
    m["small"] = sm
    m["w_in"] = inp["w_in"][0]
    m["w_glu"] = inp["w_glu"][0]
    m["w_uq"] = inp["w_uq"][0].reshape(Q_LORA, 768)
    m["w_oa"], m["w_os"], m["w_out"] = inp["w_oa"][0], inp["w_os"][0], inp["w_out"][0]
    m["w_up"], m["w_down"] = inp["w_up"][0], inp["w_down"][0]
    m["w_pg"], m["w_pp"] = inp["w_ple_gate"][0], inp["w_ple_proj"][0]
    pp_ = inp["p_prompt"][0, b, NPT * j:NPT * (j + 1)]
    ps_ = inp["p_sample"][0, SEQ_PER_CORE * c:SEQ_PER_CORE * (c + 1)].reshape(NST, PLE)
    m["pT"] = np.ascontiguousarray(np.concatenate([pp_, ps_], 0).T)
    sc = inp["state_conv"][0, SEQ_PER_CORE * c:SEQ_PER_CORE * (c + 1)]
    m["scT"] = np.ascontiguousarray(sc.reshape(16, 2, 44, 128).transpose(3, 2, 0, 1))
    m["w_uk"] = inp["w_uk"][0].reshape(KV_LORA, 512)
    m["w_uv"] = inp["w_uv"][0].reshape(KV_LORA, 512)
    tri = (np.arange(128)[:, None] <= np.arange(128)[None, :]).astype(np.float32)
    md = np.zeros((128, 4, 128), np.float32)
    for r in range(4):
        vis, full = float(r <= j), float(r < j)
        md[:, r, :] = full + (vis - full) * tri
    m["maskd"] = md
    pt_ = inp["page_table"][SEQ_PER_CORE * c:SEQ_PER_CORE * (c + 1)].astype(np.int32).reshape(16, 16, 4)
    m["ptab"] = np.ascontiguousarray(np.repeat(pt_.transpose(2, 0, 1).reshape(4, 256), 32, axis=0))
    m["p32c"] = (np.arange(128, dtype=np.int32) % 32).reshape(128, 1)
    m["w_ukT"] = np.ascontiguousarray(inp["w_uk"][0].transpose(2, 1, 0).reshape(64, 8 * KV_LORA))
    cs_p, sn_p = _rope_tables(np.arange(PAST))
    rp = np.stack([cs_p, sn_p], 0).reshape(2, 16, 4, 32, 4, 16)
    m["ropeP"] = np.ascontiguousarray(rp.transpose(2, 3, 0, 1, 4, 5).reshape(128, 2, NPAGES, 16))
    m["gk_rep"] = np.tile(inp["g_k"][0, 64:96][None], (128, 1)).astype(np.float32)
    hs_ = np.zeros((128, 4, 32), np.float32)
    for mm in range(4):
        for hh in range(2):
            hs_[64 * hh:64 * hh + 64, mm, 4 * (2 * mm + hh):4 * (2 * mm + hh) + 4] = 1.0
    m["hselm"] = hs_
    cm = np.zeros((32, 4), np.float32)
    for h_ in range(8):
        for t_ in range(4):
            cm[4 * h_ + t_, :t_ + 1] = 1.0
    m["cmask"] = cm
    rot = np.zeros((96, 96), np.float32)
    for i in range(16):
        rot[80 + i, 64 + i] = -1.0
        rot[64 + i, 80 + i] = 1.0
    m["rotm"] = rot
    m["ssm_s"], m["ssm_r"] = _ssm_packs(c, inp)
    return m


def _ssm_packs(c, inp):
    j = c % 4
    a_re, a_im, logdt = inp["a_re"][0], inp["a_im"][0], inp["log_dt"][0]
    b_re, b_im, c_re, c_im = inp["b_re"][0], inp["b_im"][0], inp["c_re"][0], inp["c_im"][0]

    def st(a):
        return a.reshape(16, 2, 64).transpose(1, 2, 0).reshape(128, 16)
    ss = np.zeros((128, SSL["_n"]), np.float32)

    def put(lay, arr_, name, arr):
        o, n = lay[name]
        arr_[:, o:o + n] = arr.reshape(128, n)
    put(SSL, ss, "a_re", st(a_re))
    put(SSL, ss, "a_im", st(a_im))
    put(SSL, ss, "logdt", st(np.repeat(logdt[:, None], 64, 1)))
    for nm, cc in (("c_re", c_re), ("c_im", c_im)):
        c4 = cc.reshape(16, 2, 16, 64)
        pad = np.zeros((2, 64, 16, 2, 16), np.float32)
        for g2 in range(2):
            pad[g2, :, :, g2, :] = c4[:, g2].transpose(2, 0, 1)
        put(SSL, ss, nm, pad)
    for nm, bb in (("b_re", b_re), ("b_im", b_im)):
        b4 = bb.reshape(16, 2, 64, 16)
        pad = np.zeros((2, 64, 16, 2, 16), np.float32)
        for g2 in range(2):
            pad[g2, :, :, g2, :] = b4[:, g2].transpose(1, 0, 2)
        put(SSL, ss, nm, pad)
    for nm, key in (("h0_re", "state_ssm_re"), ("h0_im", "state_ssm_im")):
        h = inp[key][0, SEQ_PER_CORE * c:SEQ_PER_CORE * (c + 1)]
        h4 = h.reshape(16, 16, 2, 64)
        put(SSL, ss, nm, h4.transpose(2, 3, 1, 0))
    m = np.zeros((128, 12), np.float32)
    for i in range(4):
        n = j - 1 - i
        if 0 <= n <= 2:
            m[:, 3 * i + n] = 1.0
    put(SSL, ss, "msk", m)
    blk = np.zeros((128, 4), np.float32)
    for k4 in range(4):
        blk[32 * k4:32 * k4 + 32, k4] = 1.0
    put(SSL, ss, "blk", blk)
    sr = np.zeros((128, SRL["_n"]), np.float32)

    def rowrep(a):
        a4 = a.reshape(4, 4, 2, 64)
        out = np.zeros((4, 2, 16, 4, 64), np.float32)
        out[:] = a4.transpose(1, 2, 0, 3)[:, :, None, :, :]
        return out
    put(SRL, sr, "a_re", rowrep(a_re))
    put(SRL, sr, "a_im", rowrep(a_im))
    put(SRL, sr, "logdt", rowrep(np.repeat(logdt[:, None], 64, 1)))
    for nm, bb in (("b_re", b_re), ("b_im", b_im)):
        b5 = bb.reshape(4, 4, 2, 64, 16)
        pad = np.zeros((4, 2, 16, 4, 2, 64), np.float32)
        for g2 in range(2):
            pad[:, g2, :, :, g2, :] = b5[:, :, g2].transpose(1, 3, 0, 2)
        put(SRL, sr, nm, pad)
    put(SRL, sr, "ident", np.eye(128, dtype=np.float32))
    return ss, sr


_CACHE = {}


def kernel(**inputs):
    inp = {k: np.asarray(v) for k, v in inputs.items()}
    if "nc" not in _CACHE:
        _CACHE["nc"] = build()
    nc, ins_, outs_, P = _CACHE["nc"]
    in_maps = []
    cache = None
    if "cache" in ins_:
        cache = np.concatenate([inp["cache_ckv"][0], inp["cache_kr"][0]], axis=-1).reshape(-1, 4 * (KV_LORA + QK_ROPE))
    for c in range(8):
        m = _prep_core(c, inp)
        if cache is not None:
            m["cache"] = cache
        in_maps.append({k: np.ascontiguousarray(m[k]) for k in ins_})
    res = run_bass_kernel_spmd(nc, in_maps, core_ids=list(range(8)))
    R = res.results
    f32 = np.float32
    ckv_p = np.zeros((1, 2, 8192, KV_LORA), f32)
    kr_p = np.zeros((1, 2, 8192, QK_ROPE), f32)
    ckv_s = np.zeros((1, 128, 4, KV_LORA), f32)
    kr_s = np.zeros((1, 128, 4, QK_ROPE), f32)
    for c in range(8):
        b, j = c // 4, c % 4
        ck = R[c]["o_ckvT"].T
        kr = R[c]["o_krT"].T
        ckv_p[0, b, NPT * j:NPT * (j + 1)] = ck[:NPT]
        kr_p[0, b, NPT * j:NPT * (j + 1)] = kr[:NPT]
        ckv_s[0, 16 * c:16 * c + 16] = ck[NPT:].reshape(16, 4, KV_LORA)
        kr_s[0, 16 * c:16 * c + 16] = kr[NPT:].reshape(16, 4, QK_ROPE)
    yp = np.zeros((2, 8192, D), f32)
    ys = np.zeros((128, 4, D), f32)
    cv_p = np.zeros((1, 2, 2, 2 * D_FF), f32)
    cv_s = np.zeros((1, 128, 2, 2 * D_FF), f32)
    if "o_yT" in R[0]:
        for c in range(8):
            b, j = c // 4, c % 4
            y = R[c]["o_yT"].T
            yp[b, NPT * j:NPT * (j + 1)] = y[:NPT]
            ys[16 * c:16 * c + 16] = y[NPT:].reshape(16, 4, D)
            cs_ = R[c]["o_cvs"]
            cv_s[0, 16 * c:16 * c + 16] = cs_.transpose(2, 3, 1, 0).reshape(16, 2, 2 * D_FF)
            if j == 3:
                cv_p[0, b] = R[c]["o_cvp"].transpose(2, 1, 0).reshape(2, 2 * D_FF)
    z = lambda *s: np.zeros(s, f32)
    sre_p, sim_p, sre_s, sim_s = z(1, 2, 32, 64), z(1, 2, 32, 64), z(1, 128, 32, 64), z(1, 128, 32, 64)
    if "o_hp" in R[0]:
        for c in range(8):
            b, j = c // 4, c % 4
            hs_ = R[c]["o_hs"].reshape(2, 64, 2, 16, 16)
            hs_ = hs_.transpose(2, 4, 3, 0, 1).reshape(2, 16, 32, 64)
            sre_s[0, 16 * c:16 * c + 16] = hs_[0]
            sim_s[0, 16 * c:16 * c + 16] = hs_[1]
            if j == 3:
                hp_ = R[c]["o_hp"].reshape(2, 64, 2, 16).transpose(2, 3, 0, 1).reshape(2, 32, 64)
                sre_p[0, b] = hp_[0]
                sim_p[0, b] = hp_[1]
    return (yp, ys, ckv_p, kr_p, ckv_s, kr_s, sre_p, sim_p, sre_s, sim_s, cv_p, cv_s)
```

```python
import contextlib
import numpy as np
import concourse.bass as bass
import concourse.mybir as mybir
from concourse.bass_utils import run_bass_kernel_spmd

F32 = mybir.dt.float32
BF16 = mybir.dt.bfloat16
I32 = mybir.dt.int32
AF = mybir.ActivationFunctionType
ALU = mybir.AluOpType
AX = mybir.AxisListType

D = 1024
NPT = 2048
NST = 64
T = NPT + NST
KD = D // 128
N_HEADS = 8
QK_NOPE, QK_ROPE, QK_HEAD, V_HEAD = 64, 32, 96, 64
Q_LORA, KV_LORA = 384, 256
SSM_W, GROUP, N_GROUPS, STATE = 512, 16, 32, 64
D_FF = 2816
PLE = 256
EPS = 1e-6
OFF_CKV = Q_LORA
OFF_KR = OFF_CKV + KV_LORA
OFF_U = OFF_KR + QK_ROPE
OFF_GA = OFF_U + SSM_W
OFF_GS = OFF_GA + D
IN_COLS = OFF_GS + D
SCALE = QK_HEAD ** -0.5
PAST = 8192
PAGE = 128
NPAGES = 64
SEQ_PER_CORE = 16


class Buf:
    __slots__ = ("name", "last_w", "readers")

    def __init__(self, name, fence=()):
        self.name = name
        self.last_w = None
        self.readers = list(fence)


class Op:
    __slots__ = ("eng", "fn", "deps", "dma", "sem", "value", "marked", "idx")

    def __init__(self, eng, fn, dma):
        self.eng = eng
        self.fn = fn
        self.dma = dma
        self.deps = ()
        self.sem = None
        self.value = 0
        self.marked = False
        self.idx = 0


class Prog:
    ENGS = ("pe", "act", "dve", "pool", "sp")

    def __init__(self, nc):
        self.nc = nc
        self.ops = {e: [] for e in self.ENGS}
        self.stack = contextlib.ExitStack()
        self.dma_sems = {}
        self.nbuf = 0
        self.fence = []
        self.live = []

    def sbuf(self, name, shape, dtype):
        return self.stack.enter_context(self.nc.sbuf_tensor(name, list(shape), dtype))

    def psum(self, name, shape, dtype=F32):
        return self.stack.enter_context(self.nc.psum_tensor(name, list(shape), dtype))

    def sem(self, name):
        return self.stack.enter_context(self.nc.semaphore(name))

    def buf(self, name=None):
        self.nbuf += 1
        b = Buf(name or f"b{self.nbuf}", self.fence)
        self.live.append(b)
        return b

    def new_phase(self):
        f = []
        for b in self.live:
            if b.last_w is not None:
                f.append(b.last_w)
            f.extend(b.readers)
        self.fence = list(dict.fromkeys(f))[-64:] if False else list(dict.fromkeys(f))
        self.live = []

    def add(self, eng, fn, reads=(), writes=(), dma=False, semkey=None, after=()):
        op = Op(eng, fn, dma)
        deps = set(after)
        for b in reads:
            if b.last_w is not None:
                deps.add(b.last_w)
        for b in writes:
            if b.last_w is not None:
                deps.add(b.last_w)
            deps.update(b.readers)
        op.deps = tuple(deps)
        for b in reads:
            b.readers.append(op)
        for b in writes:
            b.last_w = op
            b.readers = []
        op.idx = len(self.ops[eng])
        self.ops[eng].append(op)
        if dma:
            key = semkey if semkey is not None else id(op)
            if key not in self.dma_sems:
                self.dma_sems[key] = [self.sem(f"dq{len(self.dma_sems)}"), 0]
            ent = self.dma_sems[key]
            ent[1] += (1 if dma == "cc" else 16)
            op.sem = ent[0]
            op.value = ent[1]
            op.marked = True
        return op

    def finalize(self):
        nc = self.nc
        esem = {e: self.sem(f"eng_{e}") for e in self.ENGS}
        for e in self.ENGS:
            for op in self.ops[e]:
                for d in op.deps:
                    if d.dma:
                        continue
                    if d.eng != e:
                        d.marked = True
                    elif e != "pe" and (op.idx - d.idx) <= 2:
                        d.marked = True
        for e in self.ENGS:
            c = 0
            for op in self.ops[e]:
                if op.dma:
                    continue
                if op.marked:
                    c += 1
                    op.sem = esem[e]
                    op.value = c
        self.stats = {e: len(self.ops[e]) for e in self.ENGS}

        def emit(ename, eng):
            waited = {}
            for op in self.ops[ename]:
                need = {}
                for d in op.deps:
                    if not d.marked:
                        continue
                    if (not d.dma) and d.eng == ename and (ename == "pe" or (op.idx - d.idx) > 2):
                        continue
                    k = id(d.sem)
                    if waited.get(k, 0) >= d.value:
                        continue
                    if k not in need or need[k][1] < d.value:
                        need[k] = (d.sem, d.value)
                for k, (s, v) in need.items():
                    eng.wait_ge(s, v)
                    waited[k] = v
                ins = op.fn(eng)
                if op.marked and ins is not None:
                    if op.dma == "cc":
                        ins.then_inc(op.sem, 1)
                    elif op.dma:
                        ins.then_inc(op.sem, 16)
                    else:
                        ins.then_inc(op.sem, 1)

        with nc.Block() as block:
            @block.tensor
            def _(e):
                emit("pe", e)

            @block.scalar
            def _(e):
                emit("act", e)

            @block.vector
            def _(e):
                emit("dve", e)

            @block.gpsimd
            def _(e):
                emit("pool", e)

            @block.sync
            def _(e):
                emit("sp", e)
        self.stack.close()


class Arena:
    def __init__(self, P, name, nbytes):
        self.t = P.sbuf(name, [128, nbytes // 4], F32)
        self.off = 0
        self.cap = nbytes
        self.peak = 0

    def alloc(self, shape, dtype=F32):
        esz = 2 if dtype == BF16 else 4
        n = int(np.prod(shape[1:]))
        nb = (n * esz + 31) // 32 * 32
        assert self.off + nb <= self.cap, ("arena overflow", self.off, nb, self.cap)
        v = self.t[0:shape[0], self.off // 4:(self.off + nb) // 4]
        self.off += nb
        self.peak = max(self.peak, self.off)
        if dtype != F32:
            v = v.bitcast(dtype)
        v = v[:, 0:n]
        if len(shape) > 2:
            names = "abcde"[:len(shape) - 1]
            pat = "p (" + " ".join(names) + ") -> p " + " ".join(names)
            v = v.rearrange(pat, **{c: int(d) for c, d in zip(names[1:], shape[2:])})
        return v

    def mark(self):
        return self.off

    def release(self, m):
        self.off = m


def _small_layout():
    lay = {}
    off = 0

    def put(name, n):
        nonlocal off
        lay[name] = (off, n)
        off += n
    put("g_mix", 8)
    put("g_cq", 3)
    put("g_ckv", 2)
    put("g_ffn", 8)
    put("g_ple", 8)
    put("conv_w", 3 * 44)
    put("conv_b", 44)
    put("g_q", 1)
    put("g_k", 1)
    put("d_skip", 4)
    put("vis", 4)
    put("full", 4)
    put("ident", 128)
    put("hsel", 4)
    lay["_n"] = off
    return lay


SL = _small_layout()


def _token_chunks():
    return [(0, 510), (510, 512), (1022, 512), (1534, 290), (1824, 288)]


def _pack_layout(items):
    lay, off = {}, 0
    for name, n in items:
        lay[name] = (off, n)
        off += n
    lay["_n"] = off
    return lay


SSL = _pack_layout([("a_re", 16), ("a_im", 16), ("logdt", 16), ("c_re", 512), ("c_im", 512), ("b_re", 512),
                    ("b_im", 512), ("h0_re", 256), ("h0_im", 256), ("msk", 12), ("blk", 4)])
SRL = _pack_layout([("a_re", 256), ("a_im", 256), ("logdt", 256), ("b_re", 512), ("b_im", 512), ("ident", 128)])
PWS = [1, 2, 3, 4, 8, 12, 16]
PWR = [1, 2, 3]
TWO_PI = 6.283185307179586


def build_ssm(P, AR, nc, din, dout, dint, stores, next_bank, uT, b_uT, smc, b_sm, sm, w_glu_b, b_w_glu_b, chunks):
    LOOP_ENG = "pool"
    ss_d = din("ssm_s", [128, SSL["_n"]])
    sr_d = din("ssm_r", [128, SRL["_n"]])
    o_hp = dout("o_hp", [128, 2, 16])
    o_hs = dout("o_hs", [128, 2, 16, 16])
    cc_e_in = dint("cc_e_in", [128, 32], F32)
    cc_e_out = dint("cc_e_out", [512, 32], F32)

    AR.release(0)
    P.new_phase()
    sst = AR.alloc([128, SSL["_n"]], F32)
    Wc = AR.alloc([128, 16, 4, 2, 32], BF16)
    Ktab = AR.alloc([128, 4, 4, 128], BF16)
    A4w = AR.alloc([128, 4, 4, 2, 128], BF16)
    nlim = AR.alloc([128, len(PWS), 16], F32)
    LS_pre = True
    b_sst, b_srt = P.buf("sst"), P.buf("srt")
    P.add("sp", lambda e: e.dma_start(out=sst[:], in_=ss_d), writes=[b_sst], dma=True, semkey="sst")

    def S(name, a=None):
        o, n = SSL[name]
        v = sst[:, o:o + n]
        return v if a is None else v.rearrange("p (a b) -> p a b", a=a)

    def R(name, a=None):
        o, n = SRL[name]
        v = srt[:, o:o + n]
        return v if a is None else v.rearrange("p (a b) -> p a b", a=a)

    def tt(eng, out, a, b, op, rd, wr):
        return P.add(eng, lambda e: e.tensor_tensor(out=out, in0=a, in1=b, op=op), reads=rd, writes=wr)

    def tss(eng, out, a, scalar, op, rd, wr):
        return P.add(eng, lambda e: e.tensor_single_scalar(out=out, in_=a, scalar=scalar, op=op), reads=rd, writes=wr)

    def stt(eng, out, a, scalar, b, op0, op1, rd, wr):
        return P.add("dve", lambda e: e.scalar_tensor_tensor(out=out, in0=a, scalar=scalar, in1=b, op0=op0, op1=op1),
                     reads=rd, writes=wr)

    def act(out, in_, func, rd, wr, scale=1.0, bias=0.0):
        return P.add("act", lambda e: e.activation(out=out, in_=in_, func=func, scale=scale, bias=bias), reads=rd, writes=wr)

    def cp(eng, out, in_, rd, wr):
        return P.add(eng, lambda e: e.tensor_copy(out=out, in_=in_), reads=rd, writes=wr)

    MUL, ADD, SUB = ALU.mult, ALU.add, ALU.subtract

    def lam_pow(pfx, a_re, a_im, logdt, Fd, powers, b_src, eng):
        npw = len(powers)
        bt = P.buf(pfx + "_t")
        mk = lambda nm, sh, dt_=F32: AR.alloc(sh, dt_)
        dt = mk("dt", [128, Fd]); dre = mk("dre", [128, Fd]); dim = mk("dim", [128, Fd])
        ang = mk("ang", [128, npw, Fd]); angc = mk("angc", [128, npw, Fd]); mag = mk("mag", [128, npw, Fd])
        ki = mk("ki", [128, npw, Fd], I32); kf = mk("kf", [128, npw, Fd])
        sn = mk("sn", [128, npw, Fd]); cs = mk("cs", [128, npw, Fd])
        lre = mk("lre", [128, npw, Fd]); lim = mk("lim", [128, npw, Fd])
        act(dt[:], logdt, AF.Exp, [b_src], [bt])
        tt(eng, dre[:], dt[:], a_re, MUL, [bt, b_src], [bt])
        tt(eng, dim[:], dt[:], a_im, MUL, [bt, b_src], [bt])
        for i, n in enumerate(powers):
            tss(eng, ang[:, i, :], dim[:], n / TWO_PI, MUL, [bt], [bt])
            act(mag[:, i, :], dre[:], AF.Exp, [bt], [bt], scale=float(n))
        tss(eng, angc[:], ang[:], 0.25, ADD, [bt], [bt])
        for src, dst in ((ang, sn), (angc, cs)):
            cp("dve", ki[:], src[:], [bt], [bt])
            cp("dve", kf[:], ki[:], [bt], [bt])
            tt(eng, kf[:], src[:], kf[:], SUB, [bt], [bt])
            act(dst[:], kf[:], AF.Sin, [bt], [bt], scale=6.28318)
        tt(eng, lre[:], mag[:], cs[:], MUL, [bt], [bt])
        tt(eng, lim[:], mag[:], sn[:], MUL, [bt], [bt])
        den = mk("den", [128, Fd]); t1 = mk("t1", [128, Fd]); t2 = mk("t2", [128, Fd]); nr = mk("nr", [128, Fd])
        fre = mk("fre", [128, Fd]); fim = mk("fim", [128, Fd])
        tt(eng, den[:], a_re, a_re, MUL, [b_src], [bt])
        tt(eng, t1[:], a_im, a_im, MUL, [b_src], [bt])
        tt(eng, den[:], den[:], t1[:], ADD, [bt], [bt])
        P.add("dve", lambda e: e.reciprocal(out=den[:], in_=den[:]), reads=[bt], writes=[bt])
        tss(eng, nr[:], lre[:, 0, :], -1.0, ADD, [bt], [bt])
        tt(eng, t1[:], nr[:], a_re, MUL, [bt, b_src], [bt])
        tt(eng, t2[:], lim[:, 0, :], a_im, MUL, [bt, b_src], [bt])
        tt(eng, t1[:], t1[:], t2[:], ADD, [bt], [bt])
        tt(eng, fre[:], t1[:], den[:], MUL, [bt], [bt])
        tt(eng, t1[:], lim[:, 0, :], a_re, MUL, [bt, b_src], [bt])
        tt(eng, t2[:], nr[:], a_im, MUL, [bt, b_src], [bt])
        tt(eng, t1[:], t1[:], t2[:], SUB, [bt], [bt])
        tt(eng, fim[:], t1[:], den[:], MUL, [bt], [bt])
        return dict(lre=lre, lim=lim, fre=fre, fim=fim, b=bt)

    def cmul(eng, ore, oim, are, aim, bre, bim, t1, t2, rd, wr):
        tt(eng, t1, are, bre, MUL, rd, wr)
        tt(eng, t2, aim, bim, MUL, rd, wr)
        tt(eng, ore, t1, t2, SUB, rd, wr)
        tt(eng, t1, are, bim, MUL, rd, wr)
        tt(eng, t2, aim, bre, MUL, rd, wr)
        tt(eng, oim, t1, t2, ADD, rd, wr)

    ENG = "dve"
    LS = lam_pow("ls", S("a_re"), S("a_im"), S("logdt"), 16, PWS, b_sst, ENG)
    mB0 = AR.mark()
    srt = AR.alloc([128, SRL["_n"]], F32)
    P.add("sp", lambda e: e.dma_start(out=srt[:], in_=sr_d), writes=[b_srt], dma=True, semkey="srt")
    LR = lam_pow("lr", R("a_re"), R("a_im"), R("logdt"), 256, PWR, b_srt, ENG)
    bS, bR = LS["b"], LR["b"]
    pi = {n: i for i, n in enumerate(PWS)}

    def bc(ap2, n):
        return ap2.unsqueeze(2).broadcast_to([128, 16, n])

    bbs_re = AR.alloc([128, 16, 32], F32); bbs_im = AR.alloc([128, 16, 32], F32)
    x_re = AR.alloc([128, 16, 32], F32); x_im = AR.alloc([128, 16, 32], F32)
    u1 = AR.alloc([128, 16, 32], F32); u2 = AR.alloc([128, 16, 32], F32)
    negc_im = AR.alloc([128, 16, 32], F32)
    cmul(ENG, bbs_re[:], bbs_im[:], bc(LS["fre"][:], 32), bc(LS["fim"][:], 32), S("b_re", 16), S("b_im", 16),
         u1[:], u2[:], [bS, b_sst], [bS])
    tss(ENG, negc_im[:], S("c_im", 16), -1.0, MUL, [b_sst], [bS])

    b_Wc = P.buf("Wc")
    Wc_v = Wc[:].rearrange("p (kk k4) s c n -> p kk k4 s c n", k4=4)
    u1_v = u1[:].rearrange("p (kk k4) n -> p kk k4 n", k4=4)
    u2_v = u2[:].rearrange("p (kk k4) n -> p kk k4 n", k4=4)
    for s in range(4):
        lr_, li_ = LS["lre"][:, pi[s + 1], :], LS["lim"][:, pi[s + 1], :]
        tt(ENG, u1[:], S("c_re", 16), bc(lr_, 32), MUL, [b_sst, bS], [bS])
        tt(ENG, u2[:], S("c_im", 16), bc(li_, 32), MUL, [b_sst, bS], [bS])
        for k4 in range(4):
            tt(ENG, Wc_v[:, :, k4, s, 0, :], u1_v[:, :, k4, :], u2_v[:, :, k4, :], SUB, [bS], [bS, b_Wc])
        tt(ENG, u1[:], S("c_re", 16), bc(li_, 32), MUL, [b_sst, bS], [bS])
        tt(ENG, u2[:], S("c_im", 16), bc(lr_, 32), MUL, [b_sst, bS], [bS])
        for k4 in range(4):
            stt(ENG, Wc_v[:, :, k4, s, 1, :], u1_v[:, :, k4, :], -1.0, u2_v[:, :, k4, :], MUL, SUB,
                [bS], [bS, b_Wc])

    b_Kt = P.buf("Ktab")
    Kc = AR.alloc([128, 4, 32], F32)
    Kf = AR.alloc([128, 4, 128], F32)
    b_Kc, b_Kf = P.buf("Kc"), P.buf("Kf")
    xs_re = AR.alloc([128, 4, 16, 32], F32); xs_im = AR.alloc([128, 4, 16, 32], F32)
    b_xs = P.buf("xs")
    cp(ENG, xs_re[:, 0], bbs_re[:], [bS], [b_xs])
    cp(ENG, xs_im[:, 0], bbs_im[:], [bS], [b_xs])
    for tau in range(1, 4):
        cmul(ENG, xs_re[:, tau], xs_im[:, tau], bc(LS["lre"][:, pi[tau], :], 32), bc(LS["lim"][:, pi[tau], :], 32),
             bbs_re[:], bbs_im[:], u1[:], u2[:], [bS], [bS, b_xs])
    for kk in range(4):
        ps, b_ps = next_bank()
        for k4 in range(4):
            k = 4 * kk + k4
            for tau in range(4):
                o = ps[32 * k4:32 * k4 + 32, tau * 32:tau * 32 + 32]
                P.add("pe", lambda e, o=o, k=k, tau=tau, k4=k4: e.matmul(
                    o, lhsT=xs_re[:, tau, k, :], rhs=S("c_re", 16)[:, k, :], start=True, stop=False,
                    tile_position=(0, 32 * k4)), reads=[b_xs, b_sst], writes=[b_ps])
                P.add("pe", lambda e, o=o, k=k, tau=tau, k4=k4: e.matmul(
                    o, lhsT=xs_im[:, tau, k, :], rhs=negc_im[:, k, :], start=False, stop=True,
                    tile_position=(0, 32 * k4)), reads=[b_xs, bS], writes=[b_ps])
        cp("dve", Kc[:], ps[:, 0:128].rearrange("p (t n) -> p t n", t=4), [b_ps], [b_Kc])
        for k4 in range(4):
            tss("dve", Kf[:, :, 32 * k4:32 * k4 + 32], Kc[:], S("blk")[:, k4:k4 + 1], MUL, [b_Kc, b_sst], [b_Kf])
        stt("dve", Kf[:, 0, :], R("ident"), smc("d_skip", kk), Kf[:, 0, :], MUL, ADD, [b_srt, b_sm, b_Kf], [b_Kf])
        cp("dve", Ktab[:, kk], Kf[:], [b_Kf], [b_Kt])

    b_A4 = P.buf("A4w")
    bbr_re = AR.alloc([128, 4, 2, 64], F32); bbr_im = AR.alloc([128, 4, 2, 64], F32)
    r1 = AR.alloc([128, 4, 2, 64], F32); r2 = AR.alloc([128, 4, 2, 64], F32)
    bcr = lambda t: t.rearrange("p (kk q) -> p kk q", kk=4).unsqueeze(2).broadcast_to([128, 4, 2, 64])
    v4 = lambda t: t.rearrange("p (kk g q) -> p kk g q", kk=4, g=2)
    cmul(ENG, bbr_re[:], bbr_im[:], bcr(LR["fre"][:]), bcr(LR["fim"][:]), v4(R("b_re")), v4(R("b_im")), r1[:], r2[:],
         [bR, b_srt], [bR])
    a4v = lambda s_, c_: A4w[:, :, s_, c_, :].rearrange("p kk (g q) -> p kk g q", g=2)
    cp(ENG, a4v(3, 0), bbr_re[:], [bR], [b_A4])
    cp(ENG, a4v(3, 1), bbr_im[:], [bR], [b_A4])
    pir = {n: i for i, n in enumerate(PWR)}
    for s in range(3):
        n = 3 - s
        cmul(ENG, a4v(s, 0), a4v(s, 1), bcr(LR["lre"][:, pir[n], :]), bcr(LR["lim"][:, pir[n], :]),
             bbr_re[:], bbr_im[:], r1[:], r2[:], [bR], [bR, b_A4])

    tss(ENG, nlim[:], LS["lim"][:], -1.0, MUL, [bS], [bS])
    Lre = lambda n, k: LS["lre"][:, pi[n], k:k + 1]
    Lim = lambda n, k: LS["lim"][:, pi[n], k:k + 1]
    nLim = lambda n, k: nlim[:, pi[n], k:k + 1]

    AR.release(mB0)
    P.new_phase()
    uP = [uT[:, kk, 0:NPT].rearrange("p (c e) -> p e c", e=16) for kk in range(4)]
    uS = [uT[:, kk, NPT:T].rearrange("p (q s) -> p s q", s=4) for kk in range(4)]

    S16 = [AR.alloc([128, 16, 128], F32) for c in range(2)]
    b_S16 = P.buf("S16")
    H16 = [AR.alloc([128, 16, 129], F32) for c in range(2)]
    b_H = [P.buf("H16re"), P.buf("H16im")]
    pr = [AR.alloc([128, 2, 128], F32) for i in range(2)]
    b_pr = [P.buf("pr0"), P.buf("pr1")]

    def s4_matmuls(k, n_c, usrc, nj):
        kk, k4 = divmod(k, 4)
        out = []
        for comp in range(2):
            ps, b_ps = next_bank()
            for j in range(nj):
                for s in range(4):
                    rhs = usrc[kk][32 * k4:32 * k4 + 32, 4 * j + s, :]
                    P.add("pe", lambda e, ps=ps, j=j, s=s, rhs=rhs, comp=comp, kk=kk, k4=k4: e.matmul(
                        ps[:, j * n_c:(j + 1) * n_c], lhsT=A4w[32 * k4:32 * k4 + 32, kk, s, comp, :], rhs=rhs,
                        start=(s == 0), stop=(s == 3), tile_position=(32 * k4, 0)),
                        reads=[b_A4, b_uT], writes=[b_ps])
            out.append((ps, b_ps))
        return out

    def prefix_step(k, j, src, b_src, dst, b_dst, Sre, Sim, b_sre, b_sim, n_c, o_re=None, o_im=None, b_o=()):
        ore = dst[:, 0, 0:n_c] if o_re is None else o_re
        oim = dst[:, 1, 0:n_c] if o_im is None else o_im
        wr = [b_dst] + list(b_o)
        stt("dve", dst[:, 0, 0:n_c], src[:, 0, 0:n_c], Lre(4, k), Sre, MUL, ADD, [b_src, bS, b_sre], [b_dst])
        stt("dve", ore, src[:, 1, 0:n_c], nLim(4, k), dst[:, 0, 0:n_c], MUL, ADD, [b_src, bS, b_dst], wr)
        stt("dve", dst[:, 1, 0:n_c], src[:, 0, 0:n_c], Lim(4, k), Sim, MUL, ADD, [b_src, bS, b_sim], [b_dst])
        stt("dve", oim, src[:, 1, 0:n_c], Lre(4, k), dst[:, 1, 0:n_c], MUL, ADD, [b_src, bS, b_dst], wr)

    for k in range(16):
        (pre, b_pre), (pim, b_pim) = s4_matmuls(k, 128, uP, 4)
        act(pr[0][:, 0, :], pre[:, 0:128], AF.Copy, [b_pre], [b_pr[0]])
        act(pr[0][:, 1, :], pim[:, 0:128], AF.Copy, [b_pim], [b_pr[0]])
        cur = 0
        for j in range(1, 4):
            last = (j == 3)
            prefix_step(k, j, pr[cur], b_pr[cur], pr[1 - cur], b_pr[1 - cur], pre[:, j * 128:(j + 1) * 128],
                        pim[:, j * 128:(j + 1) * 128], b_pre, b_pim, 128,
                        o_re=S16[0][:, k, :] if last else None, o_im=S16[1][:, k, :] if last else None,
                        b_o=[b_S16] if last else ())
            cur = 1 - cur

    lt = [AR.alloc([128, 16], F32) for i in range(6)]
    b_lt = [P.buf(f"lt{i}") for i in range(6)]
    L16re, L16im = LS["lre"][:, pi[16], :], LS["lim"][:, pi[16], :]

    def run_loop():
        for c in range(128):
            hre, him = H16[0][:, :, c], H16[1][:, :, c]
            tt(LOOP_ENG, lt[0][:], L16re, hre, MUL, [bS, b_H[0]], [b_lt[0]])
            tt(LOOP_ENG, lt[1][:], L16im, him, MUL, [bS, b_H[1]], [b_lt[1]])
            tt(LOOP_ENG, lt[3][:], L16re, him, MUL, [bS, b_H[1]], [b_lt[3]])
            tt(LOOP_ENG, lt[4][:], L16im, hre, MUL, [bS, b_H[0]], [b_lt[4]])
            tt(LOOP_ENG, lt[2][:], lt[0][:], lt[1][:], SUB, [b_lt[0], b_lt[1]], [b_lt[2]])
            tt(LOOP_ENG, lt[5][:], lt[3][:], lt[4][:], ADD, [b_lt[3], b_lt[4]], [b_lt[5]])
            tt(LOOP_ENG, H16[0][:, :, c + 1], lt[2][:], S16[0][:, :, c], ADD, [b_lt[2], b_S16], [b_H[0]])
            tt(LOOP_ENG, H16[1][:, :, c + 1], lt[5][:], S16[1][:, :, c], ADD, [b_lt[5], b_S16], [b_H[1]])
            yield

    P.add(LOOP_ENG, lambda e: e.memset(H16[0][:, :, 0], 0.0), writes=[b_H[0]])
    P.add(LOOP_ENG, lambda e: e.memset(H16[1][:, :, 0], 0.0), writes=[b_H[1]])
    for _ in run_loop():
        pass
    Eloc = AR.alloc([128, 2, 16], F32)
    b_E = P.buf("Eloc")
    cp(LOOP_ENG, Eloc[:, 0, :], H16[0][:, :, 128], [b_H[0]], [b_E])
    cp(LOOP_ENG, Eloc[:, 1, :], H16[1][:, :, 128], [b_H[1]], [b_E])
    b_cci, b_cco = P.buf("cc_e_in"), P.buf("cc_e_out")
    P.add("sp", lambda e: e.dma_start(out=cc_e_in, in_=Eloc[:].rearrange("p a b -> p (a b)")), reads=[b_E], writes=[b_cci],
          dma=True, semkey="cce1")
    P.add("pool", lambda e: e.collective_compute("AllGather", ALU.bypass, replica_groups=[[0, 1, 2, 3], [4, 5, 6, 7]],
                                                 ins=[cc_e_in.opt()], outs=[cc_e_out.opt()]),
          reads=[b_cci], writes=[b_cco], dma="cc", semkey="cce2")
    Eg = AR.alloc([128, 4, 2, 16], F32)
    b_Eg = P.buf("Eg")
    P.add("sp", lambda e: e.dma_start(out=Eg[:].rearrange("p r a b -> p r (a b)"),
                                      in_=cc_e_out.rearrange("(r p) f -> p r f", p=128)),
          reads=[b_cco], writes=[b_Eg], dma=True, semkey="cce3")
    Lp = AR.alloc([128, 3, 2, 16], F32)
    b_Lp = P.buf("Lp")
    sq_a = AR.alloc([128, 2, 16], F32); sq_b = AR.alloc([128, 2, 16], F32)
    q1 = AR.alloc([128, 16], F32); q2 = AR.alloc([128, 16], F32)
    b_q = P.buf("sqtmp")
    CE = "dve"
    cp(CE, sq_a[:, 0, :], L16re, [bS], [b_q])
    cp(CE, sq_a[:, 1, :], L16im, [bS], [b_q])
    src_, dst_ = sq_a, sq_b
    for it in range(7):
        o = Lp[:, 0] if it == 6 else dst_
        tt(CE, q1[:], src_[:, 0, :], src_[:, 0, :], MUL, [b_q], [b_q])
        tt(CE, q2[:], src_[:, 1, :], src_[:, 1, :], MUL, [b_q], [b_q])
        tt(CE, o[:, 0, :], q1[:], q2[:], SUB, [b_q], [b_q, b_Lp])
        stt(CE, o[:, 1, :], src_[:, 0, :], 2.0, src_[:, 1, :], MUL, MUL, [b_q], [b_q, b_Lp])
        src_, dst_ = dst_, src_
    cmul(CE, Lp[:, 1, 0, :], Lp[:, 1, 1, :], Lp[:, 0, 0, :], Lp[:, 0, 1, :], Lp[:, 0, 0, :], Lp[:, 0, 1, :], q1[:], q2[:],
         [b_Lp, b_q], [b_Lp, b_q])
    cmul(CE, Lp[:, 2, 0, :], Lp[:, 2, 1, :], Lp[:, 1, 0, :], Lp[:, 1, 1, :], Lp[:, 0, 0, :], Lp[:, 0, 1, :], q1[:], q2[:],
         [b_Lp, b_q], [b_Lp, b_q])
    hin = AR.alloc([128, 2, 16], F32)
    cf = AR.alloc([128, 2, 16], F32)
    b_hin, b_cf = P.buf("hin"), P.buf("cf")
    P.add(CE, lambda e: e.memset(hin[:], 0.0), writes=[b_hin])
    msk = S("msk")
    for i in range(4):
        m0, m1, m2 = (msk[:, 3 * i + n:3 * i + n + 1] for n in range(3))
        tss(CE, cf[:, 0, :], Lp[:, 0, 0, :], m1, MUL, [b_Lp, b_sst], [b_cf])
        stt(CE, cf[:, 0, :], Lp[:, 1, 0, :], m2, cf[:, 0, :], MUL, ADD, [b_Lp, b_sst, b_cf], [b_cf])
        tss(CE, cf[:, 0, :], cf[:, 0, :], m0, ADD, [b_cf, b_sst], [b_cf])
        tss(CE, cf[:, 1, :], Lp[:, 0, 1, :], m1, MUL, [b_Lp, b_sst], [b_cf])
        stt(CE, cf[:, 1, :], Lp[:, 1, 1, :], m2, cf[:, 1, :], MUL, ADD, [b_Lp, b_sst, b_cf], [b_cf])
        ere, eim = Eg[:, i, 0, :], Eg[:, i, 1, :]
        tt(CE, q1[:], cf[:, 0, :], ere, MUL, [b_cf, b_Eg], [b_q])
        tt(CE, hin[:, 0, :], hin[:, 0, :], q1[:], ADD, [b_q, b_hin], [b_hin])
        tt(CE, q1[:], cf[:, 1, :], eim, MUL, [b_cf, b_Eg], [b_q])
        tt(CE, hin[:, 0, :], hin[:, 0, :], q1[:], SUB, [b_q, b_hin], [b_hin])
        tt(CE, q1[:], cf[:, 0, :], eim, MUL, [b_cf, b_Eg], [b_q])
        tt(CE, hin[:, 1, :], hin[:, 1, :], q1[:], ADD, [b_q, b_hin], [b_hin])
        tt(CE, q1[:], cf[:, 1, :], ere, MUL, [b_cf, b_Eg], [b_q])
        tt(CE, hin[:, 1, :], hin[:, 1, :], q1[:], ADD, [b_q, b_hin], [b_hin])
    cp(LOOP_ENG, H16[0][:, :, 0], hin[:, 0, :], [b_hin], [b_H[0]])
    cp(LOOP_ENG, H16[1][:, :, 0], hin[:, 1, :], [b_hin], [b_H[1]])
    for _ in run_loop():
        pass
    hp = AR.alloc([128, 2, 16], F32)
    b_hp = P.buf("hp")
    cp(LOOP_ENG, hp[:, 0, :], H16[0][:, :, 128], [b_H[0]], [b_hp])
    cp(LOOP_ENG, hp[:, 1, :], H16[1][:, :, 128], [b_H[1]], [b_hp])
    stores.append(P.add("sp", lambda e: e.dma_start(out=o_hp, in_=hp[:]), reads=[b_hp], dma=True, semkey="st_hp"))

    yT = AR.alloc([128, 4, T], BF16)
    b_yT = P.buf("yT")
    H4 = AR.alloc([128, 4, 4, 2, 128], BF16)
    b_H4 = P.buf("H4")
    h0b = AR.alloc([128, 2, 16, 16], BF16)
    b_h0b = P.buf("h0b")
    cp("dve", h0b[:, 0], S("h0_re", 16), [b_sst], [b_h0b])
    cp("dve", h0b[:, 1], S("h0_im", 16), [b_sst], [b_h0b])
    hs = AR.alloc([128, 2, 16, 16], F32)
    b_hs = P.buf("hs")
    htmp = AR.alloc([128, 2, 128], F32)
    b_htmp = P.buf("htmp")
    yP = [yT[:, kk, 0:NPT].rearrange("p (c j s) -> p j s c", j=4, s=4) for kk in range(4)]
    yS = [yT[:, kk, NPT:T].rearrange("p (q s) -> p s q", s=4) for kk in range(4)]

    def out_stage(kk, j, n_c, Hsrc, b_hsrc, usrc, dst):
        ps, b_ps = next_bank()
        for s_lo in range(4):
            o = ps[:, s_lo * n_c:(s_lo + 1) * n_c]
            nmm = 8 + s_lo + 1
            idx = 0
            for k4 in range(4):
                k = 4 * kk + k4
                for comp in range(2):
                    o4 = ps[32 * k4:32 * k4 + 32, s_lo * n_c:(s_lo + 1) * n_c]
                    P.add("pe", lambda e, o4=o4, k=k, s_lo=s_lo, comp=comp, k4=k4, idx=idx, nmm=nmm: e.matmul(
                        o4, lhsT=Wc[:, k, s_lo, comp, :], rhs=Hsrc(k, k4, comp), start=(comp == 0), stop=False,
                        tile_position=(0, 32 * k4)),
                        reads=[b_Wc, b_hsrc], writes=[b_ps])
                    idx += 1
            for tau in range(s_lo + 1):
                rhs = usrc[kk][:, 4 * j + s_lo - tau, :]
                P.add("pe", lambda e, o=o, tau=tau, rhs=rhs, idx=idx, nmm=nmm, kk=kk: e.matmul(
                    o, lhsT=Ktab[:, kk, tau, :], rhs=rhs, start=False, stop=(idx == nmm - 1)),
                    reads=[b_Kt, b_uT], writes=[b_ps])
                idx += 1
        act(dst, ps[:, 0:4 * n_c].rearrange("p (s c) -> p s c", s=4), AF.Gelu_apprx_tanh, [b_ps], [b_yT])

    for kk in range(4):
        for k4 in range(4):
            k = 4 * kk + k4
            (pre, b_pre), (pim, b_pim) = s4_matmuls(k, 128, uP, 3)
            cp("dve", H4[:, k4, 0, 0, :], H16[0][:, k, 0:128], [b_H[0]], [b_H4])
            cp("dve", H4[:, k4, 0, 1, :], H16[1][:, k, 0:128], [b_H[1]], [b_H4])
            act(pr[0][:, 0, :], pre[:, 0:128], AF.Copy, [b_pre], [b_pr[0]])
            act(pr[0][:, 1, :], pim[:, 0:128], AF.Copy, [b_pim], [b_pr[0]])
            cur = 0
            for j in range(1, 4):
                n = 4 * j
                stt("dve", htmp[:, 0, :], H16[0][:, k, 0:128], Lre(n, k), pr[cur][:, 0, :], MUL, ADD,
                    [b_H[0], bS, b_pr[cur]], [b_htmp])
                stt("dve", H4[:, k4, j, 0, :], H16[1][:, k, 0:128], nLim(n, k), htmp[:, 0, :], MUL, ADD,
                    [b_H[1], bS, b_htmp], [b_H4])
                stt("dve", htmp[:, 1, :], H16[0][:, k, 0:128], Lim(n, k), pr[cur][:, 1, :], MUL, ADD,
                    [b_H[0], bS, b_pr[cur]], [b_htmp])
                stt("dve", H4[:, k4, j, 1, :], H16[1][:, k, 0:128], Lre(n, k), htmp[:, 1, :], MUL, ADD,
                    [b_H[1], bS, b_htmp], [b_H4])
                if j < 3:
                    prefix_step(k, j, pr[cur], b_pr[cur], pr[1 - cur], b_pr[1 - cur], pre[:, j * 128:(j + 1) * 128],
                                pim[:, j * 128:(j + 1) * 128], b_pre, b_pim, 128)
                    cur = 1 - cur
            (sre, b_sre), (sim, b_sim) = s4_matmuls(k, 16, uS, 1)
            h0r, h0i = S("h0_re", 16)[:, k, :], S("h0_im", 16)[:, k, :]
            stt("dve", hs[:, 0, k, :], h0r, Lre(4, k), sre[:, 0:16], MUL, ADD, [b_sst, bS, b_sre], [b_hs])
            stt("dve", hs[:, 0, k, :], h0i, nLim(4, k), hs[:, 0, k, :], MUL, ADD, [b_sst, bS, b_hs], [b_hs])
            stt("dve", hs[:, 1, k, :], h0r, Lim(4, k), sim[:, 0:16], MUL, ADD, [b_sst, bS, b_sim], [b_hs])
            stt("dve", hs[:, 1, k, :], h0i, Lre(4, k), hs[:, 1, k, :], MUL, ADD, [b_sst, bS, b_hs], [b_hs])
        for j in range(4):
            out_stage(kk, j, 128, lambda k, k4, comp, j=j: H4[:, k4, j, comp, :], b_H4, uP, yP[kk][:, j])
        out_stage(kk, 0, 16, lambda k, k4, comp: h0b[:, comp, k, :], b_h0b, uS, yS[kk])
    stores.append(P.add("sp", lambda e: e.dma_start(out=o_hs, in_=hs[:]), reads=[b_hs], dma=True, semkey="st_hs"))

    wg = AR.alloc([128, 4, 1024], BF16)
    b_wg = P.buf("wg")
    P.add("sp", lambda e: e.dma_start(out=wg[:], in_=w_glu_b.rearrange("(k p) m -> p k m", p=128)), reads=[b_w_glu_b],
          writes=[b_wg], dma=True, semkey="wg")
    ysT, b_ys = uT, b_uT
    sg = AR.alloc([128, 512], F32)
    b_sg = P.buf("sg")
    for (n0, n) in chunks:
        for m in range(4):
            pv, b_pv = next_bank()
            pg, b_pg = next_bank()
            for k in range(4):
                P.add("pe", lambda e, pv=pv, k=k, m=m, n0=n0, n=n: e.matmul(
                    pv[:, 0:n], lhsT=wg[:, k, 128 * m:128 * m + 128], rhs=yT[:, k, n0:n0 + n], start=(k == 0), stop=(k == 3)),
                    reads=[b_wg, b_yT], writes=[b_pv])
            for k in range(4):
                P.add("pe", lambda e, pg=pg, k=k, m=m, n0=n0, n=n: e.matmul(
                    pg[:, 0:n], lhsT=wg[:, k, 512 + 128 * m:512 + 128 * m + 128], rhs=yT[:, k, n0:n0 + n], start=(k == 0),
                    stop=(k == 3)), reads=[b_wg, b_yT], writes=[b_pg])
            act(sg[:, 0:n], pg[:, 0:n], AF.Sigmoid, [b_pg], [b_sg])
            tt("dve", ysT[:, m, n0:n0 + n], pv[:, 0:n], sg[:, 0:n], MUL, [b_pv, b_sg], [b_ys])
    return dict(ysT=ysT, b_ys=b_ys)


def build_attn(P, AR, nc, din, dout, dint, stores, banks, bank_bufs, cast_w, cqn, b_cqn, ckvn, b_ckvn, krK, b_krK,
               sm, b_sm, smc, ones_bf, b_ones, rope_c, rope_s, attT, b_attT, chunks, kS, b_kS):
    MUL, ADD = ALU.mult, ALU.add
    w_uq = din("w_uq", [Q_LORA, N_HEADS * QK_HEAD])
    w_uk = din("w_uk", [KV_LORA, N_HEADS * QK_NOPE])
    w_uv = din("w_uv", [KV_LORA, N_HEADS * V_HEAD])
    maskd_d = din("maskd", [128, 4, 128])
    rotm_d = din("rotm", [96, 96])
    w_uq_b = dint("w_uq_b", [Q_LORA, N_HEADS * QK_HEAD], BF16)
    w_uk_b = dint("w_uk_b", [KV_LORA, N_HEADS * QK_NOPE], BF16)
    w_uv_b = dint("w_uv_b", [KV_LORA, N_HEADS * V_HEAD], BF16)
    b_wqb = cast_w(w_uq_b, w_uq, Q_LORA, 768, "c_wuq")
    b_wkb = cast_w(w_uk_b, w_uk, KV_LORA, 512, "c_wuk")
    b_wvb = cast_w(w_uv_b, w_uv, KV_LORA, 512, "c_wuv")
    K_own = [dint(f"K_own{h}", [QK_HEAD, NPT], BF16) for h in range(N_HEADS)]
    K_all = [dint(f"K_all{h}", [4 * QK_HEAD, NPT], BF16) for h in range(N_HEADS)]
    V_own = [dint(f"V_own{h}", [128, 16 * 65], BF16) for h in range(N_HEADS)]
    V_all = [dint(f"V_all{h}", [512, 16 * 65], BF16) for h in range(N_HEADS)]
    RG = [[0, 1, 2, 3], [4, 5, 6, 7]]

    AR.release(0)
    P.new_phase()
    qT = AR.alloc([96, 8, T], BF16)
    maskd = AR.alloc([128, 4, 128], BF16)
    sel65 = AR.alloc([65, 64], F32)
    mC1 = AR.mark()
    wq = AR.alloc([128, 3, 768], BF16)
    wk = AR.alloc([128, 2, 8, 96], BF16)
    wv = AR.alloc([128, 2, 512], BF16)
    rc = AR.alloc([96, T], F32)
    rs = AR.alloc([96, T], F32)
    rotm = AR.alloc([96, 96], F32)
    maskf = AR.alloc([128, 4, 128], F32)
    b_wq, b_wk, b_wv, b_rc, b_rs, b_rot, b_mk, b_mkf, b_sel, b_qT = [P.buf(n) for n in
        ("wq", "wk", "wv", "rc", "rs", "rotm", "maskd", "maskf", "sel65", "qT")]
    P.add("sp", lambda e: e.dma_start(out=wq[:], in_=w_uq_b.rearrange("(k p) m -> p k m", p=128)), reads=[b_wqb], writes=[b_wq],
          dma=True, semkey="wq")
    P.add("pool", lambda e: e.memset(wk[:], 0.0), writes=[b_wk])
    for k in range(2):
        P.add("sp", lambda e, k=k: e.dma_start(out=wk[:, k, :, 0:64],
                                               in_=w_uk_b[128 * k:128 * k + 128, :].rearrange("p (h d) -> p h d", h=8)),
              reads=[b_wkb], writes=[b_wk], dma=True, semkey="wk")
    P.add("sp", lambda e: e.dma_start(out=wv[:], in_=w_uv_b.rearrange("(k p) m -> p k m", p=128)), reads=[b_wvb], writes=[b_wv],
          dma=True, semkey="wv")
    P.add("sp", lambda e: e.dma_start(out=rc[:], in_=rope_c), writes=[b_rc], dma=True, semkey="rc")
    P.add("sp", lambda e: e.dma_start(out=rs[:], in_=rope_s), writes=[b_rs], dma=True, semkey="rs")
    P.add("sp", lambda e: e.dma_start(out=rotm[:], in_=rotm_d), writes=[b_rot], dma=True, semkey="rotm")
    P.add("sp", lambda e: e.dma_start(out=maskf[:], in_=maskd_d), writes=[b_mkf], dma=True, semkey="maskf")
    P.add("pool", lambda e: e.tensor_copy(out=maskd[:], in_=maskf[:]), reads=[b_mkf], writes=[b_mk])
    P.add("pool", lambda e: e.memset(sel65[:], 0.0), writes=[b_sel])
    P.add("pool", lambda e: e.memset(sel65[64:65, :], 1.0), writes=[b_sel])
    ident = sm[:, SL["ident"][0]:SL["ident"][0] + 128]

    rr = [0]

    def tbank():
        i = 2 + rr[0] % 6
        rr[0] += 1
        return banks[i], bank_bufs[i]

    NT_ = 4
    tmpl = [dict(raw=AR.alloc([96, 512], F32), sqh=AR.alloc([96, 512], BF16), lnv=AR.alloc([96, 512], F32),
                 rstd=AR.alloc([96, 512], F32), qg=AR.alloc([96, 512], F32), t1=AR.alloc([96, 512], F32),
                 t2=AR.alloc([96, 512], F32)) for _ in range(NT_)]
    tmpb = [{k: P.buf(f"nr_{k}{i}") for k in ("raw", "sqh", "lnv", "rstd", "qg", "t1", "t2")} for i in range(NT_)]
    kst = [AR.alloc([96, 512], BF16) for _ in range(3)]
    b_kst = [P.buf(f"kst{i}") for i in range(3)]
    nrc = [0]

    def normrope(mm_fn, gname, out_ap, b_out, n, n0, after_fn=None):
        ti = nrc[0] % NT_
        nrc[0] += 1
        t_, b_ = tmpl[ti], tmpb[ti]
        raw, sqh, lnv, rstd, qg, t1, t2 = (t_[k] for k in ("raw", "sqh", "lnv", "rstd", "qg", "t1", "t2"))
        ps, b_ps = tbank()
        mm_fn(ps, b_ps)
        P.add("act", lambda e: e.activation(out=raw[:, 0:n], in_=ps[0:96, 0:n], func=AF.Copy), reads=[b_ps], writes=[b_["raw"]])
        P.add("dve", lambda e: e.tensor_tensor(out=sqh[:, 0:n], in0=raw[:, 0:n], in1=raw[:, 0:n], op=MUL), reads=[b_["raw"]],
              writes=[b_["sqh"]])
        yield
        p2, b_p2 = tbank()
        P.add("pe", lambda e: e.matmul(p2[0:96, 0:n], lhsT=ones_bf[0:96, 0:96], rhs=sqh[:, 0:n], start=True, stop=True),
              reads=[b_["sqh"], b_ones], writes=[b_p2])
        P.add("act", lambda e: e.activation(out=lnv[:, 0:n], in_=p2[0:96, 0:n], func=AF.Ln, scale=1.0 / QK_HEAD, bias=EPS),
              reads=[b_p2], writes=[b_["lnv"]])
        P.add("act", lambda e: e.activation(out=rstd[:, 0:n], in_=lnv[:, 0:n], func=AF.Exp, scale=-0.5), reads=[b_["lnv"]],
              writes=[b_["rstd"]])
        yield
        P.add("dve", lambda e: e.scalar_tensor_tensor(out=qg[:, 0:n], in0=raw[:, 0:n], scalar=smc(gname)[0:96, :], in1=rstd[:, 0:n],
                                                      op0=MUL, op1=MUL), reads=[b_["raw"], b_["rstd"], b_sm], writes=[b_["qg"]])
        P.add("dve", lambda e: e.tensor_tensor(out=t1[:, 0:n], in0=qg[:, 0:n], in1=rc[:, n0:n0 + n], op=MUL), reads=[b_["qg"], b_rc],
              writes=[b_["t1"]])
        yield
        p3, b_p3 = tbank()
        P.add("pe", lambda e: e.matmul(p3[0:96, 0:n], lhsT=rotm[:, :], rhs=qg[:, 0:n], start=True, stop=True),
              reads=[b_rot, b_["qg"]], writes=[b_p3])
        P.add("dve", lambda e: e.tensor_tensor(out=t2[:, 0:n], in0=p3[0:96, 0:n], in1=rs[:, n0:n0 + n], op=MUL),
              reads=[b_p3, b_rs], writes=[b_["t2"]])
        P.add("dve", lambda e: e.tensor_tensor(out=out_ap, in0=t1[:, 0:n], in1=t2[:, 0:n], op=ADD), reads=[b_["t1"], b_["t2"]],
              writes=[b_out])
        if after_fn is not None:
            after_fn()

    def run_skewed(gens, step=1):
        active = []
        it = iter(gens)
        more = True
        while more or active:
            if more:
                try:
                    active.append(next(it))
                except StopIteration:
                    more = False
            for _ in range(step):
                for gg in list(active):
                    try:
                        next(gg)
                    except StopIteration:
                        active.remove(gg)

    b_Kown = [P.buf(f"K_own{h}") for h in range(8)]
    b_Vown = [P.buf(f"V_own{h}") for h in range(8)]
    b_Kall = [P.buf(f"K_all{h}") for h in range(8)]
    b_Vall = [P.buf(f"V_all{h}") for h in range(8)]
    Vst = AR.alloc([128, 8, 16, 65], BF16)
    b_Vst = P.buf("Vst")
    P.add("pool", lambda e: e.memset(Vst[:, :, :, 64:65], 1.0), writes=[b_Vst])
    for blk in range(16):
        ps, b_ps = tbank()
        for k in range(2):
            P.add("pe", lambda e, ps=ps, k=k, blk=blk: e.matmul(ps[:, 0:512], lhsT=ckvn[:, k, 128 * blk:128 * blk + 128], rhs=wv[:, k, :],
                                                                 start=(k == 0), stop=(k == 1)), reads=[b_ckvn, b_wv], writes=[b_ps])
        P.add("act", lambda e, ps=ps, blk=blk: e.activation(out=Vst[:, :, blk, 0:64], in_=ps[:, 0:512].rearrange("p (h v) -> p h v", h=8),
                                                            func=AF.Copy), reads=[b_ps], writes=[b_Vst])
    for h in range(N_HEADS):
        P.add("sp", lambda e, h=h: e.dma_start(out=V_own[h], in_=Vst[:, h].rearrange("p b e -> p (b e)")), reads=[b_Vst],
              writes=[b_Vown[h]], dma=True, semkey=f"vst{h}")
        P.add("pool", lambda e, h=h: e.collective_compute("AllGather", ALU.bypass, replica_groups=RG, ins=[V_own[h].opt()],
                                                          outs=[V_all[h].opt()]),
              reads=[b_Vown[h]], writes=[b_Vall[h]], dma="cc", semkey=f"ccV{h}")
    kcount = [0]
    for h in range(N_HEADS):
        gens = []
        for ci, (n0, n) in enumerate(chunks):
            def q_mm(ps, b_ps, h=h, n0=n0, n=n):
                for k in range(3):
                    P.add("pe", lambda e, k=k: e.matmul(ps[0:96, 0:n], lhsT=wq[:, k, 96 * h:96 * h + 96], rhs=cqn[:, k, n0:n0 + n],
                                                        start=(k == 0), stop=(k == 2)), reads=[b_wq, b_cqn], writes=[b_ps])

            def k_mm(ps, b_ps, h=h, n0=n0, n=n):
                for k in range(2):
                    P.add("pe", lambda e, k=k: e.matmul(ps[0:96, 0:n], lhsT=wk[:, k, h, :], rhs=ckvn[:, k, n0:n0 + n], start=(k == 0),
                                                        stop=False), reads=[b_wk, b_ckvn], writes=[b_ps])
                P.add("pe", lambda e: e.matmul(ps[0:96, 0:n], lhsT=ident[64:96, 0:96], rhs=krK[64:96, n0:n0 + n], start=False, stop=True,
                                               tile_position=(64, 0)), reads=[b_sm, b_krK], writes=[b_ps])
            ki_ = kcount[0] % 3
            kcount[0] += 1
            ks, b_ks = kst[ki_], b_kst[ki_]
            npr = min(n0 + n, NPT) - n0

            def after(h=h, n0=n0, n=n, npr=npr, ks=ks, b_ks=b_ks, ki_=ki_):
                if npr > 0:
                    P.add("sp", lambda e: e.dma_start(out=K_own[h][:, n0:n0 + npr], in_=ks[:, 0:npr]), reads=[b_ks], writes=[b_Kown[h]],
                          dma=True, semkey=f"kst{ki_}")
                if npr < n:
                    P.add("pool", lambda e: e.tensor_copy(out=kS[:, h, :], in_=ks[:, npr:n]), reads=[b_ks], writes=[b_kS])
            gens.append(normrope(q_mm, "g_q", qT[:, h, n0:n0 + n], b_qT, n, n0))
            gens.append(normrope(k_mm, "g_k", ks[:, 0:n], b_ks, n, n0, after))
        run_skewed(gens)
        P.add("pool", lambda e, h=h: e.collective_compute("AllGather", ALU.bypass, replica_groups=RG, ins=[K_own[h].opt()],
                                                          outs=[K_all[h].opt()]),
              reads=[b_Kown[h]], writes=[b_Kall[h]], dma="cc", semkey=f"ccK{h}")
    import os
    if os.environ.get("CUT") == "3":
        return {}
    AR.release(mC1)
    P.new_phase()
    Kh = [AR.alloc([96, 4, NPT], BF16) for _ in range(2)]
    Vh = [AR.alloc([128, 4, 16 * 65], BF16) for _ in range(2)]
    Vvis = AR.alloc([128, 4, 16 * 65], BF16)
    Vful = AR.alloc([128, 4, 16 * 65], BF16)
    b_Kh = [P.buf("Kh0"), P.buf("Kh1")]
    b_Vh = [P.buf("Vh0"), P.buf("Vh1")]
    b_Vvis, b_Vful = P.buf("Vvis"), P.buf("Vful")
    PT = [AR.alloc([128, 512], BF16) for _ in range(6)]
    b_PT = [P.buf(f"PT{i}") for i in range(6)]
    Osb = AR.alloc([65, 512], F32); rl = AR.alloc([64, 512], F32); ast = AR.alloc([64, 512], BF16)
    b_Osb, b_rl, b_ast = P.buf("Osb"), P.buf("rl"), P.buf("ast")

    def load_head(h):
        i = h % 2
        P.add("sp", lambda e: e.dma_start(out=Kh[i][:], in_=K_all[h].rearrange("(r d) t -> d r t", r=4)), reads=[b_Kall[h]], writes=[b_Kh[i]], dma=True,
              semkey=f"Kh{i}")
        P.add("sp", lambda e: e.dma_start(out=Vh[i][:], in_=V_all[h].rearrange("(r p) x -> p r x", r=4)), reads=[b_Vall[h]], writes=[b_Vh[i]], dma=True,
              semkey=f"Vh{i}")

    load_head(0)
    pcount = 0
    for h in range(N_HEADS):
        i = h % 2
        if h + 1 < N_HEADS:
            load_head(h + 1)
        for r in range(4):
            P.add("dve", lambda e, r=r, i=i: e.tensor_scalar(out=Vvis[:, r, :], in0=Vh[i][:, r, :], scalar1=smc("vis", r), scalar2=None,
                                                              op0=MUL), reads=[b_Vh[i], b_sm], writes=[b_Vvis])
            P.add("dve", lambda e, r=r, i=i: e.tensor_scalar(out=Vful[:, r, :], in0=Vh[i][:, r, :], scalar1=smc("full", r), scalar2=None,
                                                              op0=MUL), reads=[b_Vh[i], b_sm], writes=[b_Vful])
        for qc in range(4):
            O, b_O = banks[qc % 2], bank_bufs[qc % 2]
            first = [True]
            pending = []

            def flush(keep):
                while len(pending) > keep:
                    pending.pop(0)()
            for r in range(4):
                for kb in range(16):
                    S_, b_S = tbank()
                    P.add("pe", lambda e, S_=S_, r=r, kb=kb, i=i, h=h, qc=qc: e.matmul(
                        S_[:, 0:512], lhsT=Kh[i][:, r, 128 * kb:128 * kb + 128], rhs=qT[:, h, 512 * qc:512 * qc + 512],
                        start=True, stop=True), reads=[b_Kh[i], b_qT], writes=[b_S])
                    pt, b_pt = PT[pcount % 6], b_PT[pcount % 6]
                    pcount += 1
                    P.add("act", lambda e, S_=S_, pt=pt: e.activation(out=pt[:], in_=S_[:, 0:512], func=AF.Exp, scale=SCALE),
                          reads=[b_S], writes=[b_pt])

                    def pv(r=r, kb=kb, pt=pt, b_pt=b_pt, O=O, b_O=b_O, qc=qc, i=i):
                        d = kb - 4 * qc
                        segs = []
                        if d < 0:
                            segs.append((0, 512, Vvis, b_Vvis))
                        elif d > 3:
                            segs.append((0, 512, Vful, b_Vful))
                        else:
                            if d > 0:
                                segs.append((0, 128 * d, Vful, b_Vful))
                            P.add("dve", lambda e: e.tensor_tensor(out=pt[:, 128 * d:128 * d + 128], in0=pt[:, 128 * d:128 * d + 128],
                                                                   in1=maskd[:, r, :], op=MUL), reads=[b_pt, b_mk], writes=[b_pt])
                            segs.append((128 * d, 128 * d + 128, Vh[i], b_Vh[i]))
                            if d < 3:
                                segs.append((128 * d + 128, 512, Vvis, b_Vvis))
                        for (c0, c1, Vx, b_Vx) in segs:
                            st = first[0]
                            first[0] = False
                            P.add("pe", lambda e, c0=c0, c1=c1, Vx=Vx, st=st: e.matmul(
                                O[0:65, c0:c1], lhsT=Vx[:, r, 65 * kb:65 * kb + 65], rhs=pt[:, c0:c1], start=st, stop=False),
                                reads=[b_Vx, b_pt], writes=[b_O])
                    pending.append(pv)
                    flush(3)
            flush(0)
            P.add("act", lambda e, O=O: e.activation(out=Osb[:], in_=O[0:65, 0:512], func=AF.Copy), reads=[b_O], writes=[b_Osb])
            lb, b_lb = tbank()
            P.add("pe", lambda e, lb=lb: e.matmul(lb[0:64, 0:512], lhsT=sel65[:, :], rhs=Osb[:, :], start=True, stop=True),
                  reads=[b_sel, b_Osb], writes=[b_lb])
            P.add("dve", lambda e, lb=lb: e.reciprocal(out=rl[:], in_=lb[0:64, 0:512]), reads=[b_lb], writes=[b_rl])
            cols = slice(512 * qc, 512 * qc + 512)
            if h % 2 == 0:
                P.add("dve", lambda e, h=h, cols=cols: e.tensor_tensor(out=attT[0:64, h // 2, cols], in0=Osb[0:64, :], in1=rl[:], op=MUL),
                      reads=[b_Osb, b_rl], writes=[b_attT])
            else:
                P.add("dve", lambda e: e.tensor_tensor(out=ast[:], in0=Osb[0:64, :], in1=rl[:], op=MUL), reads=[b_Osb, b_rl],
                      writes=[b_ast])
                P.add("sp", lambda e, h=h, cols=cols: e.dma_start(out=attT[64:128, h // 2, cols], in_=ast[:]), reads=[b_ast],
                      writes=[b_attT], dma=True, semkey="ast")
    return dict(mC1=mC1, qT=qT, b_qT=b_qT, w_uk_b=w_uk_b, b_wkb=b_wkb, w_uv_b=w_uv_b, b_wvb=b_wvb, rotm_d=rotm_d)


def build_tail(P, AR, nc, din, dout, dint, stores, banks, bank_bufs, cast_w, xT, w_in_b, b_w_in_b, sm, b_sm, smc, ones_bf, b_ones,
               attT, b_attT, ysT, b_ys, chunks):
    MUL, ADD = ALU.mult, ALU.add
    names = [("w_oa", 512, D), ("w_os", 512, D), ("w_out", D, D), ("w_up", D, 2 * D_FF), ("w_down", D_FF, D),
             ("w_pg", D, D), ("w_pp", PLE, D)]
    W, bW = {"w_in": w_in_b}, {"w_in": b_w_in_b}
    for nm, r, c in names:
        src = din(nm, [r, c])
        W[nm] = dint(nm + "_b", [r, c], BF16)
        bW[nm] = cast_w(W[nm], src, r, c, "c_" + nm)
    pT_d = din("pT", [PLE, T])
    scT_d = din("scT", [128, 44, 16, 2])
    o_yT = dout("o_yT", [D, T])
    o_cvp = dout("o_cvp", [128, 44, 2])
    o_cvs = dout("o_cvs", [128, 44, 16, 2])
    cc_h_in = dint("cc_h_in", [128, 16], F32)
    cc_h_out = dint("cc_h_out", [512, 16], F32)

    AR.release(0)
    P.new_phase()
    HO = 2
    x1 = AR.alloc([128, KD, 514], F32)
    x1c4 = AR.alloc([128, KD, 290], F32)
    xn = AR.alloc([128, KD, 514], BF16)
    scr = AR.alloc([128, KD, 514], BF16)
    hT = AR.alloc([128, 22, 512], BF16)
    upx = [[AR.alloc([128, 514], F32) for _ in range(2)] for _ in range(2)]
    cav = [[AR.alloc([128, 512], F32) for _ in range(2)] for _ in range(2)]
    sgt = [AR.alloc([128, 512], F32) for _ in range(2)]
    lnv = AR.alloc([128, 514], F32)
    rstd = AR.alloc([128, 514], F32)
    pTc = AR.alloc([128, 2, 512], BF16)
    scT = AR.alloc([128, 44, 16, 2], F32)
    upS = [AR.alloc([128, 16, 6], F32) for _ in range(2)]
    cvp = AR.alloc([128, 44, 2], F32)
    cvs = AR.alloc([128, 44, 16, 2], F32)
    carry = AR.alloc([128, 44, 2], F32)
    Hg = AR.alloc([128, 4, 16], F32)
    hsend = AR.alloc([128, 16], F32)
    hrecv = AR.alloc([128, 16], F32)
    NSLAB = 4
    slabs = [AR.alloc([128, 4096], BF16) for _ in range(NSLAB)]
    b_x1, b_x1c4, b_xn, b_scr, b_hT, b_lnv, b_rstd, b_pTc, b_scT, b_cvp, b_cvs, b_carry, b_Hg, b_hsend, b_hrecv = [
        P.buf(n) for n in ("x1", "x1c4", "xn", "scr", "hT", "lnv", "rstd", "pTc", "scT", "cvp", "cvs", "carry", "Hg", "hsend", "hrecv")]
    b_upx = [[P.buf(f"upx{i}{j}") for j in range(2)] for i in range(2)]
    b_cav = [[P.buf(f"cav{i}{j}") for j in range(2)] for i in range(2)]
    b_sgt = [P.buf("sgt0"), P.buf("sgt1")]
    b_upS = [P.buf("upS0"), P.buf("upS1")]
    b_slab = [P.buf(f"slab{i}") for i in range(NSLAB)]
    P.add("sp", lambda e: e.dma_start(out=scT[:], in_=scT_d), writes=[b_scT], dma=True, semkey="scT")
    P.add("pool", lambda e: e.memset(carry[:], 0.0), writes=[b_carry])

    rr = [0]

    def tbank():
        i = rr[0] % 8
        rr[0] += 1
        return banks[i], bank_bufs[i]

    sl = [0]

    def load_slab(wname, kt, c0, width):
        i = sl[0] % NSLAB
        sl[0] += 1
        v = slabs[i][:, 0:kt * width].rearrange("p (k m) -> p k m", k=kt)
        src, bsrc = W[wname], bW[wname]
        P.add("sp", lambda e: e.dma_start(out=v, in_=src.rearrange("(k p) m -> p k m", p=128)[:, :, c0:c0 + width]),
              reads=[bsrc], writes=[b_slab[i]], dma=True, semkey=f"slab{i}")
        return v, b_slab[i]

    def mm_group(ps, b_ps, n, w, b_w, kt, wc0, rhs_fn, rd):
        for k in range(kt):
            P.add("pe", lambda e, k=k: e.matmul(ps[:, 0:n], lhsT=w[:, k, wc0:wc0 + 128], rhs=rhs_fn(k), start=(k == 0), stop=(k == kt - 1)),
                  reads=[b_w] + rd, writes=[b_ps])

    def norm_to_xn(src, b_src, gname, c_lo, c_hi):
        n = c_hi - c_lo
        P.add("pool", lambda e: e.tensor_tensor(out=scr[:, :, c_lo:c_hi], in0=src[:, :, c_lo:c_hi], in1=src[:, :, c_lo:c_hi], op=MUL),
              reads=[b_src], writes=[b_scr])
        ps, b_ps = tbank()
        for k in range(KD):
            P.add("pe", lambda e, k=k: e.matmul(ps[:, 0:n], lhsT=ones_bf[:], rhs=scr[:, k, c_lo:c_hi], start=(k == 0), stop=(k == KD - 1)),
                  reads=[b_scr, b_ones], writes=[b_ps])
        P.add("act", lambda e: e.activation(out=lnv[:, 0:n], in_=ps[:, 0:n], func=AF.Ln, scale=1.0 / D, bias=EPS), reads=[b_ps],
              writes=[b_lnv])
        P.add("act", lambda e: e.activation(out=rstd[:, 0:n], in_=lnv[:, 0:n], func=AF.Exp, scale=-0.5), reads=[b_lnv], writes=[b_rstd])
        for k in range(KD):
            P.add("dve", lambda e, k=k: e.scalar_tensor_tensor(out=xn[:, k, c_lo:c_hi], in0=src[:, k, c_lo:c_hi], scalar=smc(gname, k),
                                                               in1=rstd[:, 0:n], op0=MUL, op1=MUL),
                  reads=[b_src, b_rstd, b_sm], writes=[b_xn])

    xT_v = xT.rearrange("(k p) t -> p k t", p=128)

    def phase_D(xt, b_xt, n0, n):
        lo, hi = HO, HO + n
        P.add("sp", lambda e: e.dma_start(out=xt[:, :, lo:hi], in_=xT_v[:, :, n0:n0 + n]), writes=[b_xt], dma=True, semkey="x1ld")
        norm_to_xn(xt, b_xt, "g_mix", lo, hi)
        mixed = scr
        for q in range(2):
            wga, b_wga = load_slab("w_in", KD, OFF_GA + 512 * q, 512)
            wgs, b_wgs = load_slab("w_in", KD, OFF_GS + 512 * q, 512)
            woa, b_woa = load_slab("w_oa", 4, 512 * q, 512)
            wos, b_wos = load_slab("w_os", 4, 512 * q, 512)
            for mi in range(4):
                m = 4 * q + mi
                pga, b_pga = tbank()
                mm_group(pga, b_pga, n, wga, b_wga, KD, 128 * mi, lambda k: xn[:, k, lo:hi], [b_xn])
                pgs, b_pgs = tbank()
                mm_group(pgs, b_pgs, n, wgs, b_wgs, KD, 128 * mi, lambda k: xn[:, k, lo:hi], [b_xn])
                poa, b_poa = tbank()
                mm_group(poa, b_poa, n, woa, b_woa, 4, 128 * mi, lambda k: attT[:, k, n0:n0 + n], [b_attT])
                pos_, b_pos = tbank()
                mm_group(pos_, b_pos, n, wos, b_wos, 4, 128 * mi, lambda k: ysT[:, k, n0:n0 + n], [b_ys])
                P.add("act", lambda e, pga=pga: e.activation(out=sgt[0][:, 0:n], in_=pga[:, 0:n], func=AF.Sigmoid), reads=[b_pga],
                      writes=[b_sgt[0]])
                P.add("act", lambda e, pgs=pgs: e.activation(out=sgt[1][:, 0:n], in_=pgs[:, 0:n], func=AF.Sigmoid), reads=[b_pgs],
                      writes=[b_sgt[1]])
                P.add("dve", lambda e, poa=poa: e.tensor_tensor(out=cav[0][0][:, 0:n], in0=poa[:, 0:n], in1=sgt[0][:, 0:n], op=MUL),
                      reads=[b_poa, b_sgt[0]], writes=[b_cav[0][0]])
                P.add("dve", lambda e, pos_=pos_: e.tensor_tensor(out=cav[0][1][:, 0:n], in0=pos_[:, 0:n], in1=sgt[1][:, 0:n], op=MUL),
                      reads=[b_pos, b_sgt[1]], writes=[b_cav[0][1]])
                P.add("pool", lambda e, m=m: e.tensor_tensor(out=mixed[:, m, lo:hi], in0=cav[0][0][:, 0:n], in1=cav[0][1][:, 0:n], op=ADD),
                      reads=[b_cav[0][0], b_cav[0][1]], writes=[b_scr])
        for q in range(2):
            wo, b_wo = load_slab("w_out", KD, 512 * q, 512)
            for mi in range(4):
                m = 4 * q + mi
                ps, b_ps = tbank()
                mm_group(ps, b_ps, n, wo, b_wo, KD, 128 * mi, lambda k: mixed[:, k, lo:hi], [b_scr])
                P.add("dve", lambda e, ps=ps, m=m: e.tensor_tensor(out=xt[:, m, lo:hi], in0=ps[:, 0:n], in1=xt[:, m, lo:hi], op=ADD),
                      reads=[b_ps, b_xt], writes=[b_xt])

    ucount = [0]

    def conv3(ci_, src3, dst, b_src, b_dst, tile, nn, view=None):
        w0, w1, w2 = (smc("conv_w", tap * 44 + tile) for tap in range(3))
        P.add("act", lambda e: e.activation(out=dst, in_=src3(2), func=AF.Identity, scale=w2, bias=smc("conv_b", tile)),
              reads=[b_src, b_sm], writes=[b_dst])
        P.add("dve", lambda e: e.scalar_tensor_tensor(out=dst, in0=src3(1), scalar=w1, in1=dst, op0=MUL, op1=ADD),
              reads=[b_src, b_sm, b_dst], writes=[b_dst])
        P.add("dve", lambda e: e.scalar_tensor_tensor(out=dst, in0=src3(0), scalar=w0, in1=dst, op0=MUL, op1=ADD),
              reads=[b_src, b_sm, b_dst], writes=[b_dst])

    def phase_E(xt, b_xt, n0, n, first, last):
        lo, hi = HO, HO + n
        c_lo = 0 if first else HO
        N = hi - c_lo
        npr = min(n0 + n, NPT) - n0
        ns = n - npr
        norm_to_xn(xt, b_xt, "g_ffn", c_lo, hi)
        for q in range(6):
            npair = min(4, 22 - 4 * q)
            wa, b_wa = load_slab("w_up", KD, 512 * q, 128 * npair)
            wv_, b_wv = load_slab("w_up", KD, D_FF + 512 * q, 128 * npair)
            for pi_ in range(npair):
                p = 4 * q + pi_
                ub = ucount[0] % 2
                ucount[0] += 1
                for av, (w, b_w) in enumerate(((wa, b_wa), (wv_, b_wv))):
                    tile = p + 22 * av
                    u, b_u = upx[ub][av], b_upx[ub][av]
                    c, b_c = cav[ub][av], b_cav[ub][av]
                    ps, b_ps = tbank()
                    mm_group(ps, b_ps, N, w, b_w, KD, 128 * pi_, lambda k: xn[:, k, c_lo:hi], [b_xn])
                    P.add("act", lambda e, ps=ps, u=u: e.activation(out=u[:, c_lo:hi], in_=ps[:, 0:N], func=AF.Copy), reads=[b_ps],
                          writes=[b_u])
                    if not first:
                        P.add("pool", lambda e, u=u, tile=tile: e.tensor_copy(out=u[:, 0:2], in_=carry[:, tile, :]), reads=[b_carry],
                              writes=[b_u])
                    conv3(0, lambda k, u=u: u[:, k:k + npr], c[:, 0:npr], b_u, b_c, tile, npr)
                    P.add("pool", lambda e, u=u, tile=tile: e.tensor_copy(out=carry[:, tile, :], in_=u[:, npr:npr + 2]), reads=[b_u],
                          writes=[b_carry])
                    if last:
                        P.add("pool", lambda e, u=u, tile=tile: e.tensor_copy(out=cvp[:, tile, :], in_=u[:, npr:npr + 2]), reads=[b_u],
                              writes=[b_cvp])
                    if ns:
                        us, b_us = upS[av], b_upS[av]
                        P.add("pool", lambda e, us=us, tile=tile: e.tensor_copy(out=us[:, :, 0:2], in_=scT[:, tile, :, :]), reads=[b_scT],
                              writes=[b_us])
                        P.add("pool", lambda e, us=us, u=u: e.tensor_copy(
                            out=us[:, :, 2:6], in_=u[:, HO + npr:HO + n].rearrange("p (q s) -> p q s", s=4)), reads=[b_u], writes=[b_us])
                        conv3(0, lambda k, us=us: us[:, :, k:k + 4], c[:, npr:n].rearrange("p (q s) -> p q s", s=4), b_us, b_c, tile, ns)
                        P.add("pool", lambda e, us=us, tile=tile: e.tensor_copy(out=cvs[:, tile, :, :], in_=us[:, :, 4:6]), reads=[b_us],
                              writes=[b_cvs])
                ca, cv = cav[ub][0], cav[ub][1]
                P.add("act", lambda e, ca=ca: e.activation(out=ca[:, 0:n], in_=ca[:, 0:n], func=AF.Gelu_apprx_tanh), reads=[b_cav[ub][0]],
                      writes=[b_cav[ub][0]])
                P.add("dve", lambda e, ca=ca, cv=cv, p=p: e.tensor_tensor(out=hT[:, p, 0:n], in0=ca[:, 0:n], in1=cv[:, 0:n], op=MUL),
                      reads=[b_cav[ub][0], b_cav[ub][1]], writes=[b_hT])
        for m in range(KD):
            wd, b_wd = load_slab("w_down", 22, 128 * m, 128)
            ps, b_ps = tbank()
            mm_group(ps, b_ps, n, wd, b_wd, 22, 0, lambda k: hT[:, k, 0:n], [b_hT])
            P.add("dve", lambda e, ps=ps, m=m: e.tensor_tensor(out=xt[:, m, lo:hi], in0=ps[:, 0:n], in1=xt[:, m, lo:hi], op=ADD),
                  reads=[b_ps, b_xt], writes=[b_xt])
        norm_to_xn(xt, b_xt, "g_ple", lo, hi)
        P.add("pool", lambda e: e.dma_start(out=pTc[:, :, 0:n], in_=pT_d.rearrange("(k p) t -> p k t", p=128)[:, :, n0:n0 + n]),
              writes=[b_pTc], dma=True, semkey="pTc")
        for q in range(2):
            wg_, b_wg = load_slab("w_pg", KD, 512 * q, 512)
            wp_, b_wp = load_slab("w_pp", 2, 512 * q, 512)
            for mi in range(4):
                m = 4 * q + mi
                pg, b_pg = tbank()
                mm_group(pg, b_pg, n, wg_, b_wg, KD, 128 * mi, lambda k: xn[:, k, lo:hi], [b_xn])
                pp, b_pp = tbank()
                mm_group(pp, b_pp, n, wp_, b_wp, 2, 128 * mi, lambda k: pTc[:, k, 0:n], [b_pTc])
                P.add("act", lambda e, pg=pg: e.activation(out=sgt[0][:, 0:n], in_=pg[:, 0:n], func=AF.Sigmoid), reads=[b_pg],
                      writes=[b_sgt[0]])
                P.add("dve", lambda e, pp=pp: e.tensor_tensor(out=sgt[1][:, 0:n], in0=pp[:, 0:n], in1=sgt[0][:, 0:n], op=MUL),
                      reads=[b_pp, b_sgt[0]], writes=[b_sgt[1]])
                P.add("pool", lambda e, m=m: e.tensor_tensor(out=xt[:, m, lo:hi], in0=xt[:, m, lo:hi], in1=sgt[1][:, 0:n], op=ADD),
                      reads=[b_xt, b_sgt[1]], writes=[b_xt])
        stores.append(P.add("sp", lambda e: e.dma_start(out=o_yT.rearrange("(k p) t -> p k t", p=128)[:, :, n0:n0 + n], in_=xt[:, :, lo:hi]),
                            reads=[b_xt], dma=True, semkey="st_y"))

    n0_4, n_4 = chunks[4]
    phase_D(x1c4, b_x1c4, n0_4, n_4)
    lastp = HO + (NPT - n0_4)
    P.add("pool", lambda e: e.tensor_copy(out=hsend[:].rearrange("p (k c) -> p k c", c=2), in_=x1c4[:, :, lastp - 2:lastp]),
          reads=[b_x1c4], writes=[b_hsend])
    b_hin, b_hout = P.buf("cc_h_in"), P.buf("cc_h_out")
    P.add("sp", lambda e: e.dma_start(out=cc_h_in, in_=hsend[:]), reads=[b_hsend], writes=[b_hin], dma=True, semkey="hs1")
    P.add("pool", lambda e: e.collective_compute("AllGather", ALU.bypass, replica_groups=[[0, 1, 2, 3], [4, 5, 6, 7]],
                                                 ins=[cc_h_in.opt()], outs=[cc_h_out.opt()]),
          reads=[b_hin], writes=[b_hout], dma="cc", semkey="hs2")
    P.add("sp", lambda e: e.dma_start(out=Hg[:], in_=cc_h_out.rearrange("(r p) f -> p r f", p=128)), reads=[b_hout], writes=[b_Hg],
          dma=True, semkey="hs3")
    P.add("dve", lambda e: e.tensor_scalar(out=hrecv[:], in0=Hg[:, 0, :], scalar1=smc("hsel", 0), scalar2=None, op0=MUL),
          reads=[b_Hg, b_sm], writes=[b_hrecv])
    for r in range(1, 4):
        P.add("dve", lambda e, r=r: e.scalar_tensor_tensor(out=hrecv[:], in0=Hg[:, r, :], scalar=smc("hsel", r), in1=hrecv[:],
                                                           op0=MUL, op1=ADD), reads=[b_Hg, b_sm, b_hrecv], writes=[b_hrecv])
    for ci in range(4):
        n0, n = chunks[ci]
        if ci == 0:
            P.add("pool", lambda e: e.tensor_copy(out=x1[:, :, 0:2], in_=hrecv[:].rearrange("p (k c) -> p k c", c=2)),
                  reads=[b_hrecv], writes=[b_x1])
        phase_D(x1, b_x1, n0, n)
        phase_E(x1, b_x1, n0, n, ci == 0, False)
    phase_E(x1c4, b_x1c4, n0_4, n_4, False, True)
    stores.append(P.add("sp", lambda e: e.dma_start(out=o_cvp, in_=cvp[:]), reads=[b_cvp], dma=True, semkey="st_cvp"))
    stores.append(P.add("sp", lambda e: e.dma_start(out=o_cvs, in_=cvs[:]), reads=[b_cvs], dma=True, semkey="st_cvs"))


def build_sample_attn(P, AR, nc, din, dout, dint, stores, banks, bank_bufs, sm, b_sm, smc, ones_bf, b_ones, ckvn, b_ckvn, krK, b_krK,
                      qT, b_qT, attT, b_attT, n_pool, w_uk_b, b_wkb, w_uv_b, b_wvb, rope_c, rope_s, rotm_d, mC1):
    MUL, ADD = ALU.mult, ALU.add
    CW = KV_LORA + QK_ROPE
    cache = din("cache", [n_pool * 32, 4 * CW])
    ptab = din("ptab", [128, SEQ_PER_CORE * 16], I32)
    p32c_d = din("p32c", [128, 1], I32)
    w_ukT_d = din("w_ukT", [64, 8 * KV_LORA])
    ropeP_d = din("ropeP", [128, 2, NPAGES, 16])
    gk_rep_d = din("gk_rep", [128, 32])
    hselm_d = din("hselm", [128, 4, 32])
    cmask_d = din("cmask", [32, 4])

    AR.release(mC1)
    P.new_phase()
    NB_PG = 5
    pb = [AR.alloc([128, 4, CW], BF16) for _ in range(NB_PG)]
    b_pb = [P.buf(f"pb{i}") for i in range(NB_PG)]
    kt3 = [AR.alloc([128, 4, 96], BF16) for _ in range(2)]
    b_kt3 = [P.buf("kt3_0"), P.buf("kt3_1")]
    rt = [AR.alloc([128, 4, 16], F32) for _ in range(4)]
    b_rt = P.buf("rt")
    ropeP = AR.alloc([128, 2, NPAGES, 16], F32)
    gkr = AR.alloc([128, 32], F32)
    tabs = AR.alloc([128, 4, NPAGES, 16], F32)
    ptb = AR.alloc([128, SEQ_PER_CORE * 16], I32)
    idx = AR.alloc([128, SEQ_PER_CORE * 16], I32)
    iot = AR.alloc([128, 1], I32)
    wk_sb = AR.alloc([128, 2, 512], BF16)
    wv_sb = AR.alloc([128, 2, 512], BF16)
    wukT = AR.alloc([64, 8, KV_LORA], BF16)
    wukT_f = AR.alloc([64, 8 * KV_LORA], F32)
    hselm = AR.alloc([128, 4, 32], BF16)
    hselm_f = AR.alloc([128, 4, 32], F32)
    cmask = AR.alloc([32, 4], F32)
    qgk = AR.alloc([64, 8, NST], BF16)
    Qabs = AR.alloc([128, 2, SEQ_PER_CORE, 32], BF16)
    Qrope = AR.alloc([96, SEQ_PER_CORE, 32], BF16)
    cT_sb = [AR.alloc([128, 2, 512], BF16) for _ in range(2)]
    krT_sb = [AR.alloc([96, 512], BF16) for _ in range(2)]
    kn_sb = [AR.alloc([128, 512], BF16) for _ in range(4)]
    sq_sb = [AR.alloc([128, 512], BF16) for _ in range(4)]
    sqk = AR.alloc([32, 512], BF16)
    lnr = AR.alloc([32, 512], F32)
    rr_ = AR.alloc([32, 512], F32)
    sr = AR.alloc([32, 512], F32)
    Pm = AR.alloc([32, 512], BF16)
    PT_sb = [AR.alloc([128, 4, 32], BF16) for _ in range(2)]
    Lacc = AR.alloc([32, 20], F32)
    accs = AR.alloc([32, KV_LORA], F32)
    lsum = AR.alloc([32, 1], F32)
    olat = AR.alloc([32, KV_LORA], BF16)
    olT = AR.alloc([128, 2, SEQ_PER_CORE, 32], BF16)
    knew = AR.alloc([96, NST], BF16)
    kraw = AR.alloc([96, NST], BF16)
    kg32 = AR.alloc([96, NST], F32)
    kt1 = AR.alloc([96, NST], F32)
    kt2 = AR.alloc([96, NST], F32)
    rc_s = AR.alloc([96, NST], F32)
    rs_s = AR.alloc([96, NST], F32)
    rotm = AR.alloc([96, 96], F32)
    cnew = AR.alloc([4, KV_LORA], BF16)
    names = ["kt", "ropeP", "gkr", "tabs", "ptb", "idx", "iot", "wk", "wv", "wukT", "hselm", "cmask", "qgk", "Qabs", "Qrope",
             "sqk", "lnr", "rr", "sr", "Pm", "Lacc", "accs", "lsum", "olat", "olT", "knew", "kraw", "ktmp", "rcs", "rotm", "cnew"]
    B = {n: P.buf("s_" + n) for n in names}
    b_cT = [P.buf("cT0"), P.buf("cT1")]
    b_krT = [P.buf("krT0"), P.buf("krT1")]
    b_kn = [P.buf(f"kn{i}") for i in range(4)]
    b_sq = [P.buf(f"sq{i}") for i in range(4)]
    b_PT = [P.buf("PTs0"), P.buf("PTs1")]

    rrb = [0]

    def tbank():
        i = 1 + rrb[0] % 7
        rrb[0] += 1
        return banks[i], bank_bufs[i]
    ACC, b_ACC = banks[0], bank_bufs[0]

    ld = lambda out, in_, wr, key, rd=(): P.add("sp", lambda e: e.dma_start(out=out, in_=in_), reads=list(rd), writes=[wr], dma=True,
                                                semkey=key)
    ld(ropeP[:], ropeP_d, B["ropeP"], "s_ropeP")
    ld(gkr[:], gk_rep_d, B["gkr"], "s_gkr")
    ld(ptb[:], ptab, B["ptb"], "s_ptb")
    ld(iot[:], p32c_d, B["iot"], "s_iot")
    ld(wk_sb[:], w_uk_b.rearrange("(k p) m -> p k m", p=128), B["wk"], "s_wk", [b_wkb])
    ld(wv_sb[:], w_uv_b.rearrange("(k p) m -> p k m", p=128), B["wv"], "s_wv", [b_wvb])
    ld(wukT_f[:], w_ukT_d, B["wukT"], "s_wukT")
    ld(hselm_f[:], hselm_d, B["hselm"], "s_hselm")
    ld(cmask[:], cmask_d, B["cmask"], "s_cmask")
    ld(rc_s[:], rope_c[:, NPT:T], B["rcs"], "s_rcs")
    ld(rs_s[:], rope_s[:, NPT:T], B["rcs"], "s_rss")
    ld(rotm[:], rotm_d, B["rotm"], "s_rotm")
    P.add("pool", lambda e: e.tensor_copy(out=wukT[:].rearrange("p h l -> p (h l)"), in_=wukT_f[:]), reads=[B["wukT"]], writes=[B["wukT"]])
    P.add("pool", lambda e: e.tensor_copy(out=hselm[:], in_=hselm_f[:]), reads=[B["hselm"]], writes=[B["hselm"]])
    P.add("dve", lambda e: e.tensor_scalar(out=idx[:], in0=ptb[:], scalar1=32.0, scalar2=iot[:, 0:1], op0=MUL, op1=ADD),
          reads=[B["ptb"], B["iot"]], writes=[B["idx"]])
    g1 = gkr[:, 0:16].unsqueeze(1).broadcast_to([128, NPAGES, 16])
    g2 = gkr[:, 16:32].unsqueeze(1).broadcast_to([128, NPAGES, 16])
    tt_ = lambda o, a, b_, op: P.add("pool", lambda e: e.tensor_tensor(out=o, in0=a, in1=b_, op=op), reads=[B["ropeP"], B["gkr"], B["tabs"]],
                                     writes=[B["tabs"]])
    tt_(tabs[:, 0], ropeP[:, 0], g1, MUL)
    tt_(tabs[:, 1], ropeP[:, 0], g2, MUL)
    tt_(tabs[:, 2], ropeP[:, 1], g2, MUL)
    P.add("pool", lambda e: e.tensor_single_scalar(out=tabs[:, 2], in_=tabs[:, 2], scalar=-1.0, op=MUL), reads=[B["tabs"]],
          writes=[B["tabs"]])
    tt_(tabs[:, 3], ropeP[:, 1], g1, MUL)
    for i in range(2):
        P.add("pool", lambda e, i=i: e.memset(kt3[i][:], 0.0), writes=[b_kt3[i]])

    P.add("dve", lambda e: e.tensor_scalar(out=qgk[:], in0=qT[0:64, :, NPT:T], scalar1=smc("g_k")[0:64, :], scalar2=None, op0=MUL),
          reads=[b_qT, b_sm], writes=[B["qgk"]])
    for h in range(N_HEADS):
        for kt in range(2):
            ps, b_ps = tbank()
            P.add("pe", lambda e, ps=ps, h=h, kt=kt: e.matmul(ps[:, 0:NST], lhsT=wukT[:, h, 128 * kt:128 * kt + 128], rhs=qgk[:, h, :],
                                                              start=True, stop=True), reads=[B["wukT"], B["qgk"]], writes=[b_ps])
            P.add("act", lambda e, ps=ps, h=h, kt=kt: e.activation(out=Qabs[:, kt, :, 4 * h:4 * h + 4],
                                                                   in_=ps[:, 0:NST].rearrange("p (q t) -> p q t", t=4), func=AF.Copy),
                  reads=[b_ps], writes=[B["Qabs"]])
        P.add("pool", lambda e, h=h: e.tensor_copy(out=Qrope[64:96, :, 4 * h:4 * h + 4],
                                                   in_=qT[64:96, h, NPT:T].rearrange("p (q t) -> p q t", t=4)),
              reads=[b_qT], writes=[B["Qrope"]])
    P.add("act", lambda e: e.activation(out=kraw[64:96, :], in_=krK[64:96, NPT:T], func=AF.Copy), reads=[b_krK], writes=[B["kraw"]])
    P.add("dve", lambda e: e.tensor_scalar(out=kg32[64:96, :], in0=krK[64:96, NPT:T], scalar1=smc("g_k")[64:96, :], scalar2=None, op0=MUL),
          reads=[b_krK, b_sm], writes=[B["ktmp"]])
    ps, b_ps = tbank()
    P.add("pe", lambda e, ps=ps: e.matmul(ps[0:96, 0:NST], lhsT=rotm[64:96, 0:96], rhs=kg32[64:96, :], start=True, stop=True,
                                          tile_position=(64, 0)), reads=[B["rotm"], B["ktmp"]], writes=[b_ps])
    P.add("dve", lambda e: e.tensor_tensor(out=kt1[64:96, :], in0=kg32[64:96, :], in1=rc_s[64:96, :], op=MUL), reads=[B["ktmp"], B["rcs"]],
          writes=[B["ktmp"]])
    P.add("dve", lambda e, ps=ps: e.tensor_tensor(out=kt2[64:96, :], in0=ps[64:96, 0:NST], in1=rs_s[64:96, :], op=MUL),
          reads=[b_ps, B["rcs"]], writes=[B["ktmp"]])
    P.add("dve", lambda e: e.tensor_tensor(out=knew[64:96, :], in0=kt1[64:96, :], in1=kt2[64:96, :], op=ADD), reads=[B["ktmp"]],
          writes=[B["knew"]])

    idb = AR.alloc([128, 128], BF16)
    b_idb = P.buf("idb")
    P.add("pool", lambda e: e.tensor_copy(out=idb[:], in_=sm[:, SL["ident"][0]:SL["ident"][0] + 128]), reads=[b_sm], writes=[b_idb])
    sqk2 = [sqk, AR.alloc([32, 512], BF16)]
    lnr2 = [lnr, AR.alloc([32, 512], F32)]
    rr2 = [rr_, AR.alloc([32, 512], F32)]
    sr2 = [sr, AR.alloc([32, 512], F32)]
    Pm2 = [Pm, AR.alloc([32, 512], BF16)]
    Lacc2 = [Lacc, AR.alloc([32, 20], F32)]
    Bq = [{n: P.buf(f"s2_{n}{i}") for n in ("sqk", "lnr", "rr", "sr", "Pm")} for i in range(2)]
    b_Lacc2 = [P.buf("Lacc0"), P.buf("Lacc1")]
    cnt = [0]
    gcnt = [0]

    def tbank2():
        i = 2 + rrb[0] % 6
        rrb[0] += 1
        return banks[i], bank_bufs[i]

    def chunk(q, col, npos, cT, b_cTs, kraw_ap, b_kraw, krop_ap, b_krop, crows, first, mask):
        n = npos
        ACC, b_ACC = banks[q % 2], bank_bufs[q % 2]
        Lq, b_Lq = Lacc2[q % 2], b_Lacc2[q % 2]
        ci = gcnt[0] % 2
        gcnt[0] += 1
        sqk_, lnr_, rr__, sr_, Pm_ = sqk2[ci], lnr2[ci], rr2[ci], sr2[ci], Pm2[ci]
        Bc = Bq[ci]
        pss_l = []
        for m in range(4):
            ps, b_ps = tbank2()
            for kt in range(2):
                P.add("pe", lambda e, ps=ps, m=m, kt=kt: e.matmul(ps[:, 0:n], lhsT=wk_sb[:, kt, 128 * m:128 * m + 128], rhs=cT[:, kt, 0:n],
                                                                  start=(kt == 0), stop=(kt == 1)), reads=[B["wk"]] + b_cTs, writes=[b_ps])
            pss_l.append((ps, b_ps))
        for m in range(4):
            ps, b_ps = pss_l[m]
            if m < 2:
                P.add("act", lambda e, ps=ps, m=m: e.activation(out=kn_sb[m][:, 0:n], in_=ps[:, 0:n], func=AF.Copy), reads=[b_ps],
                      writes=[b_kn[m]])
            else:
                P.add("dve", lambda e, ps=ps, m=m: e.tensor_copy(out=kn_sb[m][:, 0:n], in_=ps[:, 0:n]), reads=[b_ps], writes=[b_kn[m]])
            P.add("dve", lambda e, m=m: e.tensor_tensor(out=sq_sb[m][:, 0:n], in0=kn_sb[m][:, 0:n], in1=kn_sb[m][:, 0:n], op=MUL),
                  reads=[b_kn[m]], writes=[b_sq[m]])
        P.add("pool", lambda e: e.tensor_tensor(out=sqk_[:, 0:n], in0=kraw_ap, in1=kraw_ap, op=MUL), reads=b_kraw, writes=[Bc["sqk"]])
        yield
        pss, b_pss = tbank2()
        for m in range(4):
            P.add("pe", lambda e, m=m: e.matmul(pss[0:32, 0:n], lhsT=hselm[:, m, :], rhs=sq_sb[m][:, 0:n], start=(m == 0), stop=False),
                  reads=[B["hselm"], b_sq[m]], writes=[b_pss])
        P.add("pe", lambda e: e.matmul(pss[0:32, 0:n], lhsT=ones_bf[0:32, 0:32], rhs=sqk_[:, 0:n], start=False, stop=True),
              reads=[b_ones, Bc["sqk"]], writes=[b_pss])
        psc, b_psc = tbank2()
        for kt in range(2):
            P.add("pe", lambda e, kt=kt: e.matmul(psc[0:32, 0:n], lhsT=Qabs[:, kt, q, :], rhs=cT[:, kt, 0:n], start=(kt == 0), stop=False),
                  reads=[B["Qabs"]] + b_cTs, writes=[b_psc])
        P.add("pe", lambda e: e.matmul(psc[0:32, 0:n], lhsT=Qrope[64:96, q, :], rhs=krop_ap, start=False, stop=True, tile_position=(64, 0)),
              reads=[B["Qrope"]] + b_krop, writes=[b_psc])
        P.add("act", lambda e: e.activation(out=lnr_[:, 0:n], in_=pss[0:32, 0:n], func=AF.Ln, scale=1.0 / QK_HEAD, bias=EPS),
              reads=[b_pss], writes=[Bc["lnr"]])
        P.add("act", lambda e: e.activation(out=rr__[:, 0:n], in_=lnr_[:, 0:n], func=AF.Exp, scale=-0.5), reads=[Bc["lnr"]],
              writes=[Bc["rr"]])
        P.add("dve", lambda e: e.tensor_tensor(out=sr_[:, 0:n], in0=psc[0:32, 0:n], in1=rr__[:, 0:n], op=MUL), reads=[b_psc, Bc["rr"]],
              writes=[Bc["sr"]])
        if mask:
            P.add("act", lambda e: e.activation(out=sr_[:, 0:n], in_=sr_[:, 0:n], func=AF.Exp, scale=SCALE), reads=[Bc["sr"]],
                  writes=[Bc["sr"]])
            P.add("dve", lambda e: e.tensor_tensor(out=sr_[:, 0:n], in0=sr_[:, 0:n], in1=cmask[:, 0:n], op=MUL), reads=[Bc["sr"], B["cmask"]],
                  writes=[Bc["sr"]])
            P.add("dve", lambda e: e.tensor_copy(out=Pm_[:, 0:n], in_=sr_[:, 0:n]), reads=[Bc["sr"]], writes=[Bc["Pm"]])
            P.add("dve", lambda e: e.reduce_sum(out=Lq[:, col:col + 1], in_=sr_[:, 0:n], axis=AX.X), reads=[Bc["sr"]], writes=[b_Lq])
        else:
            P.add("act", lambda e: e.activation(out=Pm_[:, 0:n], in_=sr_[:, 0:n], func=AF.Exp, scale=SCALE, accum_out=Lq[:, col:col + 1]),
                  reads=[Bc["sr"]], writes=[Bc["Pm"], b_Lq])
        yield
        pT_, b_pT = tbank2()
        pTb = pT_[:].bitcast(BF16)
        nblk = len(crows)
        for bi, (cap, b_cap, c0, c1) in enumerate(crows):
            P.add("pe", lambda e, bi=bi, c0=c0, c1=c1: e.transpose(pTb[0:c1 - c0, 32 * bi:32 * bi + 32], Pm_[:, c0:c1], idb[0:32, 0:32]),
                  reads=[Bc["Pm"], b_idb], writes=[b_pT])
        pi_ = cnt[0] % 2
        cnt[0] += 1
        rows = crows[0][3] - crows[0][2]
        P.add("act", lambda e, pi_=pi_: e.activation(out=PT_sb[pi_][0:rows, 0:nblk, :],
                                                     in_=pTb[0:rows, 0:32 * nblk].rearrange("p (b c) -> p b c", c=32), func=AF.Copy),
              reads=[b_pT], writes=[b_PT[pi_]])
        yield
        for bi, (cap, b_cap, c0, c1) in enumerate(crows):
            P.add("pe", lambda e, bi=bi, cap=cap, c0=c0, c1=c1, pi_=pi_, st=(first and bi == 0): e.matmul(
                ACC[0:32, 0:KV_LORA], lhsT=PT_sb[pi_][0:c1 - c0, bi, :], rhs=cap, start=st, stop=False),
                reads=[b_PT[pi_]] + b_cap, writes=[b_ACC])

    pgc = [0]
    kcn = [0]

    def page_chunk(q, g):
        bi_ = pgc[0] % NB_PG
        ki = pgc[0] % 2
        ci_ = pgc[0] % 2
        pgc[0] += 1
        pbt, b_pbt = pb[bi_], b_pb[bi_]
        ch = q * 16 + g
        P.add("pool", lambda e: e.indirect_dma_start(
            out=pbt[:].rearrange("p a c -> p (a c)"), out_offset=None, in_=cache,
            in_offset=bass.IndirectOffsetOnAxis(ap=idx[:, ch:ch + 1], axis=0)),
            reads=[B["idx"]], writes=[b_pbt], dma=True, semkey=f"pb{bi_}")
        yield
        yield
        yield
        yield
        ki = kcn[0] % 2
        kcn[0] += 1
        k3, b_k3 = kt3[ki], b_kt3[ki]
        kr1, kr2 = pbt[:, :, KV_LORA:KV_LORA + 16], pbt[:, :, KV_LORA + 16:KV_LORA + 32]
        pgs = slice(4 * g, 4 * g + 4)
        pl = lambda fn, rd, wr: P.add("pool", fn, reads=rd, writes=wr)
        pl(lambda e: e.tensor_copy(out=k3[:, :, 0:32], in_=pbt[:, :, KV_LORA:CW]), [b_pbt], [b_k3])
        pl(lambda e: e.tensor_tensor(out=rt[0][:], in0=kr1, in1=tabs[:, 0, pgs, :], op=MUL), [b_pbt, B["tabs"]], [b_rt])
        pl(lambda e: e.tensor_tensor(out=rt[1][:], in0=kr2, in1=tabs[:, 2, pgs, :], op=MUL), [b_pbt, B["tabs"]], [b_rt])
        pl(lambda e: e.tensor_tensor(out=k3[:, :, 64:80], in0=rt[0][:], in1=rt[1][:], op=ADD), [b_rt], [b_k3])
        pl(lambda e: e.tensor_tensor(out=rt[2][:], in0=kr2, in1=tabs[:, 1, pgs, :], op=MUL), [b_pbt, B["tabs"]], [b_rt])
        pl(lambda e: e.tensor_tensor(out=rt[3][:], in0=kr1, in1=tabs[:, 3, pgs, :], op=MUL), [b_pbt, B["tabs"]], [b_rt])
        pl(lambda e: e.tensor_tensor(out=k3[:, :, 80:96], in0=rt[2][:], in1=rt[3][:], op=ADD), [b_rt], [b_k3])
        yield
        psT, b_psT = tbank2()
        psTb = psT[:].bitcast(BF16).rearrange("p (k n) -> p k n", k=2)
        psK, b_psK = tbank2()
        psKb = psK[:].bitcast(BF16)
        for pg in range(4):
            for kt in range(2):
                P.add("pe", lambda e, pg=pg, kt=kt: e.transpose(psTb[:, kt, 128 * pg:128 * pg + 128], pbt[:, pg, 128 * kt:128 * kt + 128],
                                                                idb[:, :]), reads=[b_pbt, b_idb], writes=[b_psT])
            P.add("pe", lambda e, pg=pg: e.transpose(psKb[0:96, 128 * pg:128 * pg + 128], k3[:, pg, :], idb[:, :]), reads=[b_k3, b_idb],
                  writes=[b_psK])
        P.add("dve", lambda e: e.tensor_copy(out=cT_sb[ci_][:], in_=psTb), reads=[b_psT], writes=[b_cT[ci_]])
        P.add("act", lambda e: e.activation(out=krT_sb[ci_][:], in_=psKb[0:96, 0:512], func=AF.Copy), reads=[b_psK], writes=[b_krT[ci_]])
        yield
        crows = [(pbt[:, pg, 0:KV_LORA], [b_pbt], 128 * pg, 128 * pg + 128) for pg in range(4)]
        yield from chunk(q, g, 512, cT_sb[ci_], [b_cT[ci_]], krT_sb[ci_][0:32, 0:512], [b_krT[ci_]], krT_sb[ci_][64:96, 0:512],
                         [b_krT[ci_]], crows, g == 0, False)

    def run_pipelined(gens, step=2):
        active = []
        it = iter(gens)
        more = True
        while more or active:
            if more:
                try:
                    active.append(next(it))
                except StopIteration:
                    more = False
            for _ in range(step):
                for gg in list(active):
                    try:
                        next(gg)
                    except StopIteration:
                        active.remove(gg)

    for q in range(SEQ_PER_CORE):
        run_pipelined([page_chunk(q, g) for g in range(NPAGES // 4)])
        ACC, b_ACC = banks[q % 2], bank_bufs[q % 2]
        Lq, b_Lq = Lacc2[q % 2], b_Lacc2[q % 2]
        c0 = NPT + 4 * q
        psn, b_psn = tbank2()
        psnb = psn[:].bitcast(BF16)
        for kt in range(2):
            P.add("pe", lambda e, kt=kt, c0=c0, psnb=psnb: e.transpose(psnb[0:4, 128 * kt:128 * kt + 128], ckvn[:, kt, c0:c0 + 4], idb[:, :]),
                  reads=[b_ckvn, b_idb], writes=[b_psn])
        P.add("act", lambda e, psnb=psnb: e.activation(out=cnew[:], in_=psnb[0:4, 0:KV_LORA], func=AF.Copy), reads=[b_psn], writes=[B["cnew"]])
        for _ in chunk(q, 16, 4, ckvn[:, :, c0:c0 + 4], [b_ckvn], kraw[64:96, 4 * q:4 * q + 4], [B["kraw"]], knew[64:96, 4 * q:4 * q + 4],
                       [B["knew"]], [(cnew[:, :], [B["cnew"]], 0, 4)], False, True):
            pass
        P.add("act", lambda e, ACC=ACC: e.activation(out=accs[:], in_=ACC[0:32, 0:KV_LORA], func=AF.Copy), reads=[b_ACC], writes=[B["accs"]])
        P.add("dve", lambda e, Lq=Lq: e.reduce_sum(out=lsum[:], in_=Lq[:, 0:17], axis=AX.X), reads=[b_Lq], writes=[B["lsum"]])
        P.add("dve", lambda e: e.reciprocal(out=lsum[:], in_=lsum[:]), reads=[B["lsum"]], writes=[B["lsum"]])
        P.add("dve", lambda e: e.tensor_scalar(out=olat[:], in0=accs[:], scalar1=lsum[:, 0:1], scalar2=None, op0=MUL),
              reads=[B["accs"], B["lsum"]], writes=[B["olat"]])
        pso, b_pso = tbank2()
        psob = pso[:].bitcast(BF16)
        for kt in range(2):
            P.add("pe", lambda e, kt=kt, psob=psob: e.transpose(psob[:, 32 * kt:32 * kt + 32], olat[:, 128 * kt:128 * kt + 128],
                                                                idb[0:32, 0:32]), reads=[B["olat"], b_idb], writes=[b_pso])
        P.add("act", lambda e, q=q, psob=psob: e.activation(out=olT[:, :, q, :], in_=psob[:, 0:64].rearrange("p (k c) -> p k c", k=2),
                                                            func=AF.Copy), reads=[b_pso], writes=[B["olT"]])
    for hp in range(4):
        ps, b_ps = tbank()
        for hh in range(2):
            h = 2 * hp + hh
            for kt in range(2):
                P.add("pe", lambda e, ps=ps, hh=hh, h=h, kt=kt: e.matmul(
                    ps[64 * hh:64 * hh + 64, 0:NST], lhsT=wv_sb[:, kt, 64 * h:64 * h + 64], rhs=olT[:, kt, :, 4 * h:4 * h + 4],
                    start=(kt == 0), stop=(kt == 1), tile_position=(0, 64 * hh)), reads=[B["wv"], B["olT"]], writes=[b_ps])
        P.add("act", lambda e, ps=ps, hp=hp: e.activation(out=attT[:, hp, NPT:T], in_=ps[:, 0:NST], func=AF.Copy), reads=[b_ps],
              writes=[b_attT])


def build(stage=99, n_pool=10240, dbg=False):
    nc = bass.Bass("TRN2", target_bir_lowering=False)
    P = Prog(nc)
    ins_, outs_ = {}, {}

    def din(name, shape, dt=F32):
        ins_[name] = nc.dram_tensor(name, list(shape), dt, kind="ExternalInput").ap()
        return ins_[name]

    def dout(name, shape, dt=F32):
        outs_[name] = nc.dram_tensor(name, list(shape), dt, kind="ExternalOutput").ap()
        return outs_[name]

    def dint(name, shape, dt):
        return nc.dram_tensor(name, list(shape), dt).ap()

    xT = din("xT", [D, T])
    small = din("small", [128, SL["_n"]])
    rope_c = din("rope_c", [96, T])
    rope_s = din("rope_s", [96, T])
    w_in = din("w_in", [D, IN_COLS])
    o_ckvT = dout("o_ckvT", [KV_LORA, T])
    o_krT = dout("o_krT", [QK_ROPE, T])

    w_in_b = dint("w_in_b", [D, IN_COLS], BF16)

    stores = []
    pool_q = "pool"

    def cast_w(dst, src, rows, cols, key):
        a = 1
        while cols // a > 2048 or cols % a:
            a += 1
        s2 = src.rearrange("k (a m) -> (k a) m", a=a) if a > 1 else src
        d2 = dst.rearrange("k (a m) -> (k a) m", a=a) if a > 1 else dst
        b = P.buf(key)
        P.add(pool_q, lambda e: e.dma_start(out=d2, in_=s2), writes=[b], dma=True, semkey=key)
        return b

    b_w_in_b = cast_w(w_in_b, w_in, D, IN_COLS, "c_w_in")
    w_glu = din("w_glu", [SSM_W, 2 * SSM_W])
    w_glu_b = dint("w_glu_b", [SSM_W, 2 * SSM_W], BF16)
    b_w_glu_b = cast_w(w_glu_b, w_glu, SSM_W, 2 * SSM_W, "c_w_glu")

    ones_bf = P.sbuf("ones_bf", [128, 128], BF16)
    b_ones = P.buf("ones")
    P.add("pool", lambda e: e.memset(ones_bf[:], 1.0), writes=[b_ones])
    sm = P.sbuf("sm", [128, SL["_n"]], F32)
    b_sm = P.buf("sm")
    P.add("sp", lambda e: e.dma_start(out=sm[:], in_=small), writes=[b_sm], dma=True, semkey="sm")

    def smc(name, i=0):
        o = SL[name][0] + i
        return sm[:, o:o + 1]

    NB = 8
    banks = [P.psum(f"ps{i}", [128, 512], F32) for i in range(NB)]
    bank_bufs = [P.buf(f"ps{i}") for i in range(NB)]
    bank_rr = [0]

    def next_bank():
        i = bank_rr[0] % NB
        bank_rr[0] += 1
        return banks[i], bank_bufs[i]

    cqn = P.sbuf("cqn", [128, 3, T], BF16)
    ckvn = P.sbuf("ckvn", [128, 2, T], BF16)
    krK = P.sbuf("krK", [96, T], F32)
    uT = P.sbuf("uT", [128, 4, T], BF16)
    b_cqn, b_ckvn, b_krT, b_uT = P.buf("cqn"), P.buf("ckvn"), P.buf("krT"), P.buf("uT")

    chunks = _token_chunks()
    AR = Arena(P, "arena", 142 * 1024)

    NA = OFF_GA
    wA = AR.alloc([128, KD, NA], BF16)
    b_wA = P.buf("wA")
    P.add("sp", lambda e: e.dma_start(out=wA[:], in_=w_in_b.rearrange("(k p) m -> p k m", p=128)[:, :, 0:NA]),
          reads=[b_w_in_b], writes=[b_wA], dma=True, semkey="wA")

    xc = [AR.alloc([128, KD, 512], F32) for i in range(2)]
    b_xc = [P.buf(f"xc{i}") for i in range(2)]
    sq = AR.alloc([128, KD, 512], BF16)
    b_sq = P.buf("sq")
    lnv = AR.alloc([128, 512], F32)
    b_lnv = P.buf("lnv")
    rstd = AR.alloc([128, 512], F32)
    b_rstd = P.buf("rstd")
    xn = AR.alloc([128, KD, 512], BF16)
    b_xn = P.buf("xn")
    cqf = AR.alloc([128, 3, 512], F32)
    b_cqf = P.buf("cqf")
    ckvf = AR.alloc([128, 2, 512], F32)
    b_ckvf = P.buf("ckvf")
    ckvo = AR.alloc([128, 2, 512], F32)
    b_ckvo = P.buf("ckvo")

    xT_v = xT.rearrange("(k p) t -> p k t", p=128)

    def rms_rstd(src, b_src, nk, n, nfeat, dst=rstd, b_dst=b_rstd):
        P.add("dve", lambda e: e.tensor_tensor(out=sq[:, 0:nk, 0:n], in0=src[:, 0:nk, 0:n], in1=src[:, 0:nk, 0:n],
                                               op=ALU.mult), reads=[b_src], writes=[b_sq])
        ps, b_ps = next_bank()
        for k in range(nk):
            P.add("pe", lambda e, k=k: e.matmul(ps[:, 0:n], lhsT=ones_bf[:], rhs=sq[:, k, 0:n], start=(k == 0),
                                                stop=(k == nk - 1)), reads=[b_sq, b_ones], writes=[b_ps])
        P.add("act", lambda e: e.activation(out=lnv[:, 0:n], in_=ps[:, 0:n], func=AF.Ln, scale=1.0 / nfeat, bias=EPS),
              reads=[b_ps], writes=[b_lnv])
        P.add("act", lambda e: e.activation(out=dst[:, 0:n], in_=lnv[:, 0:n], func=AF.Exp, scale=-0.5),
              reads=[b_lnv], writes=[b_dst])

    for ci, (n0, n) in enumerate(chunks):
        xb, b_x = xc[ci % 2], b_xc[ci % 2]
        P.add("sp", lambda e, xb=xb, n0=n0, n=n: e.dma_start(out=xb[:, :, 0:n], in_=xT_v[:, :, n0:n0 + n]),
              writes=[b_x], dma=True, semkey=f"xc{ci % 2}")
        rms_rstd(xb, b_x, KD, n, D)
        for k in range(KD):
            P.add("dve", lambda e, k=k, xb=xb, n=n: e.scalar_tensor_tensor(
                out=xn[:, k, 0:n], in0=xb[:, k, 0:n], scalar=smc("g_mix", k), in1=rstd[:, 0:n],
                op0=ALU.mult, op1=ALU.mult), reads=[b_x, b_rstd, b_sm], writes=[b_xn])
        groups = [("cq", i, OFF_CKV * 0 + 128 * i, 128) for i in range(3)] + \
                 [("ckv", i, OFF_CKV + 128 * i, 128) for i in range(2)] + \
                 [("kr", 0, OFF_KR, 32)] + [("u", i, OFF_U + 128 * i, 128) for i in range(4)]
        for kind, i, c0, m in groups:
            ps, b_ps = next_bank()
            for k in range(KD):
                if kind == "kr":
                    P.add("pe", lambda e, k=k, c0=c0, m=m, ps=ps, n=n: e.matmul(
                        ps[64:96, 0:n], lhsT=wA[:, k, c0:c0 + m], rhs=xn[:, k, 0:n], start=(k == 0), stop=(k == KD - 1),
                        tile_position=(0, 64)), reads=[b_wA, b_xn], writes=[b_ps])
                    continue
                P.add("pe", lambda e, k=k, c0=c0, m=m, ps=ps, n=n: e.matmul(
                    ps[0:m, 0:n], lhsT=wA[:, k, c0:c0 + m], rhs=xn[:, k, 0:n], start=(k == 0), stop=(k == KD - 1)),
                    reads=[b_wA, b_xn], writes=[b_ps])
            if kind == "cq":
                P.add("act", lambda e, i=i, ps=ps, n=n: e.activation(out=cqf[:, i, 0:n], in_=ps[:, 0:n], func=AF.Copy),
                      reads=[b_ps], writes=[b_cqf])
            elif kind == "ckv":
                P.add("act", lambda e, i=i, ps=ps, n=n: e.activation(out=ckvf[:, i, 0:n], in_=ps[:, 0:n], func=AF.Copy),
                      reads=[b_ps], writes=[b_ckvf])
            elif kind == "kr":
                P.add("act", lambda e, ps=ps, n=n, n0=n0: e.activation(out=krK[64:96, n0:n0 + n], in_=ps[64:96, 0:n], func=AF.Copy),
                      reads=[b_ps], writes=[b_krT])
            else:
                P.add("act", lambda e, i=i, ps=ps, n=n, n0=n0: e.activation(out=uT[:, i, n0:n0 + n], in_=ps[:, 0:n], func=AF.Copy),
                      reads=[b_ps], writes=[b_uT])
        rms_rstd(cqf, b_cqf, 3, n, Q_LORA)
        for i in range(3):
            P.add("dve", lambda e, i=i, n=n, n0=n0: e.scalar_tensor_tensor(
                out=cqn[:, i, n0:n0 + n], in0=cqf[:, i, 0:n], scalar=smc("g_cq", i), in1=rstd[:, 0:n],
                op0=ALU.mult, op1=ALU.mult), reads=[b_cqf, b_rstd, b_sm], writes=[b_cqn])
        rms_rstd(ckvf, b_ckvf, 2, n, KV_LORA)
        for i in range(2):
            P.add("dve", lambda e, i=i, n=n: e.scalar_tensor_tensor(
                out=ckvo[:, i, 0:n], in0=ckvf[:, i, 0:n], scalar=smc("g_ckv", i), in1=rstd[:, 0:n],
                op0=ALU.mult, op1=ALU.mult), reads=[b_ckvf, b_rstd, b_sm], writes=[b_ckvo])
        P.add("pool", lambda e, n=n, n0=n0: e.tensor_copy(out=ckvn[:, :, n0:n0 + n], in_=ckvo[:, :, 0:n]),
              reads=[b_ckvo], writes=[b_ckvn])
        stores.append(P.add("sp", lambda e, n=n, n0=n0: e.dma_start(
            out=o_ckvT.rearrange("(k p) t -> p k t", p=128)[:, :, n0:n0 + n], in_=ckvo[:, :, 0:n]),
            reads=[b_ckvo], dma=True, semkey="st_ckv"))
    stores.append(P.add("sp", lambda e: e.dma_start(out=o_krT, in_=krK[64:96, :]), reads=[b_krT], dma=True, semkey="st_kr"))


    if stage >= 2:
        ssm = build_ssm(P, AR, nc, din, dout, dint, stores, next_bank, uT, b_uT, smc, b_sm, sm, w_glu_b, b_w_glu_b, chunks)
        if dbg:
            o_dbg_ys = dout("o_dbg_ys", [128, 4, T], BF16)
            stores.append(P.add("sp", lambda e: e.dma_start(out=o_dbg_ys, in_=uT[:]), reads=[b_uT], dma=True, semkey="dbg_ys"))


    if stage >= 3:
        attT = P.sbuf("attT", [128, 4, T], BF16)
        b_attT = P.buf("attT")
        kS = P.sbuf("kS", [96, 8, NST], BF16)
        b_kS = P.buf("kS")
        att = build_attn(P, AR, nc, din, dout, dint, stores, banks, bank_bufs, cast_w, cqn, b_cqn, ckvn, b_ckvn, krK, b_krT,
                         sm, b_sm, smc, ones_bf, b_ones, rope_c, rope_s, attT, b_attT, chunks, kS, b_kS)
        if stage >= 5:
            build_sample_attn(P, AR, nc, din, dout, dint, stores, banks, bank_bufs, sm, b_sm, smc, ones_bf, b_ones, ckvn, b_ckvn,
                              krK, b_krT, att["qT"], att["b_qT"], attT, b_attT, n_pool, att["w_uk_b"], att["b_wkb"], att["w_uv_b"],
                              att["b_wvb"], rope_c, rope_s, att["rotm_d"], att["mC1"])
        if dbg:
            o_dbg_att = dout("o_dbg_att", [128, 4, T], BF16)
            stores.append(P.add("sp", lambda e: e.dma_start(out=o_dbg_att, in_=attT[:]), reads=[b_attT], dma=True, semkey="dbg_att"))


    if stage >= 4:
        build_tail(P, AR, nc, din, dout, dint, stores, banks, bank_bufs, cast_w, xT, w_in_b, b_w_in_b, sm, b_sm, smc, ones_bf, b_ones,
                   attT, b_attT, ssm["ysT"], ssm["b_ys"], chunks)

    P.add("sp", lambda e: None, after=stores)
    P.finalize()
    return nc, ins_, outs_, P


def _rope_tables(pos):
    inv_freq = np.power(np.float32(10000.0), -np.arange(0, QK_ROPE, 2, dtype=np.float32) / np.float32(QK_ROPE)).astype(np.float32)
    ang = pos.astype(np.float32)[:, None] * inv_freq[None, :]
    return np.cos(ang).astype(np.float32), np.sin(ang).astype(np.float32)


def _prep_core(c, inp):
    b, j = c // 4, c % 4
    m = {}
    xp = inp["x_prompt"][b, NPT * j:NPT * (j + 1)]
    xs = inp["x_sample"][SEQ_PER_CORE * c:SEQ_PER_CORE * (c + 1)].reshape(NST, D)
    m["xT"] = np.ascontiguousarray(np.concatenate([xp, xs], 0).T)
    pos = np.concatenate([NPT * j + np.arange(NPT), np.tile(PAST + np.arange(4), SEQ_PER_CORE)])
    cs, sn = _rope_tables(pos)
    rc = np.ones((96, T), np.float32)
    rs = np.zeros((96, T), np.float32)
    rc[64:80] = cs.T
    rc[80:96] = cs.T
    rs[64:80] = sn.T
    rs[80:96] = sn.T
    m["rope_c"], m["rope_s"] = rc, rs
    sm = np.zeros((128, SL["_n"]), np.float32)

    def put(name, arr):
        o, n = SL[name]
        sm[:arr.shape[0], o:o + arr.shape[1]] = arr
    put("g_mix", inp["g_mix"][0].reshape(8, 128).T)
    put("g_cq", inp["g_cq"][0].reshape(3, 128).T)
    put("g_ckv", inp["g_ckv"][0].reshape(2, 128).T)
    put("g_ffn", inp["g_ffn"][0].reshape(8, 128).T)
    put("g_ple", inp["g_ple"][0].reshape(8, 128).T)
    cw = inp["conv_w"][0].reshape(3, 44, 128)
    put("conv_w", cw.transpose(2, 0, 1).reshape(128, 132))
    put("conv_b", inp["conv_b"][0].reshape(44, 128).T)
    put("g_q", inp["g_q"][0].reshape(96, 1))
    put("g_k", inp["g_k"][0].reshape(96, 1))
    put("d_skip", inp["d_skip"][0].reshape(4, 128).T)
    put("vis", np.tile((np.arange(4) <= j).astype(np.float32)[None], (128, 1)))
    put("full", np.tile((np.arange(4) < j).astype(np.float32)[None], (128, 1)))
    put("ident", np.eye(128, dtype=np.float32))
    put("hsel", np.tile((np.arange(4) == j - 1).astype(np.float32)[None], (128, 1)))
    m["small"] = sm
    m["w_in"] = inp["w_in"][0]
    m["w_glu"] = inp["w_glu"][0]
    m["w_uq"] = inp["w_uq"][0].reshape(Q_LORA, 768)
    m["w_oa"], m["w_os"], m["w_out"] = inp["w_oa"][0], inp["w_os"][0], inp["w_out"][0]
    m["w_up"], m["w_down"] = inp["w_up"][0], inp["w_down"][0]
    m["w_pg"], m["w_pp"] = inp["w_ple_gate"][0], inp["w_ple_proj"][0]
    pp_ = inp["p_prompt"][0, b, NPT * j:NPT * (j + 1)]
    ps_ = inp["p_sample"][0, SEQ_PER_CORE * c:SEQ_PER_CORE * (c + 1)].reshape(NST, PLE)
    m["pT"] = np.ascontiguousarray(np.concatenate([pp_, ps_], 0).T)
    sc = inp["state_conv"][0, SEQ_PER_CORE * c:SEQ_PER_CORE * (c + 1)]
    m["scT"] = np.ascontiguousarray(sc.reshape(16, 2, 44, 128).transpose(3, 2, 0, 1))
    m["w_uk"] = inp["w_uk"][0].reshape(KV_LORA, 512)
    m["w_uv"] = inp["w_uv"][0].reshape(KV_LORA, 512)
    tri = (np.arange(128)[:, None] <= np.arange(128)[None, :]).astype(np.float32)
    md = np.zeros((128, 4, 128), np.float32)
    for r in range(4):
        vis, full = float(r <= j), float(r < j)
        md[:, r, :] = full + (vis - full) * tri
    m["maskd"] = md
    pt_ = inp["page_table"][SEQ_PER_CORE * c:SEQ_PER_CORE * (c + 1)].astype(np.int32).reshape(16, 16, 4)
    m["ptab"] = np.ascontiguousarray(np.repeat(pt_.transpose(2, 0, 1).reshape(4, 256), 32, axis=0))
    m["p32c"] = (np.arange(128, dtype=np.int32) % 32).reshape(128, 1)
    m["w_ukT"] = np.ascontiguousarray(inp["w_uk"][0].transpose(2, 1, 0).reshape(64, 8 * KV_LORA))
    cs_p, sn_p = _rope_tables(np.arange(PAST))
    rp = np.stack([cs_p, sn_p], 0).reshape(2, 16, 4, 32, 4, 16)
    m["ropeP"] = np.ascontiguousarray(rp.transpose(2, 3, 0, 1, 4, 5).reshape(128, 2, NPAGES, 16))
    m["gk_rep"] = np.tile(inp["g_k"][0, 64:96][None], (128, 1)).astype(np.float32)
    hs_ = np.zeros((128, 4, 32), np.float32)
    for mm in range(4):
        for hh in range(2):
            hs_[64 * hh:64 * hh + 64, mm, 4 * (2 * mm + hh):4 * (2 * mm + hh) + 4] = 1.0
    m["hselm"] = hs_
    cm = np.zeros((32, 4), np.float32)
    for h_ in range(8):
        for t_ in range(4):
            cm[4 * h_ + t_, :t_ + 1] = 1.0
    m["cmask"] = cm
    rot = np.zeros((96, 96), np.float32)
    for i in range(16):
        rot[80 + i, 64 + i] = -1.0
        rot[64 + i, 80 + i] = 1.0
    m["rotm"] = rot
    m["ssm_s"], m["ssm_r"] = _ssm_packs(c, inp)
    return m


def _ssm_packs(c, inp):
    j = c % 4
    a_re, a_im, logdt = inp["a_re"][0], inp["a_im"][0], inp["log_dt"][0]
    b_re, b_im, c_re, c_im = inp["b_re"][0], inp["b_im"][0], inp["c_re"][0], inp["c_im"][0]

    def st(a):
        return a.reshape(16, 2, 64).transpose(1, 2, 0).reshape(128, 16)
    ss = np.zeros((128, SSL["_n"]), np.float32)

    def put(lay, arr_, name, arr):
        o, n = lay[name]
        arr_[:, o:o + n] = arr.reshape(128, n)
    put(SSL, ss, "a_re", st(a_re))
    put(SSL, ss, "a_im", st(a_im))
    put(SSL, ss, "logdt", st(np.repeat(logdt[:, None], 64, 1)))
    for nm, cc in (("c_re", c_re), ("c_im", c_im)):
        c4 = cc.reshape(16, 2, 16, 64)
        pad = np.zeros((2, 64, 16, 2, 16), np.float32)
        for g2 in range(2):
            pad[g2, :, :, g2, :] = c4[:, g2].transpose(2, 0, 1)
        put(SSL, ss, nm, pad)
    for nm, bb in (("b_re", b_re), ("b_im", b_im)):
        b4 = bb.reshape(16, 2, 64, 16)
        pad = np.zeros((2, 64, 16, 2, 16), np.float32)
        for g2 in range(2):
            pad[g2, :, :, g2, :] = b4[:, g2].transpose(1, 0, 2)
        put(SSL, ss, nm, pad)
    for nm, key in (("h0_re", "state_ssm_re"), ("h0_im", "state_ssm_im")):
        h = inp[key][0, SEQ_PER_CORE * c:SEQ_PER_CORE * (c + 1)]
        h4 = h.reshape(16, 16, 2, 64)
        put(SSL, ss, nm, h4.transpose(2, 3, 1, 0))
    m = np.zeros((128, 12), np.float32)
    for i in range(4):
        n = j - 1 - i
        if 0 <= n <= 2:
            m[:, 3 * i + n] = 1.0
    put(SSL, ss, "msk", m)
    blk = np.zeros((128, 4), np.float32)
    for k4 in range(4):
        blk[32 * k4:32 * k4 + 32, k4] = 1.0
    put(SSL, ss, "blk", blk)
    sr = np.zeros((128, SRL["_n"]), np.float32)

    def rowrep(a):
        a4 = a.reshape(4, 4, 2, 64)
        out = np.zeros((4, 2, 16, 4, 64), np.float32)
        out[:] = a4.transpose(1, 2, 0, 3)[:, :, None, :, :]
        return out
    put(SRL, sr, "a_re", rowrep(a_re))
    put(SRL, sr, "a_im", rowrep(a_im))
    put(SRL, sr, "logdt", rowrep(np.repeat(logdt[:, None], 64, 1)))
    for nm, bb in (("b_re", b_re), ("b_im", b_im)):
        b5 = bb.reshape(4, 4, 2, 64, 16)
        pad = np.zeros((4, 2, 16, 4, 2, 64), np.float32)
        for g2 in range(2):
            pad[:, g2, :, :, g2, :] = b5[:, :, g2].transpose(1, 3, 0, 2)
        put(SRL, sr, nm, pad)
    put(SRL, sr, "ident", np.eye(128, dtype=np.float32))
    return ss, sr


_CACHE = {}


def kernel(**inputs):
    inp = {k: np.asarray(v) for k, v in inputs.items()}
    if "nc" not in _CACHE:
        _CACHE["nc"] = build()
    nc, ins_, outs_, P = _CACHE["nc"]
    in_maps = []
    cache = None
    if "cache" in ins_:
        cache = np.concatenate([inp["cache_ckv"][0], inp["cache_kr"][0]], axis=-1).reshape(-1, 4 * (KV_LORA + QK_ROPE))
    for c in range(8):
        m = _prep_core(c, inp)
        if cache is not None:
            m["cache"] = cache
        in_maps.append({k: np.ascontiguousarray(m[k]) for k in ins_})
    res = run_bass_kernel_spmd(nc, in_maps, core_ids=list(range(8)))
    R = res.results
    f32 = np.float32
    ckv_p = np.zeros((1, 2, 8192, KV_LORA), f32)
    kr_p = np.zeros((1, 2, 8192, QK_ROPE), f32)
    ckv_s = np.zeros((1, 128, 4, KV_LORA), f32)
    kr_s = np.zeros((1, 128, 4, QK_ROPE), f32)
    for c in range(8):
        b, j = c // 4, c % 4
        ck = R[c]["o_ckvT"].T
        kr = R[c]["o_krT"].T
        ckv_p[0, b, NPT * j:NPT * (j + 1)] = ck[:NPT]
        kr_p[0, b, NPT * j:NPT * (j + 1)] = kr[:NPT]
        ckv_s[0, 16 * c:16 * c + 16] = ck[NPT:].reshape(16, 4, KV_LORA)
        kr_s[0, 16 * c:16 * c + 16] = kr[NPT:].reshape(16, 4, QK_ROPE)
    yp = np.zeros((2, 8192, D), f32)
    ys = np.zeros((128, 4, D), f32)
    cv_p = np.zeros((1, 2, 2, 2 * D_FF), f32)
    cv_s = np.zeros((1, 128, 2, 2 * D_FF), f32)
    if "o_yT" in R[0]:
        for c in range(8):
            b, j = c // 4, c % 4
            y = R[c]["o_yT"].T
            yp[b, NPT * j:NPT * (j + 1)] = y[:NPT]
            ys[16 * c:16 * c + 16] = y[NPT:].reshape(16, 4, D)
            cs_ = R[c]["o_cvs"]
            cv_s[0, 16 * c:16 * c + 16] = cs_.transpose(2, 3, 1, 0).reshape(16, 2, 2 * D_FF)
            if j == 3:
                cv_p[0, b] = R[c]["o_cvp"].transpose(2, 1, 0).reshape(2, 2 * D_FF)
    z = lambda *s: np.zeros(s, f32)
    sre_p, sim_p, sre_s, sim_s = z(1, 2, 32, 64), z(1, 2, 32, 64), z(1, 128, 32, 64), z(1, 128, 32, 64)
    if "o_hp" in R[0]:
        for c in range(8):
            b, j = c // 4, c % 4
            hs_ = R[c]["o_hs"].reshape(2, 64, 2, 16, 16)
            hs_ = hs_.transpose(2, 4, 3, 0, 1).reshape(2, 16, 32, 64)
            sre_s[0, 16 * c:16 * c + 16] = hs_[0]
            sim_s[0, 16 * c:16 * c + 16] = hs_[1]
            if j == 3:
                hp_ = R[c]["o_hp"].reshape(2, 64, 2, 16).transpose(2, 3, 0, 1).reshape(2, 32, 64)
                sre_p[0, b] = hp_[0]
                sim_p[0, b] = hp_[1]
    return (yp, ys, ckv_p, kr_p, ckv_s, kr_s, sre_p, sim_p, sre_s, sim_s, cv_p, cv_s)
```

```python
import contextlib
import numpy as np
import concourse.bass as bass
import concourse.mybir as mybir
from concourse.bass_utils import run_bass_kernel_spmd

F32 = mybir.dt.float32
BF16 = mybir.dt.bfloat16
I32 = mybir.dt.int32
AF = mybir.ActivationFunctionType
ALU = mybir.AluOpType
AX = mybir.AxisListType

D = 1024
NPT = 2048
NST = 64
T = NPT + NST
KD = D // 128
N_HEADS = 8
QK_NOPE, QK_ROPE, QK_HEAD, V_HEAD = 64, 32, 96, 64
Q_LORA, KV_LORA = 384, 256
SSM_W, GROUP, N_GROUPS, STATE = 512, 16, 32, 64
D_FF = 2816
PLE = 256
EPS = 1e-6
OFF_CKV = Q_LORA
OFF_KR = OFF_CKV + KV_LORA
OFF_U = OFF_KR + QK_ROPE
OFF_GA = OFF_U + SSM_W
OFF_GS = OFF_GA + D
IN_COLS = OFF_GS + D
SCALE = QK_HEAD ** -0.5
PAST = 8192
PAGE = 128
NPAGES = 64
SEQ_PER_CORE = 16


class Buf:
    __slots__ = ("name", "last_w", "readers")

    def __init__(self, name, fence=()):
        self.name = name
        self.last_w = None
        self.readers = list(fence)


class Op:
    __slots__ = ("eng", "fn", "deps", "dma", "sem", "value", "marked", "idx")

    def __init__(self, eng, fn, dma):
        self.eng = eng
        self.fn = fn
        self.dma = dma
        self.deps = ()
        self.sem = None
        self.value = 0
        self.marked = False
        self.idx = 0


class Prog:
    ENGS = ("pe", "act", "dve", "pool", "sp")

    def __init__(self, nc):
        self.nc = nc
        self.ops = {e: [] for e in self.ENGS}
        self.stack = contextlib.ExitStack()
        self.dma_sems = {}
        self.nbuf = 0
        self.fence = []
        self.live = []

    def sbuf(self, name, shape, dtype):
        return self.stack.enter_context(self.nc.sbuf_tensor(name, list(shape), dtype))

    def psum(self, name, shape, dtype=F32):
        return self.stack.enter_context(self.nc.psum_tensor(name, list(shape), dtype))

    def sem(self, name):
        return self.stack.enter_context(self.nc.semaphore(name))

    def buf(self, name=None):
        self.nbuf += 1
        b = Buf(name or f"b{self.nbuf}", self.fence)
        self.live.append(b)
        return b

    def new_phase(self):
        f = []
        for b in self.live:
            if b.last_w is not None:
                f.append(b.last_w)
            f.extend(b.readers)
        self.fence = list(dict.fromkeys(f))[-64:] if False else list(dict.fromkeys(f))
        self.live = []

    def add(self, eng, fn, reads=(), writes=(), dma=False, semkey=None, after=()):
        op = Op(eng, fn, dma)
        deps = set(after)
        for b in reads:
            if b.last_w is not None:
                deps.add(b.last_w)
        for b in writes:
            if b.last_w is not None:
                deps.add(b.last_w)
            deps.update(b.readers)
        op.deps = tuple(deps)
        for b in reads:
            b.readers.append(op)
        for b in writes:
            b.last_w = op
            b.readers = []
        op.idx = len(self.ops[eng])
        self.ops[eng].append(op)
        if dma:
            key = semkey if semkey is not None else id(op)
            if key not in self.dma_sems:
                self.dma_sems[key] = [self.sem(f"dq{len(self.dma_sems)}"), 0]
            ent = self.dma_sems[key]
            ent[1] += (1 if dma == "cc" else 16)
            op.sem = ent[0]
            op.value = ent[1]
            op.marked = True
        return op

    def finalize(self):
        nc = self.nc
        esem = {e: self.sem(f"eng_{e}") for e in self.ENGS}
        for e in self.ENGS:
            for op in self.ops[e]:
                for d in op.deps:
                    if d.dma:
                        continue
                    if d.eng != e:
                        d.marked = True
                    elif e != "pe" and (op.idx - d.idx) <= 2:
                        d.marked = True
        for e in self.ENGS:
            c = 0
            for op in self.ops[e]:
                if op.dma:
                    continue
                if op.marked:
                    c += 1
                    op.sem = esem[e]
                    op.value = c
        self.stats = {e: len(self.ops[e]) for e in self.ENGS}

        def emit(ename, eng):
            waited = {}
            for op in self.ops[ename]:
                need = {}
                for d in op.deps:
                    if not d.marked:
                        continue
                    if (not d.dma) and d.eng == ename and (ename == "pe" or (op.idx - d.idx) > 2):
                        continue
                    k = id(d.sem)
                    if waited.get(k, 0) >= d.value:
                        continue
                    if k not in need or need[k][1] < d.value:
                        need[k] = (d.sem, d.value)
                for k, (s, v) in need.items():
                    eng.wait_ge(s, v)
                    waited[k] = v
                ins = op.fn(eng)
                if op.marked and ins is not None:
                    if op.dma == "cc":
                        ins.then_inc(op.sem, 1)
                    elif op.dma:
                        ins.then_inc(op.sem, 16)
                    else:
                        ins.then_inc(op.sem, 1)

        with nc.Block() as block:
            @block.tensor
            def _(e):
                emit("pe", e)

            @block.scalar
            def _(e):
                emit("act", e)

            @block.vector
            def _(e):
                emit("dve", e)

            @block.gpsimd
            def _(e):
                emit("pool", e)

            @block.sync
            def _(e):
                emit("sp", e)
        self.stack.close()


class Arena:
    def __init__(self, P, name, nbytes):
        self.t = P.sbuf(name, [128, nbytes // 4], F32)
        self.off = 0
        self.cap = nbytes
        self.peak = 0

    def alloc(self, shape, dtype=F32):
        esz = 2 if dtype == BF16 else 4
        n = int(np.prod(shape[1:]))
        nb = (n * esz + 31) // 32 * 32
        assert self.off + nb <= self.cap, ("arena overflow", self.off, nb, self.cap)
        v = self.t[0:shape[0], self.off // 4:(self.off + nb) // 4]
        self.off += nb
        self.peak = max(self.peak, self.off)
        if dtype != F32:
            v = v.bitcast(dtype)
        v = v[:, 0:n]
        if len(shape) > 2:
            names = "abcde"[:len(shape) - 1]
            pat = "p (" + " ".join(names) + ") -> p " + " ".join(names)
            v = v.rearrange(pat, **{c: int(d) for c, d in zip(names[1:], shape[2:])})
        return v

    def mark(self):
        return self.off

    def release(self, m):
        self.off = m


def _small_layout():
    lay = {}
    off = 0

    def put(name, n):
        nonlocal off
        lay[name] = (off, n)
        off += n
    put("g_mix", 8)
    put("g_cq", 3)
    put("g_ckv", 2)
    put("g_ffn", 8)
    put("g_ple", 8)
    put("conv_w", 3 * 44)
    put("conv_b", 44)
    put("g_q", 1)
    put("g_k", 1)
    put("d_skip", 4)
    put("vis", 4)
    put("full", 4)
    put("ident", 128)
    put("hsel", 4)
    lay["_n"] = off
    return lay


SL = _small_layout()


def _token_chunks():
    return [(0, 510), (510, 512), (1022, 512), (1534, 290), (1824, 288)]


def _pack_layout(items):
    lay, off = {}, 0
    for name, n in items:
        lay[name] = (off, n)
        off += n
    lay["_n"] = off
    return lay


SSL = _pack_layout([("a_re", 16), ("a_im", 16), ("logdt", 16), ("c_re", 512), ("c_im", 512), ("b_re", 512),
                    ("b_im", 512), ("h0_re", 256), ("h0_im", 256), ("msk", 12), ("blk", 4)])
SRL = _pack_layout([("a_re", 256), ("a_im", 256), ("logdt", 256), ("b_re", 512), ("b_im", 512), ("ident", 128)])
PWS = [1, 2, 3, 4, 8, 12, 16]
PWR = [1, 2, 3]
TWO_PI = 6.283185307179586


def build_ssm(P, AR, nc, din, dout, dint, stores, next_bank, uT, b_uT, smc, b_sm, sm, w_glu_b, b_w_glu_b, chunks):
    LOOP_ENG = "pool"
    ss_d = din("ssm_s", [128, SSL["_n"]])
    sr_d = din("ssm_r", [128, SRL["_n"]])
    o_hp = dout("o_hp", [128, 2, 16])
    o_hs = dout("o_hs", [128, 2, 16, 16])
    cc_e_in = dint("cc_e_in", [128, 32], F32)
    cc_e_out = dint("cc_e_out", [512, 32], F32)

    AR.release(0)
    P.new_phase()
    sst = AR.alloc([128, SSL["_n"]], F32)
    Wc = AR.alloc([128, 16, 4, 2, 32], BF16)
    Ktab = AR.alloc([128, 4, 4, 128], BF16)
    A4w = AR.alloc([128, 4, 4, 2, 128], BF16)
    nlim = AR.alloc([128, len(PWS), 16], F32)
    LS_pre = True
    b_sst, b_srt = P.buf("sst"), P.buf("srt")
    P.add("sp", lambda e: e.dma_start(out=sst[:], in_=ss_d), writes=[b_sst], dma=True, semkey="sst")

    def S(name, a=None):
        o, n = SSL[name]
        v = sst[:, o:o + n]
        return v if a is None else v.rearrange("p (a b) -> p a b", a=a)

    def R(name, a=None):
        o, n = SRL[name]
        v = srt[:, o:o + n]
        return v if a is None else v.rearrange("p (a b) -> p a b", a=a)

    def tt(eng, out, a, b, op, rd, wr):
        return P.add(eng, lambda e: e.tensor_tensor(out=out, in0=a, in1=b, op=op), reads=rd, writes=wr)

    def tss(eng, out, a, scalar, op, rd, wr):
        return P.add(eng, lambda e: e.tensor_single_scalar(out=out, in_=a, scalar=scalar, op=op), reads=rd, writes=wr)

    def stt(eng, out, a, scalar, b, op0, op1, rd, wr):
        return P.add("dve", lambda e: e.scalar_tensor_tensor(out=out, in0=a, scalar=scalar, in1=b, op0=op0, op1=op1),
                     reads=rd, writes=wr)

    def act(out, in_, func, rd, wr, scale=1.0, bias=0.0):
        return P.add("act", lambda e: e.activation(out=out, in_=in_, func=func, scale=scale, bias=bias), reads=rd, writes=wr)

    def cp(eng, out, in_, rd, wr):
        return P.add(eng, lambda e: e.tensor_copy(out=out, in_=in_), reads=rd, writes=wr)

    MUL, ADD, SUB = ALU.mult, ALU.add, ALU.subtract

    def lam_pow(pfx, a_re, a_im, logdt, Fd, powers, b_src, eng):
        npw = len(powers)
        bt = P.buf(pfx + "_t")
        mk = lambda nm, sh, dt_=F32: AR.alloc(sh, dt_)
        dt = mk("dt", [128, Fd]); dre = mk("dre", [128, Fd]); dim = mk("dim", [128, Fd])
        ang = mk("ang", [128, npw, Fd]); angc = mk("angc", [128, npw, Fd]); mag = mk("mag", [128, npw, Fd])
        ki = mk("ki", [128, npw, Fd], I32); kf = mk("kf", [128, npw, Fd])
        sn = mk("sn", [128, npw, Fd]); cs = mk("cs", [128, npw, Fd])
        lre = mk("lre", [128, npw, Fd]); lim = mk("lim", [128, npw, Fd])
        act(dt[:], logdt, AF.Exp, [b_src], [bt])
        tt(eng, dre[:], dt[:], a_re, MUL, [bt, b_src], [bt])
        tt(eng, dim[:], dt[:], a_im, MUL, [bt, b_src], [bt])
        for i, n in enumerate(powers):
            tss(eng, ang[:, i, :], dim[:], n / TWO_PI, MUL, [bt], [bt])
            act(mag[:, i, :], dre[:], AF.Exp, [bt], [bt], scale=float(n))
        tss(eng, angc[:], ang[:], 0.25, ADD, [bt], [bt])
        for src, dst in ((ang, sn), (angc, cs)):
            cp("dve", ki[:], src[:], [bt], [bt])
            cp("dve", kf[:], ki[:], [bt], [bt])
            tt(eng, kf[:], src[:], kf[:], SUB, [bt], [bt])
            act(dst[:], kf[:], AF.Sin, [bt], [bt], scale=6.28318)
        tt(eng, lre[:], mag[:], cs[:], MUL, [bt], [bt])
        tt(eng, lim[:], mag[:], sn[:], MUL, [bt], [bt])
        den = mk("den", [128, Fd]); t1 = mk("t1", [128, Fd]); t2 = mk("t2", [128, Fd]); nr = mk("nr", [128, Fd])
        fre = mk("fre", [128, Fd]); fim = mk("fim", [128, Fd])
        tt(eng, den[:], a_re, a_re, MUL, [b_src], [bt])
        tt(eng, t1[:], a_im, a_im, MUL, [b_src], [bt])
        tt(eng, den[:], den[:], t1[:], ADD, [bt], [bt])
        P.add("dve", lambda e: e.reciprocal(out=den[:], in_=den[:]), reads=[bt], writes=[bt])
        tss(eng, nr[:], lre[:, 0, :], -1.0, ADD, [bt], [bt])
        tt(eng, t1[:], nr[:], a_re, MUL, [bt, b_src], [bt])
        tt(eng, t2[:], lim[:, 0, :], a_im, MUL, [bt, b_src], [bt])
        tt(eng, t1[:], t1[:], t2[:], ADD, [bt], [bt])
        tt(eng, fre[:], t1[:], den[:], MUL, [bt], [bt])
        tt(eng, t1[:], lim[:, 0, :], a_re, MUL, [bt, b_src], [bt])
        tt(eng, t2[:], nr[:], a_im, MUL, [bt, b_src], [bt])
        tt(eng, t1[:], t1[:], t2[:], SUB, [bt], [bt])
        tt(eng, fim[:], t1[:], den[:], MUL, [bt], [bt])
        return dict(lre=lre, lim=lim, fre=fre, fim=fim, b=bt)

    def cmul(eng, ore, oim, are, aim, bre, bim, t1, t2, rd, wr):
        tt(eng, t1, are, bre, MUL, rd, wr)
        tt(eng, t2, aim, bim, MUL, rd, wr)
        tt(eng, ore, t1, t2, SUB, rd, wr)
        tt(eng, t1, are, bim, MUL, rd, wr)
        tt(eng, t2, aim, bre, MUL, rd, wr)
        tt(eng, oim, t1, t2, ADD, rd, wr)

    ENG = "dve"
    LS = lam_pow("ls", S("a_re"), S("a_im"), S("logdt"), 16, PWS, b_sst, ENG)
    mB0 = AR.mark()
    srt = AR.alloc([128, SRL["_n"]], F32)
    P.add("sp", lambda e: e.dma_start(out=srt[:], in_=sr_d), writes=[b_srt], dma=True, semkey="srt")
    LR = lam_pow("lr", R("a_re"), R("a_im"), R("logdt"), 256, PWR, b_srt, ENG)
    bS, bR = LS["b"], LR["b"]
    pi = {n: i for i, n in enumerate(PWS)}

    def bc(ap2, n):
        return ap2.unsqueeze(2).broadcast_to([128, 16, n])

    bbs_re = AR.alloc([128, 16, 32], F32); bbs_im = AR.alloc([128, 16, 32], F32)
    x_re = AR.alloc([128, 16, 32], F32); x_im = AR.alloc([128, 16, 32], F32)
    u1 = AR.alloc([128, 16, 32], F32); u2 = AR.alloc([128, 16, 32], F32)
    negc_im = AR.alloc([128, 16, 32], F32)
    cmul(ENG, bbs_re[:], bbs_im[:], bc(LS["fre"][:], 32), bc(LS["fim"][:], 32), S("b_re", 16), S("b_im", 16),
         u1[:], u2[:], [bS, b_sst], [bS])
    tss(ENG, negc_im[:], S("c_im", 16), -1.0, MUL, [b_sst], [bS])

    b_Wc = P.buf("Wc")
    Wc_v = Wc[:].rearrange("p (kk k4) s c n -> p kk k4 s c n", k4=4)
    u1_v = u1[:].rearrange("p (kk k4) n -> p kk k4 n", k4=4)
    u2_v = u2[:].rearrange("p (kk k4) n -> p kk k4 n", k4=4)
    for s in range(4):
        lr_, li_ = LS["lre"][:, pi[s + 1], :], LS["lim"][:, pi[s + 1], :]
        tt(ENG, u1[:], S("c_re", 16), bc(lr_, 32), MUL, [b_sst, bS], [bS])
        tt(ENG, u2[:], S("c_im", 16), bc(li_, 32), MUL, [b_sst, bS], [bS])
        for k4 in range(4):
            tt(ENG, Wc_v[:, :, k4, s, 0, :], u1_v[:, :, k4, :], u2_v[:, :, k4, :], SUB, [bS], [bS, b_Wc])
        tt(ENG, u1[:], S("c_re", 16), bc(li_, 32), MUL, [b_sst, bS], [bS])
        tt(ENG, u2[:], S("c_im", 16), bc(lr_, 32), MUL, [b_sst, bS], [bS])
        for k4 in range(4):
            stt(ENG, Wc_v[:, :, k4, s, 1, :], u1_v[:, :, k4, :], -1.0, u2_v[:, :, k4, :], MUL, SUB,
                [bS], [bS, b_Wc])

    b_Kt = P.buf("Ktab")
    Kc = AR.alloc([128, 4, 32], F32)
    Kf = AR.alloc([128, 4, 128], F32)
    b_Kc, b_Kf = P.buf("Kc"), P.buf("Kf")
    xs_re = AR.alloc([128, 4, 16, 32], F32); xs_im = AR.alloc([128, 4, 16, 32], F32)
    b_xs = P.buf("xs")
    cp(ENG, xs_re[:, 0], bbs_re[:], [bS], [b_xs])
    cp(ENG, xs_im[:, 0], bbs_im[:], [bS], [b_xs])
    for tau in range(1, 4):
        cmul(ENG, xs_re[:, tau], xs_im[:, tau], bc(LS["lre"][:, pi[tau], :], 32), bc(LS["lim"][:, pi[tau], :], 32),
             bbs_re[:], bbs_im[:], u1[:], u2[:], [bS], [bS, b_xs])
    for kk in range(4):
        ps, b_ps = next_bank()
        for k4 in range(4):
            k = 4 * kk + k4
            for tau in range(4):
                o = ps[32 * k4:32 * k4 + 32, tau * 32:tau * 32 + 32]
                P.add("pe", lambda e, o=o, k=k, tau=tau, k4=k4: e.matmul(
                    o, lhsT=xs_re[:, tau, k, :], rhs=S("c_re", 16)[:, k, :], start=True, stop=False,
                    tile_position=(0, 32 * k4)), reads=[b_xs, b_sst], writes=[b_ps])
                P.add("pe", lambda e, o=o, k=k, tau=tau, k4=k4: e.matmul(
                    o, lhsT=xs_im[:, tau, k, :], rhs=negc_im[:, k, :], start=False, stop=True,
                    tile_position=(0, 32 * k4)), reads=[b_xs, bS], writes=[b_ps])
        cp("dve", Kc[:], ps[:, 0:128].rearrange("p (t n) -> p t n", t=4), [b_ps], [b_Kc])
        for k4 in range(4):
            tss("dve", Kf[:, :, 32 * k4:32 * k4 + 32], Kc[:], S("blk")[:, k4:k4 + 1], MUL, [b_Kc, b_sst], [b_Kf])
        stt("dve", Kf[:, 0, :], R("ident"), smc("d_skip", kk), Kf[:, 0, :], MUL, ADD, [b_srt, b_sm, b_Kf], [b_Kf])
        cp("dve", Ktab[:, kk], Kf[:], [b_Kf], [b_Kt])

    b_A4 = P.buf("A4w")
    bbr_re = AR.alloc([128, 4, 2, 64], F32); bbr_im = AR.alloc([128, 4, 2, 64], F32)
    r1 = AR.alloc([128, 4, 2, 64], F32); r2 = AR.alloc([128, 4, 2, 64], F32)
    bcr = lambda t: t.rearrange("p (kk q) -> p kk q", kk=4).unsqueeze(2).broadcast_to([128, 4, 2, 64])
    v4 = lambda t: t.rearrange("p (kk g q) -> p kk g q", kk=4, g=2)
    cmul(ENG, bbr_re[:], bbr_im[:], bcr(LR["fre"][:]), bcr(LR["fim"][:]), v4(R("b_re")), v4(R("b_im")), r1[:], r2[:],
         [bR, b_srt], [bR])
    a4v = lambda s_, c_: A4w[:, :, s_, c_, :].rearrange("p kk (g q) -> p kk g q", g=2)
    cp(ENG, a4v(3, 0), bbr_re[:], [bR], [b_A4])
    cp(ENG, a4v(3, 1), bbr_im[:], [bR], [b_A4])
    pir = {n: i for i, n in enumerate(PWR)}
    for s in range(3):
        n = 3 - s
        cmul(ENG, a4v(s, 0), a4v(s, 1), bcr(LR["lre"][:, pir[n], :]), bcr(LR["lim"][:, pir[n], :]),
             bbr_re[:], bbr_im[:], r1[:], r2[:], [bR], [bR, b_A4])

    tss(ENG, nlim[:], LS["lim"][:], -1.0, MUL, [bS], [bS])
    Lre = lambda n, k: LS["lre"][:, pi[n], k:k + 1]
    Lim = lambda n, k: LS["lim"][:, pi[n], k:k + 1]
    nLim = lambda n, k: nlim[:, pi[n], k:k + 1]

    AR.release(mB0)
    P.new_phase()
    uP = [uT[:, kk, 0:NPT].rearrange("p (c e) -> p e c", e=16) for kk in range(4)]
    uS = [uT[:, kk, NPT:T].rearrange("p (q s) -> p s q", s=4) for kk in range(4)]

    S16 = [AR.alloc([128, 16, 128], F32) for c in range(2)]
    b_S16 = P.buf("S16")
    H16 = [AR.alloc([128, 16, 129], F32) for c in range(2)]
    b_H = [P.buf("H16re"), P.buf("H16im")]
    pr = [AR.alloc([128, 2, 128], F32) for i in range(2)]
    b_pr = [P.buf("pr0"), P.buf("pr1")]

    def s4_matmuls(k, n_c, usrc, nj):
        kk, k4 = divmod(k, 4)
        out = []
        for comp in range(2):
            ps, b_ps = next_bank()
            for j in range(nj):
                for s in range(4):
                    rhs = usrc[kk][32 * k4:32 * k4 + 32, 4 * j + s, :]
                    P.add("pe", lambda e, ps=ps, j=j, s=s, rhs=rhs, comp=comp, kk=kk, k4=k4: e.matmul(
                        ps[:, j * n_c:(j + 1) * n_c], lhsT=A4w[32 * k4:32 * k4 + 32, kk, s, comp, :], rhs=rhs,
                        start=(s == 0), stop=(s == 3), tile_position=(32 * k4, 0)),
                        reads=[b_A4, b_uT], writes=[b_ps])
            out.append((ps, b_ps))
        return out

    def prefix_step(k, j, src, b_src, dst, b_dst, Sre, Sim, b_sre, b_sim, n_c, o_re=None, o_im=None, b_o=()):
        ore = dst[:, 0, 0:n_c] if o_re is None else o_re
        oim = dst[:, 1, 0:n_c] if o_im is None else o_im
        wr = [b_dst] + list(b_o)
        stt("dve", dst[:, 0, 0:n_c], src[:, 0, 0:n_c], Lre(4, k), Sre, MUL, ADD, [b_src, bS, b_sre], [b_dst])
        stt("dve", ore, src[:, 1, 0:n_c], nLim(4, k), dst[:, 0, 0:n_c], MUL, ADD, [b_src, bS, b_dst], wr)
        stt("dve", dst[:, 1, 0:n_c], src[:, 0, 0:n_c], Lim(4, k), Sim, MUL, ADD, [b_src, bS, b_sim], [b_dst])
        stt("dve", oim, src[:, 1, 0:n_c], Lre(4, k), dst[:, 1, 0:n_c], MUL, ADD, [b_src, bS, b_dst], wr)

    for k in range(16):
        (pre, b_pre), (pim, b_pim) = s4_matmuls(k, 128, uP, 4)
        act(pr[0][:, 0, :], pre[:, 0:128], AF.Copy, [b_pre], [b_pr[0]])
        act(pr[0][:, 1, :], pim[:, 0:128], AF.Copy, [b_pim], [b_pr[0]])
        cur = 0
        for j in range(1, 4):
            last = (j == 3)
            prefix_step(k, j, pr[cur], b_pr[cur], pr[1 - cur], b_pr[1 - cur], pre[:, j * 128:(j + 1) * 128],
                        pim[:, j * 128:(j + 1) * 128], b_pre, b_pim, 128,
                        o_re=S16[0][:, k, :] if last else None, o_im=S16[1][:, k, :] if last else None,
                        b_o=[b_S16] if last else ())
            cur = 1 - cur

    lt = [AR.alloc([128, 16], F32) for i in range(6)]
    b_lt = [P.buf(f"lt{i}") for i in range(6)]
    L16re, L16im = LS["lre"][:, pi[16], :], LS["lim"][:, pi[16], :]

    def run_loop():
        for c in range(128):
            hre, him = H16[0][:, :, c], H16[1][:, :, c]
            tt(LOOP_ENG, lt[0][:], L16re, hre, MUL, [bS, b_H[0]], [b_lt[0]])
            tt(LOOP_ENG, lt[1][:], L16im, him, MUL, [bS, b_H[1]], [b_lt[1]])
            tt(LOOP_ENG, lt[3][:], L16re, him, MUL, [bS, b_H[1]], [b_lt[3]])
            tt(LOOP_ENG, lt[4][:], L16im, hre, MUL, [bS, b_H[0]], [b_lt[4]])
            tt(LOOP_ENG, lt[2][:], lt[0][:], lt[1][:], SUB, [b_lt[0], b_lt[1]], [b_lt[2]])
            tt(LOOP_ENG, lt[5][:], lt[3][:], lt[4][:], ADD, [b_lt[3], b_lt[4]], [b_lt[5]])
            tt(LOOP_ENG, H16[0][:, :, c + 1], lt[2][:], S16[0][:, :, c], ADD, [b_lt[2], b_S16], [b_H[0]])
            tt(LOOP_ENG, H16[1][:, :, c + 1], lt[5][:], S16[1][:, :, c], ADD, [b_lt[5], b_S16], [b_H[1]])
            yield

    P.add(LOOP_ENG, lambda e: e.memset(H16[0][:, :, 0], 0.0), writes=[b_H[0]])
    P.add(LOOP_ENG, lambda e: e.memset(H16[1][:, :, 0], 0.0), writes=[b_H[1]])
    for _ in run_loop():
        pass
    Eloc = AR.alloc([128, 2, 16], F32)
    b_E = P.buf("Eloc")
    cp(LOOP_ENG, Eloc[:, 0, :], H16[0][:, :, 128], [b_H[0]], [b_E])
    cp(LOOP_ENG, Eloc[:, 1, :], H16[1][:, :, 128], [b_H[1]], [b_E])
    b_cci, b_cco = P.buf("cc_e_in"), P.buf("cc_e_out")
    P.add("sp", lambda e: e.dma_start(out=cc_e_in, in_=Eloc[:].rearrange("p a b -> p (a b)")), reads=[b_E], writes=[b_cci],
          dma=True, semkey="cce1")
    P.add("pool", lambda e: e.collective_compute("AllGather", ALU.bypass, replica_groups=[[0, 1, 2, 3], [4, 5, 6, 7]],
                                                 ins=[cc_e_in.opt()], outs=[cc_e_out.opt()]),
          reads=[b_cci], writes=[b_cco], dma="cc", semkey="cce2")
    Eg = AR.alloc([128, 4, 2, 16], F32)
    b_Eg = P.buf("Eg")
    P.add("sp", lambda e: e.dma_start(out=Eg[:].rearrange("p r a b -> p r (a b)"),
                                      in_=cc_e_out.rearrange("(r p) f -> p r f", p=128)),
          reads=[b_cco], writes=[b_Eg], dma=True, semkey="cce3")
    Lp = AR.alloc([128, 3, 2, 16], F32)
    b_Lp = P.buf("Lp")
    sq_a = AR.alloc([128, 2, 16], F32); sq_b = AR.alloc([128, 2, 16], F32)
    q1 = AR.alloc([128, 16], F32); q2 = AR.alloc([128, 16], F32)
    b_q = P.buf("sqtmp")
    CE = "dve"
    cp(CE, sq_a[:, 0, :], L16re, [bS], [b_q])
    cp(CE, sq_a[:, 1, :], L16im, [bS], [b_q])
    src_, dst_ = sq_a, sq_b
    PW = [AR.alloc([128, 16, 129], F32) for _ in range(2)]
    pwt = [AR.alloc([128, 16, 64], F32) for _ in range(2)]
    b_PW, b_pwt = P.buf("PW"), P.buf("pwt")
    P.add(CE, lambda e: e.memset(PW[0][:, :, 0:1], 1.0), writes=[b_PW])
    P.add(CE, lambda e: e.memset(PW[1][:, :, 0:1], 0.0), writes=[b_PW])
    for it in range(7):
        n_ = 1 << it
        cmul(CE, PW[0][:, :, n_:2 * n_], PW[1][:, :, n_:2 * n_], PW[0][:, :, 0:n_], PW[1][:, :, 0:n_],
             bc(src_[:, 0, :], n_), bc(src_[:, 1, :], n_), pwt[0][:, :, 0:n_], pwt[1][:, :, 0:n_],
             [b_PW, b_q], [b_PW, b_pwt])
        o = Lp[:, 0] if it == 6 else dst_
        tt(CE, q1[:], src_[:, 0, :], src_[:, 0, :], MUL, [b_q], [b_q])
        tt(CE, q2[:], src_[:, 1, :], src_[:, 1, :], MUL, [b_q], [b_q])
        tt(CE, o[:, 0, :], q1[:], q2[:], SUB, [b_q], [b_q, b_Lp])
        stt(CE, o[:, 1, :], src_[:, 0, :], 2.0, src_[:, 1, :], MUL, MUL, [b_q], [b_q, b_Lp])
        src_, dst_ = dst_, src_
    cmul(CE, Lp[:, 1, 0, :], Lp[:, 1, 1, :], Lp[:, 0, 0, :], Lp[:, 0, 1, :], Lp[:, 0, 0, :], Lp[:, 0, 1, :], q1[:], q2[:],
         [b_Lp, b_q], [b_Lp, b_q])
    cmul(CE, Lp[:, 2, 0, :], Lp[:, 2, 1, :], Lp[:, 1, 0, :], Lp[:, 1, 1, :], Lp[:, 0, 0, :], Lp[:, 0, 1, :], q1[:], q2[:],
         [b_Lp, b_q], [b_Lp, b_q])
    hin = AR.alloc([128, 2, 16], F32)
    cf = AR.alloc([128, 2, 16], F32)
    b_hin, b_cf = P.buf("hin"), P.buf("cf")
    P.add(CE, lambda e: e.memset(hin[:], 0.0), writes=[b_hin])
    msk = S("msk")
    for i in range(4):
        m0, m1, m2 = (msk[:, 3 * i + n:3 * i + n + 1] for n in range(3))
        tss(CE, cf[:, 0, :], Lp[:, 0, 0, :], m1, MUL, [b_Lp, b_sst], [b_cf])
        stt(CE, cf[:, 0, :], Lp[:, 1, 0, :], m2, cf[:, 0, :], MUL, ADD, [b_Lp, b_sst, b_cf], [b_cf])
        tss(CE, cf[:, 0, :], cf[:, 0, :], m0, ADD, [b_cf, b_sst], [b_cf])
        tss(CE, cf[:, 1, :], Lp[:, 0, 1, :], m1, MUL, [b_Lp, b_sst], [b_cf])
        stt(CE, cf[:, 1, :], Lp[:, 1, 1, :], m2, cf[:, 1, :], MUL, ADD, [b_Lp, b_sst, b_cf], [b_cf])
        ere, eim = Eg[:, i, 0, :], Eg[:, i, 1, :]
        tt(CE, q1[:], cf[:, 0, :], ere, MUL, [b_cf, b_Eg], [b_q])
        tt(CE, hin[:, 0, :], hin[:, 0, :], q1[:], ADD, [b_q, b_hin], [b_hin])
        tt(CE, q1[:], cf[:, 1, :], eim, MUL, [b_cf, b_Eg], [b_q])
        tt(CE, hin[:, 0, :], hin[:, 0, :], q1[:], SUB, [b_q, b_hin], [b_hin])
        tt(CE, q1[:], cf[:, 0, :], eim, MUL, [b_cf, b_Eg], [b_q])
        tt(CE, hin[:, 1, :], hin[:, 1, :], q1[:], ADD, [b_q, b_hin], [b_hin])
        tt(CE, q1[:], cf[:, 1, :], ere, MUL, [b_cf, b_Eg], [b_q])
        tt(CE, hin[:, 1, :], hin[:, 1, :], q1[:], ADD, [b_q, b_hin], [b_hin])
    cp(CE, PW[0][:, :, 128], Lp[:, 0, 0, :], [b_Lp], [b_PW])
    cp(CE, PW[1][:, :, 128], Lp[:, 0, 1, :], [b_Lp], [b_PW])
    fx = S16
    b_fx = [b_S16, P.buf("fx1")]
    hb = lambda comp: hin[:, comp, :].unsqueeze(2).broadcast_to([128, 16, 128])
    PWr, PWi = PW[0][:, :, 1:129], PW[1][:, :, 1:129]
    Hr, Hi = H16[0][:, :, 1:129], H16[1][:, :, 1:129]
    tt(CE, fx[0][:], PWr, hb(0), MUL, [b_PW, b_hin], [b_fx[0]])
    tt(CE, fx[1][:], PWi, hb(1), MUL, [b_PW, b_hin], [b_fx[0], b_fx[1]])
    tt(CE, fx[0][:], fx[0][:], fx[1][:], SUB, [b_fx[0], b_fx[1]], [b_fx[0]])
    tt(CE, Hr, Hr, fx[0][:], ADD, [b_fx[0], b_H[0]], [b_H[0]])
    tt(CE, fx[0][:], PWr, hb(1), MUL, [b_PW, b_hin], [b_fx[0]])
    tt(CE, fx[1][:], PWi, hb(0), MUL, [b_PW, b_hin], [b_fx[0], b_fx[1]])
    tt(CE, fx[0][:], fx[0][:], fx[1][:], ADD, [b_fx[0], b_fx[1]], [b_fx[0]])
    tt(CE, Hi, Hi, fx[0][:], ADD, [b_fx[0], b_H[1]], [b_H[1]])
    cp(CE, H16[0][:, :, 0], hin[:, 0, :], [b_hin], [b_H[0]])
    cp(CE, H16[1][:, :, 0], hin[:, 1, :], [b_hin], [b_H[1]])
    hp = AR.alloc([128, 2, 16], F32)
    b_hp = P.buf("hp")
    cp(LOOP_ENG, hp[:, 0, :], H16[0][:, :, 128], [b_H[0]], [b_hp])
    cp(LOOP_ENG, hp[:, 1, :], H16[1][:, :, 128], [b_H[1]], [b_hp])
    stores.append(P.add("sp", lambda e: e.dma_start(out=o_hp, in_=hp[:]), reads=[b_hp], dma=True, semkey="st_hp"))

    yT = AR.alloc([128, 4, T], BF16)
    b_yT = P.buf("yT")
    H4 = AR.alloc([128, 4, 4, 2, 128], BF16)
    b_H4 = P.buf("H4")
    h0b = AR.alloc([128, 2, 16, 16], BF16)
    b_h0b = P.buf("h0b")
    cp("dve", h0b[:, 0], S("h0_re", 16), [b_sst], [b_h0b])
    cp("dve", h0b[:, 1], S("h0_im", 16), [b_sst], [b_h0b])
    hs = AR.alloc([128, 2, 16, 16], F32)
    b_hs = P.buf("hs")
    htmp = AR.alloc([128, 2, 128], F32)
    b_htmp = P.buf("htmp")
    yP = [yT[:, kk, 0:NPT].rearrange("p (c j s) -> p j s c", j=4, s=4) for kk in range(4)]
    yS = [yT[:, kk, NPT:T].rearrange("p (q s) -> p s q", s=4) for kk in range(4)]

    def out_stage(kk, j, n_c, Hsrc, b_hsrc, usrc, dst):
        ps, b_ps = next_bank()
        for s_lo in range(4):
            o = ps[:, s_lo * n_c:(s_lo + 1) * n_c]
            nmm = 8 + s_lo + 1
            idx = 0
            for k4 in range(4):
                k = 4 * kk + k4
                for comp in range(2):
                    o4 = ps[32 * k4:32 * k4 + 32, s_lo * n_c:(s_lo + 1) * n_c]
                    P.add("pe", lambda e, o4=o4, k=k, s_lo=s_lo, comp=comp, k4=k4, idx=idx, nmm=nmm: e.matmul(
                        o4, lhsT=Wc[:, k, s_lo, comp, :], rhs=Hsrc(k, k4, comp), start=(comp == 0), stop=False,
                        tile_position=(0, 32 * k4)),
                        reads=[b_Wc, b_hsrc], writes=[b_ps])
                    idx += 1
            for tau in range(s_lo + 1):
                rhs = usrc[kk][:, 4 * j + s_lo - tau, :]
                P.add("pe", lambda e, o=o, tau=tau, rhs=rhs, idx=idx, nmm=nmm, kk=kk: e.matmul(
                    o, lhsT=Ktab[:, kk, tau, :], rhs=rhs, start=False, stop=(idx == nmm - 1)),
                    reads=[b_Kt, b_uT], writes=[b_ps])
                idx += 1
        act(dst, ps[:, 0:4 * n_c].rearrange("p (s c) -> p s c", s=4), AF.Gelu_apprx_tanh, [b_ps], [b_yT])

    for kk in range(4):
        for k4 in range(4):
            k = 4 * kk + k4
            (pre, b_pre), (pim, b_pim) = s4_matmuls(k, 128, uP, 3)
            cp("dve", H4[:, k4, 0, 0, :], H16[0][:, k, 0:128], [b_H[0]], [b_H4])
            cp("dve", H4[:, k4, 0, 1, :], H16[1][:, k, 0:128], [b_H[1]], [b_H4])
            act(pr[0][:, 0, :], pre[:, 0:128], AF.Copy, [b_pre], [b_pr[0]])
            act(pr[0][:, 1, :], pim[:, 0:128], AF.Copy, [b_pim], [b_pr[0]])
            cur = 0
            for j in range(1, 4):
                n = 4 * j
                stt("dve", htmp[:, 0, :], H16[0][:, k, 0:128], Lre(n, k), pr[cur][:, 0, :], MUL, ADD,
                    [b_H[0], bS, b_pr[cur]], [b_htmp])
                stt("dve", H4[:, k4, j, 0, :], H16[1][:, k, 0:128], nLim(n, k), htmp[:, 0, :], MUL, ADD,
                    [b_H[1], bS, b_htmp], [b_H4])
                stt("dve", htmp[:, 1, :], H16[0][:, k, 0:128], Lim(n, k), pr[cur][:, 1, :], MUL, ADD,
                    [b_H[0], bS, b_pr[cur]], [b_htmp])
                stt("dve", H4[:, k4, j, 1, :], H16[1][:, k, 0:128], Lre(n, k), htmp[:, 1, :], MUL, ADD,
                    [b_H[1], bS, b_htmp], [b_H4])
                if j < 3:
                    prefix_step(k, j, pr[cur], b_pr[cur], pr[1 - cur], b_pr[1 - cur], pre[:, j * 128:(j + 1) * 128],
                                pim[:, j * 128:(j + 1) * 128], b_pre, b_pim, 128)
                    cur = 1 - cur
            (sre, b_sre), (sim, b_sim) = s4_matmuls(k, 16, uS, 1)
            h0r, h0i = S("h0_re", 16)[:, k, :], S("h0_im", 16)[:, k, :]
            stt("dve", hs[:, 0, k, :], h0r, Lre(4, k), sre[:, 0:16], MUL, ADD, [b_sst, bS, b_sre], [b_hs])
            stt("dve", hs[:, 0, k, :], h0i, nLim(4, k), hs[:, 0, k, :], MUL, ADD, [b_sst, bS, b_hs], [b_hs])
            stt("dve", hs[:, 1, k, :], h0r, Lim(4, k), sim[:, 0:16], MUL, ADD, [b_sst, bS, b_sim], [b_hs])
            stt("dve", hs[:, 1, k, :], h0i, Lre(4, k), hs[:, 1, k, :], MUL, ADD, [b_sst, bS, b_hs], [b_hs])
        for j in range(4):
            out_stage(kk, j, 128, lambda k, k4, comp, j=j: H4[:, k4, j, comp, :], b_H4, uP, yP[kk][:, j])
        out_stage(kk, 0, 16, lambda k, k4, comp: h0b[:, comp, k, :], b_h0b, uS, yS[kk])
    stores.append(P.add("sp", lambda e: e.dma_start(out=o_hs, in_=hs[:]), reads=[b_hs], dma=True, semkey="st_hs"))

    wg = AR.alloc([128, 4, 1024], BF16)
    b_wg = P.buf("wg")
    P.add("sp", lambda e: e.dma_start(out=wg[:], in_=w_glu_b.rearrange("(k p) m -> p k m", p=128)), reads=[b_w_glu_b],
          writes=[b_wg], dma=True, semkey="wg")
    ysT, b_ys = uT, b_uT
    sg = AR.alloc([128, 512], F32)
    b_sg = P.buf("sg")
    for (n0, n) in chunks:
        for m in range(4):
            pv, b_pv = next_bank()
            pg, b_pg = next_bank()
            for k in range(4):
                P.add("pe", lambda e, pv=pv, k=k, m=m, n0=n0, n=n: e.matmul(
                    pv[:, 0:n], lhsT=wg[:, k, 128 * m:128 * m + 128], rhs=yT[:, k, n0:n0 + n], start=(k == 0), stop=(k == 3)),
                    reads=[b_wg, b_yT], writes=[b_pv])
            for k in range(4):
                P.add("pe", lambda e, pg=pg, k=k, m=m, n0=n0, n=n: e.matmul(
                    pg[:, 0:n], lhsT=wg[:, k, 512 + 128 * m:512 + 128 * m + 128], rhs=yT[:, k, n0:n0 + n], start=(k == 0),
                    stop=(k == 3)), reads=[b_wg, b_yT], writes=[b_pg])
            act(sg[:, 0:n], pg[:, 0:n], AF.Sigmoid, [b_pg], [b_sg])
            tt("dve", ysT[:, m, n0:n0 + n], pv[:, 0:n], sg[:, 0:n], MUL, [b_pv, b_sg], [b_ys])
    return dict(ysT=ysT, b_ys=b_ys)


def build_attn(P, AR, nc, din, dout, dint, stores, banks, bank_bufs, cast_w, cqn, b_cqn, ckvn, b_ckvn, krK, b_krK,
               sm, b_sm, smc, ones_bf, b_ones, rope_c, rope_s, attT, b_attT, chunks, kS, b_kS):
    MUL, ADD = ALU.mult, ALU.add
    w_uq = din("w_uq", [Q_LORA, N_HEADS * QK_HEAD])
    w_uk = din("w_uk", [KV_LORA, N_HEADS * QK_NOPE])
    w_uv = din("w_uv", [KV_LORA, N_HEADS * V_HEAD])
    maskd_d = din("maskd", [128, 4, 128])
    rotm_d = din("rotm", [96, 96])
    w_uq_b = dint("w_uq_b", [Q_LORA, N_HEADS * QK_HEAD], BF16)
    w_uk_b = dint("w_uk_b", [KV_LORA, N_HEADS * QK_NOPE], BF16)
    w_uv_b = dint("w_uv_b", [KV_LORA, N_HEADS * V_HEAD], BF16)
    b_wqb = cast_w(w_uq_b, w_uq, Q_LORA, 768, "c_wuq")
    b_wkb = cast_w(w_uk_b, w_uk, KV_LORA, 512, "c_wuk")
    b_wvb = cast_w(w_uv_b, w_uv, KV_LORA, 512, "c_wuv")
    K_own = [dint(f"K_own{h}", [QK_HEAD, NPT], BF16) for h in range(N_HEADS)]
    K_all = [dint(f"K_all{h}", [4 * QK_HEAD, NPT], BF16) for h in range(N_HEADS)]
    V_own = [dint(f"V_own{h}", [128, 16 * 65], BF16) for h in range(N_HEADS)]
    V_all = [dint(f"V_all{h}", [512, 16 * 65], BF16) for h in range(N_HEADS)]
    RG = [[0, 1, 2, 3], [4, 5, 6, 7]]

    AR.release(0)
    P.new_phase()
    qT = AR.alloc([96, 8, T], BF16)
    maskd = AR.alloc([128, 4, 128], BF16)
    sel65 = AR.alloc([65, 64], F32)
    mC1 = AR.mark()
    wq = AR.alloc([128, 3, 768], BF16)
    wk = AR.alloc([128, 2, 8, 96], BF16)
    wv = AR.alloc([128, 2, 512], BF16)
    rc = AR.alloc([96, T], F32)
    rs = AR.alloc([96, T], F32)
    rotm = AR.alloc([96, 96], F32)
    maskf = AR.alloc([128, 4, 128], F32)
    b_wq, b_wk, b_wv, b_rc, b_rs, b_rot, b_mk, b_mkf, b_sel, b_qT = [P.buf(n) for n in
        ("wq", "wk", "wv", "rc", "rs", "rotm", "maskd", "maskf", "sel65", "qT")]
    P.add("sp", lambda e: e.dma_start(out=wq[:], in_=w_uq_b.rearrange("(k p) m -> p k m", p=128)), reads=[b_wqb], writes=[b_wq],
          dma=True, semkey="wq")
    P.add("pool", lambda e: e.memset(wk[:], 0.0), writes=[b_wk])
    for k in range(2):
        P.add("sp", lambda e, k=k: e.dma_start(out=wk[:, k, :, 0:64],
                                               in_=w_uk_b[128 * k:128 * k + 128, :].rearrange("p (h d) -> p h d", h=8)),
              reads=[b_wkb], writes=[b_wk], dma=True, semkey="wk")
    P.add("sp", lambda e: e.dma_start(out=wv[:], in_=w_uv_b.rearrange("(k p) m -> p k m", p=128)), reads=[b_wvb], writes=[b_wv],
          dma=True, semkey="wv")
    P.add("sp", lambda e: e.dma_start(out=rc[:], in_=rope_c), writes=[b_rc], dma=True, semkey="rc")
    P.add("sp", lambda e: e.dma_start(out=rs[:], in_=rope_s), writes=[b_rs], dma=True, semkey="rs")
    P.add("sp", lambda e: e.dma_start(out=rotm[:], in_=rotm_d), writes=[b_rot], dma=True, semkey="rotm")
    P.add("sp", lambda e: e.dma_start(out=maskf[:], in_=maskd_d), writes=[b_mkf], dma=True, semkey="maskf")
    P.add("pool", lambda e: e.tensor_copy(out=maskd[:], in_=maskf[:]), reads=[b_mkf], writes=[b_mk])
    P.add("pool", lambda e: e.memset(sel65[:], 0.0), writes=[b_sel])
    P.add("pool", lambda e: e.memset(sel65[64:65, :], 1.0), writes=[b_sel])
    ident = sm[:, SL["ident"][0]:SL["ident"][0] + 128]

    rr = [0]

    def tbank():
        i = 2 + rr[0] % 6
        rr[0] += 1
        return banks[i], bank_bufs[i]

    NT_ = 4
    tmpl = [dict(raw=AR.alloc([96, 512], F32), sqh=AR.alloc([96, 512], BF16), lnv=AR.alloc([96, 512], F32),
                 rstd=AR.alloc([96, 512], F32), qg=AR.alloc([96, 512], F32), t1=AR.alloc([96, 512], F32),
                 t2=AR.alloc([96, 512], F32)) for _ in range(NT_)]
    tmpb = [{k: P.buf(f"nr_{k}{i}") for k in ("raw", "sqh", "lnv", "rstd", "qg", "t1", "t2")} for i in range(NT_)]
    kst = [AR.alloc([96, 512], BF16) for _ in range(3)]
    b_kst = [P.buf(f"kst{i}") for i in range(3)]
    nrc = [0]

    def normrope(mm_fn, gname, out_ap, b_out, n, n0, after_fn=None):
        ti = nrc[0] % NT_
        nrc[0] += 1
        t_, b_ = tmpl[ti], tmpb[ti]
        raw, sqh, lnv, rstd, qg, t1, t2 = (t_[k] for k in ("raw", "sqh", "lnv", "rstd", "qg", "t1", "t2"))
        ps, b_ps = tbank()
        mm_fn(ps, b_ps)
        P.add("act", lambda e: e.activation(out=raw[:, 0:n], in_=ps[0:96, 0:n], func=AF.Copy), reads=[b_ps], writes=[b_["raw"]])
        P.add("dve", lambda e: e.tensor_tensor(out=sqh[:, 0:n], in0=raw[:, 0:n], in1=raw[:, 0:n], op=MUL), reads=[b_["raw"]],
              writes=[b_["sqh"]])
        yield
        p2, b_p2 = tbank()
        P.add("pe", lambda e: e.matmul(p2[0:96, 0:n], lhsT=ones_bf[0:96, 0:96], rhs=sqh[:, 0:n], start=True, stop=True),
              reads=[b_["sqh"], b_ones], writes=[b_p2])
        P.add("act", lambda e: e.activation(out=lnv[:, 0:n], in_=p2[0:96, 0:n], func=AF.Ln, scale=1.0 / QK_HEAD, bias=EPS),
              reads=[b_p2], writes=[b_["lnv"]])
        P.add("act", lambda e: e.activation(out=rstd[:, 0:n], in_=lnv[:, 0:n], func=AF.Exp, scale=-0.5), reads=[b_["lnv"]],
              writes=[b_["rstd"]])
        yield
        P.add("dve", lambda e: e.scalar_tensor_tensor(out=qg[:, 0:n], in0=raw[:, 0:n], scalar=smc(gname)[0:96, :], in1=rstd[:, 0:n],
                                                      op0=MUL, op1=MUL), reads=[b_["raw"], b_["rstd"], b_sm], writes=[b_["qg"]])
        P.add("dve", lambda e: e.tensor_tensor(out=t1[:, 0:n], in0=qg[:, 0:n], in1=rc[:, n0:n0 + n], op=MUL), reads=[b_["qg"], b_rc],
              writes=[b_["t1"]])
        yield
        p3, b_p3 = tbank()
        P.add("pe", lambda e: e.matmul(p3[0:96, 0:n], lhsT=rotm[:, :], rhs=qg[:, 0:n], start=True, stop=True),
              reads=[b_rot, b_["qg"]], writes=[b_p3])
        P.add("dve", lambda e: e.tensor_tensor(out=t2[:, 0:n], in0=p3[0:96, 0:n], in1=rs[:, n0:n0 + n], op=MUL),
              reads=[b_p3, b_rs], writes=[b_["t2"]])
        P.add("dve", lambda e: e.tensor_tensor(out=out_ap, in0=t1[:, 0:n], in1=t2[:, 0:n], op=ADD), reads=[b_["t1"], b_["t2"]],
              writes=[b_out])
        if after_fn is not None:
            after_fn()

    def run_skewed(gens, step=1):
        active = []
        it = iter(gens)
        more = True
        while more or active:
            if more:
                try:
                    active.append(next(it))
                except StopIteration:
                    more = False
            for _ in range(step):
                for gg in list(active):
                    try:
                        next(gg)
                    except StopIteration:
                        active.remove(gg)

    b_Kown = [P.buf(f"K_own{h}") for h in range(8)]
    b_Vown = [P.buf(f"V_own{h}") for h in range(8)]
    b_Kall = [P.buf(f"K_all{h}") for h in range(8)]
    b_Vall = [P.buf(f"V_all{h}") for h in range(8)]
    Vst = AR.alloc([128, 8, 16, 65], BF16)
    b_Vst = P.buf("Vst")
    P.add("pool", lambda e: e.memset(Vst[:, :, :, 64:65], 1.0), writes=[b_Vst])
    for blk in range(16):
        ps, b_ps = tbank()
        for k in range(2):
            P.add("pe", lambda e, ps=ps, k=k, blk=blk: e.matmul(ps[:, 0:512], lhsT=ckvn[:, k, 128 * blk:128 * blk + 128], rhs=wv[:, k, :],
                                                                 start=(k == 0), stop=(k == 1)), reads=[b_ckvn, b_wv], writes=[b_ps])
        P.add("act", lambda e, ps=ps, blk=blk: e.activation(out=Vst[:, :, blk, 0:64], in_=ps[:, 0:512].rearrange("p (h v) -> p h v", h=8),
                                                            func=AF.Copy), reads=[b_ps], writes=[b_Vst])
    for h in range(N_HEADS):
        P.add("sp", lambda e, h=h: e.dma_start(out=V_own[h], in_=Vst[:, h].rearrange("p b e -> p (b e)")), reads=[b_Vst],
              writes=[b_Vown[h]], dma=True, semkey=f"vst{h}")
        P.add("pool", lambda e, h=h: e.collective_compute("AllGather", ALU.bypass, replica_groups=RG, ins=[V_own[h].opt()],
                                                          outs=[V_all[h].opt()]),
              reads=[b_Vown[h]], writes=[b_Vall[h]], dma="cc", semkey=f"ccV{h}")
    kcount = [0]
    for h in range(N_HEADS):
        gens = []
        for ci, (n0, n) in enumerate(chunks):
            def q_mm(ps, b_ps, h=h, n0=n0, n=n):
                for k in range(3):
                    P.add("pe", lambda e, k=k: e.matmul(ps[0:96, 0:n], lhsT=wq[:, k, 96 * h:96 * h + 96], rhs=cqn[:, k, n0:n0 + n],
                                                        start=(k == 0), stop=(k == 2)), reads=[b_wq, b_cqn], writes=[b_ps])

            def k_mm(ps, b_ps, h=h, n0=n0, n=n):
                for k in range(2):
                    P.add("pe", lambda e, k=k: e.matmul(ps[0:96, 0:n], lhsT=wk[:, k, h, :], rhs=ckvn[:, k, n0:n0 + n], start=(k == 0),
                                                        stop=False), reads=[b_wk, b_ckvn], writes=[b_ps])
                P.add("pe", lambda e: e.matmul(ps[0:96, 0:n], lhsT=ident[64:96, 0:96], rhs=krK[64:96, n0:n0 + n], start=False, stop=True,
                                               tile_position=(64, 0)), reads=[b_sm, b_krK], writes=[b_ps])
            ki_ = kcount[0] % 3
            kcount[0] += 1
            ks, b_ks = kst[ki_], b_kst[ki_]
            npr = min(n0 + n, NPT) - n0

            def after(h=h, n0=n0, n=n, npr=npr, ks=ks, b_ks=b_ks, ki_=ki_):
                if npr > 0:
                    P.add("sp", lambda e: e.dma_start(out=K_own[h][:, n0:n0 + npr], in_=ks[:, 0:npr]), reads=[b_ks], writes=[b_Kown[h]],
                          dma=True, semkey=f"kst{ki_}")
                if npr < n:
                    P.add("pool", lambda e: e.tensor_copy(out=kS[:, h, :], in_=ks[:, npr:n]), reads=[b_ks], writes=[b_kS])
            gens.append(normrope(q_mm, "g_q", qT[:, h, n0:n0 + n], b_qT, n, n0))
            gens.append(normrope(k_mm, "g_k", ks[:, 0:n], b_ks, n, n0, after))
        run_skewed(gens)
        P.add("pool", lambda e, h=h: e.collective_compute("AllGather", ALU.bypass, replica_groups=RG, ins=[K_own[h].opt()],
                                                          outs=[K_all[h].opt()]),
              reads=[b_Kown[h]], writes=[b_Kall[h]], dma="cc", semkey=f"ccK{h}")
    import os
    if os.environ.get("CUT") == "3":
        return {}
    AR.release(mC1)
    P.new_phase()
    Kh = [AR.alloc([96, 4, NPT], BF16) for _ in range(2)]
    Vh = [AR.alloc([128, 4, 16 * 65], BF16) for _ in range(2)]
    Vvis = AR.alloc([128, 4, 16 * 65], BF16)
    Vful = AR.alloc([128, 4, 16 * 65], BF16)
    b_Kh = [P.buf("Kh0"), P.buf("Kh1")]
    b_Vh = [P.buf("Vh0"), P.buf("Vh1")]
    b_Vvis, b_Vful = P.buf("Vvis"), P.buf("Vful")
    PT = [AR.alloc([128, 512], BF16) for _ in range(6)]
    b_PT = [P.buf(f"PT{i}") for i in range(6)]
    Osb = AR.alloc([65, 512], F32); rl = AR.alloc([64, 512], F32); ast = AR.alloc([64, 512], BF16)
    b_Osb, b_rl, b_ast = P.buf("Osb"), P.buf("rl"), P.buf("ast")

    def load_head(h):
        i = h % 2
        P.add("sp", lambda e: e.dma_start(out=Kh[i][:], in_=K_all[h].rearrange("(r d) t -> d r t", r=4)), reads=[b_Kall[h]], writes=[b_Kh[i]], dma=True,
              semkey=f"Kh{i}")
        P.add("sp", lambda e: e.dma_start(out=Vh[i][:], in_=V_all[h].rearrange("(r p) x -> p r x", r=4)), reads=[b_Vall[h]], writes=[b_Vh[i]], dma=True,
              semkey=f"Vh{i}")

    load_head(0)
    pcount = 0
    for h in range(N_HEADS):
        i = h % 2
        if h + 1 < N_HEADS:
            load_head(h + 1)
        for r in range(4):
            P.add("dve", lambda e, r=r, i=i: e.tensor_scalar(out=Vvis[:, r, :], in0=Vh[i][:, r, :], scalar1=smc("vis", r), scalar2=None,
                                                              op0=MUL), reads=[b_Vh[i], b_sm], writes=[b_Vvis])
            P.add("dve", lambda e, r=r, i=i: e.tensor_scalar(out=Vful[:, r, :], in0=Vh[i][:, r, :], scalar1=smc("full", r), scalar2=None,
                                                              op0=MUL), reads=[b_Vh[i], b_sm], writes=[b_Vful])
        for qc in range(4):
            O, b_O = banks[qc % 2], bank_bufs[qc % 2]
            first = [True]
            pending = []

            def flush(keep):
                while len(pending) > keep:
                    pending.pop(0)()
            for r in range(4):
                for kb in range(16):
                    S_, b_S = tbank()
                    P.add("pe", lambda e, S_=S_, r=r, kb=kb, i=i, h=h, qc=qc: e.matmul(
                        S_[:, 0:512], lhsT=Kh[i][:, r, 128 * kb:128 * kb + 128], rhs=qT[:, h, 512 * qc:512 * qc + 512],
                        start=True, stop=True), reads=[b_Kh[i], b_qT], writes=[b_S])
                    pt, b_pt = PT[pcount % 6], b_PT[pcount % 6]
                    pcount += 1
                    P.add("act", lambda e, S_=S_, pt=pt: e.activation(out=pt[:], in_=S_[:, 0:512], func=AF.Exp, scale=SCALE),
                          reads=[b_S], writes=[b_pt])

                    def pv(r=r, kb=kb, pt=pt, b_pt=b_pt, O=O, b_O=b_O, qc=qc, i=i):
                        d = kb - 4 * qc
                        segs = []
                        if d < 0:
                            segs.append((0, 512, Vvis, b_Vvis))
                        elif d > 3:
                            segs.append((0, 512, Vful, b_Vful))
                        else:
                            if d > 0:
                                segs.append((0, 128 * d, Vful, b_Vful))
                            P.add("dve", lambda e: e.tensor_tensor(out=pt[:, 128 * d:128 * d + 128], in0=pt[:, 128 * d:128 * d + 128],
                                                                   in1=maskd[:, r, :], op=MUL), reads=[b_pt, b_mk], writes=[b_pt])
                            segs.append((128 * d, 128 * d + 128, Vh[i], b_Vh[i]))
                            if d < 3:
                                segs.append((128 * d + 128, 512, Vvis, b_Vvis))
                        for (c0, c1, Vx, b_Vx) in segs:
                            st = first[0]
                            first[0] = False
                            P.add("pe", lambda e, c0=c0, c1=c1, Vx=Vx, st=st: e.matmul(
                                O[0:65, c0:c1], lhsT=Vx[:, r, 65 * kb:65 * kb + 65], rhs=pt[:, c0:c1], start=st, stop=False),
                                reads=[b_Vx, b_pt], writes=[b_O])
                    pending.append(pv)
                    flush(3)
            flush(0)
            P.add("act", lambda e, O=O: e.activation(out=Osb[:], in_=O[0:65, 0:512], func=AF.Copy), reads=[b_O], writes=[b_Osb])
            lb, b_lb = tbank()
            P.add("pe", lambda e, lb=lb: e.matmul(lb[0:64, 0:512], lhsT=sel65[:, :], rhs=Osb[:, :], start=True, stop=True),
                  reads=[b_sel, b_Osb], writes=[b_lb])
            P.add("dve", lambda e, lb=lb: e.reciprocal(out=rl[:], in_=lb[0:64, 0:512]), reads=[b_lb], writes=[b_rl])
            cols = slice(512 * qc, 512 * qc + 512)
            if h % 2 == 0:
                P.add("dve", lambda e, h=h, cols=cols: e.tensor_tensor(out=attT[0:64, h // 2, cols], in0=Osb[0:64, :], in1=rl[:], op=MUL),
                      reads=[b_Osb, b_rl], writes=[b_attT])
            else:
                P.add("dve", lambda e: e.tensor_tensor(out=ast[:], in0=Osb[0:64, :], in1=rl[:], op=MUL), reads=[b_Osb, b_rl],
                      writes=[b_ast])
                P.add("sp", lambda e, h=h, cols=cols: e.dma_start(out=attT[64:128, h // 2, cols], in_=ast[:]), reads=[b_ast],
                      writes=[b_attT], dma=True, semkey="ast")
    return dict(mC1=mC1, qT=qT, b_qT=b_qT, w_uk_b=w_uk_b, b_wkb=b_wkb, w_uv_b=w_uv_b, b_wvb=b_wvb, rotm_d=rotm_d)


def build_tail(P, AR, nc, din, dout, dint, stores, banks, bank_bufs, cast_w, xT, w_in_b, b_w_in_b, sm, b_sm, smc, ones_bf, b_ones,
               attT, b_attT, ysT, b_ys, chunks):
    MUL, ADD = ALU.mult, ALU.add
    names = [("w_oa", 512, D), ("w_os", 512, D), ("w_out", D, D), ("w_up", D, 2 * D_FF), ("w_down", D_FF, D),
             ("w_pg", D, D), ("w_pp", PLE, D)]
    W, bW = {"w_in": w_in_b}, {"w_in": b_w_in_b}
    for nm, r, c in names:
        src = din(nm, [r, c])
        W[nm] = dint(nm + "_b", [r, c], BF16)
        bW[nm] = cast_w(W[nm], src, r, c, "c_" + nm)
    pT_d = din("pT", [PLE, T])
    scT_d = din("scT", [128, 44, 16, 2])
    o_yT = dout("o_yT", [D, T])
    o_cvp = dout("o_cvp", [128, 44, 2])
    o_cvs = dout("o_cvs", [128, 44, 16, 2])
    cc_h_in = dint("cc_h_in", [128, 16], F32)
    cc_h_out = dint("cc_h_out", [512, 16], F32)

    AR.release(0)
    P.new_phase()
    HO = 2
    x1 = AR.alloc([128, KD, 514], F32)
    x1c4 = AR.alloc([128, KD, 290], F32)
    xn = AR.alloc([128, KD, 514], BF16)
    scr = AR.alloc([128, KD, 514], BF16)
    hT = AR.alloc([128, 22, 512], BF16)
    upx = [[AR.alloc([128, 514], F32) for _ in range(2)] for _ in range(2)]
    cav = [[AR.alloc([128, 512], F32) for _ in range(2)] for _ in range(2)]
    sgt = [AR.alloc([128, 512], F32) for _ in range(2)]
    lnv = AR.alloc([128, 514], F32)
    rstd = AR.alloc([128, 514], F32)
    pTc = AR.alloc([128, 2, 512], BF16)
    scT = AR.alloc([128, 44, 16, 2], F32)
    upS = [AR.alloc([128, 16, 6], F32) for _ in range(2)]
    cvp = AR.alloc([128, 44, 2], F32)
    cvs = AR.alloc([128, 44, 16, 2], F32)
    carry = AR.alloc([128, 44, 2], F32)
    Hg = AR.alloc([128, 4, 16], F32)
    hsend = AR.alloc([128, 16], F32)
    hrecv = AR.alloc([128, 16], F32)
    NSLAB = 4
    slabs = [AR.alloc([128, 4096], BF16) for _ in range(NSLAB)]
    b_x1, b_x1c4, b_xn, b_scr, b_hT, b_lnv, b_rstd, b_pTc, b_scT, b_cvp, b_cvs, b_carry, b_Hg, b_hsend, b_hrecv = [
        P.buf(n) for n in ("x1", "x1c4", "xn", "scr", "hT", "lnv", "rstd", "pTc", "scT", "cvp", "cvs", "carry", "Hg", "hsend", "hrecv")]
    b_upx = [[P.buf(f"upx{i}{j}") for j in range(2)] for i in range(2)]
    b_cav = [[P.buf(f"cav{i}{j}") for j in range(2)] for i in range(2)]
    b_sgt = [P.buf("sgt0"), P.buf("sgt1")]
    b_upS = [P.buf("upS0"), P.buf("upS1")]
    b_slab = [P.buf(f"slab{i}") for i in range(NSLAB)]
    P.add("sp", lambda e: e.dma_start(out=scT[:], in_=scT_d), writes=[b_scT], dma=True, semkey="scT")
    P.add("pool", lambda e: e.memset(carry[:], 0.0), writes=[b_carry])

    rr = [0]

    def tbank():
        i = rr[0] % 8
        rr[0] += 1
        return banks[i], bank_bufs[i]

    sl = [0]

    def load_slab(wname, kt, c0, width):
        i = sl[0] % NSLAB
        sl[0] += 1
        v = slabs[i][:, 0:kt * width].rearrange("p (k m) -> p k m", k=kt)
        src, bsrc = W[wname], bW[wname]
        P.add("sp", lambda e: e.dma_start(out=v, in_=src.rearrange("(k p) m -> p k m", p=128)[:, :, c0:c0 + width]),
              reads=[bsrc], writes=[b_slab[i]], dma=True, semkey=f"slab{i}")
        return v, b_slab[i]

    def mm_group(ps, b_ps, n, w, b_w, kt, wc0, rhs_fn, rd):
        for k in range(kt):
            P.add("pe", lambda e, k=k: e.matmul(ps[:, 0:n], lhsT=w[:, k, wc0:wc0 + 128], rhs=rhs_fn(k), start=(k == 0), stop=(k == kt - 1)),
                  reads=[b_w] + rd, writes=[b_ps])

    def norm_to_xn(src, b_src, gname, c_lo, c_hi):
        n = c_hi - c_lo
        P.add("pool", lambda e: e.tensor_tensor(out=scr[:, :, c_lo:c_hi], in0=src[:, :, c_lo:c_hi], in1=src[:, :, c_lo:c_hi], op=MUL),
              reads=[b_src], writes=[b_scr])
        ps, b_ps = tbank()
        for k in range(KD):
            P.add("pe", lambda e, k=k: e.matmul(ps[:, 0:n], lhsT=ones_bf[:], rhs=scr[:, k, c_lo:c_hi], start=(k == 0), stop=(k == KD - 1)),
                  reads=[b_scr, b_ones], writes=[b_ps])
        P.add("act", lambda e: e.activation(out=lnv[:, 0:n], in_=ps[:, 0:n], func=AF.Ln, scale=1.0 / D, bias=EPS), reads=[b_ps],
              writes=[b_lnv])
        P.add("act", lambda e: e.activation(out=rstd[:, 0:n], in_=lnv[:, 0:n], func=AF.Exp, scale=-0.5), reads=[b_lnv], writes=[b_rstd])
        for k in range(KD):
            P.add("dve", lambda e, k=k: e.scalar_tensor_tensor(out=xn[:, k, c_lo:c_hi], in0=src[:, k, c_lo:c_hi], scalar=smc(gname, k),
                                                               in1=rstd[:, 0:n], op0=MUL, op1=MUL),
                  reads=[b_src, b_rstd, b_sm], writes=[b_xn])

    xT_v = xT.rearrange("(k p) t -> p k t", p=128)

    def phase_D(xt, b_xt, n0, n):
        lo, hi = HO, HO + n
        P.add("sp", lambda e: e.dma_start(out=xt[:, :, lo:hi], in_=xT_v[:, :, n0:n0 + n]), writes=[b_xt], dma=True, semkey="x1ld")
        norm_to_xn(xt, b_xt, "g_mix", lo, hi)
        mixed = scr
        for q in range(2):
            wga, b_wga = load_slab("w_in", KD, OFF_GA + 512 * q, 512)
            wgs, b_wgs = load_slab("w_in", KD, OFF_GS + 512 * q, 512)
            woa, b_woa = load_slab("w_oa", 4, 512 * q, 512)
            wos, b_wos = load_slab("w_os", 4, 512 * q, 512)
            for mi in range(4):
                m = 4 * q + mi
                pga, b_pga = tbank()
                mm_group(pga, b_pga, n, wga, b_wga, KD, 128 * mi, lambda k: xn[:, k, lo:hi], [b_xn])
                pgs, b_pgs = tbank()
                mm_group(pgs, b_pgs, n, wgs, b_wgs, KD, 128 * mi, lambda k: xn[:, k, lo:hi], [b_xn])
                poa, b_poa = tbank()
                mm_group(poa, b_poa, n, woa, b_woa, 4, 128 * mi, lambda k: attT[:, k, n0:n0 + n], [b_attT])
                pos_, b_pos = tbank()
                mm_group(pos_, b_pos, n, wos, b_wos, 4, 128 * mi, lambda k: ysT[:, k, n0:n0 + n], [b_ys])
                P.add("act", lambda e, pga=pga: e.activation(out=sgt[0][:, 0:n], in_=pga[:, 0:n], func=AF.Sigmoid), reads=[b_pga],
                      writes=[b_sgt[0]])
                P.add("act", lambda e, pgs=pgs: e.activation(out=sgt[1][:, 0:n], in_=pgs[:, 0:n], func=AF.Sigmoid), reads=[b_pgs],
                      writes=[b_sgt[1]])
                P.add("dve", lambda e, poa=poa: e.tensor_tensor(out=cav[0][0][:, 0:n], in0=poa[:, 0:n], in1=sgt[0][:, 0:n], op=MUL),
                      reads=[b_poa, b_sgt[0]], writes=[b_cav[0][0]])
                P.add("dve", lambda e, pos_=pos_: e.tensor_tensor(out=cav[0][1][:, 0:n], in0=pos_[:, 0:n], in1=sgt[1][:, 0:n], op=MUL),
                      reads=[b_pos, b_sgt[1]], writes=[b_cav[0][1]])
                P.add("pool", lambda e, m=m: e.tensor_tensor(out=mixed[:, m, lo:hi], in0=cav[0][0][:, 0:n], in1=cav[0][1][:, 0:n], op=ADD),
                      reads=[b_cav[0][0], b_cav[0][1]], writes=[b_scr])
        for q in range(2):
            wo, b_wo = load_slab("w_out", KD, 512 * q, 512)
            for mi in range(4):
                m = 4 * q + mi
                ps, b_ps = tbank()
                mm_group(ps, b_ps, n, wo, b_wo, KD, 128 * mi, lambda k: mixed[:, k, lo:hi], [b_scr])
                P.add("dve", lambda e, ps=ps, m=m: e.tensor_tensor(out=xt[:, m, lo:hi], in0=ps[:, 0:n], in1=xt[:, m, lo:hi], op=ADD),
                      reads=[b_ps, b_xt], writes=[b_xt])

    ucount = [0]

    def conv3(ci_, src3, dst, b_src, b_dst, tile, nn, view=None):
        w0, w1, w2 = (smc("conv_w", tap * 44 + tile) for tap in range(3))
        P.add("act", lambda e: e.activation(out=dst, in_=src3(2), func=AF.Identity, scale=w2, bias=smc("conv_b", tile)),
              reads=[b_src, b_sm], writes=[b_dst])
        P.add("dve", lambda e: e.scalar_tensor_tensor(out=dst, in0=src3(1), scalar=w1, in1=dst, op0=MUL, op1=ADD),
              reads=[b_src, b_sm, b_dst], writes=[b_dst])
        P.add("dve", lambda e: e.scalar_tensor_tensor(out=dst, in0=src3(0), scalar=w0, in1=dst, op0=MUL, op1=ADD),
              reads=[b_src, b_sm, b_dst], writes=[b_dst])

    def phase_E(xt, b_xt, n0, n, first, last):
        lo, hi = HO, HO + n
        c_lo = 0 if first else HO
        N = hi - c_lo
        npr = min(n0 + n, NPT) - n0
        ns = n - npr
        norm_to_xn(xt, b_xt, "g_ffn", c_lo, hi)
        for q in range(6):
            npair = min(4, 22 - 4 * q)
            wa, b_wa = load_slab("w_up", KD, 512 * q, 128 * npair)
            wv_, b_wv = load_slab("w_up", KD, D_FF + 512 * q, 128 * npair)
            for pi_ in range(npair):
                p = 4 * q + pi_
                ub = ucount[0] % 2
                ucount[0] += 1
                for av, (w, b_w) in enumerate(((wa, b_wa), (wv_, b_wv))):
                    tile = p + 22 * av
                    u, b_u = upx[ub][av], b_upx[ub][av]
                    c, b_c = cav[ub][av], b_cav[ub][av]
                    ps, b_ps = tbank()
                    mm_group(ps, b_ps, N, w, b_w, KD, 128 * pi_, lambda k: xn[:, k, c_lo:hi], [b_xn])
                    P.add("act", lambda e, ps=ps, u=u: e.activation(out=u[:, c_lo:hi], in_=ps[:, 0:N], func=AF.Copy), reads=[b_ps],
                          writes=[b_u])
                    if not first:
                        P.add("pool", lambda e, u=u, tile=tile: e.tensor_copy(out=u[:, 0:2], in_=carry[:, tile, :]), reads=[b_carry],
                              writes=[b_u])
                    conv3(0, lambda k, u=u: u[:, k:k + npr], c[:, 0:npr], b_u, b_c, tile, npr)
                    P.add("pool", lambda e, u=u, tile=tile: e.tensor_copy(out=carry[:, tile, :], in_=u[:, npr:npr + 2]), reads=[b_u],
                          writes=[b_carry])
                    if last:
                        P.add("pool", lambda e, u=u, tile=tile: e.tensor_copy(out=cvp[:, tile, :], in_=u[:, npr:npr + 2]), reads=[b_u],
                              writes=[b_cvp])
                    if ns:
                        us, b_us = upS[av], b_upS[av]
                        P.add("pool", lambda e, us=us, tile=tile: e.tensor_copy(out=us[:, :, 0:2], in_=scT[:, tile, :, :]), reads=[b_scT],
                              writes=[b_us])
                        P.add("pool", lambda e, us=us, u=u: e.tensor_copy(
                            out=us[:, :, 2:6], in_=u[:, HO + npr:HO + n].rearrange("p (q s) -> p q s", s=4)), reads=[b_u], writes=[b_us])
                        conv3(0, lambda k, us=us: us[:, :, k:k + 4], c[:, npr:n].rearrange("p (q s) -> p q s", s=4), b_us, b_c, tile, ns)
                        P.add("pool", lambda e, us=us, tile=tile: e.tensor_copy(out=cvs[:, tile, :, :], in_=us[:, :, 4:6]), reads=[b_us],
                              writes=[b_cvs])
                ca, cv = cav[ub][0], cav[ub][1]
                P.add("act", lambda e, ca=ca: e.activation(out=ca[:, 0:n], in_=ca[:, 0:n], func=AF.Gelu_apprx_tanh), reads=[b_cav[ub][0]],
                      writes=[b_cav[ub][0]])
                P.add("dve", lambda e, ca=ca, cv=cv, p=p: e.tensor_tensor(out=hT[:, p, 0:n], in0=ca[:, 0:n], in1=cv[:, 0:n], op=MUL),
                      reads=[b_cav[ub][0], b_cav[ub][1]], writes=[b_hT])
        for m in range(KD):
            wd, b_wd = load_slab("w_down", 22, 128 * m, 128)
            ps, b_ps = tbank()
            mm_group(ps, b_ps, n, wd, b_wd, 22, 0, lambda k: hT[:, k, 0:n], [b_hT])
            P.add("dve", lambda e, ps=ps, m=m: e.tensor_tensor(out=xt[:, m, lo:hi], in0=ps[:, 0:n], in1=xt[:, m, lo:hi], op=ADD),
                  reads=[b_ps, b_xt], writes=[b_xt])
        norm_to_xn(xt, b_xt, "g_ple", lo, hi)
        P.add("pool", lambda e: e.dma_start(out=pTc[:, :, 0:n], in_=pT_d.rearrange("(k p) t -> p k t", p=128)[:, :, n0:n0 + n]),
              writes=[b_pTc], dma=True, semkey="pTc")
        for q in range(2):
            wg_, b_wg = load_slab("w_pg", KD, 512 * q, 512)
            wp_, b_wp = load_slab("w_pp", 2, 512 * q, 512)
            for mi in range(4):
                m = 4 * q + mi
                pg, b_pg = tbank()
                mm_group(pg, b_pg, n, wg_, b_wg, KD, 128 * mi, lambda k: xn[:, k, lo:hi], [b_xn])
                pp, b_pp = tbank()
                mm_group(pp, b_pp, n, wp_, b_wp, 2, 128 * mi, lambda k: pTc[:, k, 0:n], [b_pTc])
                P.add("act", lambda e, pg=pg: e.activation(out=sgt[0][:, 0:n], in_=pg[:, 0:n], func=AF.Sigmoid), reads=[b_pg],
                      writes=[b_sgt[0]])
                P.add("dve", lambda e, pp=pp: e.tensor_tensor(out=sgt[1][:, 0:n], in0=pp[:, 0:n], in1=sgt[0][:, 0:n], op=MUL),
                      reads=[b_pp, b_sgt[0]], writes=[b_sgt[1]])
                P.add("pool", lambda e, m=m: e.tensor_tensor(out=xt[:, m, lo:hi], in0=xt[:, m, lo:hi], in1=sgt[1][:, 0:n], op=ADD),
                      reads=[b_xt, b_sgt[1]], writes=[b_xt])
        stores.append(P.add("sp", lambda e: e.dma_start(out=o_yT.rearrange("(k p) t -> p k t", p=128)[:, :, n0:n0 + n], in_=xt[:, :, lo:hi]),
                            reads=[b_xt], dma=True, semkey="st_y"))

    n0_4, n_4 = chunks[4]
    phase_D(x1c4, b_x1c4, n0_4, n_4)
    lastp = HO + (NPT - n0_4)
    P.add("pool", lambda e: e.tensor_copy(out=hsend[:].rearrange("p (k c) -> p k c", c=2), in_=x1c4[:, :, lastp - 2:lastp]),
          reads=[b_x1c4], writes=[b_hsend])
    b_hin, b_hout = P.buf("cc_h_in"), P.buf("cc_h_out")
    P.add("sp", lambda e: e.dma_start(out=cc_h_in, in_=hsend[:]), reads=[b_hsend], writes=[b_hin], dma=True, semkey="hs1")
    P.add("pool", lambda e: e.collective_compute("AllGather", ALU.bypass, replica_groups=[[0, 1, 2, 3], [4, 5, 6, 7]],
                                                 ins=[cc_h_in.opt()], outs=[cc_h_out.opt()]),
          reads=[b_hin], writes=[b_hout], dma="cc", semkey="hs2")
    P.add("sp", lambda e: e.dma_start(out=Hg[:], in_=cc_h_out.rearrange("(r p) f -> p r f", p=128)), reads=[b_hout], writes=[b_Hg],
          dma=True, semkey="hs3")
    P.add("dve", lambda e: e.tensor_scalar(out=hrecv[:], in0=Hg[:, 0, :], scalar1=smc("hsel", 0), scalar2=None, op0=MUL),
          reads=[b_Hg, b_sm], writes=[b_hrecv])
    for r in range(1, 4):
        P.add("dve", lambda e, r=r: e.scalar_tensor_tensor(out=hrecv[:], in0=Hg[:, r, :], scalar=smc("hsel", r), in1=hrecv[:],
                                                           op0=MUL, op1=ADD), reads=[b_Hg, b_sm, b_hrecv], writes=[b_hrecv])
    for ci in range(4):
        n0, n = chunks[ci]
        if ci == 0:
            P.add("pool", lambda e: e.tensor_copy(out=x1[:, :, 0:2], in_=hrecv[:].rearrange("p (k c) -> p k c", c=2)),
                  reads=[b_hrecv], writes=[b_x1])
        phase_D(x1, b_x1, n0, n)
        phase_E(x1, b_x1, n0, n, ci == 0, False)
    phase_E(x1c4, b_x1c4, n0_4, n_4, False, True)
    stores.append(P.add("sp", lambda e: e.dma_start(out=o_cvp, in_=cvp[:]), reads=[b_cvp], dma=True, semkey="st_cvp"))
    stores.append(P.add("sp", lambda e: e.dma_start(out=o_cvs, in_=cvs[:]), reads=[b_cvs], dma=True, semkey="st_cvs"))


def build_sample_attn(P, AR, nc, din, dout, dint, stores, banks, bank_bufs, sm, b_sm, smc, ones_bf, b_ones, ckvn, b_ckvn, krK, b_krK,
                      qT, b_qT, attT, b_attT, n_pool, w_uk_b, b_wkb, w_uv_b, b_wvb, rope_c, rope_s, rotm_d, mC1):
    MUL, ADD = ALU.mult, ALU.add
    CW = KV_LORA + QK_ROPE
    cache = din("cache", [n_pool * 32, 4 * CW])
    ptab = din("ptab", [128, SEQ_PER_CORE * 16], I32)
    p32c_d = din("p32c", [128, 1], I32)
    w_ukT_d = din("w_ukT", [64, 8 * KV_LORA])
    ropeP_d = din("ropeP", [128, 2, NPAGES, 16])
    gk_rep_d = din("gk_rep", [128, 32])
    hselm_d = din("hselm", [128, 4, 32])
    cmask_d = din("cmask", [32, 4])

    AR.release(mC1)
    P.new_phase()
    NB_PG = 5
    pb = [AR.alloc([128, 4, CW], BF16) for _ in range(NB_PG)]
    b_pb = [P.buf(f"pb{i}") for i in range(NB_PG)]
    kt3 = [AR.alloc([128, 4, 96], BF16) for _ in range(2)]
    b_kt3 = [P.buf("kt3_0"), P.buf("kt3_1")]
    rt = [AR.alloc([128, 4, 16], F32) for _ in range(4)]
    b_rt = P.buf("rt")
    ropeP = AR.alloc([128, 2, NPAGES, 16], F32)
    gkr = AR.alloc([128, 32], F32)
    tabs = AR.alloc([128, 4, NPAGES, 16], F32)
    ptb = AR.alloc([128, SEQ_PER_CORE * 16], I32)
    idx = AR.alloc([128, SEQ_PER_CORE * 16], I32)
    iot = AR.alloc([128, 1], I32)
    wk_sb = AR.alloc([128, 2, 512], BF16)
    wv_sb = AR.alloc([128, 2, 512], BF16)
    wukT = AR.alloc([64, 8, KV_LORA], BF16)
    wukT_f = AR.alloc([64, 8 * KV_LORA], F32)
    hselm = AR.alloc([128, 4, 32], BF16)
    hselm_f = AR.alloc([128, 4, 32], F32)
    cmask = AR.alloc([32, 4], F32)
    qgk = AR.alloc([64, 8, NST], BF16)
    Qabs = AR.alloc([128, 2, SEQ_PER_CORE, 32], BF16)
    Qrope = AR.alloc([96, SEQ_PER_CORE, 32], BF16)
    cT_sb = [AR.alloc([128, 2, 512], BF16) for _ in range(2)]
    krT_sb = [AR.alloc([96, 512], BF16) for _ in range(2)]
    kn_sb = [AR.alloc([128, 512], BF16) for _ in range(4)]
    sq_sb = [AR.alloc([128, 512], BF16) for _ in range(4)]
    sqk = AR.alloc([32, 512], BF16)
    lnr = AR.alloc([32, 512], F32)
    rr_ = AR.alloc([32, 512], F32)
    sr = AR.alloc([32, 512], F32)
    Pm = AR.alloc([32, 512], BF16)
    PT_sb = [AR.alloc([128, 4, 32], BF16) for _ in range(2)]
    Lacc = AR.alloc([32, 20], F32)
    accs = AR.alloc([32, KV_LORA], F32)
    lsum = AR.alloc([32, 1], F32)
    olat = AR.alloc([32, KV_LORA], BF16)
    olT = AR.alloc([128, 2, SEQ_PER_CORE, 32], BF16)
    knew = AR.alloc([96, NST], BF16)
    kraw = AR.alloc([96, NST], BF16)
    kg32 = AR.alloc([96, NST], F32)
    kt1 = AR.alloc([96, NST], F32)
    kt2 = AR.alloc([96, NST], F32)
    rc_s = AR.alloc([96, NST], F32)
    rs_s = AR.alloc([96, NST], F32)
    rotm = AR.alloc([96, 96], F32)
    cnew = AR.alloc([4, KV_LORA], BF16)
    names = ["kt", "ropeP", "gkr", "tabs", "ptb", "idx", "iot", "wk", "wv", "wukT", "hselm", "cmask", "qgk", "Qabs", "Qrope",
             "sqk", "lnr", "rr", "sr", "Pm", "Lacc", "accs", "lsum", "olat", "olT", "knew", "kraw", "ktmp", "rcs", "rotm", "cnew"]
    B = {n: P.buf("s_" + n) for n in names}
    b_cT = [P.buf("cT0"), P.buf("cT1")]
    b_krT = [P.buf("krT0"), P.buf("krT1")]
    b_kn = [P.buf(f"kn{i}") for i in range(4)]
    b_sq = [P.buf(f"sq{i}") for i in range(4)]
    b_PT = [P.buf("PTs0"), P.buf("PTs1")]

    rrb = [0]

    def tbank():
        i = 1 + rrb[0] % 7
        rrb[0] += 1
        return banks[i], bank_bufs[i]
    ACC, b_ACC = banks[0], bank_bufs[0]

    ld = lambda out, in_, wr, key, rd=(): P.add("sp", lambda e: e.dma_start(out=out, in_=in_), reads=list(rd), writes=[wr], dma=True,
                                                semkey=key)
    ld(ropeP[:], ropeP_d, B["ropeP"], "s_ropeP")
    ld(gkr[:], gk_rep_d, B["gkr"], "s_gkr")
    ld(ptb[:], ptab, B["ptb"], "s_ptb")
    ld(iot[:], p32c_d, B["iot"], "s_iot")
    ld(wk_sb[:], w_uk_b.rearrange("(k p) m -> p k m", p=128), B["wk"], "s_wk", [b_wkb])
    ld(wv_sb[:], w_uv_b.rearrange("(k p) m -> p k m", p=128), B["wv"], "s_wv", [b_wvb])
    ld(wukT_f[:], w_ukT_d, B["wukT"], "s_wukT")
    ld(hselm_f[:], hselm_d, B["hselm"], "s_hselm")
    ld(cmask[:], cmask_d, B["cmask"], "s_cmask")
    ld(rc_s[:], rope_c[:, NPT:T], B["rcs"], "s_rcs")
    ld(rs_s[:], rope_s[:, NPT:T], B["rcs"], "s_rss")
    ld(rotm[:], rotm_d, B["rotm"], "s_rotm")
    P.add("pool", lambda e: e.tensor_copy(out=wukT[:].rearrange("p h l -> p (h l)"), in_=wukT_f[:]), reads=[B["wukT"]], writes=[B["wukT"]])
    P.add("pool", lambda e: e.tensor_copy(out=hselm[:], in_=hselm_f[:]), reads=[B["hselm"]], writes=[B["hselm"]])
    P.add("dve", lambda e: e.tensor_scalar(out=idx[:], in0=ptb[:], scalar1=32.0, scalar2=iot[:, 0:1], op0=MUL, op1=ADD),
          reads=[B["ptb"], B["iot"]], writes=[B["idx"]])
    g1 = gkr[:, 0:16].unsqueeze(1).broadcast_to([128, NPAGES, 16])
    g2 = gkr[:, 16:32].unsqueeze(1).broadcast_to([128, NPAGES, 16])
    tt_ = lambda o, a, b_, op: P.add("pool", lambda e: e.tensor_tensor(out=o, in0=a, in1=b_, op=op), reads=[B["ropeP"], B["gkr"], B["tabs"]],
                                     writes=[B["tabs"]])
    tt_(tabs[:, 0], ropeP[:, 0], g1, MUL)
    tt_(tabs[:, 1], ropeP[:, 0], g2, MUL)
    tt_(tabs[:, 2], ropeP[:, 1], g2, MUL)
    P.add("pool", lambda e: e.tensor_single_scalar(out=tabs[:, 2], in_=tabs[:, 2], scalar=-1.0, op=MUL), reads=[B["tabs"]],
          writes=[B["tabs"]])
    tt_(tabs[:, 3], ropeP[:, 1], g1, MUL)
    for i in range(2):
        P.add("pool", lambda e, i=i: e.memset(kt3[i][:], 0.0), writes=[b_kt3[i]])

    P.add("dve", lambda e: e.tensor_scalar(out=qgk[:], in0=qT[0:64, :, NPT:T], scalar1=smc("g_k")[0:64, :], scalar2=None, op0=MUL),
          reads=[b_qT, b_sm], writes=[B["qgk"]])
    for h in range(N_HEADS):
        for kt in range(2):
            ps, b_ps = tbank()
            P.add("pe", lambda e, ps=ps, h=h, kt=kt: e.matmul(ps[:, 0:NST], lhsT=wukT[:, h, 128 * kt:128 * kt + 128], rhs=qgk[:, h, :],
                                                              start=True, stop=True), reads=[B["wukT"], B["qgk"]], writes=[b_ps])
            P.add("act", lambda e, ps=ps, h=h, kt=kt: e.activation(out=Qabs[:, kt, :, 4 * h:4 * h + 4],
                                                                   in_=ps[:, 0:NST].rearrange("p (q t) -> p q t", t=4), func=AF.Copy),
                  reads=[b_ps], writes=[B["Qabs"]])
        P.add("pool", lambda e, h=h: e.tensor_copy(out=Qrope[64:96, :, 4 * h:4 * h + 4],
                                                   in_=qT[64:96, h, NPT:T].rearrange("p (q t) -> p q t", t=4)),
              reads=[b_qT], writes=[B["Qrope"]])
    P.add("act", lambda e: e.activation(out=kraw[64:96, :], in_=krK[64:96, NPT:T], func=AF.Copy), reads=[b_krK], writes=[B["kraw"]])
    P.add("dve", lambda e: e.tensor_scalar(out=kg32[64:96, :], in0=krK[64:96, NPT:T], scalar1=smc("g_k")[64:96, :], scalar2=None, op0=MUL),
          reads=[b_krK, b_sm], writes=[B["ktmp"]])
    ps, b_ps = tbank()
    P.add("pe", lambda e, ps=ps: e.matmul(ps[0:96, 0:NST], lhsT=rotm[64:96, 0:96], rhs=kg32[64:96, :], start=True, stop=True,
                                          tile_position=(64, 0)), reads=[B["rotm"], B["ktmp"]], writes=[b_ps])
    P.add("dve", lambda e: e.tensor_tensor(out=kt1[64:96, :], in0=kg32[64:96, :], in1=rc_s[64:96, :], op=MUL), reads=[B["ktmp"], B["rcs"]],
          writes=[B["ktmp"]])
    P.add("dve", lambda e, ps=ps: e.tensor_tensor(out=kt2[64:96, :], in0=ps[64:96, 0:NST], in1=rs_s[64:96, :], op=MUL),
          reads=[b_ps, B["rcs"]], writes=[B["ktmp"]])
    P.add("dve", lambda e: e.tensor_tensor(out=knew[64:96, :], in0=kt1[64:96, :], in1=kt2[64:96, :], op=ADD), reads=[B["ktmp"]],
          writes=[B["knew"]])

    idb = AR.alloc([128, 128], BF16)
    b_idb = P.buf("idb")
    P.add("pool", lambda e: e.tensor_copy(out=idb[:], in_=sm[:, SL["ident"][0]:SL["ident"][0] + 128]), reads=[b_sm], writes=[b_idb])
    sqk2 = [sqk, AR.alloc([32, 512], BF16)]
    lnr2 = [lnr, AR.alloc([32, 512], F32)]
    rr2 = [rr_, AR.alloc([32, 512], F32)]
    sr2 = [sr, AR.alloc([32, 512], F32)]
    Pm2 = [Pm, AR.alloc([32, 512], BF16)]
    Lacc2 = [Lacc, AR.alloc([32, 20], F32)]
    Bq = [{n: P.buf(f"s2_{n}{i}") for n in ("sqk", "lnr", "rr", "sr", "Pm")} for i in range(2)]
    b_Lacc2 = [P.buf("Lacc0"), P.buf("Lacc1")]
    cnt = [0]
    gcnt = [0]

    def tbank2():
        i = 2 + rrb[0] % 6
        rrb[0] += 1
        return banks[i], bank_bufs[i]

    def chunk(q, col, npos, cT, b_cTs, kraw_ap, b_kraw, krop_ap, b_krop, crows, first, mask):
        n = npos
        ACC, b_ACC = banks[q % 2], bank_bufs[q % 2]
        Lq, b_Lq = Lacc2[q % 2], b_Lacc2[q % 2]
        ci = gcnt[0] % 2
        gcnt[0] += 1
        sqk_, lnr_, rr__, sr_, Pm_ = sqk2[ci], lnr2[ci], rr2[ci], sr2[ci], Pm2[ci]
        Bc = Bq[ci]
        pss_l = []
        for m in range(4):
            ps, b_ps = tbank2()
            for kt in range(2):
                P.add("pe", lambda e, ps=ps, m=m, kt=kt: e.matmul(ps[:, 0:n], lhsT=wk_sb[:, kt, 128 * m:128 * m + 128], rhs=cT[:, kt, 0:n],
                                                                  start=(kt == 0), stop=(kt == 1)), reads=[B["wk"]] + b_cTs, writes=[b_ps])
            pss_l.append((ps, b_ps))
        for m in range(4):
            ps, b_ps = pss_l[m]
            if m < 2:
                P.add("act", lambda e, ps=ps, m=m: e.activation(out=kn_sb[m][:, 0:n], in_=ps[:, 0:n], func=AF.Copy), reads=[b_ps],
                      writes=[b_kn[m]])
            else:
                P.add("dve", lambda e, ps=ps, m=m: e.tensor_copy(out=kn_sb[m][:, 0:n], in_=ps[:, 0:n]), reads=[b_ps], writes=[b_kn[m]])
            P.add("dve", lambda e, m=m: e.tensor_tensor(out=sq_sb[m][:, 0:n], in0=kn_sb[m][:, 0:n], in1=kn_sb[m][:, 0:n], op=MUL),
                  reads=[b_kn[m]], writes=[b_sq[m]])
        P.add("pool", lambda e: e.tensor_tensor(out=sqk_[:, 0:n], in0=kraw_ap, in1=kraw_ap, op=MUL), reads=b_kraw, writes=[Bc["sqk"]])
        yield
        pss, b_pss = tbank2()
        for m in range(4):
            P.add("pe", lambda e, m=m: e.matmul(pss[0:32, 0:n], lhsT=hselm[:, m, :], rhs=sq_sb[m][:, 0:n], start=(m == 0), stop=False),
                  reads=[B["hselm"], b_sq[m]], writes=[b_pss])
        P.add("pe", lambda e: e.matmul(pss[0:32, 0:n], lhsT=ones_bf[0:32, 0:32], rhs=sqk_[:, 0:n], start=False, stop=True),
              reads=[b_ones, Bc["sqk"]], writes=[b_pss])
        psc, b_psc = tbank2()
        for kt in range(2):
            P.add("pe", lambda e, kt=kt: e.matmul(psc[0:32, 0:n], lhsT=Qabs[:, kt, q, :], rhs=cT[:, kt, 0:n], start=(kt == 0), stop=False),
                  reads=[B["Qabs"]] + b_cTs, writes=[b_psc])
        P.add("pe", lambda e: e.matmul(psc[0:32, 0:n], lhsT=Qrope[64:96, q, :], rhs=krop_ap, start=False, stop=True, tile_position=(64, 0)),
              reads=[B["Qrope"]] + b_krop, writes=[b_psc])
        P.add("act", lambda e: e.activation(out=lnr_[:, 0:n], in_=pss[0:32, 0:n], func=AF.Ln, scale=1.0 / QK_HEAD, bias=EPS),
              reads=[b_pss], writes=[Bc["lnr"]])
        P.add("act", lambda e: e.activation(out=rr__[:, 0:n], in_=lnr_[:, 0:n], func=AF.Exp, scale=-0.5), reads=[Bc["lnr"]],
              writes=[Bc["rr"]])
        P.add("dve", lambda e: e.tensor_tensor(out=sr_[:, 0:n], in0=psc[0:32, 0:n], in1=rr__[:, 0:n], op=MUL), reads=[b_psc, Bc["rr"]],
              writes=[Bc["sr"]])
        if mask:
            P.add("act", lambda e: e.activation(out=sr_[:, 0:n], in_=sr_[:, 0:n], func=AF.Exp, scale=SCALE), reads=[Bc["sr"]],
                  writes=[Bc["sr"]])
            P.add("dve", lambda e: e.tensor_tensor(out=sr_[:, 0:n], in0=sr_[:, 0:n], in1=cmask[:, 0:n], op=MUL), reads=[Bc["sr"], B["cmask"]],
                  writes=[Bc["sr"]])
            P.add("dve", lambda e: e.tensor_copy(out=Pm_[:, 0:n], in_=sr_[:, 0:n]), reads=[Bc["sr"]], writes=[Bc["Pm"]])
            P.add("dve", lambda e: e.reduce_sum(out=Lq[:, col:col + 1], in_=sr_[:, 0:n], axis=AX.X), reads=[Bc["sr"]], writes=[b_Lq])
        else:
            P.add("act", lambda e: e.activation(out=Pm_[:, 0:n], in_=sr_[:, 0:n], func=AF.Exp, scale=SCALE, accum_out=Lq[:, col:col + 1]),
                  reads=[Bc["sr"]], writes=[Bc["Pm"], b_Lq])
        yield
        pT_, b_pT = tbank2()
        pTb = pT_[:].bitcast(BF16)
        nblk = len(crows)
        for bi, (cap, b_cap, c0, c1) in enumerate(crows):
            P.add("pe", lambda e, bi=bi, c0=c0, c1=c1: e.transpose(pTb[0:c1 - c0, 32 * bi:32 * bi + 32], Pm_[:, c0:c1], idb[0:32, 0:32]),
                  reads=[Bc["Pm"], b_idb], writes=[b_pT])
        pi_ = cnt[0] % 2
        cnt[0] += 1
        rows = crows[0][3] - crows[0][2]
        P.add("act", lambda e, pi_=pi_: e.activation(out=PT_sb[pi_][0:rows, 0:nblk, :],
                                                     in_=pTb[0:rows, 0:32 * nblk].rearrange("p (b c) -> p b c", c=32), func=AF.Copy),
              reads=[b_pT], writes=[b_PT[pi_]])
        yield
        for bi, (cap, b_cap, c0, c1) in enumerate(crows):
            P.add("pe", lambda e, bi=bi, cap=cap, c0=c0, c1=c1, pi_=pi_, st=(first and bi == 0): e.matmul(
                ACC[0:32, 0:KV_LORA], lhsT=PT_sb[pi_][0:c1 - c0, bi, :], rhs=cap, start=st, stop=False),
                reads=[b_PT[pi_]] + b_cap, writes=[b_ACC])

    pgc = [0]
    kcn = [0]

    def page_chunk(q, g):
        bi_ = pgc[0] % NB_PG
        ki = pgc[0] % 2
        ci_ = pgc[0] % 2
        pgc[0] += 1
        pbt, b_pbt = pb[bi_], b_pb[bi_]
        ch = q * 16 + g
        P.add("pool", lambda e: e.indirect_dma_start(
            out=pbt[:].rearrange("p a c -> p (a c)"), out_offset=None, in_=cache,
            in_offset=bass.IndirectOffsetOnAxis(ap=idx[:, ch:ch + 1], axis=0)),
            reads=[B["idx"]], writes=[b_pbt], dma=True, semkey=f"pb{bi_}")
        yield
        yield
        yield
        yield
        ki = kcn[0] % 2
        kcn[0] += 1
        k3, b_k3 = kt3[ki], b_kt3[ki]
        kr1, kr2 = pbt[:, :, KV_LORA:KV_LORA + 16], pbt[:, :, KV_LORA + 16:KV_LORA + 32]
        pgs = slice(4 * g, 4 * g + 4)
        pl = lambda fn, rd, wr: P.add("pool", fn, reads=rd, writes=wr)
        pl(lambda e: e.tensor_copy(out=k3[:, :, 0:32], in_=pbt[:, :, KV_LORA:CW]), [b_pbt], [b_k3])
        pl(lambda e: e.tensor_tensor(out=rt[0][:], in0=kr1, in1=tabs[:, 0, pgs, :], op=MUL), [b_pbt, B["tabs"]], [b_rt])
        pl(lambda e: e.tensor_tensor(out=rt[1][:], in0=kr2, in1=tabs[:, 2, pgs, :], op=MUL), [b_pbt, B["tabs"]], [b_rt])
        pl(lambda e: e.tensor_tensor(out=k3[:, :, 64:80], in0=rt[0][:], in1=rt[1][:], op=ADD), [b_rt], [b_k3])
        pl(lambda e: e.tensor_tensor(out=rt[2][:], in0=kr2, in1=tabs[:, 1, pgs, :], op=MUL), [b_pbt, B["tabs"]], [b_rt])
        pl(lambda e: e.tensor_tensor(out=rt[3][:], in0=kr1, in1=tabs[:, 3, pgs, :], op=MUL), [b_pbt, B["tabs"]], [b_rt])
        pl(lambda e: e.tensor_tensor(out=k3[:, :, 80:96], in0=rt[2][:], in1=rt[3][:], op=ADD), [b_rt], [b_k3])
        yield
        psT, b_psT = tbank2()
        psTb = psT[:].bitcast(BF16).rearrange("p (k n) -> p k n", k=2)
        psK, b_psK = tbank2()
        psKb = psK[:].bitcast(BF16)
        for pg in range(4):
            for kt in range(2):
                P.add("pe", lambda e, pg=pg, kt=kt: e.transpose(psTb[:, kt, 128 * pg:128 * pg + 128], pbt[:, pg, 128 * kt:128 * kt + 128],
                                                                idb[:, :]), reads=[b_pbt, b_idb], writes=[b_psT])
            P.add("pe", lambda e, pg=pg: e.transpose(psKb[0:96, 128 * pg:128 * pg + 128], k3[:, pg, :], idb[:, :]), reads=[b_k3, b_idb],
                  writes=[b_psK])
        P.add("dve", lambda e: e.tensor_copy(out=cT_sb[ci_][:], in_=psTb), reads=[b_psT], writes=[b_cT[ci_]])
        P.add("act", lambda e: e.activation(out=krT_sb[ci_][:], in_=psKb[0:96, 0:512], func=AF.Copy), reads=[b_psK], writes=[b_krT[ci_]])
        yield
        crows = [(pbt[:, pg, 0:KV_LORA], [b_pbt], 128 * pg, 128 * pg + 128) for pg in range(4)]
        yield from chunk(q, g, 512, cT_sb[ci_], [b_cT[ci_]], krT_sb[ci_][0:32, 0:512], [b_krT[ci_]], krT_sb[ci_][64:96, 0:512],
                         [b_krT[ci_]], crows, g == 0, False)

    def run_pipelined(gens, step=2):
        active = []
        it = iter(gens)
        more = True
        while more or active:
            if more:
                try:
                    active.append(next(it))
                except StopIteration:
                    more = False
            for _ in range(step):
                for gg in list(active):
                    try:
                        next(gg)
                    except StopIteration:
                        active.remove(gg)

    for q in range(SEQ_PER_CORE):
        run_pipelined([page_chunk(q, g) for g in range(NPAGES // 4)])
        ACC, b_ACC = banks[q % 2], bank_bufs[q % 2]
        Lq, b_Lq = Lacc2[q % 2], b_Lacc2[q % 2]
        c0 = NPT + 4 * q
        psn, b_psn = tbank2()
        psnb = psn[:].bitcast(BF16)
        for kt in range(2):
            P.add("pe", lambda e, kt=kt, c0=c0, psnb=psnb: e.transpose(psnb[0:4, 128 * kt:128 * kt + 128], ckvn[:, kt, c0:c0 + 4], idb[:, :]),
                  reads=[b_ckvn, b_idb], writes=[b_psn])
        P.add("act", lambda e, psnb=psnb: e.activation(out=cnew[:], in_=psnb[0:4, 0:KV_LORA], func=AF.Copy), reads=[b_psn], writes=[B["cnew"]])
        for _ in chunk(q, 16, 4, ckvn[:, :, c0:c0 + 4], [b_ckvn], kraw[64:96, 4 * q:4 * q + 4], [B["kraw"]], knew[64:96, 4 * q:4 * q + 4],
                       [B["knew"]], [(cnew[:, :], [B["cnew"]], 0, 4)], False, True):
            pass
        P.add("act", lambda e, ACC=ACC: e.activation(out=accs[:], in_=ACC[0:32, 0:KV_LORA], func=AF.Copy), reads=[b_ACC], writes=[B["accs"]])
        P.add("dve", lambda e, Lq=Lq: e.reduce_sum(out=lsum[:], in_=Lq[:, 0:17], axis=AX.X), reads=[b_Lq], writes=[B["lsum"]])
        P.add("dve", lambda e: e.reciprocal(out=lsum[:], in_=lsum[:]), reads=[B["lsum"]], writes=[B["lsum"]])
        P.add("dve", lambda e: e.tensor_scalar(out=olat[:], in0=accs[:], scalar1=lsum[:, 0:1], scalar2=None, op0=MUL),
              reads=[B["accs"], B["lsum"]], writes=[B["olat"]])
        pso, b_pso = tbank2()
        psob = pso[:].bitcast(BF16)
        for kt in range(2):
            P.add("pe", lambda e, kt=kt, psob=psob: e.transpose(psob[:, 32 * kt:32 * kt + 32], olat[:, 128 * kt:128 * kt + 128],
                                                                idb[0:32, 0:32]), reads=[B["olat"], b_idb], writes=[b_pso])
        P.add("act", lambda e, q=q, psob=psob: e.activation(out=olT[:, :, q, :], in_=psob[:, 0:64].rearrange("p (k c) -> p k c", k=2),
                                                            func=AF.Copy), reads=[b_pso], writes=[B["olT"]])
    for hp in range(4):
        ps, b_ps = tbank()
        for hh in range(2):
            h = 2 * hp + hh
            for kt in range(2):
                P.add("pe", lambda e, ps=ps, hh=hh, h=h, kt=kt: e.matmul(
                    ps[64 * hh:64 * hh + 64, 0:NST], lhsT=wv_sb[:, kt, 64 * h:64 * h + 64], rhs=olT[:, kt, :, 4 * h:4 * h + 4],
                    start=(kt == 0), stop=(kt == 1), tile_position=(0, 64 * hh)), reads=[B["wv"], B["olT"]], writes=[b_ps])
        P.add("act", lambda e, ps=ps, hp=hp: e.activation(out=attT[:, hp, NPT:T], in_=ps[:, 0:NST], func=AF.Copy), reads=[b_ps],
              writes=[b_attT])


def build(stage=99, n_pool=10240, dbg=False):
    nc = bass.Bass("TRN2", target_bir_lowering=False)
    P = Prog(nc)
    ins_, outs_ = {}, {}

    def din(name, shape, dt=F32):
        ins_[name] = nc.dram_tensor(name, list(shape), dt, kind="ExternalInput").ap()
        return ins_[name]

    def dout(name, shape, dt=F32):
        outs_[name] = nc.dram_tensor(name, list(shape), dt, kind="ExternalOutput").ap()
        return outs_[name]

    def dint(name, shape, dt):
        return nc.dram_tensor(name, list(shape), dt).ap()

    xT = din("xT", [D, T])
    small = din("small", [128, SL["_n"]])
    rope_c = din("rope_c", [96, T])
    rope_s = din("rope_s", [96, T])
    w_in = din("w_in", [D, IN_COLS])
    o_ckvT = dout("o_ckvT", [KV_LORA, T])
    o_krT = dout("o_krT", [QK_ROPE, T])

    w_in_b = dint("w_in_b", [D, IN_COLS], BF16)

    stores = []
    pool_q = "pool"

    def cast_w(dst, src, rows, cols, key):
        a = 1
        while cols // a > 2048 or cols % a:
            a += 1
        s2 = src.rearrange("k (a m) -> (k a) m", a=a) if a > 1 else src
        d2 = dst.rearrange("k (a m) -> (k a) m", a=a) if a > 1 else dst
        b = P.buf(key)
        P.add(pool_q, lambda e: e.dma_start(out=d2, in_=s2), writes=[b], dma=True, semkey=key)
        return b

    b_w_in_b = cast_w(w_in_b, w_in, D, IN_COLS, "c_w_in")
    w_glu = din("w_glu", [SSM_W, 2 * SSM_W])
    w_glu_b = dint("w_glu_b", [SSM_W, 2 * SSM_W], BF16)
    b_w_glu_b = cast_w(w_glu_b, w_glu, SSM_W, 2 * SSM_W, "c_w_glu")

    ones_bf = P.sbuf("ones_bf", [128, 128], BF16)
    b_ones = P.buf("ones")
    P.add("pool", lambda e: e.memset(ones_bf[:], 1.0), writes=[b_ones])
    sm = P.sbuf("sm", [128, SL["_n"]], F32)
    b_sm = P.buf("sm")
    P.add("sp", lambda e: e.dma_start(out=sm[:], in_=small), writes=[b_sm], dma=True, semkey="sm")

    def smc(name, i=0):
        o = SL[name][0] + i
        return sm[:, o:o + 1]

    NB = 8
    banks = [P.psum(f"ps{i}", [128, 512], F32) for i in range(NB)]
    bank_bufs = [P.buf(f"ps{i}") for i in range(NB)]
    bank_rr = [0]

    def next_bank():
        i = bank_rr[0] % NB
        bank_rr[0] += 1
        return banks[i], bank_bufs[i]

    cqn = P.sbuf("cqn", [128, 3, T], BF16)
    ckvn = P.sbuf("ckvn", [128, 2, T], BF16)
    krK = P.sbuf("krK", [96, T], F32)
    uT = P.sbuf("uT", [128, 4, T], BF16)
    b_cqn, b_ckvn, b_krT, b_uT = P.buf("cqn"), P.buf("ckvn"), P.buf("krT"), P.buf("uT")

    chunks = _token_chunks()
    AR = Arena(P, "arena", 142 * 1024)

    NA = OFF_GA
    wA = AR.alloc([128, KD, NA], BF16)
    b_wA = P.buf("wA")
    P.add("sp", lambda e: e.dma_start(out=wA[:], in_=w_in_b.rearrange("(k p) m -> p k m", p=128)[:, :, 0:NA]),
          reads=[b_w_in_b], writes=[b_wA], dma=True, semkey="wA")

    xc = [AR.alloc([128, KD, 512], F32) for i in range(2)]
    b_xc = [P.buf(f"xc{i}") for i in range(2)]
    sq = AR.alloc([128, KD, 512], BF16)
    b_sq = P.buf("sq")
    lnv = AR.alloc([128, 512], F32)
    b_lnv = P.buf("lnv")
    rstd = AR.alloc([128, 512], F32)
    b_rstd = P.buf("rstd")
    xn = AR.alloc([128, KD, 512], BF16)
    b_xn = P.buf("xn")
    cqf = AR.alloc([128, 3, 512], F32)
    b_cqf = P.buf("cqf")
    ckvf = AR.alloc([128, 2, 512], F32)
    b_ckvf = P.buf("ckvf")
    ckvo = AR.alloc([128, 2, 512], F32)
    b_ckvo = P.buf("ckvo")

    xT_v = xT.rearrange("(k p) t -> p k t", p=128)

    def rms_rstd(src, b_src, nk, n, nfeat, dst=rstd, b_dst=b_rstd):
        P.add("dve", lambda e: e.tensor_tensor(out=sq[:, 0:nk, 0:n], in0=src[:, 0:nk, 0:n], in1=src[:, 0:nk, 0:n],
                                               op=ALU.mult), reads=[b_src], writes=[b_sq])
        ps, b_ps = next_bank()
        for k in range(nk):
            P.add("pe", lambda e, k=k: e.matmul(ps[:, 0:n], lhsT=ones_bf[:], rhs=sq[:, k, 0:n], start=(k == 0),
                                                stop=(k == nk - 1)), reads=[b_sq, b_ones], writes=[b_ps])
        P.add("act", lambda e: e.activation(out=lnv[:, 0:n], in_=ps[:, 0:n], func=AF.Ln, scale=1.0 / nfeat, bias=EPS),
              reads=[b_ps], writes=[b_lnv])
        P.add("act", lambda e: e.activation(out=dst[:, 0:n], in_=lnv[:, 0:n], func=AF.Exp, scale=-0.5),
              reads=[b_lnv], writes=[b_dst])

    for ci, (n0, n) in enumerate(chunks):
        xb, b_x = xc[ci % 2], b_xc[ci % 2]
        P.add("sp", lambda e, xb=xb, n0=n0, n=n: e.dma_start(out=xb[:, :, 0:n], in_=xT_v[:, :, n0:n0 + n]),
              writes=[b_x], dma=True, semkey=f"xc{ci % 2}")
        rms_rstd(xb, b_x, KD, n, D)
        for k in range(KD):
            P.add("dve", lambda e, k=k, xb=xb, n=n: e.scalar_tensor_tensor(
                out=xn[:, k, 0:n], in0=xb[:, k, 0:n], scalar=smc("g_mix", k), in1=rstd[:, 0:n],
                op0=ALU.mult, op1=ALU.mult), reads=[b_x, b_rstd, b_sm], writes=[b_xn])
        groups = [("cq", i, OFF_CKV * 0 + 128 * i, 128) for i in range(3)] + \
                 [("ckv", i, OFF_CKV + 128 * i, 128) for i in range(2)] + \
                 [("kr", 0, OFF_KR, 32)] + [("u", i, OFF_U + 128 * i, 128) for i in range(4)]
        for kind, i, c0, m in groups:
            ps, b_ps = next_bank()
            for k in range(KD):
                if kind == "kr":
                    P.add("pe", lambda e, k=k, c0=c0, m=m, ps=ps, n=n: e.matmul(
                        ps[64:96, 0:n], lhsT=wA[:, k, c0:c0 + m], rhs=xn[:, k, 0:n], start=(k == 0), stop=(k == KD - 1),
                        tile_position=(0, 64)), reads=[b_wA, b_xn], writes=[b_ps])
                    continue
                P.add("pe", lambda e, k=k, c0=c0, m=m, ps=ps, n=n: e.matmul(
                    ps[0:m, 0:n], lhsT=wA[:, k, c0:c0 + m], rhs=xn[:, k, 0:n], start=(k == 0), stop=(k == KD - 1)),
                    reads=[b_wA, b_xn], writes=[b_ps])
            if kind == "cq":
                P.add("act", lambda e, i=i, ps=ps, n=n: e.activation(out=cqf[:, i, 0:n], in_=ps[:, 0:n], func=AF.Copy),
                      reads=[b_ps], writes=[b_cqf])
            elif kind == "ckv":
                P.add("act", lambda e, i=i, ps=ps, n=n: e.activation(out=ckvf[:, i, 0:n], in_=ps[:, 0:n], func=AF.Copy),
                      reads=[b_ps], writes=[b_ckvf])
            elif kind == "kr":
                P.add("act", lambda e, ps=ps, n=n, n0=n0: e.activation(out=krK[64:96, n0:n0 + n], in_=ps[64:96, 0:n], func=AF.Copy),
                      reads=[b_ps], writes=[b_krT])
            else:
                P.add("act", lambda e, i=i, ps=ps, n=n, n0=n0: e.activation(out=uT[:, i, n0:n0 + n], in_=ps[:, 0:n], func=AF.Copy),
                      reads=[b_ps], writes=[b_uT])
        rms_rstd(cqf, b_cqf, 3, n, Q_LORA)
        for i in range(3):
            P.add("dve", lambda e, i=i, n=n, n0=n0: e.scalar_tensor_tensor(
                out=cqn[:, i, n0:n0 + n], in0=cqf[:, i, 0:n], scalar=smc("g_cq", i), in1=rstd[:, 0:n],
                op0=ALU.mult, op1=ALU.mult), reads=[b_cqf, b_rstd, b_sm], writes=[b_cqn])
        rms_rstd(ckvf, b_ckvf, 2, n, KV_LORA)
        for i in range(2):
            P.add("dve", lambda e, i=i, n=n: e.scalar_tensor_tensor(
                out=ckvo[:, i, 0:n], in0=ckvf[:, i, 0:n], scalar=smc("g_ckv", i), in1=rstd[:, 0:n],
                op0=ALU.mult, op1=ALU.mult), reads=[b_ckvf, b_rstd, b_sm], writes=[b_ckvo])
        P.add("pool", lambda e, n=n, n0=n0: e.tensor_copy(out=ckvn[:, :, n0:n0 + n], in_=ckvo[:, :, 0:n]),
              reads=[b_ckvo], writes=[b_ckvn])
        stores.append(P.add("sp", lambda e, n=n, n0=n0: e.dma_start(
            out=o_ckvT.rearrange("(k p) t -> p k t", p=128)[:, :, n0:n0 + n], in_=ckvo[:, :, 0:n]),
            reads=[b_ckvo], dma=True, semkey="st_ckv"))
    stores.append(P.add("sp", lambda e: e.dma_start(out=o_krT, in_=krK[64:96, :]), reads=[b_krT], dma=True, semkey="st_kr"))


    if stage >= 2:
        ssm = build_ssm(P, AR, nc, din, dout, dint, stores, next_bank, uT, b_uT, smc, b_sm, sm, w_glu_b, b_w_glu_b, chunks)
        if dbg:
            o_dbg_ys = dout("o_dbg_ys", [128, 4, T], BF16)
            stores.append(P.add("sp", lambda e: e.dma_start(out=o_dbg_ys, in_=uT[:]), reads=[b_uT], dma=True, semkey="dbg_ys"))


    if stage >= 3:
        attT = P.sbuf("attT", [128, 4, T], BF16)
        b_attT = P.buf("attT")
        kS = P.sbuf("kS", [96, 8, NST], BF16)
        b_kS = P.buf("kS")
        att = build_attn(P, AR, nc, din, dout, dint, stores, banks, bank_bufs, cast_w, cqn, b_cqn, ckvn, b_ckvn, krK, b_krT,
                         sm, b_sm, smc, ones_bf, b_ones, rope_c, rope_s, attT, b_attT, chunks, kS, b_kS)
        if stage >= 5:
            build_sample_attn(P, AR, nc, din, dout, dint, stores, banks, bank_bufs, sm, b_sm, smc, ones_bf, b_ones, ckvn, b_ckvn,
                              krK, b_krT, att["qT"], att["b_qT"], attT, b_attT, n_pool, att["w_uk_b"], att["b_wkb"], att["w_uv_b"],
                              att["b_wvb"], rope_c, rope_s, att["rotm_d"], att["mC1"])
        if dbg:
            o_dbg_att = dout("o_dbg_att", [128, 4, T], BF16)
            stores.append(P.add("sp", lambda e: e.dma_start(out=o_dbg_att, in_=attT[:]), reads=[b_attT], dma=True, semkey="dbg_att"))


    if stage >= 4:
        build_tail(P, AR, nc, din, dout, dint, stores, banks, bank_bufs, cast_w, xT, w_in_b, b_w_in_b, sm, b_sm, smc, ones_bf, b_ones,
                   attT, b_attT, ssm["ysT"], ssm["b_ys"], chunks)

    P.add("sp", lambda e: None, after=stores)
    P.finalize()
    return nc, ins_, outs_, P


def _rope_tables(pos):
    inv_freq = np.power(np.float32(10000.0), -np.arange(0, QK_ROPE, 2, dtype=np.float32) / np.float32(QK_ROPE)).astype(np.float32)
    ang = pos.astype(np.float32)[:, None] * inv_freq[None, :]
    return np.cos(ang).astype(np.float32), np.sin(ang).astype(np.float32)


def _prep_core(c, inp):
    b, j = c // 4, c % 4
    m = {}
    xp = inp["x_prompt"][b, NPT * j:NPT * (j + 1)]
    xs = inp["x_sample"][SEQ_PER_CORE * c:SEQ_PER_CORE * (c + 1)].reshape(NST, D)
    m["xT"] = np.ascontiguousarray(np.concatenate([xp, xs], 0).T)
    pos = np.concatenate([NPT * j + np.arange(NPT), np.tile(PAST + np.arange(4), SEQ_PER_CORE)])
    cs, sn = _rope_tables(pos)
    rc = np.ones((96, T), np.float32)
    rs = np.zeros((96, T), np.float32)
    rc[64:80] = cs.T
    rc[80:96] = cs.T
    rs[64:80] = sn.T
    rs[80:96] = sn.T
    m["rope_c"], m["rope_s"] = rc, rs
    sm = np.zeros((128, SL["_n"]), np.float32)

    def put(name, arr):
        o, n = SL[name]
        sm[:arr.shape[0], o:o + arr.shape[1]] = arr
    put("g_mix", inp["g_mix"][0].reshape(8, 128).T)
    put("g_cq", inp["g_cq"][0].reshape(3, 128).T)
    put("g_ckv", inp["g_ckv"][0].reshape(2, 128).T)
    put("g_ffn", inp["g_ffn"][0].reshape(8, 128).T)
    put("g_ple", inp["g_ple"][0].reshape(8, 128).T)
    cw = inp["conv_w"][0].reshape(3, 44, 128)
    put("conv_w", cw.transpose(2, 0, 1).reshape(128, 132))
    put("conv_b", inp["conv_b"][0].reshape(44, 128).T)
    put("g_q", inp["g_q"][0].reshape(96, 1))
    put("g_k", inp["g_k"][0].reshape(96, 1))
    put("d_skip", inp["d_skip"][0].reshape(4, 128).T)
    put("vis", np.tile((np.arange(4) <= j).astype(np.float32)[None], (128, 1)))
    put("full", np.tile((np.arange(4) < j).astype(np.float32)[None], (128, 1)))
    put("ident", np.eye(128, dtype=np.float32))
    put("hsel", np.tile((np.arange(4) == j - 1).astype(np.float32)[None], (128, 1)))
    m["small"] = sm
    m["w_in"] = inp["w_in"][0]
    m["w_glu"] = inp["w_glu"][0]
    m["w_uq"] = inp["w_uq"][0].reshape(Q_LORA, 768)
    m["w_oa"], m["w_os"], m["w_out"] = inp["w_oa"][0], inp["w_os"][0], inp["w_out"][0]
    m["w_up"], m["w_down"] = inp["w_up"][0], inp["w_down"][0]
    m["w_pg"], m["w_pp"] = inp["w_ple_gate"][0], inp["w_ple_proj"][0]
    pp_ = inp["p_prompt"][0, b, NPT * j:NPT * (j + 1)]
    ps_ = inp["p_sample"][0, SEQ_PER_CORE * c:SEQ_PER_CORE * (c + 1)].reshape(NST, PLE)
    m["pT"] = np.ascontiguousarray(np.concatenate([pp_, ps_], 0).T)
    sc = inp["state_conv"][0, SEQ_PER_CORE * c:SEQ_PER_CORE * (c + 1)]
    m["scT"] = np.ascontiguousarray(sc.reshape(16, 2, 44, 128).transpose(3, 2, 0, 1))
    m["w_uk"] = inp["w_uk"][0].reshape(KV_LORA, 512)
    m["w_uv"] = inp["w_uv"][0].reshape(KV_LORA, 512)
    tri = (np.arange(128)[:, None] <= np.arange(128)[None, :]).astype(np.float32)
    md = np.zeros((128, 4, 128), np.float32)
    for r in range(4):
        vis, full = float(r <= j), float(r < j)
        md[:, r, :] = full + (vis - full) * tri
    m["maskd"] = md
    pt_ = inp["page_table"][SEQ_PER_CORE * c:SEQ_PER_CORE * (c + 1)].astype(np.int32).reshape(16, 16, 4)
    m["ptab"] = np.ascontiguousarray(np.repeat(pt_.transpose(2, 0, 1).reshape(4, 256), 32, axis=0))
    m["p32c"] = (np.arange(128, dtype=np.int32) % 32).reshape(128, 1)
    m["w_ukT"] = np.ascontiguousarray(inp["w_uk"][0].transpose(2, 1, 0).reshape(64, 8 * KV_LORA))
    cs_p, sn_p = _rope_tables(np.arange(PAST))
    rp = np.stack([cs_p, sn_p], 0).reshape(2, 16, 4, 32, 4, 16)
    m["ropeP"] = np.ascontiguousarray(rp.transpose(2, 3, 0, 1, 4, 5).reshape(128, 2, NPAGES, 16))
    m["gk_rep"] = np.tile(inp["g_k"][0, 64:96][None], (128, 1)).astype(np.float32)
    hs_ = np.zeros((128, 4, 32), np.float32)
    for mm in range(4):
        for hh in range(2):
            hs_[64 * hh:64 * hh + 64, mm, 4 * (2 * mm + hh):4 * (2 * mm + hh) + 4] = 1.0
    m["hselm"] = hs_
    cm = np.zeros((32, 4), np.float32)
    for h_ in range(8):
        for t_ in range(4):
            cm[4 * h_ + t_, :t_ + 1] = 1.0
    m["cmask"] = cm
    rot = np.zeros((96, 96), np.float32)
    for i in range(16):
        rot[80 + i, 64 + i] = -1.0
        rot[64 + i, 80 + i] = 1.0
    m["rotm"] = rot
    m["ssm_s"], m["ssm_r"] = _ssm_packs(c, inp)
    return m


def _ssm_packs(c, inp):
    j = c % 4
    a_re, a_im, logdt = inp["a_re"][0], inp["a_im"][0], inp["log_dt"][0]
    b_re, b_im, c_re, c_im = inp["b_re"][0], inp["b_im"][0], inp["c_re"][0], inp["c_im"][0]

    def st(a):
        return a.reshape(16, 2, 64).transpose(1, 2, 0).reshape(128, 16)
    ss = np.zeros((128, SSL["_n"]), np.float32)

    def put(lay, arr_, name, arr):
        o, n = lay[name]
        arr_[:, o:o + n] = arr.reshape(128, n)
    put(SSL, ss, "a_re", st(a_re))
    put(SSL, ss, "a_im", st(a_im))
    put(SSL, ss, "logdt", st(np.repeat(logdt[:, None], 64, 1)))
    for nm, cc in (("c_re", c_re), ("c_im", c_im)):
        c4 = cc.reshape(16, 2, 16, 64)
        pad = np.zeros((2, 64, 16, 2, 16), np.float32)
        for g2 in range(2):
            pad[g2, :, :, g2, :] = c4[:, g2].transpose(2, 0, 1)
        put(SSL, ss, nm, pad)
    for nm, bb in (("b_re", b_re), ("b_im", b_im)):
        b4 = bb.reshape(16, 2, 64, 16)
        pad = np.zeros((2, 64, 16, 2, 16), np.float32)
        for g2 in range(2):
            pad[g2, :, :, g2, :] = b4[:, g2].transpose(1, 0, 2)
        put(SSL, ss, nm, pad)
    for nm, key in (("h0_re", "state_ssm_re"), ("h0_im", "state_ssm_im")):
        h = inp[key][0, SEQ_PER_CORE * c:SEQ_PER_CORE * (c + 1)]
        h4 = h.reshape(16, 16, 2, 64)
        put(SSL, ss, nm, h4.transpose(2, 3, 1, 0))
    m = np.zeros((128, 12), np.float32)
    for i in range(4):
        n = j - 1 - i
        if 0 <= n <= 2:
            m[:, 3 * i + n] = 1.0
    put(SSL, ss, "msk", m)
    blk = np.zeros((128, 4), np.float32)
    for k4 in range(4):
        blk[32 * k4:32 * k4 + 32, k4] = 1.0
    put(SSL, ss, "blk", blk)
    sr = np.zeros((128, SRL["_n"]), np.float32)

    def rowrep(a):
        a4 = a.reshape(4, 4, 2, 64)
        out = np.zeros((4, 2, 16, 4, 64), np.float32)
        out[:] = a4.transpose(1, 2, 0, 3)[:, :, None, :, :]
        return out
    put(SRL, sr, "a_re", rowrep(a_re))
    put(SRL, sr, "a_im", rowrep(a_im))
    put(SRL, sr, "logdt", rowrep(np.repeat(logdt[:, None], 64, 1)))
    for nm, bb in (("b_re", b_re), ("b_im", b_im)):
        b5 = bb.reshape(4, 4, 2, 64, 16)
        pad = np.zeros((4, 2, 16, 4, 2, 64), np.float32)
        for g2 in range(2):
            pad[:, g2, :, :, g2, :] = b5[:, :, g2].transpose(1, 3, 0, 2)
        put(SRL, sr, nm, pad)
    put(SRL, sr, "ident", np.eye(128, dtype=np.float32))
    return ss, sr


_CACHE = {}


def kernel(**inputs):
    inp = {k: np.asarray(v) for k, v in inputs.items()}
    if "nc" not in _CACHE:
        _CACHE["nc"] = build()
    nc, ins_, outs_, P = _CACHE["nc"]
    in_maps = []
    cache = None
    if "cache" in ins_:
        cache = np.concatenate([inp["cache_ckv"][0], inp["cache_kr"][0]], axis=-1).reshape(-1, 4 * (KV_LORA + QK_ROPE))
    for c in range(8):
        m = _prep_core(c, inp)
        if cache is not None:
            m["cache"] = cache
        in_maps.append({k: np.ascontiguousarray(m[k]) for k in ins_})
    res = run_bass_kernel_spmd(nc, in_maps, core_ids=list(range(8)))
    R = res.results
    f32 = np.float32
    ckv_p = np.zeros((1, 2, 8192, KV_LORA), f32)
    kr_p = np.zeros((1, 2, 8192, QK_ROPE), f32)
    ckv_s = np.zeros((1, 128, 4, KV_LORA), f32)
    kr_s = np.zeros((1, 128, 4, QK_ROPE), f32)
    for c in range(8):
        b, j = c // 4, c % 4
        ck = R[c]["o_ckvT"].T
        kr = R[c]["o_krT"].T
        ckv_p[0, b, NPT * j:NPT * (j + 1)] = ck[:NPT]
        kr_p[0, b, NPT * j:NPT * (j + 1)] = kr[:NPT]
        ckv_s[0, 16 * c:16 * c + 16] = ck[NPT:].reshape(16, 4, KV_LORA)
        kr_s[0, 16 * c:16 * c + 16] = kr[NPT:].reshape(16, 4, QK_ROPE)
    yp = np.zeros((2, 8192, D), f32)
    ys = np.zeros((128, 4, D), f32)
    cv_p = np.zeros((1, 2, 2, 2 * D_FF), f32)
    cv_s = np.zeros((1, 128, 2, 2 * D_FF), f32)
    if "o_yT" in R[0]:
        for c in range(8):
            b, j = c // 4, c % 4
            y = R[c]["o_yT"].T
            yp[b, NPT * j:NPT * (j + 1)] = y[:NPT]
            ys[16 * c:16 * c + 16] = y[NPT:].reshape(16, 4, D)
            cs_ = R[c]["o_cvs"]
            cv_s[0, 16 * c:16 * c + 16] = cs_.transpose(2, 3, 1, 0).reshape(16, 2, 2 * D_FF)
            if j == 3:
                cv_p[0, b] = R[c]["o_cvp"].transpose(2, 1, 0).reshape(2, 2 * D_FF)
    z = lambda *s: np.zeros(s, f32)
    sre_p, sim_p, sre_s, sim_s = z(1, 2, 32, 64), z(1, 2, 32, 64), z(1, 128, 32, 64), z(1, 128, 32, 64)
    if "o_hp" in R[0]:
        for c in range(8):
            b, j = c // 4, c % 4
            hs_ = R[c]["o_hs"].reshape(2, 64, 2, 16, 16)
            hs_ = hs_.transpose(2, 4, 3, 0, 1).reshape(2, 16, 32, 64)
            sre_s[0, 16 * c:16 * c + 16] = hs_[0]
            sim_s[0, 16 * c:16 * c + 16] = hs_[1]
            if j == 3:
                hp_ = R[c]["o_hp"].reshape(2, 64, 2, 16).transpose(2, 3, 0, 1).reshape(2, 32, 64)
                sre_p[0, b] = hp_[0]
                sim_p[0, b] = hp_[1]
    return (yp, ys, ckv_p, kr_p, ckv_s, kr_s, sre_p, sim_p, sre_s, sim_s, cv_p, cv_s)
```

```python
import contextlib
import numpy as np
import concourse.bass as bass
import concourse.mybir as mybir
from concourse.bass_utils import run_bass_kernel_spmd

F32 = mybir.dt.float32
BF16 = mybir.dt.bfloat16
I32 = mybir.dt.int32
AF = mybir.ActivationFunctionType
ALU = mybir.AluOpType
AX = mybir.AxisListType

D = 1024
NPT = 2048
NST = 64
T = NPT + NST
KD = D // 128
N_HEADS = 8
QK_NOPE, QK_ROPE, QK_HEAD, V_HEAD = 64, 32, 96, 64
Q_LORA, KV_LORA = 384, 256
SSM_W, GROUP, N_GROUPS, STATE = 512, 16, 32, 64
D_FF = 2816
PLE = 256
EPS = 1e-6
OFF_CKV = Q_LORA
OFF_KR = OFF_CKV + KV_LORA
OFF_U = OFF_KR + QK_ROPE
OFF_GA = OFF_U + SSM_W
OFF_GS = OFF_GA + D
IN_COLS = OFF_GS + D
SCALE = QK_HEAD ** -0.5
PAST = 8192
PAGE = 128
NPAGES = 64
SEQ_PER_CORE = 16


class Buf:
    __slots__ = ("name", "last_w", "readers")

    def __init__(self, name, fence=()):
        self.name = name
        self.last_w = None
        self.readers = list(fence)


class Op:
    __slots__ = ("eng", "fn", "deps", "dma", "sem", "value", "marked", "idx")

    def __init__(self, eng, fn, dma):
        self.eng = eng
        self.fn = fn
        self.dma = dma
        self.deps = ()
        self.sem = None
        self.value = 0
        self.marked = False
        self.idx = 0


class Prog:
    ENGS = ("pe", "act", "dve", "pool", "sp")

    def __init__(self, nc):
        self.nc = nc
        self.ops = {e: [] for e in self.ENGS}
        self.stack = contextlib.ExitStack()
        self.dma_sems = {}
        self.nbuf = 0
        self.fence = []
        self.live = []

    def sbuf(self, name, shape, dtype):
        return self.stack.enter_context(self.nc.sbuf_tensor(name, list(shape), dtype))

    def psum(self, name, shape, dtype=F32):
        return self.stack.enter_context(self.nc.psum_tensor(name, list(shape), dtype))

    def sem(self, name):
        return self.stack.enter_context(self.nc.semaphore(name))

    def buf(self, name=None):
        self.nbuf += 1
        b = Buf(name or f"b{self.nbuf}", self.fence)
        self.live.append(b)
        return b

    def new_phase(self):
        f = []
        for b in self.live:
            if b.last_w is not None:
                f.append(b.last_w)
            f.extend(b.readers)
        self.fence = list(dict.fromkeys(f))[-64:] if False else list(dict.fromkeys(f))
        self.live = []

    def add(self, eng, fn, reads=(), writes=(), dma=False, semkey=None, after=()):
        op = Op(eng, fn, dma)
        deps = set(after)
        for b in reads:
            if b.last_w is not None:
                deps.add(b.last_w)
        for b in writes:
            if b.last_w is not None:
                deps.add(b.last_w)
            deps.update(b.readers)
        op.deps = tuple(deps)
        for b in reads:
            b.readers.append(op)
        for b in writes:
            b.last_w = op
            b.readers = []
        op.idx = len(self.ops[eng])
        self.ops[eng].append(op)
        if dma:
            key = semkey if semkey is not None else id(op)
            if key not in self.dma_sems:
                self.dma_sems[key] = [self.sem(f"dq{len(self.dma_sems)}"), 0]
            ent = self.dma_sems[key]
            ent[1] += (1 if dma == "cc" else 16)
            op.sem = ent[0]
            op.value = ent[1]
            op.marked = True
        return op

    def finalize(self):
        nc = self.nc
        esem = {e: self.sem(f"eng_{e}") for e in self.ENGS}
        for e in self.ENGS:
            for op in self.ops[e]:
                for d in op.deps:
                    if d.dma:
                        continue
                    if d.eng != e:
                        d.marked = True
                    elif e != "pe" and (op.idx - d.idx) <= 2:
                        d.marked = True
        for e in self.ENGS:
            c = 0
            for op in self.ops[e]:
                if op.dma:
                    continue
                if op.marked:
                    c += 1
                    op.sem = esem[e]
                    op.value = c
        self.stats = {e: len(self.ops[e]) for e in self.ENGS}

        def emit(ename, eng):
            waited = {}
            for op in self.ops[ename]:
                need = {}
                for d in op.deps:
                    if not d.marked:
                        continue
                    if (not d.dma) and d.eng == ename and (ename == "pe" or (op.idx - d.idx) > 2):
                        continue
                    k = id(d.sem)
                    if waited.get(k, 0) >= d.value:
                        continue
                    if k not in need or need[k][1] < d.value:
                        need[k] = (d.sem, d.value)
                for k, (s, v) in need.items():
                    eng.wait_ge(s, v)
                    waited[k] = v
                ins = op.fn(eng)
                if op.marked and ins is not None:
                    if op.dma == "cc":
                        ins.then_inc(op.sem, 1)
                    elif op.dma:
                        ins.then_inc(op.sem, 16)
                    else:
                        ins.then_inc(op.sem, 1)

        with nc.Block() as block:
            @block.tensor
            def _(e):
                emit("pe", e)

            @block.scalar
            def _(e):
                emit("act", e)

            @block.vector
            def _(e):
                emit("dve", e)

            @block.gpsimd
            def _(e):
                emit("pool", e)

            @block.sync
            def _(e):
                emit("sp", e)
        self.stack.close()


class Arena:
    def __init__(self, P, name, nbytes):
        self.t = P.sbuf(name, [128, nbytes // 4], F32)
        self.off = 0
        self.cap = nbytes
        self.peak = 0

    def alloc(self, shape, dtype=F32):
        esz = 2 if dtype == BF16 else 4
        n = int(np.prod(shape[1:]))
        nb = (n * esz + 31) // 32 * 32
        assert self.off + nb <= self.cap, ("arena overflow", self.off, nb, self.cap)
        v = self.t[0:shape[0], self.off // 4:(self.off + nb) // 4]
        self.off += nb
        self.peak = max(self.peak, self.off)
        if dtype != F32:
            v = v.bitcast(dtype)
        v = v[:, 0:n]
        if len(shape) > 2:
            names = "abcde"[:len(shape) - 1]
            pat = "p (" + " ".join(names) + ") -> p " + " ".join(names)
            v = v.rearrange(pat, **{c: int(d) for c, d in zip(names[1:], shape[2:])})
        return v

    def mark(self):
        return self.off

    def release(self, m):
        self.off = m


def _small_layout():
    lay = {}
    off = 0

    def put(name, n):
        nonlocal off
        lay[name] = (off, n)
        off += n
    put("g_mix", 8)
    put("g_cq", 3)
    put("g_ckv", 2)
    put("g_ffn", 8)
    put("g_ple", 8)
    put("conv_w", 3 * 44)
    put("conv_b", 44)
    put("g_q", 1)
    put("g_k", 1)
    put("d_skip", 4)
    put("vis", 4)
    put("full", 4)
    put("ident", 128)
    put("hsel", 4)
    lay["_n"] = off
    return lay


SL = _small_layout()


def _token_chunks():
    return [(0, 510), (510, 512), (1022, 512), (1534, 290), (1824, 288)]


def _pack_layout(items):
    lay, off = {}, 0
    for name, n in items:
        lay[name] = (off, n)
        off += n
    lay["_n"] = off
    return lay


SSL = _pack_layout([("a_re", 16), ("a_im", 16), ("logdt", 16), ("c_re", 512), ("c_im", 512), ("b_re", 512),
                    ("b_im", 512), ("h0_re", 256), ("h0_im", 256), ("msk", 12), ("blk", 4)])
SRL = _pack_layout([("a_re", 256), ("a_im", 256), ("logdt", 256), ("b_re", 512), ("b_im", 512), ("ident", 128)])
PWS = [1, 2, 3, 4, 8, 12, 16]
PWR = [1, 2, 3]
TWO_PI = 6.283185307179586


def build_ssm(P, AR, nc, din, dout, dint, stores, next_bank, uT, b_uT, smc, b_sm, sm, w_glu_b, b_w_glu_b, chunks):
    LOOP_ENG = "pool"
    ss_d = din("ssm_s", [128, SSL["_n"]])
    sr_d = din("ssm_r", [128, SRL["_n"]])
    o_hp = dout("o_hp", [128, 2, 16])
    o_hs = dout("o_hs", [128, 2, 16, 16])
    cc_e_in = dint("cc_e_in", [128, 32], F32)
    cc_e_out = dint("cc_e_out", [512, 32], F32)

    AR.release(0)
    P.new_phase()
    sst = AR.alloc([128, SSL["_n"]], F32)
    Wc = AR.alloc([128, 16, 4, 2, 32], BF16)
    Ktab = AR.alloc([128, 4, 4, 128], BF16)
    A4w = AR.alloc([128, 4, 4, 2, 128], BF16)
    nlim = AR.alloc([128, len(PWS), 16], F32)
    LS_pre = True
    b_sst, b_srt = P.buf("sst"), P.buf("srt")
    P.add("sp", lambda e: e.dma_start(out=sst[:], in_=ss_d), writes=[b_sst], dma=True, semkey="sst")

    def S(name, a=None):
        o, n = SSL[name]
        v = sst[:, o:o + n]
        return v if a is None else v.rearrange("p (a b) -> p a b", a=a)

    def R(name, a=None):
        o, n = SRL[name]
        v = srt[:, o:o + n]
        return v if a is None else v.rearrange("p (a b) -> p a b", a=a)

    def tt(eng, out, a, b, op, rd, wr):
        return P.add(eng, lambda e: e.tensor_tensor(out=out, in0=a, in1=b, op=op), reads=rd, writes=wr)

    def tss(eng, out, a, scalar, op, rd, wr):
        return P.add(eng, lambda e: e.tensor_single_scalar(out=out, in_=a, scalar=scalar, op=op), reads=rd, writes=wr)

    def stt(eng, out, a, scalar, b, op0, op1, rd, wr):
        return P.add("dve", lambda e: e.scalar_tensor_tensor(out=out, in0=a, scalar=scalar, in1=b, op0=op0, op1=op1),
                     reads=rd, writes=wr)

    def act(out, in_, func, rd, wr, scale=1.0, bias=0.0):
        return P.add("act", lambda e: e.activation(out=out, in_=in_, func=func, scale=scale, bias=bias), reads=rd, writes=wr)

    def cp(eng, out, in_, rd, wr):
        return P.add(eng, lambda e: e.tensor_copy(out=out, in_=in_), reads=rd, writes=wr)

    MUL, ADD, SUB = ALU.mult, ALU.add, ALU.subtract

    def lam_pow(pfx, a_re, a_im, logdt, Fd, powers, b_src, eng):
        npw = len(powers)
        bt = P.buf(pfx + "_t")
        mk = lambda nm, sh, dt_=F32: AR.alloc(sh, dt_)
        dt = mk("dt", [128, Fd]); dre = mk("dre", [128, Fd]); dim = mk("dim", [128, Fd])
        ang = mk("ang", [128, npw, Fd]); angc = mk("angc", [128, npw, Fd]); mag = mk("mag", [128, npw, Fd])
        ki = mk("ki", [128, npw, Fd], I32); kf = mk("kf", [128, npw, Fd])
        sn = mk("sn", [128, npw, Fd]); cs = mk("cs", [128, npw, Fd])
        lre = mk("lre", [128, npw, Fd]); lim = mk("lim", [128, npw, Fd])
        act(dt[:], logdt, AF.Exp, [b_src], [bt])
        tt(eng, dre[:], dt[:], a_re, MUL, [bt, b_src], [bt])
        tt(eng, dim[:], dt[:], a_im, MUL, [bt, b_src], [bt])
        for i, n in enumerate(powers):
            tss(eng, ang[:, i, :], dim[:], n / TWO_PI, MUL, [bt], [bt])
            act(mag[:, i, :], dre[:], AF.Exp, [bt], [bt], scale=float(n))
        tss(eng, angc[:], ang[:], 0.25, ADD, [bt], [bt])
        for src, dst in ((ang, sn), (angc, cs)):
            cp("dve", ki[:], src[:], [bt], [bt])
            cp("dve", kf[:], ki[:], [bt], [bt])
            tt(eng, kf[:], src[:], kf[:], SUB, [bt], [bt])
            act(dst[:], kf[:], AF.Sin, [bt], [bt], scale=6.28318)
        tt(eng, lre[:], mag[:], cs[:], MUL, [bt], [bt])
        tt(eng, lim[:], mag[:], sn[:], MUL, [bt], [bt])
        den = mk("den", [128, Fd]); t1 = mk("t1", [128, Fd]); t2 = mk("t2", [128, Fd]); nr = mk("nr", [128, Fd])
        fre = mk("fre", [128, Fd]); fim = mk("fim", [128, Fd])
        tt(eng, den[:], a_re, a_re, MUL, [b_src], [bt])
        tt(eng, t1[:], a_im, a_im, MUL, [b_src], [bt])
        tt(eng, den[:], den[:], t1[:], ADD, [bt], [bt])
        P.add("dve", lambda e: e.reciprocal(out=den[:], in_=den[:]), reads=[bt], writes=[bt])
        tss(eng, nr[:], lre[:, 0, :], -1.0, ADD, [bt], [bt])
        tt(eng, t1[:], nr[:], a_re, MUL, [bt, b_src], [bt])
        tt(eng, t2[:], lim[:, 0, :], a_im, MUL, [bt, b_src], [bt])
        tt(eng, t1[:], t1[:], t2[:], ADD, [bt], [bt])
        tt(eng, fre[:], t1[:], den[:], MUL, [bt], [bt])
        tt(eng, t1[:], lim[:, 0, :], a_re, MUL, [bt, b_src], [bt])
        tt(eng, t2[:], nr[:], a_im, MUL, [bt, b_src], [bt])
        tt(eng, t1[:], t1[:], t2[:], SUB, [bt], [bt])
        tt(eng, fim[:], t1[:], den[:], MUL, [bt], [bt])
        return dict(lre=lre, lim=lim, fre=fre, fim=fim, b=bt)

    def cmul(eng, ore, oim, are, aim, bre, bim, t1, t2, rd, wr):
        tt(eng, t1, are, bre, MUL, rd, wr)
        tt(eng, t2, aim, bim, MUL, rd, wr)
        tt(eng, ore, t1, t2, SUB, rd, wr)
        tt(eng, t1, are, bim, MUL, rd, wr)
        tt(eng, t2, aim, bre, MUL, rd, wr)
        tt(eng, oim, t1, t2, ADD, rd, wr)

    ENG = "dve"
    LS = lam_pow("ls", S("a_re"), S("a_im"), S("logdt"), 16, PWS, b_sst, ENG)
    mB0 = AR.mark()
    srt = AR.alloc([128, SRL["_n"]], F32)
    P.add("sp", lambda e: e.dma_start(out=srt[:], in_=sr_d), writes=[b_srt], dma=True, semkey="srt")
    LR = lam_pow("lr", R("a_re"), R("a_im"), R("logdt"), 256, PWR, b_srt, ENG)
    bS, bR = LS["b"], LR["b"]
    pi = {n: i for i, n in enumerate(PWS)}

    def bc(ap2, n):
        return ap2.unsqueeze(2).broadcast_to([128, 16, n])

    bbs_re = AR.alloc([128, 16, 32], F32); bbs_im = AR.alloc([128, 16, 32], F32)
    x_re = AR.alloc([128, 16, 32], F32); x_im = AR.alloc([128, 16, 32], F32)
    u1 = AR.alloc([128, 16, 32], F32); u2 = AR.alloc([128, 16, 32], F32)
    negc_im = AR.alloc([128, 16, 32], F32)
    cmul(ENG, bbs_re[:], bbs_im[:], bc(LS["fre"][:], 32), bc(LS["fim"][:], 32), S("b_re", 16), S("b_im", 16),
         u1[:], u2[:], [bS, b_sst], [bS])
    tss(ENG, negc_im[:], S("c_im", 16), -1.0, MUL, [b_sst], [bS])

    b_Wc = P.buf("Wc")
    Wc_v = Wc[:].rearrange("p (kk k4) s c n -> p kk k4 s c n", k4=4)
    u1_v = u1[:].rearrange("p (kk k4) n -> p kk k4 n", k4=4)
    u2_v = u2[:].rearrange("p (kk k4) n -> p kk k4 n", k4=4)
    for s in range(4):
        lr_, li_ = LS["lre"][:, pi[s + 1], :], LS["lim"][:, pi[s + 1], :]
        tt(ENG, u1[:], S("c_re", 16), bc(lr_, 32), MUL, [b_sst, bS], [bS])
        tt(ENG, u2[:], S("c_im", 16), bc(li_, 32), MUL, [b_sst, bS], [bS])
        for k4 in range(4):
            tt(ENG, Wc_v[:, :, k4, s, 0, :], u1_v[:, :, k4, :], u2_v[:, :, k4, :], SUB, [bS], [bS, b_Wc])
        tt(ENG, u1[:], S("c_re", 16), bc(li_, 32), MUL, [b_sst, bS], [bS])
        tt(ENG, u2[:], S("c_im", 16), bc(lr_, 32), MUL, [b_sst, bS], [bS])
        for k4 in range(4):
            stt(ENG, Wc_v[:, :, k4, s, 1, :], u1_v[:, :, k4, :], -1.0, u2_v[:, :, k4, :], MUL, SUB,
                [bS], [bS, b_Wc])

    b_Kt = P.buf("Ktab")
    Kc = AR.alloc([128, 4, 32], F32)
    Kf = AR.alloc([128, 4, 128], F32)
    b_Kc, b_Kf = P.buf("Kc"), P.buf("Kf")
    xs_re = AR.alloc([128, 4, 16, 32], F32); xs_im = AR.alloc([128, 4, 16, 32], F32)
    b_xs = P.buf("xs")
    cp(ENG, xs_re[:, 0], bbs_re[:], [bS], [b_xs])
    cp(ENG, xs_im[:, 0], bbs_im[:], [bS], [b_xs])
    for tau in range(1, 4):
        cmul(ENG, xs_re[:, tau], xs_im[:, tau], bc(LS["lre"][:, pi[tau], :], 32), bc(LS["lim"][:, pi[tau], :], 32),
             bbs_re[:], bbs_im[:], u1[:], u2[:], [bS], [bS, b_xs])
    for kk in range(4):
        ps, b_ps = next_bank()
        for k4 in range(4):
            k = 4 * kk + k4
            for tau in range(4):
                o = ps[32 * k4:32 * k4 + 32, tau * 32:tau * 32 + 32]
                P.add("pe", lambda e, o=o, k=k, tau=tau, k4=k4: e.matmul(
                    o, lhsT=xs_re[:, tau, k, :], rhs=S("c_re", 16)[:, k, :], start=True, stop=False,
                    tile_position=(0, 32 * k4)), reads=[b_xs, b_sst], writes=[b_ps])
                P.add("pe", lambda e, o=o, k=k, tau=tau, k4=k4: e.matmul(
                    o, lhsT=xs_im[:, tau, k, :], rhs=negc_im[:, k, :], start=False, stop=True,
                    tile_position=(0, 32 * k4)), reads=[b_xs, bS], writes=[b_ps])
        cp("dve", Kc[:], ps[:, 0:128].rearrange("p (t n) -> p t n", t=4), [b_ps], [b_Kc])
        for k4 in range(4):
            tss("dve", Kf[:, :, 32 * k4:32 * k4 + 32], Kc[:], S("blk")[:, k4:k4 + 1], MUL, [b_Kc, b_sst], [b_Kf])
        stt("dve", Kf[:, 0, :], R("ident"), smc("d_skip", kk), Kf[:, 0, :], MUL, ADD, [b_srt, b_sm, b_Kf], [b_Kf])
        cp("dve", Ktab[:, kk], Kf[:], [b_Kf], [b_Kt])

    b_A4 = P.buf("A4w")
    bbr_re = AR.alloc([128, 4, 2, 64], F32); bbr_im = AR.alloc([128, 4, 2, 64], F32)
    r1 = AR.alloc([128, 4, 2, 64], F32); r2 = AR.alloc([128, 4, 2, 64], F32)
    bcr = lambda t: t.rearrange("p (kk q) -> p kk q", kk=4).unsqueeze(2).broadcast_to([128, 4, 2, 64])
    v4 = lambda t: t.rearrange("p (kk g q) -> p kk g q", kk=4, g=2)
    cmul(ENG, bbr_re[:], bbr_im[:], bcr(LR["fre"][:]), bcr(LR["fim"][:]), v4(R("b_re")), v4(R("b_im")), r1[:], r2[:],
         [bR, b_srt], [bR])
    a4v = lambda s_, c_: A4w[:, :, s_, c_, :].rearrange("p kk (g q) -> p kk g q", g=2)
    cp(ENG, a4v(3, 0), bbr_re[:], [bR], [b_A4])
    cp(ENG, a4v(3, 1), bbr_im[:], [bR], [b_A4])
    pir = {n: i for i, n in enumerate(PWR)}
    for s in range(3):
        n = 3 - s
        cmul(ENG, a4v(s, 0), a4v(s, 1), bcr(LR["lre"][:, pir[n], :]), bcr(LR["lim"][:, pir[n], :]),
             bbr_re[:], bbr_im[:], r1[:], r2[:], [bR], [bR, b_A4])

    tss(ENG, nlim[:], LS["lim"][:], -1.0, MUL, [bS], [bS])
    Lre = lambda n, k: LS["lre"][:, pi[n], k:k + 1]
    Lim = lambda n, k: LS["lim"][:, pi[n], k:k + 1]
    nLim = lambda n, k: nlim[:, pi[n], k:k + 1]

    AR.release(mB0)
    P.new_phase()
    uP = [uT[:, kk, 0:NPT].rearrange("p (c e) -> p e c", e=16) for kk in range(4)]
    uS = [uT[:, kk, NPT:T].rearrange("p (q s) -> p s q", s=4) for kk in range(4)]

    S16 = [AR.alloc([128, 16, 128], F32) for c in range(2)]
    b_S16 = P.buf("S16")
    H16 = [AR.alloc([128, 16, 129], F32) for c in range(2)]
    b_H = [P.buf("H16re"), P.buf("H16im")]
    pr = [AR.alloc([128, 2, 128], F32) for i in range(2)]
    b_pr = [P.buf("pr0"), P.buf("pr1")]

    def s4_matmuls(k, n_c, usrc, nj):
        kk, k4 = divmod(k, 4)
        out = []
        for comp in range(2):
            ps, b_ps = next_bank()
            for j in range(nj):
                for s in range(4):
                    rhs = usrc[kk][32 * k4:32 * k4 + 32, 4 * j + s, :]
                    P.add("pe", lambda e, ps=ps, j=j, s=s, rhs=rhs, comp=comp, kk=kk, k4=k4: e.matmul(
                        ps[:, j * n_c:(j + 1) * n_c], lhsT=A4w[32 * k4:32 * k4 + 32, kk, s, comp, :], rhs=rhs,
                        start=(s == 0), stop=(s == 3), tile_position=(32 * k4, 0)),
                        reads=[b_A4, b_uT], writes=[b_ps])
            out.append((ps, b_ps))
        return out

    def prefix_step(k, j, src, b_src, dst, b_dst, Sre, Sim, b_sre, b_sim, n_c, o_re=None, o_im=None, b_o=()):
        ore = dst[:, 0, 0:n_c] if o_re is None else o_re
        oim = dst[:, 1, 0:n_c] if o_im is None else o_im
        wr = [b_dst] + list(b_o)
        stt("dve", dst[:, 0, 0:n_c], src[:, 0, 0:n_c], Lre(4, k), Sre, MUL, ADD, [b_src, bS, b_sre], [b_dst])
        stt("dve", ore, src[:, 1, 0:n_c], nLim(4, k), dst[:, 0, 0:n_c], MUL, ADD, [b_src, bS, b_dst], wr)
        stt("dve", dst[:, 1, 0:n_c], src[:, 0, 0:n_c], Lim(4, k), Sim, MUL, ADD, [b_src, bS, b_sim], [b_dst])
        stt("dve", oim, src[:, 1, 0:n_c], Lre(4, k), dst[:, 1, 0:n_c], MUL, ADD, [b_src, bS, b_dst], wr)

    for k in range(16):
        (pre, b_pre), (pim, b_pim) = s4_matmuls(k, 128, uP, 4)
        act(pr[0][:, 0, :], pre[:, 0:128], AF.Copy, [b_pre], [b_pr[0]])
        act(pr[0][:, 1, :], pim[:, 0:128], AF.Copy, [b_pim], [b_pr[0]])
        cur = 0
        for j in range(1, 4):
            last = (j == 3)
            prefix_step(k, j, pr[cur], b_pr[cur], pr[1 - cur], b_pr[1 - cur], pre[:, j * 128:(j + 1) * 128],
                        pim[:, j * 128:(j + 1) * 128], b_pre, b_pim, 128,
                        o_re=S16[0][:, k, :] if last else None, o_im=S16[1][:, k, :] if last else None,
                        b_o=[b_S16] if last else ())
            cur = 1 - cur

    lt = [AR.alloc([128, 16], F32) for i in range(6)]
    b_lt = [P.buf(f"lt{i}") for i in range(6)]
    L16re, L16im = LS["lre"][:, pi[16], :], LS["lim"][:, pi[16], :]

    def run_loop(eng, c0, c1, ltb, b_ltb, bH):
        for c in range(c0, c1):
            hre, him = H16[0][:, :, c], H16[1][:, :, c]
            tt(eng, ltb[0][:], L16re, hre, MUL, [bS, bH[0]], [b_ltb[0]])
            tt(eng, ltb[1][:], L16im, him, MUL, [bS, bH[1]], [b_ltb[1]])
            tt(eng, ltb[3][:], L16re, him, MUL, [bS, bH[1]], [b_ltb[3]])
            tt(eng, ltb[4][:], L16im, hre, MUL, [bS, bH[0]], [b_ltb[4]])
            tt(eng, ltb[2][:], ltb[0][:], ltb[1][:], SUB, [b_ltb[0], b_ltb[1]], [b_ltb[2]])
            tt(eng, ltb[5][:], ltb[3][:], ltb[4][:], ADD, [b_ltb[3], b_ltb[4]], [b_ltb[5]])
            tt(eng, H16[0][:, :, c + 1], ltb[2][:], S16[0][:, :, c], ADD, [b_ltb[2], b_S16], [bH[0]])
            tt(eng, H16[1][:, :, c + 1], ltb[5][:], S16[1][:, :, c], ADD, [b_ltb[5], b_S16], [bH[1]])

    Lp = AR.alloc([128, 3, 2, 16], F32)
    b_Lp = P.buf("Lp")
    sq_a = AR.alloc([128, 2, 16], F32); sq_b = AR.alloc([128, 2, 16], F32)
    q1 = AR.alloc([128, 16], F32); q2 = AR.alloc([128, 16], F32)
    b_q = P.buf("sqtmp")
    CE = "dve"
    cp(CE, sq_a[:, 0, :], L16re, [bS], [b_q])
    cp(CE, sq_a[:, 1, :], L16im, [bS], [b_q])
    src_, dst_ = sq_a, sq_b
    PW = [AR.alloc([128, 16, 129], F32) for _ in range(2)]
    pwt = [AR.alloc([128, 16, 64], F32) for _ in range(2)]
    b_PW, b_pwt = P.buf("PW"), P.buf("pwt")
    P.add(CE, lambda e: e.memset(PW[0][:, :, 0:1], 1.0), writes=[b_PW])
    P.add(CE, lambda e: e.memset(PW[1][:, :, 0:1], 0.0), writes=[b_PW])
    for it in range(7):
        n_ = 1 << it
        cmul(CE, PW[0][:, :, n_:2 * n_], PW[1][:, :, n_:2 * n_], PW[0][:, :, 0:n_], PW[1][:, :, 0:n_],
             bc(src_[:, 0, :], n_), bc(src_[:, 1, :], n_), pwt[0][:, :, 0:n_], pwt[1][:, :, 0:n_],
             [b_PW, b_q], [b_PW, b_pwt])
        o = Lp[:, 0] if it == 6 else dst_
        tt(CE, q1[:], src_[:, 0, :], src_[:, 0, :], MUL, [b_q], [b_q])
        tt(CE, q2[:], src_[:, 1, :], src_[:, 1, :], MUL, [b_q], [b_q])
        tt(CE, o[:, 0, :], q1[:], q2[:], SUB, [b_q], [b_q, b_Lp])
        stt(CE, o[:, 1, :], src_[:, 0, :], 2.0, src_[:, 1, :], MUL, MUL, [b_q], [b_q, b_Lp])
        src_, dst_ = dst_, src_
    lt2 = [AR.alloc([128, 16], F32) for i in range(6)]
    b_lt2 = [P.buf(f"lt2{i}") for i in range(6)]
    b_H2 = [P.buf("H2re"), P.buf("H2im")]
    HC = 64
    P.add(LOOP_ENG, lambda e: e.memset(H16[0][:, :, 0], 0.0), writes=[b_H[0]])
    P.add(LOOP_ENG, lambda e: e.memset(H16[1][:, :, 0], 0.0), writes=[b_H[1]])
    run_loop(LOOP_ENG, 0, HC, lt, b_lt, b_H)
    cp(CE, H16[0][:, :, HC + 1], S16[0][:, :, HC], [b_S16], [b_H2[0]])
    cp(CE, H16[1][:, :, HC + 1], S16[1][:, :, HC], [b_S16], [b_H2[1]])
    run_loop(CE, HC + 1, 128, lt2, b_lt2, b_H2)
    hb2 = lambda comp: H16[comp][:, :, HC].unsqueeze(2).broadcast_to([128, 16, 64])
    PWr2, PWi2 = PW[0][:, :, 1:65], PW[1][:, :, 1:65]
    H2r, H2i = H16[0][:, :, HC + 1:129], H16[1][:, :, HC + 1:129]
    tt(CE, pwt[0][:], PWr2, hb2(0), MUL, [b_PW, b_H[0]], [b_pwt])
    tt(CE, pwt[1][:], PWi2, hb2(1), MUL, [b_PW, b_H[1]], [b_pwt])
    tt(CE, pwt[0][:], pwt[0][:], pwt[1][:], SUB, [b_pwt], [b_pwt])
    tt(CE, H2r, H2r, pwt[0][:], ADD, [b_pwt, b_H2[0]], [b_H2[0], b_H[0]])
    tt(CE, pwt[0][:], PWr2, hb2(1), MUL, [b_PW, b_H[1]], [b_pwt])
    tt(CE, pwt[1][:], PWi2, hb2(0), MUL, [b_PW, b_H[0]], [b_pwt])
    tt(CE, pwt[0][:], pwt[0][:], pwt[1][:], ADD, [b_pwt], [b_pwt])
    tt(CE, H2i, H2i, pwt[0][:], ADD, [b_pwt, b_H2[1]], [b_H2[1], b_H[1]])
    Eloc = AR.alloc([128, 2, 16], F32)
    b_E = P.buf("Eloc")
    cp(LOOP_ENG, Eloc[:, 0, :], H16[0][:, :, 128], [b_H[0]], [b_E])
    cp(LOOP_ENG, Eloc[:, 1, :], H16[1][:, :, 128], [b_H[1]], [b_E])
    b_cci, b_cco = P.buf("cc_e_in"), P.buf("cc_e_out")
    P.add("sp", lambda e: e.dma_start(out=cc_e_in, in_=Eloc[:].rearrange("p a b -> p (a b)")), reads=[b_E], writes=[b_cci],
          dma=True, semkey="cce1")
    P.add("pool", lambda e: e.collective_compute("AllGather", ALU.bypass, replica_groups=[[0, 1, 2, 3], [4, 5, 6, 7]],
                                                 ins=[cc_e_in.opt()], outs=[cc_e_out.opt()]),
          reads=[b_cci], writes=[b_cco], dma="cc", semkey="cce2")
    Eg = AR.alloc([128, 4, 2, 16], F32)
    b_Eg = P.buf("Eg")
    P.add("sp", lambda e: e.dma_start(out=Eg[:].rearrange("p r a b -> p r (a b)"),
                                      in_=cc_e_out.rearrange("(r p) f -> p r f", p=128)),
          reads=[b_cco], writes=[b_Eg], dma=True, semkey="cce3")
    cmul(CE, Lp[:, 1, 0, :], Lp[:, 1, 1, :], Lp[:, 0, 0, :], Lp[:, 0, 1, :], Lp[:, 0, 0, :], Lp[:, 0, 1, :], q1[:], q2[:],
         [b_Lp, b_q], [b_Lp, b_q])
    cmul(CE, Lp[:, 2, 0, :], Lp[:, 2, 1, :], Lp[:, 1, 0, :], Lp[:, 1, 1, :], Lp[:, 0, 0, :], Lp[:, 0, 1, :], q1[:], q2[:],
         [b_Lp, b_q], [b_Lp, b_q])
    hin = AR.alloc([128, 2, 16], F32)
    cf = AR.alloc([128, 2, 16], F32)
    b_hin, b_cf = P.buf("hin"), P.buf("cf")
    P.add(CE, lambda e: e.memset(hin[:], 0.0), writes=[b_hin])
    msk = S("msk")
    for i in range(4):
        m0, m1, m2 = (msk[:, 3 * i + n:3 * i + n + 1] for n in range(3))
        tss(CE, cf[:, 0, :], Lp[:, 0, 0, :], m1, MUL, [b_Lp, b_sst], [b_cf])
        stt(CE, cf[:, 0, :], Lp[:, 1, 0, :], m2, cf[:, 0, :], MUL, ADD, [b_Lp, b_sst, b_cf], [b_cf])
        tss(CE, cf[:, 0, :], cf[:, 0, :], m0, ADD, [b_cf, b_sst], [b_cf])
        tss(CE, cf[:, 1, :], Lp[:, 0, 1, :], m1, MUL, [b_Lp, b_sst], [b_cf])
        stt(CE, cf[:, 1, :], Lp[:, 1, 1, :], m2, cf[:, 1, :], MUL, ADD, [b_Lp, b_sst, b_cf], [b_cf])
        ere, eim = Eg[:, i, 0, :], Eg[:, i, 1, :]
        tt(CE, q1[:], cf[:, 0, :], ere, MUL, [b_cf, b_Eg], [b_q])
        tt(CE, hin[:, 0, :], hin[:, 0, :], q1[:], ADD, [b_q, b_hin], [b_hin])
        tt(CE, q1[:], cf[:, 1, :], eim, MUL, [b_cf, b_Eg], [b_q])
        tt(CE, hin[:, 0, :], hin[:, 0, :], q1[:], SUB, [b_q, b_hin], [b_hin])
        tt(CE, q1[:], cf[:, 0, :], eim, MUL, [b_cf, b_Eg], [b_q])
        tt(CE, hin[:, 1, :], hin[:, 1, :], q1[:], ADD, [b_q, b_hin], [b_hin])
        tt(CE, q1[:], cf[:, 1, :], ere, MUL, [b_cf, b_Eg], [b_q])
        tt(CE, hin[:, 1, :], hin[:, 1, :], q1[:], ADD, [b_q, b_hin], [b_hin])
    cp(CE, PW[0][:, :, 128], Lp[:, 0, 0, :], [b_Lp], [b_PW])
    cp(CE, PW[1][:, :, 128], Lp[:, 0, 1, :], [b_Lp], [b_PW])
    fx = S16
    b_fx = [b_S16, P.buf("fx1")]
    hb = lambda comp: hin[:, comp, :].unsqueeze(2).broadcast_to([128, 16, 128])
    PWr, PWi = PW[0][:, :, 1:129], PW[1][:, :, 1:129]
    Hr, Hi = H16[0][:, :, 1:129], H16[1][:, :, 1:129]
    tt(CE, fx[0][:], PWr, hb(0), MUL, [b_PW, b_hin], [b_fx[0]])
    tt(CE, fx[1][:], PWi, hb(1), MUL, [b_PW, b_hin], [b_fx[0], b_fx[1]])
    tt(CE, fx[0][:], fx[0][:], fx[1][:], SUB, [b_fx[0], b_fx[1]], [b_fx[0]])
    tt(CE, Hr, Hr, fx[0][:], ADD, [b_fx[0], b_H[0]], [b_H[0]])
    tt(CE, fx[0][:], PWr, hb(1), MUL, [b_PW, b_hin], [b_fx[0]])
    tt(CE, fx[1][:], PWi, hb(0), MUL, [b_PW, b_hin], [b_fx[0], b_fx[1]])
    tt(CE, fx[0][:], fx[0][:], fx[1][:], ADD, [b_fx[0], b_fx[1]], [b_fx[0]])
    tt(CE, Hi, Hi, fx[0][:], ADD, [b_fx[0], b_H[1]], [b_H[1]])
    cp(CE, H16[0][:, :, 0], hin[:, 0, :], [b_hin], [b_H[0]])
    cp(CE, H16[1][:, :, 0], hin[:, 1, :], [b_hin], [b_H[1]])
    hp = AR.alloc([128, 2, 16], F32)
    b_hp = P.buf("hp")
    cp(LOOP_ENG, hp[:, 0, :], H16[0][:, :, 128], [b_H[0]], [b_hp])
    cp(LOOP_ENG, hp[:, 1, :], H16[1][:, :, 128], [b_H[1]], [b_hp])
    stores.append(P.add("sp", lambda e: e.dma_start(out=o_hp, in_=hp[:]), reads=[b_hp], dma=True, semkey="st_hp"))

    yT = AR.alloc([128, 4, T], BF16)
    b_yT = P.buf("yT")
    H4 = AR.alloc([128, 4, 4, 2, 128], BF16)
    b_H4 = P.buf("H4")
    h0b = AR.alloc([128, 2, 16, 16], BF16)
    b_h0b = P.buf("h0b")
    cp("dve", h0b[:, 0], S("h0_re", 16), [b_sst], [b_h0b])
    cp("dve", h0b[:, 1], S("h0_im", 16), [b_sst], [b_h0b])
    hs = AR.alloc([128, 2, 16, 16], F32)
    b_hs = P.buf("hs")
    htmp = AR.alloc([128, 2, 128], F32)
    b_htmp = P.buf("htmp")
    yP = [yT[:, kk, 0:NPT].rearrange("p (c j s) -> p j s c", j=4, s=4) for kk in range(4)]
    yS = [yT[:, kk, NPT:T].rearrange("p (q s) -> p s q", s=4) for kk in range(4)]

    def out_stage(kk, j, n_c, Hsrc, b_hsrc, usrc, dst):
        ps, b_ps = next_bank()
        for s_lo in range(4):
            o = ps[:, s_lo * n_c:(s_lo + 1) * n_c]
            nmm = 8 + s_lo + 1
            idx = 0
            for k4 in range(4):
                k = 4 * kk + k4
                for comp in range(2):
                    o4 = ps[32 * k4:32 * k4 + 32, s_lo * n_c:(s_lo + 1) * n_c]
                    P.add("pe", lambda e, o4=o4, k=k, s_lo=s_lo, comp=comp, k4=k4, idx=idx, nmm=nmm: e.matmul(
                        o4, lhsT=Wc[:, k, s_lo, comp, :], rhs=Hsrc(k, k4, comp), start=(comp == 0), stop=False,
                        tile_position=(0, 32 * k4)),
                        reads=[b_Wc, b_hsrc], writes=[b_ps])
                    idx += 1
            for tau in range(s_lo + 1):
                rhs = usrc[kk][:, 4 * j + s_lo - tau, :]
                P.add("pe", lambda e, o=o, tau=tau, rhs=rhs, idx=idx, nmm=nmm, kk=kk: e.matmul(
                    o, lhsT=Ktab[:, kk, tau, :], rhs=rhs, start=False, stop=(idx == nmm - 1)),
                    reads=[b_Kt, b_uT], writes=[b_ps])
                idx += 1
        act(dst, ps[:, 0:4 * n_c].rearrange("p (s c) -> p s c", s=4), AF.Gelu_apprx_tanh, [b_ps], [b_yT])

    for kk in range(4):
        for k4 in range(4):
            k = 4 * kk + k4
            (pre, b_pre), (pim, b_pim) = s4_matmuls(k, 128, uP, 3)
            cp("dve", H4[:, k4, 0, 0, :], H16[0][:, k, 0:128], [b_H[0]], [b_H4])
            cp("dve", H4[:, k4, 0, 1, :], H16[1][:, k, 0:128], [b_H[1]], [b_H4])
            act(pr[0][:, 0, :], pre[:, 0:128], AF.Copy, [b_pre], [b_pr[0]])
            act(pr[0][:, 1, :], pim[:, 0:128], AF.Copy, [b_pim], [b_pr[0]])
            cur = 0
            for j in range(1, 4):
                n = 4 * j
                stt("dve", htmp[:, 0, :], H16[0][:, k, 0:128], Lre(n, k), pr[cur][:, 0, :], MUL, ADD,
                    [b_H[0], bS, b_pr[cur]], [b_htmp])
                stt("dve", H4[:, k4, j, 0, :], H16[1][:, k, 0:128], nLim(n, k), htmp[:, 0, :], MUL, ADD,
                    [b_H[1], bS, b_htmp], [b_H4])
                stt("dve", htmp[:, 1, :], H16[0][:, k, 0:128], Lim(n, k), pr[cur][:, 1, :], MUL, ADD,
                    [b_H[0], bS, b_pr[cur]], [b_htmp])
                stt("dve", H4[:, k4, j, 1, :], H16[1][:, k, 0:128], Lre(n, k), htmp[:, 1, :], MUL, ADD,
                    [b_H[1], bS, b_htmp], [b_H4])
                if j < 3:
                    prefix_step(k, j, pr[cur], b_pr[cur], pr[1 - cur], b_pr[1 - cur], pre[:, j * 128:(j + 1) * 128],
                                pim[:, j * 128:(j + 1) * 128], b_pre, b_pim, 128)
                    cur = 1 - cur
            (sre, b_sre), (sim, b_sim) = s4_matmuls(k, 16, uS, 1)
            h0r, h0i = S("h0_re", 16)[:, k, :], S("h0_im", 16)[:, k, :]
            stt("dve", hs[:, 0, k, :], h0r, Lre(4, k), sre[:, 0:16], MUL, ADD, [b_sst, bS, b_sre], [b_hs])
            stt("dve", hs[:, 0, k, :], h0i, nLim(4, k), hs[:, 0, k, :], MUL, ADD, [b_sst, bS, b_hs], [b_hs])
            stt("dve", hs[:, 1, k, :], h0r, Lim(4, k), sim[:, 0:16], MUL, ADD, [b_sst, bS, b_sim], [b_hs])
            stt("dve", hs[:, 1, k, :], h0i, Lre(4, k), hs[:, 1, k, :], MUL, ADD, [b_sst, bS, b_hs], [b_hs])
        for j in range(4):
            out_stage(kk, j, 128, lambda k, k4, comp, j=j: H4[:, k4, j, comp, :], b_H4, uP, yP[kk][:, j])
        out_stage(kk, 0, 16, lambda k, k4, comp: h0b[:, comp, k, :], b_h0b, uS, yS[kk])
    stores.append(P.add("sp", lambda e: e.dma_start(out=o_hs, in_=hs[:]), reads=[b_hs], dma=True, semkey="st_hs"))

    wg = AR.alloc([128, 4, 1024], BF16)
    b_wg = P.buf("wg")
    P.add("sp", lambda e: e.dma_start(out=wg[:], in_=w_glu_b.rearrange("(k p) m -> p k m", p=128)), reads=[b_w_glu_b],
          writes=[b_wg], dma=True, semkey="wg")
    ysT, b_ys = uT, b_uT
    sg = AR.alloc([128, 512], F32)
    b_sg = P.buf("sg")
    for (n0, n) in chunks:
        for m in range(4):
            pv, b_pv = next_bank()
            pg, b_pg = next_bank()
            for k in range(4):
                P.add("pe", lambda e, pv=pv, k=k, m=m, n0=n0, n=n: e.matmul(
                    pv[:, 0:n], lhsT=wg[:, k, 128 * m:128 * m + 128], rhs=yT[:, k, n0:n0 + n], start=(k == 0), stop=(k == 3)),
                    reads=[b_wg, b_yT], writes=[b_pv])
            for k in range(4):
                P.add("pe", lambda e, pg=pg, k=k, m=m, n0=n0, n=n: e.matmul(
                    pg[:, 0:n], lhsT=wg[:, k, 512 + 128 * m:512 + 128 * m + 128], rhs=yT[:, k, n0:n0 + n], start=(k == 0),
                    stop=(k == 3)), reads=[b_wg, b_yT], writes=[b_pg])
            act(sg[:, 0:n], pg[:, 0:n], AF.Sigmoid, [b_pg], [b_sg])
            tt("dve", ysT[:, m, n0:n0 + n], pv[:, 0:n], sg[:, 0:n], MUL, [b_pv, b_sg], [b_ys])
    return dict(ysT=ysT, b_ys=b_ys)


def build_attn(P, AR, nc, din, dout, dint, stores, banks, bank_bufs, cast_w, cqn, b_cqn, ckvn, b_ckvn, krK, b_krK,
               sm, b_sm, smc, ones_bf, b_ones, rope_c, rope_s, attT, b_attT, chunks, kS, b_kS):
    MUL, ADD = ALU.mult, ALU.add
    w_uq = din("w_uq", [Q_LORA, N_HEADS * QK_HEAD])
    w_uk = din("w_uk", [KV_LORA, N_HEADS * QK_NOPE])
    w_uv = din("w_uv", [KV_LORA, N_HEADS * V_HEAD])
    maskd_d = din("maskd", [128, 4, 128])
    rotm_d = din("rotm", [96, 96])
    w_uq_b = dint("w_uq_b", [Q_LORA, N_HEADS * QK_HEAD], BF16)
    w_uk_b = dint("w_uk_b", [KV_LORA, N_HEADS * QK_NOPE], BF16)
    w_uv_b = dint("w_uv_b", [KV_LORA, N_HEADS * V_HEAD], BF16)
    b_wqb = cast_w(w_uq_b, w_uq, Q_LORA, 768, "c_wuq")
    b_wkb = cast_w(w_uk_b, w_uk, KV_LORA, 512, "c_wuk")
    b_wvb = cast_w(w_uv_b, w_uv, KV_LORA, 512, "c_wuv")
    K_own = [dint(f"K_own{h}", [QK_HEAD, NPT], BF16) for h in range(N_HEADS)]
    K_all = [dint(f"K_all{h}", [4 * QK_HEAD, NPT], BF16) for h in range(N_HEADS)]
    V_own = [dint(f"V_own{h}", [128, 16 * 65], BF16) for h in range(N_HEADS)]
    V_all = [dint(f"V_all{h}", [512, 16 * 65], BF16) for h in range(N_HEADS)]
    RG = [[0, 1, 2, 3], [4, 5, 6, 7]]

    AR.release(0)
    P.new_phase()
    qT = AR.alloc([96, 8, T], BF16)
    maskd = AR.alloc([128, 4, 128], BF16)
    sel65 = AR.alloc([65, 64], F32)
    mC1 = AR.mark()
    wq = AR.alloc([128, 3, 768], BF16)
    wk = AR.alloc([128, 2, 8, 96], BF16)
    wv = AR.alloc([128, 2, 512], BF16)
    rc = AR.alloc([96, T], F32)
    rs = AR.alloc([96, T], F32)
    rotm = AR.alloc([96, 96], F32)
    maskf = AR.alloc([128, 4, 128], F32)
    b_wq, b_wk, b_wv, b_rc, b_rs, b_rot, b_mk, b_mkf, b_sel, b_qT = [P.buf(n) for n in
        ("wq", "wk", "wv", "rc", "rs", "rotm", "maskd", "maskf", "sel65", "qT")]
    P.add("sp", lambda e: e.dma_start(out=wq[:], in_=w_uq_b.rearrange("(k p) m -> p k m", p=128)), reads=[b_wqb], writes=[b_wq],
          dma=True, semkey="wq")
    P.add("pool", lambda e: e.memset(wk[:], 0.0), writes=[b_wk])
    for k in range(2):
        P.add("sp", lambda e, k=k: e.dma_start(out=wk[:, k, :, 0:64],
                                               in_=w_uk_b[128 * k:128 * k + 128, :].rearrange("p (h d) -> p h d", h=8)),
              reads=[b_wkb], writes=[b_wk], dma=True, semkey="wk")
    P.add("sp", lambda e: e.dma_start(out=wv[:], in_=w_uv_b.rearrange("(k p) m -> p k m", p=128)), reads=[b_wvb], writes=[b_wv],
          dma=True, semkey="wv")
    P.add("sp", lambda e: e.dma_start(out=rc[:], in_=rope_c), writes=[b_rc], dma=True, semkey="rc")
    P.add("sp", lambda e: e.dma_start(out=rs[:], in_=rope_s), writes=[b_rs], dma=True, semkey="rs")
    P.add("sp", lambda e: e.dma_start(out=rotm[:], in_=rotm_d), writes=[b_rot], dma=True, semkey="rotm")
    P.add("sp", lambda e: e.dma_start(out=maskf[:], in_=maskd_d), writes=[b_mkf], dma=True, semkey="maskf")
    P.add("pool", lambda e: e.tensor_copy(out=maskd[:], in_=maskf[:]), reads=[b_mkf], writes=[b_mk])
    P.add("pool", lambda e: e.memset(sel65[:], 0.0), writes=[b_sel])
    P.add("pool", lambda e: e.memset(sel65[64:65, :], 1.0), writes=[b_sel])
    ident = sm[:, SL["ident"][0]:SL["ident"][0] + 128]

    rr = [0]

    def tbank():
        i = 2 + rr[0] % 6
        rr[0] += 1
        return banks[i], bank_bufs[i]

    NT_ = 4
    tmpl = [dict(raw=AR.alloc([96, 512], F32), sqh=AR.alloc([96, 512], BF16), lnv=AR.alloc([96, 512], F32),
                 rstd=AR.alloc([96, 512], F32), qg=AR.alloc([96, 512], F32), t1=AR.alloc([96, 512], F32),
                 t2=AR.alloc([96, 512], F32)) for _ in range(NT_)]
    tmpb = [{k: P.buf(f"nr_{k}{i}") for k in ("raw", "sqh", "lnv", "rstd", "qg", "t1", "t2")} for i in range(NT_)]
    kst = [AR.alloc([96, 512], BF16) for _ in range(3)]
    b_kst = [P.buf(f"kst{i}") for i in range(3)]
    nrc = [0]

    def normrope(mm_fn, gname, out_ap, b_out, n, n0, after_fn=None):
        ti = nrc[0] % NT_
        nrc[0] += 1
        t_, b_ = tmpl[ti], tmpb[ti]
        raw, sqh, lnv, rstd, qg, t1, t2 = (t_[k] for k in ("raw", "sqh", "lnv", "rstd", "qg", "t1", "t2"))
        ps, b_ps = tbank()
        mm_fn(ps, b_ps)
        P.add("act", lambda e: e.activation(out=raw[:, 0:n], in_=ps[0:96, 0:n], func=AF.Copy), reads=[b_ps], writes=[b_["raw"]])
        P.add("dve", lambda e: e.tensor_tensor(out=sqh[:, 0:n], in0=raw[:, 0:n], in1=raw[:, 0:n], op=MUL), reads=[b_["raw"]],
              writes=[b_["sqh"]])
        yield
        p2, b_p2 = tbank()
        P.add("pe", lambda e: e.matmul(p2[0:96, 0:n], lhsT=ones_bf[0:96, 0:96], rhs=sqh[:, 0:n], start=True, stop=True),
              reads=[b_["sqh"], b_ones], writes=[b_p2])
        P.add("act", lambda e: e.activation(out=lnv[:, 0:n], in_=p2[0:96, 0:n], func=AF.Ln, scale=1.0 / QK_HEAD, bias=EPS),
              reads=[b_p2], writes=[b_["lnv"]])
        P.add("act", lambda e: e.activation(out=rstd[:, 0:n], in_=lnv[:, 0:n], func=AF.Exp, scale=-0.5), reads=[b_["lnv"]],
              writes=[b_["rstd"]])
        yield
        P.add("dve", lambda e: e.scalar_tensor_tensor(out=qg[:, 0:n], in0=raw[:, 0:n], scalar=smc(gname)[0:96, :], in1=rstd[:, 0:n],
                                                      op0=MUL, op1=MUL), reads=[b_["raw"], b_["rstd"], b_sm], writes=[b_["qg"]])
        P.add("dve", lambda e: e.tensor_tensor(out=t1[:, 0:n], in0=qg[:, 0:n], in1=rc[:, n0:n0 + n], op=MUL), reads=[b_["qg"], b_rc],
              writes=[b_["t1"]])
        yield
        p3, b_p3 = tbank()
        P.add("pe", lambda e: e.matmul(p3[0:96, 0:n], lhsT=rotm[:, :], rhs=qg[:, 0:n], start=True, stop=True),
              reads=[b_rot, b_["qg"]], writes=[b_p3])
        P.add("dve", lambda e: e.tensor_tensor(out=t2[:, 0:n], in0=p3[0:96, 0:n], in1=rs[:, n0:n0 + n], op=MUL),
              reads=[b_p3, b_rs], writes=[b_["t2"]])
        P.add("dve", lambda e: e.tensor_tensor(out=out_ap, in0=t1[:, 0:n], in1=t2[:, 0:n], op=ADD), reads=[b_["t1"], b_["t2"]],
              writes=[b_out])
        if after_fn is not None:
            after_fn()

    def run_skewed(gens, step=1):
        active = []
        it = iter(gens)
        more = True
        while more or active:
            if more:
                try:
                    active.append(next(it))
                except StopIteration:
                    more = False
            for _ in range(step):
                for gg in list(active):
                    try:
                        next(gg)
                    except StopIteration:
                        active.remove(gg)

    b_Kown = [P.buf(f"K_own{h}") for h in range(8)]
    b_Vown = [P.buf(f"V_own{h}") for h in range(8)]
    b_Kall = [P.buf(f"K_all{h}") for h in range(8)]
    b_Vall = [P.buf(f"V_all{h}") for h in range(8)]
    Vst = AR.alloc([128, 8, 16, 65], BF16)
    b_Vst = P.buf("Vst")
    P.add("pool", lambda e: e.memset(Vst[:, :, :, 64:65], 1.0), writes=[b_Vst])
    for blk in range(16):
        ps, b_ps = tbank()
        for k in range(2):
            P.add("pe", lambda e, ps=ps, k=k, blk=blk: e.matmul(ps[:, 0:512], lhsT=ckvn[:, k, 128 * blk:128 * blk + 128], rhs=wv[:, k, :],
                                                                 start=(k == 0), stop=(k == 1)), reads=[b_ckvn, b_wv], writes=[b_ps])
        P.add("act", lambda e, ps=ps, blk=blk: e.activation(out=Vst[:, :, blk, 0:64], in_=ps[:, 0:512].rearrange("p (h v) -> p h v", h=8),
                                                            func=AF.Copy), reads=[b_ps], writes=[b_Vst])
    for h in range(N_HEADS):
        P.add("sp", lambda e, h=h: e.dma_start(out=V_own[h], in_=Vst[:, h].rearrange("p b e -> p (b e)")), reads=[b_Vst],
              writes=[b_Vown[h]], dma=True, semkey=f"vst{h}")
        P.add("pool", lambda e, h=h: e.collective_compute("AllGather", ALU.bypass, replica_groups=RG, ins=[V_own[h].opt()],
                                                          outs=[V_all[h].opt()]),
              reads=[b_Vown[h]], writes=[b_Vall[h]], dma="cc", semkey=f"ccV{h}")
    kcount = [0]
    for h in range(N_HEADS):
        gens = []
        for ci, (n0, n) in enumerate(chunks):
            def q_mm(ps, b_ps, h=h, n0=n0, n=n):
                for k in range(3):
                    P.add("pe", lambda e, k=k: e.matmul(ps[0:96, 0:n], lhsT=wq[:, k, 96 * h:96 * h + 96], rhs=cqn[:, k, n0:n0 + n],
                                                        start=(k == 0), stop=(k == 2)), reads=[b_wq, b_cqn], writes=[b_ps])

            def k_mm(ps, b_ps, h=h, n0=n0, n=n):
                for k in range(2):
                    P.add("pe", lambda e, k=k: e.matmul(ps[0:96, 0:n], lhsT=wk[:, k, h, :], rhs=ckvn[:, k, n0:n0 + n], start=(k == 0),
                                                        stop=False), reads=[b_wk, b_ckvn], writes=[b_ps])
                P.add("pe", lambda e: e.matmul(ps[0:96, 0:n], lhsT=ident[64:96, 0:96], rhs=krK[64:96, n0:n0 + n], start=False, stop=True,
                                               tile_position=(64, 0)), reads=[b_sm, b_krK], writes=[b_ps])
            ki_ = kcount[0] % 3
            kcount[0] += 1
            ks, b_ks = kst[ki_], b_kst[ki_]
            npr = min(n0 + n, NPT) - n0

            def after(h=h, n0=n0, n=n, npr=npr, ks=ks, b_ks=b_ks, ki_=ki_):
                if npr > 0:
                    P.add("sp", lambda e: e.dma_start(out=K_own[h][:, n0:n0 + npr], in_=ks[:, 0:npr]), reads=[b_ks], writes=[b_Kown[h]],
                          dma=True, semkey=f"kst{ki_}")
                if npr < n:
                    P.add("pool", lambda e: e.tensor_copy(out=kS[:, h, :], in_=ks[:, npr:n]), reads=[b_ks], writes=[b_kS])
            gens.append(normrope(q_mm, "g_q", qT[:, h, n0:n0 + n], b_qT, n, n0))
            gens.append(normrope(k_mm, "g_k", ks[:, 0:n], b_ks, n, n0, after))
        run_skewed(gens)
        P.add("pool", lambda e, h=h: e.collective_compute("AllGather", ALU.bypass, replica_groups=RG, ins=[K_own[h].opt()],
                                                          outs=[K_all[h].opt()]),
              reads=[b_Kown[h]], writes=[b_Kall[h]], dma="cc", semkey=f"ccK{h}")
    import os
    if os.environ.get("CUT") == "3":
        return {}
    AR.release(mC1)
    P.new_phase()
    Kh = [AR.alloc([96, 4, NPT], BF16) for _ in range(2)]
    Vh = [AR.alloc([128, 4, 16 * 65], BF16) for _ in range(2)]
    Vvis = AR.alloc([128, 4, 16 * 65], BF16)
    Vful = AR.alloc([128, 4, 16 * 65], BF16)
    b_Kh = [P.buf("Kh0"), P.buf("Kh1")]
    b_Vh = [P.buf("Vh0"), P.buf("Vh1")]
    b_Vvis, b_Vful = P.buf("Vvis"), P.buf("Vful")
    PT = [AR.alloc([128, 512], BF16) for _ in range(6)]
    b_PT = [P.buf(f"PT{i}") for i in range(6)]
    Osb = AR.alloc([65, 512], F32); rl = AR.alloc([64, 512], F32); ast = AR.alloc([64, 512], BF16)
    b_Osb, b_rl, b_ast = P.buf("Osb"), P.buf("rl"), P.buf("ast")

    def load_head(h):
        i = h % 2
        P.add("sp", lambda e: e.dma_start(out=Kh[i][:], in_=K_all[h].rearrange("(r d) t -> d r t", r=4)), reads=[b_Kall[h]], writes=[b_Kh[i]], dma=True,
              semkey=f"Kh{i}")
        P.add("sp", lambda e: e.dma_start(out=Vh[i][:], in_=V_all[h].rearrange("(r p) x -> p r x", r=4)), reads=[b_Vall[h]], writes=[b_Vh[i]], dma=True,
              semkey=f"Vh{i}")

    load_head(0)
    pcount = 0
    for h in range(N_HEADS):
        i = h % 2
        if h + 1 < N_HEADS:
            load_head(h + 1)
        for r in range(4):
            P.add("dve", lambda e, r=r, i=i: e.tensor_scalar(out=Vvis[:, r, :], in0=Vh[i][:, r, :], scalar1=smc("vis", r), scalar2=None,
                                                              op0=MUL), reads=[b_Vh[i], b_sm], writes=[b_Vvis])
            P.add("dve", lambda e, r=r, i=i: e.tensor_scalar(out=Vful[:, r, :], in0=Vh[i][:, r, :], scalar1=smc("full", r), scalar2=None,
                                                              op0=MUL), reads=[b_Vh[i], b_sm], writes=[b_Vful])
        for qc in range(4):
            O, b_O = banks[qc % 2], bank_bufs[qc % 2]
            first = [True]
            pending = []

            def flush(keep):
                while len(pending) > keep:
                    pending.pop(0)()
            for r in range(4):
                for kb in range(16):
                    S_, b_S = tbank()
                    P.add("pe", lambda e, S_=S_, r=r, kb=kb, i=i, h=h, qc=qc: e.matmul(
                        S_[:, 0:512], lhsT=Kh[i][:, r, 128 * kb:128 * kb + 128], rhs=qT[:, h, 512 * qc:512 * qc + 512],
                        start=True, stop=True), reads=[b_Kh[i], b_qT], writes=[b_S])
                    pt, b_pt = PT[pcount % 6], b_PT[pcount % 6]
                    pcount += 1
                    P.add("act", lambda e, S_=S_, pt=pt: e.activation(out=pt[:], in_=S_[:, 0:512], func=AF.Exp, scale=SCALE),
                          reads=[b_S], writes=[b_pt])

                    def pv(r=r, kb=kb, pt=pt, b_pt=b_pt, O=O, b_O=b_O, qc=qc, i=i):
                        d = kb - 4 * qc
                        segs = []
                        if d < 0:
                            segs.append((0, 512, Vvis, b_Vvis))
                        elif d > 3:
                            segs.append((0, 512, Vful, b_Vful))
                        else:
                            if d > 0:
                                segs.append((0, 128 * d, Vful, b_Vful))
                            P.add("dve", lambda e: e.tensor_tensor(out=pt[:, 128 * d:128 * d + 128], in0=pt[:, 128 * d:128 * d + 128],
                                                                   in1=maskd[:, r, :], op=MUL), reads=[b_pt, b_mk], writes=[b_pt])
                            segs.append((128 * d, 128 * d + 128, Vh[i], b_Vh[i]))
                            if d < 3:
                                segs.append((128 * d + 128, 512, Vvis, b_Vvis))
                        for (c0, c1, Vx, b_Vx) in segs:
                            st = first[0]
                            first[0] = False
                            P.add("pe", lambda e, c0=c0, c1=c1, Vx=Vx, st=st: e.matmul(
                                O[0:65, c0:c1], lhsT=Vx[:, r, 65 * kb:65 * kb + 65], rhs=pt[:, c0:c1], start=st, stop=False),
                                reads=[b_Vx, b_pt], writes=[b_O])
                    pending.append(pv)
                    flush(3)
            flush(0)
            P.add("act", lambda e, O=O: e.activation(out=Osb[:], in_=O[0:65, 0:512], func=AF.Copy), reads=[b_O], writes=[b_Osb])
            lb, b_lb = tbank()
            P.add("pe", lambda e, lb=lb: e.matmul(lb[0:64, 0:512], lhsT=sel65[:, :], rhs=Osb[:, :], start=True, stop=True),
                  reads=[b_sel, b_Osb], writes=[b_lb])
            P.add("dve", lambda e, lb=lb: e.reciprocal(out=rl[:], in_=lb[0:64, 0:512]), reads=[b_lb], writes=[b_rl])
            cols = slice(512 * qc, 512 * qc + 512)
            if h % 2 == 0:
                P.add("dve", lambda e, h=h, cols=cols: e.tensor_tensor(out=attT[0:64, h // 2, cols], in0=Osb[0:64, :], in1=rl[:], op=MUL),
                      reads=[b_Osb, b_rl], writes=[b_attT])
            else:
                P.add("dve", lambda e: e.tensor_tensor(out=ast[:], in0=Osb[0:64, :], in1=rl[:], op=MUL), reads=[b_Osb, b_rl],
                      writes=[b_ast])
                P.add("sp", lambda e, h=h, cols=cols: e.dma_start(out=attT[64:128, h // 2, cols], in_=ast[:]), reads=[b_ast],
                      writes=[b_attT], dma=True, semkey="ast")
    return dict(mC1=mC1, qT=qT, b_qT=b_qT, w_uk_b=w_uk_b, b_wkb=b_wkb, w_uv_b=w_uv_b, b_wvb=b_wvb, rotm_d=rotm_d)


def build_tail(P, AR, nc, din, dout, dint, stores, banks, bank_bufs, cast_w, xT, w_in_b, b_w_in_b, sm, b_sm, smc, ones_bf, b_ones,
               attT, b_attT, ysT, b_ys, chunks):
    MUL, ADD = ALU.mult, ALU.add
    names = [("w_oa", 512, D), ("w_os", 512, D), ("w_out", D, D), ("w_up", D, 2 * D_FF), ("w_down", D_FF, D),
             ("w_pg", D, D), ("w_pp", PLE, D)]
    W, bW = {"w_in": w_in_b}, {"w_in": b_w_in_b}
    for nm, r, c in names:
        src = din(nm, [r, c])
        W[nm] = dint(nm + "_b", [r, c], BF16)
        bW[nm] = cast_w(W[nm], src, r, c, "c_" + nm)
    pT_d = din("pT", [PLE, T])
    scT_d = din("scT", [128, 44, 16, 2])
    o_yT = dout("o_yT", [D, T])
    o_cvp = dout("o_cvp", [128, 44, 2])
    o_cvs = dout("o_cvs", [128, 44, 16, 2])
    cc_h_in = dint("cc_h_in", [128, 16], F32)
    cc_h_out = dint("cc_h_out", [512, 16], F32)

    AR.release(0)
    P.new_phase()
    HO = 2
    x1 = AR.alloc([128, KD, 514], F32)
    x1c4 = AR.alloc([128, KD, 290], F32)
    xn = AR.alloc([128, KD, 514], BF16)
    scr = AR.alloc([128, KD, 514], BF16)
    hT = AR.alloc([128, 22, 512], BF16)
    upx = [[AR.alloc([128, 514], F32) for _ in range(2)] for _ in range(2)]
    cav = [[AR.alloc([128, 512], F32) for _ in range(2)] for _ in range(2)]
    sgt = [AR.alloc([128, 512], F32) for _ in range(2)]
    lnv = AR.alloc([128, 514], F32)
    rstd = AR.alloc([128, 514], F32)
    pTc = AR.alloc([128, 2, 512], BF16)
    scT = AR.alloc([128, 44, 16, 2], F32)
    upS = [AR.alloc([128, 16, 6], F32) for _ in range(2)]
    cvp = AR.alloc([128, 44, 2], F32)
    cvs = AR.alloc([128, 44, 16, 2], F32)
    carry = AR.alloc([128, 44, 2], F32)
    Hg = AR.alloc([128, 4, 16], F32)
    hsend = AR.alloc([128, 16], F32)
    hrecv = AR.alloc([128, 16], F32)
    NSLAB = 4
    slabs = [AR.alloc([128, 4096], BF16) for _ in range(NSLAB)]
    b_x1, b_x1c4, b_xn, b_scr, b_hT, b_lnv, b_rstd, b_pTc, b_scT, b_cvp, b_cvs, b_carry, b_Hg, b_hsend, b_hrecv = [
        P.buf(n) for n in ("x1", "x1c4", "xn", "scr", "hT", "lnv", "rstd", "pTc", "scT", "cvp", "cvs", "carry", "Hg", "hsend", "hrecv")]
    b_upx = [[P.buf(f"upx{i}{j}") for j in range(2)] for i in range(2)]
    b_cav = [[P.buf(f"cav{i}{j}") for j in range(2)] for i in range(2)]
    b_sgt = [P.buf("sgt0"), P.buf("sgt1")]
    b_upS = [P.buf("upS0"), P.buf("upS1")]
    b_slab = [P.buf(f"slab{i}") for i in range(NSLAB)]
    P.add("sp", lambda e: e.dma_start(out=scT[:], in_=scT_d), writes=[b_scT], dma=True, semkey="scT")
    P.add("pool", lambda e: e.memset(carry[:], 0.0), writes=[b_carry])

    rr = [0]

    def tbank():
        i = rr[0] % 8
        rr[0] += 1
        return banks[i], bank_bufs[i]

    sl = [0]

    def load_slab(wname, kt, c0, width):
        i = sl[0] % NSLAB
        sl[0] += 1
        v = slabs[i][:, 0:kt * width].rearrange("p (k m) -> p k m", k=kt)
        src, bsrc = W[wname], bW[wname]
        P.add("sp", lambda e: e.dma_start(out=v, in_=src.rearrange("(k p) m -> p k m", p=128)[:, :, c0:c0 + width]),
              reads=[bsrc], writes=[b_slab[i]], dma=True, semkey=f"slab{i}")
        return v, b_slab[i]

    def mm_group(ps, b_ps, n, w, b_w, kt, wc0, rhs_fn, rd):
        for k in range(kt):
            P.add("pe", lambda e, k=k: e.matmul(ps[:, 0:n], lhsT=w[:, k, wc0:wc0 + 128], rhs=rhs_fn(k), start=(k == 0), stop=(k == kt - 1)),
                  reads=[b_w] + rd, writes=[b_ps])

    def norm_to_xn(src, b_src, gname, c_lo, c_hi):
        n = c_hi - c_lo
        P.add("pool", lambda e: e.tensor_tensor(out=scr[:, :, c_lo:c_hi], in0=src[:, :, c_lo:c_hi], in1=src[:, :, c_lo:c_hi], op=MUL),
              reads=[b_src], writes=[b_scr])
        ps, b_ps = tbank()
        for k in range(KD):
            P.add("pe", lambda e, k=k: e.matmul(ps[:, 0:n], lhsT=ones_bf[:], rhs=scr[:, k, c_lo:c_hi], start=(k == 0), stop=(k == KD - 1)),
                  reads=[b_scr, b_ones], writes=[b_ps])
        P.add("act", lambda e: e.activation(out=lnv[:, 0:n], in_=ps[:, 0:n], func=AF.Ln, scale=1.0 / D, bias=EPS), reads=[b_ps],
              writes=[b_lnv])
        P.add("act", lambda e: e.activation(out=rstd[:, 0:n], in_=lnv[:, 0:n], func=AF.Exp, scale=-0.5), reads=[b_lnv], writes=[b_rstd])
        for k in range(KD):
            P.add("dve", lambda e, k=k: e.scalar_tensor_tensor(out=xn[:, k, c_lo:c_hi], in0=src[:, k, c_lo:c_hi], scalar=smc(gname, k),
                                                               in1=rstd[:, 0:n], op0=MUL, op1=MUL),
                  reads=[b_src, b_rstd, b_sm], writes=[b_xn])

    xT_v = xT.rearrange("(k p) t -> p k t", p=128)

    def phase_D(xt, b_xt, n0, n):
        lo, hi = HO, HO + n
        P.add("sp", lambda e: e.dma_start(out=xt[:, :, lo:hi], in_=xT_v[:, :, n0:n0 + n]), writes=[b_xt], dma=True, semkey="x1ld")
        norm_to_xn(xt, b_xt, "g_mix", lo, hi)
        mixed = scr
        for q in range(2):
            wga, b_wga = load_slab("w_in", KD, OFF_GA + 512 * q, 512)
            wgs, b_wgs = load_slab("w_in", KD, OFF_GS + 512 * q, 512)
            woa, b_woa = load_slab("w_oa", 4, 512 * q, 512)
            wos, b_wos = load_slab("w_os", 4, 512 * q, 512)
            for mi in range(4):
                m = 4 * q + mi
                pga, b_pga = tbank()
                mm_group(pga, b_pga, n, wga, b_wga, KD, 128 * mi, lambda k: xn[:, k, lo:hi], [b_xn])
                pgs, b_pgs = tbank()
                mm_group(pgs, b_pgs, n, wgs, b_wgs, KD, 128 * mi, lambda k: xn[:, k, lo:hi], [b_xn])
                poa, b_poa = tbank()
                mm_group(poa, b_poa, n, woa, b_woa, 4, 128 * mi, lambda k: attT[:, k, n0:n0 + n], [b_attT])
                pos_, b_pos = tbank()
                mm_group(pos_, b_pos, n, wos, b_wos, 4, 128 * mi, lambda k: ysT[:, k, n0:n0 + n], [b_ys])
                P.add("act", lambda e, pga=pga: e.activation(out=sgt[0][:, 0:n], in_=pga[:, 0:n], func=AF.Sigmoid), reads=[b_pga],
                      writes=[b_sgt[0]])
                P.add("act", lambda e, pgs=pgs: e.activation(out=sgt[1][:, 0:n], in_=pgs[:, 0:n], func=AF.Sigmoid), reads=[b_pgs],
                      writes=[b_sgt[1]])
                P.add("dve", lambda e, poa=poa: e.tensor_tensor(out=cav[0][0][:, 0:n], in0=poa[:, 0:n], in1=sgt[0][:, 0:n], op=MUL),
                      reads=[b_poa, b_sgt[0]], writes=[b_cav[0][0]])
                P.add("dve", lambda e, pos_=pos_: e.tensor_tensor(out=cav[0][1][:, 0:n], in0=pos_[:, 0:n], in1=sgt[1][:, 0:n], op=MUL),
                      reads=[b_pos, b_sgt[1]], writes=[b_cav[0][1]])
                P.add("pool", lambda e, m=m: e.tensor_tensor(out=mixed[:, m, lo:hi], in0=cav[0][0][:, 0:n], in1=cav[0][1][:, 0:n], op=ADD),
                      reads=[b_cav[0][0], b_cav[0][1]], writes=[b_scr])
        for q in range(2):
            wo, b_wo = load_slab("w_out", KD, 512 * q, 512)
            for mi in range(4):
                m = 4 * q + mi
                ps, b_ps = tbank()
                mm_group(ps, b_ps, n, wo, b_wo, KD, 128 * mi, lambda k: mixed[:, k, lo:hi], [b_scr])
                P.add("dve", lambda e, ps=ps, m=m: e.tensor_tensor(out=xt[:, m, lo:hi], in0=ps[:, 0:n], in1=xt[:, m, lo:hi], op=ADD),
                      reads=[b_ps, b_xt], writes=[b_xt])

    ucount = [0]

    def conv3(ci_, src3, dst, b_src, b_dst, tile, nn, view=None):
        w0, w1, w2 = (smc("conv_w", tap * 44 + tile) for tap in range(3))
        P.add("act", lambda e: e.activation(out=dst, in_=src3(2), func=AF.Identity, scale=w2, bias=smc("conv_b", tile)),
              reads=[b_src, b_sm], writes=[b_dst])
        P.add("dve", lambda e: e.scalar_tensor_tensor(out=dst, in0=src3(1), scalar=w1, in1=dst, op0=MUL, op1=ADD),
              reads=[b_src, b_sm, b_dst], writes=[b_dst])
        P.add("dve", lambda e: e.scalar_tensor_tensor(out=dst, in0=src3(0), scalar=w0, in1=dst, op0=MUL, op1=ADD),
              reads=[b_src, b_sm, b_dst], writes=[b_dst])

    def phase_E(xt, b_xt, n0, n, first, last):
        lo, hi = HO, HO + n
        c_lo = 0 if first else HO
        N = hi - c_lo
        npr = min(n0 + n, NPT) - n0
        ns = n - npr
        norm_to_xn(xt, b_xt, "g_ffn", c_lo, hi)
        for q in range(6):
            npair = min(4, 22 - 4 * q)
            wa, b_wa = load_slab("w_up", KD, 512 * q, 128 * npair)
            wv_, b_wv = load_slab("w_up", KD, D_FF + 512 * q, 128 * npair)
            for pi_ in range(npair):
                p = 4 * q + pi_
                ub = ucount[0] % 2
                ucount[0] += 1
                for av, (w, b_w) in enumerate(((wa, b_wa), (wv_, b_wv))):
                    tile = p + 22 * av
                    u, b_u = upx[ub][av], b_upx[ub][av]
                    c, b_c = cav[ub][av], b_cav[ub][av]
                    ps, b_ps = tbank()
                    mm_group(ps, b_ps, N, w, b_w, KD, 128 * pi_, lambda k: xn[:, k, c_lo:hi], [b_xn])
                    P.add("act", lambda e, ps=ps, u=u: e.activation(out=u[:, c_lo:hi], in_=ps[:, 0:N], func=AF.Copy), reads=[b_ps],
                          writes=[b_u])
                    if not first:
                        P.add("pool", lambda e, u=u, tile=tile: e.tensor_copy(out=u[:, 0:2], in_=carry[:, tile, :]), reads=[b_carry],
                              writes=[b_u])
                    conv3(0, lambda k, u=u: u[:, k:k + npr], c[:, 0:npr], b_u, b_c, tile, npr)
                    P.add("pool", lambda e, u=u, tile=tile: e.tensor_copy(out=carry[:, tile, :], in_=u[:, npr:npr + 2]), reads=[b_u],
                          writes=[b_carry])
                    if last:
                        P.add("pool", lambda e, u=u, tile=tile: e.tensor_copy(out=cvp[:, tile, :], in_=u[:, npr:npr + 2]), reads=[b_u],
                              writes=[b_cvp])
                    if ns:
                        us, b_us = upS[av], b_upS[av]
                        P.add("pool", lambda e, us=us, tile=tile: e.tensor_copy(out=us[:, :, 0:2], in_=scT[:, tile, :, :]), reads=[b_scT],
                              writes=[b_us])
                        P.add("pool", lambda e, us=us, u=u: e.tensor_copy(
                            out=us[:, :, 2:6], in_=u[:, HO + npr:HO + n].rearrange("p (q s) -> p q s", s=4)), reads=[b_u], writes=[b_us])
                        conv3(0, lambda k, us=us: us[:, :, k:k + 4], c[:, npr:n].rearrange("p (q s) -> p q s", s=4), b_us, b_c, tile, ns)
                        P.add("pool", lambda e, us=us, tile=tile: e.tensor_copy(out=cvs[:, tile, :, :], in_=us[:, :, 4:6]), reads=[b_us],
                              writes=[b_cvs])
                ca, cv = cav[ub][0], cav[ub][1]
                P.add("act", lambda e, ca=ca: e.activation(out=ca[:, 0:n], in_=ca[:, 0:n], func=AF.Gelu_apprx_tanh), reads=[b_cav[ub][0]],
                      writes=[b_cav[ub][0]])
                P.add("dve", lambda e, ca=ca, cv=cv, p=p: e.tensor_tensor(out=hT[:, p, 0:n], in0=ca[:, 0:n], in1=cv[:, 0:n], op=MUL),
                      reads=[b_cav[ub][0], b_cav[ub][1]], writes=[b_hT])
        for m in range(KD):
            wd, b_wd = load_slab("w_down", 22, 128 * m, 128)
            ps, b_ps = tbank()
            mm_group(ps, b_ps, n, wd, b_wd, 22, 0, lambda k: hT[:, k, 0:n], [b_hT])
            P.add("dve", lambda e, ps=ps, m=m: e.tensor_tensor(out=xt[:, m, lo:hi], in0=ps[:, 0:n], in1=xt[:, m, lo:hi], op=ADD),
                  reads=[b_ps, b_xt], writes=[b_xt])
        norm_to_xn(xt, b_xt, "g_ple", lo, hi)
        P.add("pool", lambda e: e.dma_start(out=pTc[:, :, 0:n], in_=pT_d.rearrange("(k p) t -> p k t", p=128)[:, :, n0:n0 + n]),
              writes=[b_pTc], dma=True, semkey="pTc")
        for q in range(2):
            wg_, b_wg = load_slab("w_pg", KD, 512 * q, 512)
            wp_, b_wp = load_slab("w_pp", 2, 512 * q, 512)
            for mi in range(4):
                m = 4 * q + mi
                pg, b_pg = tbank()
                mm_group(pg, b_pg, n, wg_, b_wg, KD, 128 * mi, lambda k: xn[:, k, lo:hi], [b_xn])
                pp, b_pp = tbank()
                mm_group(pp, b_pp, n, wp_, b_wp, 2, 128 * mi, lambda k: pTc[:, k, 0:n], [b_pTc])
                P.add("act", lambda e, pg=pg: e.activation(out=sgt[0][:, 0:n], in_=pg[:, 0:n], func=AF.Sigmoid), reads=[b_pg],
                      writes=[b_sgt[0]])
                P.add("dve", lambda e, pp=pp: e.tensor_tensor(out=sgt[1][:, 0:n], in0=pp[:, 0:n], in1=sgt[0][:, 0:n], op=MUL),
                      reads=[b_pp, b_sgt[0]], writes=[b_sgt[1]])
                P.add("pool", lambda e, m=m: e.tensor_tensor(out=xt[:, m, lo:hi], in0=xt[:, m, lo:hi], in1=sgt[1][:, 0:n], op=ADD),
                      reads=[b_xt, b_sgt[1]], writes=[b_xt])
        stores.append(P.add("sp", lambda e: e.dma_start(out=o_yT.rearrange("(k p) t -> p k t", p=128)[:, :, n0:n0 + n], in_=xt[:, :, lo:hi]),
                            reads=[b_xt], dma=True, semkey="st_y"))

    n0_4, n_4 = chunks[4]
    phase_D(x1c4, b_x1c4, n0_4, n_4)
    lastp = HO + (NPT - n0_4)
    P.add("pool", lambda e: e.tensor_copy(out=hsend[:].rearrange("p (k c) -> p k c", c=2), in_=x1c4[:, :, lastp - 2:lastp]),
          reads=[b_x1c4], writes=[b_hsend])
    b_hin, b_hout = P.buf("cc_h_in"), P.buf("cc_h_out")
    P.add("sp", lambda e: e.dma_start(out=cc_h_in, in_=hsend[:]), reads=[b_hsend], writes=[b_hin], dma=True, semkey="hs1")
    P.add("pool", lambda e: e.collective_compute("AllGather", ALU.bypass, replica_groups=[[0, 1, 2, 3], [4, 5, 6, 7]],
                                                 ins=[cc_h_in.opt()], outs=[cc_h_out.opt()]),
          reads=[b_hin], writes=[b_hout], dma="cc", semkey="hs2")
    P.add("sp", lambda e: e.dma_start(out=Hg[:], in_=cc_h_out.rearrange("(r p) f -> p r f", p=128)), reads=[b_hout], writes=[b_Hg],
          dma=True, semkey="hs3")
    P.add("dve", lambda e: e.tensor_scalar(out=hrecv[:], in0=Hg[:, 0, :], scalar1=smc("hsel", 0), scalar2=None, op0=MUL),
          reads=[b_Hg, b_sm], writes=[b_hrecv])
    for r in range(1, 4):
        P.add("dve", lambda e, r=r: e.scalar_tensor_tensor(out=hrecv[:], in0=Hg[:, r, :], scalar=smc("hsel", r), in1=hrecv[:],
                                                           op0=MUL, op1=ADD), reads=[b_Hg, b_sm, b_hrecv], writes=[b_hrecv])
    for ci in range(4):
        n0, n = chunks[ci]
        if ci == 0:
            P.add("pool", lambda e: e.tensor_copy(out=x1[:, :, 0:2], in_=hrecv[:].rearrange("p (k c) -> p k c", c=2)),
                  reads=[b_hrecv], writes=[b_x1])
        phase_D(x1, b_x1, n0, n)
        phase_E(x1, b_x1, n0, n, ci == 0, False)
    phase_E(x1c4, b_x1c4, n0_4, n_4, False, True)
    stores.append(P.add("sp", lambda e: e.dma_start(out=o_cvp, in_=cvp[:]), reads=[b_cvp], dma=True, semkey="st_cvp"))
    stores.append(P.add("sp", lambda e: e.dma_start(out=o_cvs, in_=cvs[:]), reads=[b_cvs], dma=True, semkey="st_cvs"))


def build_sample_attn(P, AR, nc, din, dout, dint, stores, banks, bank_bufs, sm, b_sm, smc, ones_bf, b_ones, ckvn, b_ckvn, krK, b_krK,
                      qT, b_qT, attT, b_attT, n_pool, w_uk_b, b_wkb, w_uv_b, b_wvb, rope_c, rope_s, rotm_d, mC1):
    MUL, ADD = ALU.mult, ALU.add
    CW = KV_LORA + QK_ROPE
    cache = din("cache", [n_pool * 32, 4 * CW])
    ptab = din("ptab", [128, SEQ_PER_CORE * 16], I32)
    p32c_d = din("p32c", [128, 1], I32)
    w_ukT_d = din("w_ukT", [64, 8 * KV_LORA])
    ropeP_d = din("ropeP", [128, 2, NPAGES, 16])
    gk_rep_d = din("gk_rep", [128, 32])
    hselm_d = din("hselm", [128, 4, 32])
    cmask_d = din("cmask", [32, 4])

    AR.release(mC1)
    P.new_phase()
    NB_PG = 5
    pb = [AR.alloc([128, 4, CW], BF16) for _ in range(NB_PG)]
    b_pb = [P.buf(f"pb{i}") for i in range(NB_PG)]
    kt3 = [AR.alloc([128, 4, 96], BF16) for _ in range(2)]
    b_kt3 = [P.buf("kt3_0"), P.buf("kt3_1")]
    rt = [AR.alloc([128, 4, 16], F32) for _ in range(4)]
    b_rt = P.buf("rt")
    ropeP = AR.alloc([128, 2, NPAGES, 16], F32)
    gkr = AR.alloc([128, 32], F32)
    tabs = AR.alloc([128, 4, NPAGES, 16], F32)
    ptb = AR.alloc([128, SEQ_PER_CORE * 16], I32)
    idx = AR.alloc([128, SEQ_PER_CORE * 16], I32)
    iot = AR.alloc([128, 1], I32)
    wk_sb = AR.alloc([128, 2, 512], BF16)
    wv_sb = AR.alloc([128, 2, 512], BF16)
    wukT = AR.alloc([64, 8, KV_LORA], BF16)
    wukT_f = AR.alloc([64, 8 * KV_LORA], F32)
    hselm = AR.alloc([128, 4, 32], BF16)
    hselm_f = AR.alloc([128, 4, 32], F32)
    cmask = AR.alloc([32, 4], F32)
    qgk = AR.alloc([64, 8, NST], BF16)
    Qabs = AR.alloc([128, 2, SEQ_PER_CORE, 32], BF16)
    Qrope = AR.alloc([96, SEQ_PER_CORE, 32], BF16)
    cT_sb = [AR.alloc([128, 2, 512], BF16) for _ in range(2)]
    krT_sb = [AR.alloc([96, 512], BF16) for _ in range(2)]
    kn_sb = [AR.alloc([128, 512], BF16) for _ in range(4)]
    sq_sb = [AR.alloc([128, 512], BF16) for _ in range(4)]
    sqk = AR.alloc([32, 512], BF16)
    lnr = AR.alloc([32, 512], F32)
    rr_ = AR.alloc([32, 512], F32)
    sr = AR.alloc([32, 512], F32)
    Pm = AR.alloc([32, 512], BF16)
    PT_sb = [AR.alloc([128, 4, 32], BF16) for _ in range(2)]
    Lacc = AR.alloc([32, 20], F32)
    accs = AR.alloc([32, KV_LORA], F32)
    lsum = AR.alloc([32, 1], F32)
    olat = AR.alloc([32, KV_LORA], BF16)
    olT = AR.alloc([128, 2, SEQ_PER_CORE, 32], BF16)
    knew = AR.alloc([96, NST], BF16)
    kraw = AR.alloc([96, NST], BF16)
    kg32 = AR.alloc([96, NST], F32)
    kt1 = AR.alloc([96, NST], F32)
    kt2 = AR.alloc([96, NST], F32)
    rc_s = AR.alloc([96, NST], F32)
    rs_s = AR.alloc([96, NST], F32)
    rotm = AR.alloc([96, 96], F32)
    cnew = AR.alloc([4, KV_LORA], BF16)
    names = ["kt", "ropeP", "gkr", "tabs", "ptb", "idx", "iot", "wk", "wv", "wukT", "hselm", "cmask", "qgk", "Qabs", "Qrope",
             "sqk", "lnr", "rr", "sr", "Pm", "Lacc", "accs", "lsum", "olat", "olT", "knew", "kraw", "ktmp", "rcs", "rotm", "cnew"]
    B = {n: P.buf("s_" + n) for n in names}
    b_cT = [P.buf("cT0"), P.buf("cT1")]
    b_krT = [P.buf("krT0"), P.buf("krT1")]
    b_kn = [P.buf(f"kn{i}") for i in range(4)]
    b_sq = [P.buf(f"sq{i}") for i in range(4)]
    b_PT = [P.buf("PTs0"), P.buf("PTs1")]

    rrb = [0]

    def tbank():
        i = 1 + rrb[0] % 7
        rrb[0] += 1
        return banks[i], bank_bufs[i]
    ACC, b_ACC = banks[0], bank_bufs[0]

    ld = lambda out, in_, wr, key, rd=(): P.add("sp", lambda e: e.dma_start(out=out, in_=in_), reads=list(rd), writes=[wr], dma=True,
                                                semkey=key)
    ld(ropeP[:], ropeP_d, B["ropeP"], "s_ropeP")
    ld(gkr[:], gk_rep_d, B["gkr"], "s_gkr")
    ld(ptb[:], ptab, B["ptb"], "s_ptb")
    ld(iot[:], p32c_d, B["iot"], "s_iot")
    ld(wk_sb[:], w_uk_b.rearrange("(k p) m -> p k m", p=128), B["wk"], "s_wk", [b_wkb])
    ld(wv_sb[:], w_uv_b.rearrange("(k p) m -> p k m", p=128), B["wv"], "s_wv", [b_wvb])
    ld(wukT_f[:], w_ukT_d, B["wukT"], "s_wukT")
    ld(hselm_f[:], hselm_d, B["hselm"], "s_hselm")
    ld(cmask[:], cmask_d, B["cmask"], "s_cmask")
    ld(rc_s[:], rope_c[:, NPT:T], B["rcs"], "s_rcs")
    ld(rs_s[:], rope_s[:, NPT:T], B["rcs"], "s_rss")
    ld(rotm[:], rotm_d, B["rotm"], "s_rotm")
    P.add("pool", lambda e: e.tensor_copy(out=wukT[:].rearrange("p h l -> p (h l)"), in_=wukT_f[:]), reads=[B["wukT"]], writes=[B["wukT"]])
    P.add("pool", lambda e: e.tensor_copy(out=hselm[:], in_=hselm_f[:]), reads=[B["hselm"]], writes=[B["hselm"]])
    P.add("dve", lambda e: e.tensor_scalar(out=idx[:], in0=ptb[:], scalar1=32.0, scalar2=iot[:, 0:1], op0=MUL, op1=ADD),
          reads=[B["ptb"], B["iot"]], writes=[B["idx"]])
    g1 = gkr[:, 0:16].unsqueeze(1).broadcast_to([128, NPAGES, 16])
    g2 = gkr[:, 16:32].unsqueeze(1).broadcast_to([128, NPAGES, 16])
    tt_ = lambda o, a, b_, op: P.add("pool", lambda e: e.tensor_tensor(out=o, in0=a, in1=b_, op=op), reads=[B["ropeP"], B["gkr"], B["tabs"]],
                                     writes=[B["tabs"]])
    tt_(tabs[:, 0], ropeP[:, 0], g1, MUL)
    tt_(tabs[:, 1], ropeP[:, 0], g2, MUL)
    tt_(tabs[:, 2], ropeP[:, 1], g2, MUL)
    P.add("pool", lambda e: e.tensor_single_scalar(out=tabs[:, 2], in_=tabs[:, 2], scalar=-1.0, op=MUL), reads=[B["tabs"]],
          writes=[B["tabs"]])
    tt_(tabs[:, 3], ropeP[:, 1], g1, MUL)
    for i in range(2):
        P.add("pool", lambda e, i=i: e.memset(kt3[i][:], 0.0), writes=[b_kt3[i]])

    P.add("dve", lambda e: e.tensor_scalar(out=qgk[:], in0=qT[0:64, :, NPT:T], scalar1=smc("g_k")[0:64, :], scalar2=None, op0=MUL),
          reads=[b_qT, b_sm], writes=[B["qgk"]])
    for h in range(N_HEADS):
        for kt in range(2):
            ps, b_ps = tbank()
            P.add("pe", lambda e, ps=ps, h=h, kt=kt: e.matmul(ps[:, 0:NST], lhsT=wukT[:, h, 128 * kt:128 * kt + 128], rhs=qgk[:, h, :],
                                                              start=True, stop=True), reads=[B["wukT"], B["qgk"]], writes=[b_ps])
            P.add("act", lambda e, ps=ps, h=h, kt=kt: e.activation(out=Qabs[:, kt, :, 4 * h:4 * h + 4],
                                                                   in_=ps[:, 0:NST].rearrange("p (q t) -> p q t", t=4), func=AF.Copy),
                  reads=[b_ps], writes=[B["Qabs"]])
        P.add("pool", lambda e, h=h: e.tensor_copy(out=Qrope[64:96, :, 4 * h:4 * h + 4],
                                                   in_=qT[64:96, h, NPT:T].rearrange("p (q t) -> p q t", t=4)),
              reads=[b_qT], writes=[B["Qrope"]])
    P.add("act", lambda e: e.activation(out=kraw[64:96, :], in_=krK[64:96, NPT:T], func=AF.Copy), reads=[b_krK], writes=[B["kraw"]])
    P.add("dve", lambda e: e.tensor_scalar(out=kg32[64:96, :], in0=krK[64:96, NPT:T], scalar1=smc("g_k")[64:96, :], scalar2=None, op0=MUL),
          reads=[b_krK, b_sm], writes=[B["ktmp"]])
    ps, b_ps = tbank()
    P.add("pe", lambda e, ps=ps: e.matmul(ps[0:96, 0:NST], lhsT=rotm[64:96, 0:96], rhs=kg32[64:96, :], start=True, stop=True,
                                          tile_position=(64, 0)), reads=[B["rotm"], B["ktmp"]], writes=[b_ps])
    P.add("dve", lambda e: e.tensor_tensor(out=kt1[64:96, :], in0=kg32[64:96, :], in1=rc_s[64:96, :], op=MUL), reads=[B["ktmp"], B["rcs"]],
          writes=[B["ktmp"]])
    P.add("dve", lambda e, ps=ps: e.tensor_tensor(out=kt2[64:96, :], in0=ps[64:96, 0:NST], in1=rs_s[64:96, :], op=MUL),
          reads=[b_ps, B["rcs"]], writes=[B["ktmp"]])
    P.add("dve", lambda e: e.tensor_tensor(out=knew[64:96, :], in0=kt1[64:96, :], in1=kt2[64:96, :], op=ADD), reads=[B["ktmp"]],
          writes=[B["knew"]])

    idb = AR.alloc([128, 128], BF16)
    b_idb = P.buf("idb")
    P.add("pool", lambda e: e.tensor_copy(out=idb[:], in_=sm[:, SL["ident"][0]:SL["ident"][0] + 128]), reads=[b_sm], writes=[b_idb])
    sqk2 = [sqk, AR.alloc([32, 512], BF16)]
    lnr2 = [lnr, AR.alloc([32, 512], F32)]
    rr2 = [rr_, AR.alloc([32, 512], F32)]
    sr2 = [sr, AR.alloc([32, 512], F32)]
    Pm2 = [Pm, AR.alloc([32, 512], BF16)]
    Lacc2 = [Lacc, AR.alloc([32, 20], F32)]
    Bq = [{n: P.buf(f"s2_{n}{i}") for n in ("sqk", "lnr", "rr", "sr", "Pm")} for i in range(2)]
    b_Lacc2 = [P.buf("Lacc0"), P.buf("Lacc1")]
    cnt = [0]
    gcnt = [0]

    def tbank2():
        i = 2 + rrb[0] % 6
        rrb[0] += 1
        return banks[i], bank_bufs[i]

    def chunk(q, col, npos, cT, b_cTs, kraw_ap, b_kraw, krop_ap, b_krop, crows, first, mask):
        n = npos
        ACC, b_ACC = banks[q % 2], bank_bufs[q % 2]
        Lq, b_Lq = Lacc2[q % 2], b_Lacc2[q % 2]
        ci = gcnt[0] % 2
        gcnt[0] += 1
        sqk_, lnr_, rr__, sr_, Pm_ = sqk2[ci], lnr2[ci], rr2[ci], sr2[ci], Pm2[ci]
        Bc = Bq[ci]
        pss_l = []
        for m in range(4):
            ps, b_ps = tbank2()
            for kt in range(2):
                P.add("pe", lambda e, ps=ps, m=m, kt=kt: e.matmul(ps[:, 0:n], lhsT=wk_sb[:, kt, 128 * m:128 * m + 128], rhs=cT[:, kt, 0:n],
                                                                  start=(kt == 0), stop=(kt == 1)), reads=[B["wk"]] + b_cTs, writes=[b_ps])
            pss_l.append((ps, b_ps))
        for m in range(4):
            ps, b_ps = pss_l[m]
            if m < 2:
                P.add("act", lambda e, ps=ps, m=m: e.activation(out=kn_sb[m][:, 0:n], in_=ps[:, 0:n], func=AF.Copy), reads=[b_ps],
                      writes=[b_kn[m]])
            else:
                P.add("dve", lambda e, ps=ps, m=m: e.tensor_copy(out=kn_sb[m][:, 0:n], in_=ps[:, 0:n]), reads=[b_ps], writes=[b_kn[m]])
            P.add("dve", lambda e, m=m: e.tensor_tensor(out=sq_sb[m][:, 0:n], in0=kn_sb[m][:, 0:n], in1=kn_sb[m][:, 0:n], op=MUL),
                  reads=[b_kn[m]], writes=[b_sq[m]])
        P.add("pool", lambda e: e.tensor_tensor(out=sqk_[:, 0:n], in0=kraw_ap, in1=kraw_ap, op=MUL), reads=b_kraw, writes=[Bc["sqk"]])
        yield
        pss, b_pss = tbank2()
        for m in range(4):
            P.add("pe", lambda e, m=m: e.matmul(pss[0:32, 0:n], lhsT=hselm[:, m, :], rhs=sq_sb[m][:, 0:n], start=(m == 0), stop=False),
                  reads=[B["hselm"], b_sq[m]], writes=[b_pss])
        P.add("pe", lambda e: e.matmul(pss[0:32, 0:n], lhsT=ones_bf[0:32, 0:32], rhs=sqk_[:, 0:n], start=False, stop=True),
              reads=[b_ones, Bc["sqk"]], writes=[b_pss])
        psc, b_psc = tbank2()
        for kt in range(2):
            P.add("pe", lambda e, kt=kt: e.matmul(psc[0:32, 0:n], lhsT=Qabs[:, kt, q, :], rhs=cT[:, kt, 0:n], start=(kt == 0), stop=False),
                  reads=[B["Qabs"]] + b_cTs, writes=[b_psc])
        P.add("pe", lambda e: e.matmul(psc[0:32, 0:n], lhsT=Qrope[64:96, q, :], rhs=krop_ap, start=False, stop=True, tile_position=(64, 0)),
              reads=[B["Qrope"]] + b_krop, writes=[b_psc])
        P.add("act", lambda e: e.activation(out=lnr_[:, 0:n], in_=pss[0:32, 0:n], func=AF.Ln, scale=1.0 / QK_HEAD, bias=EPS),
              reads=[b_pss], writes=[Bc["lnr"]])
        P.add("act", lambda e: e.activation(out=rr__[:, 0:n], in_=lnr_[:, 0:n], func=AF.Exp, scale=-0.5), reads=[Bc["lnr"]],
              writes=[Bc["rr"]])
        P.add("dve", lambda e: e.tensor_tensor(out=sr_[:, 0:n], in0=psc[0:32, 0:n], in1=rr__[:, 0:n], op=MUL), reads=[b_psc, Bc["rr"]],
              writes=[Bc["sr"]])
        if mask:
            P.add("act", lambda e: e.activation(out=sr_[:, 0:n], in_=sr_[:, 0:n], func=AF.Exp, scale=SCALE), reads=[Bc["sr"]],
                  writes=[Bc["sr"]])
            P.add("dve", lambda e: e.tensor_tensor(out=sr_[:, 0:n], in0=sr_[:, 0:n], in1=cmask[:, 0:n], op=MUL), reads=[Bc["sr"], B["cmask"]],
                  writes=[Bc["sr"]])
            P.add("dve", lambda e: e.tensor_copy(out=Pm_[:, 0:n], in_=sr_[:, 0:n]), reads=[Bc["sr"]], writes=[Bc["Pm"]])
            P.add("dve", lambda e: e.reduce_sum(out=Lq[:, col:col + 1], in_=sr_[:, 0:n], axis=AX.X), reads=[Bc["sr"]], writes=[b_Lq])
        else:
            P.add("act", lambda e: e.activation(out=Pm_[:, 0:n], in_=sr_[:, 0:n], func=AF.Exp, scale=SCALE, accum_out=Lq[:, col:col + 1]),
                  reads=[Bc["sr"]], writes=[Bc["Pm"], b_Lq])
        yield
        pT_, b_pT = tbank2()
        pTb = pT_[:].bitcast(BF16)
        nblk = len(crows)
        for bi, (cap, b_cap, c0, c1) in enumerate(crows):
            P.add("pe", lambda e, bi=bi, c0=c0, c1=c1: e.transpose(pTb[0:c1 - c0, 32 * bi:32 * bi + 32], Pm_[:, c0:c1], idb[0:32, 0:32]),
                  reads=[Bc["Pm"], b_idb], writes=[b_pT])
        pi_ = cnt[0] % 2
        cnt[0] += 1
        rows = crows[0][3] - crows[0][2]
        P.add("act", lambda e, pi_=pi_: e.activation(out=PT_sb[pi_][0:rows, 0:nblk, :],
                                                     in_=pTb[0:rows, 0:32 * nblk].rearrange("p (b c) -> p b c", c=32), func=AF.Copy),
              reads=[b_pT], writes=[b_PT[pi_]])
        yield
        for bi, (cap, b_cap, c0, c1) in enumerate(crows):
            P.add("pe", lambda e, bi=bi, cap=cap, c0=c0, c1=c1, pi_=pi_, st=(first and bi == 0): e.matmul(
                ACC[0:32, 0:KV_LORA], lhsT=PT_sb[pi_][0:c1 - c0, bi, :], rhs=cap, start=st, stop=False),
                reads=[b_PT[pi_]] + b_cap, writes=[b_ACC])

    pgc = [0]
    kcn = [0]

    def page_chunk(q, g):
        bi_ = pgc[0] % NB_PG
        ki = pgc[0] % 2
        ci_ = pgc[0] % 2
        pgc[0] += 1
        pbt, b_pbt = pb[bi_], b_pb[bi_]
        ch = q * 16 + g
        P.add("pool", lambda e: e.indirect_dma_start(
            out=pbt[:].rearrange("p a c -> p (a c)"), out_offset=None, in_=cache,
            in_offset=bass.IndirectOffsetOnAxis(ap=idx[:, ch:ch + 1], axis=0)),
            reads=[B["idx"]], writes=[b_pbt], dma=True, semkey=f"pb{bi_}")
        yield
        yield
        yield
        yield
        ki = kcn[0] % 2
        kcn[0] += 1
        k3, b_k3 = kt3[ki], b_kt3[ki]
        kr1, kr2 = pbt[:, :, KV_LORA:KV_LORA + 16], pbt[:, :, KV_LORA + 16:KV_LORA + 32]
        pgs = slice(4 * g, 4 * g + 4)
        pl = lambda fn, rd, wr: P.add("pool", fn, reads=rd, writes=wr)
        pl(lambda e: e.tensor_copy(out=k3[:, :, 0:32], in_=pbt[:, :, KV_LORA:CW]), [b_pbt], [b_k3])
        pl(lambda e: e.tensor_tensor(out=rt[0][:], in0=kr1, in1=tabs[:, 0, pgs, :], op=MUL), [b_pbt, B["tabs"]], [b_rt])
        pl(lambda e: e.tensor_tensor(out=rt[1][:], in0=kr2, in1=tabs[:, 2, pgs, :], op=MUL), [b_pbt, B["tabs"]], [b_rt])
        pl(lambda e: e.tensor_tensor(out=k3[:, :, 64:80], in0=rt[0][:], in1=rt[1][:], op=ADD), [b_rt], [b_k3])
        pl(lambda e: e.tensor_tensor(out=rt[2][:], in0=kr2, in1=tabs[:, 1, pgs, :], op=MUL), [b_pbt, B["tabs"]], [b_rt])
        pl(lambda e: e.tensor_tensor(out=rt[3][:], in0=kr1, in1=tabs[:, 3, pgs, :], op=MUL), [b_pbt, B["tabs"]], [b_rt])
        pl(lambda e: e.tensor_tensor(out=k3[:, :, 80:96], in0=rt[2][:], in1=rt[3][:], op=ADD), [b_rt], [b_k3])
        yield
        psT, b_psT = tbank2()
        psTb = psT[:].bitcast(BF16).rearrange("p (k n) -> p k n", k=2)
        psK, b_psK = tbank2()
        psKb = psK[:].bitcast(BF16)
        for pg in range(4):
            for kt in range(2):
                P.add("pe", lambda e, pg=pg, kt=kt: e.transpose(psTb[:, kt, 128 * pg:128 * pg + 128], pbt[:, pg, 128 * kt:128 * kt + 128],
                                                                idb[:, :]), reads=[b_pbt, b_idb], writes=[b_psT])
            P.add("pe", lambda e, pg=pg: e.transpose(psKb[0:96, 128 * pg:128 * pg + 128], k3[:, pg, :], idb[:, :]), reads=[b_k3, b_idb],
                  writes=[b_psK])
        P.add("dve", lambda e: e.tensor_copy(out=cT_sb[ci_][:], in_=psTb), reads=[b_psT], writes=[b_cT[ci_]])
        P.add("act", lambda e: e.activation(out=krT_sb[ci_][:], in_=psKb[0:96, 0:512], func=AF.Copy), reads=[b_psK], writes=[b_krT[ci_]])
        yield
        crows = [(pbt[:, pg, 0:KV_LORA], [b_pbt], 128 * pg, 128 * pg + 128) for pg in range(4)]
        yield from chunk(q, g, 512, cT_sb[ci_], [b_cT[ci_]], krT_sb[ci_][0:32, 0:512], [b_krT[ci_]], krT_sb[ci_][64:96, 0:512],
                         [b_krT[ci_]], crows, g == 0, False)

    def run_pipelined(gens, step=2):
        active = []
        it = iter(gens)
        more = True
        while more or active:
            if more:
                try:
                    active.append(next(it))
                except StopIteration:
                    more = False
            for _ in range(step):
                for gg in list(active):
                    try:
                        next(gg)
                    except StopIteration:
                        active.remove(gg)

    for q in range(SEQ_PER_CORE):
        run_pipelined([page_chunk(q, g) for g in range(NPAGES // 4)])
        ACC, b_ACC = banks[q % 2], bank_bufs[q % 2]
        Lq, b_Lq = Lacc2[q % 2], b_Lacc2[q % 2]
        c0 = NPT + 4 * q
        psn, b_psn = tbank2()
        psnb = psn[:].bitcast(BF16)
        for kt in range(2):
            P.add("pe", lambda e, kt=kt, c0=c0, psnb=psnb: e.transpose(psnb[0:4, 128 * kt:128 * kt + 128], ckvn[:, kt, c0:c0 + 4], idb[:, :]),
                  reads=[b_ckvn, b_idb], writes=[b_psn])
        P.add("act", lambda e, psnb=psnb: e.activation(out=cnew[:], in_=psnb[0:4, 0:KV_LORA], func=AF.Copy), reads=[b_psn], writes=[B["cnew"]])
        for _ in chunk(q, 16, 4, ckvn[:, :, c0:c0 + 4], [b_ckvn], kraw[64:96, 4 * q:4 * q + 4], [B["kraw"]], knew[64:96, 4 * q:4 * q + 4],
                       [B["knew"]], [(cnew[:, :], [B["cnew"]], 0, 4)], False, True):
            pass
        P.add("act", lambda e, ACC=ACC: e.activation(out=accs[:], in_=ACC[0:32, 0:KV_LORA], func=AF.Copy), reads=[b_ACC], writes=[B["accs"]])
        P.add("dve", lambda e, Lq=Lq: e.reduce_sum(out=lsum[:], in_=Lq[:, 0:17], axis=AX.X), reads=[b_Lq], writes=[B["lsum"]])
        P.add("dve", lambda e: e.reciprocal(out=lsum[:], in_=lsum[:]), reads=[B["lsum"]], writes=[B["lsum"]])
        P.add("dve", lambda e: e.tensor_scalar(out=olat[:], in0=accs[:], scalar1=lsum[:, 0:1], scalar2=None, op0=MUL),
              reads=[B["accs"], B["lsum"]], writes=[B["olat"]])
        pso, b_pso = tbank2()
        psob = pso[:].bitcast(BF16)
        for kt in range(2):
            P.add("pe", lambda e, kt=kt, psob=psob: e.transpose(psob[:, 32 * kt:32 * kt + 32], olat[:, 128 * kt:128 * kt + 128],
                                                                idb[0:32, 0:32]), reads=[B["olat"], b_idb], writes=[b_pso])
        P.add("act", lambda e, q=q, psob=psob: e.activation(out=olT[:, :, q, :], in_=psob[:, 0:64].rearrange("p (k c) -> p k c", k=2),
                                                            func=AF.Copy), reads=[b_pso], writes=[B["olT"]])
    for hp in range(4):
        ps, b_ps = tbank()
        for hh in range(2):
            h = 2 * hp + hh
            for kt in range(2):
                P.add("pe", lambda e, ps=ps, hh=hh, h=h, kt=kt: e.matmul(
                    ps[64 * hh:64 * hh + 64, 0:NST], lhsT=wv_sb[:, kt, 64 * h:64 * h + 64], rhs=olT[:, kt, :, 4 * h:4 * h + 4],
                    start=(kt == 0), stop=(kt == 1), tile_position=(0, 64 * hh)), reads=[B["wv"], B["olT"]], writes=[b_ps])
        P.add("act", lambda e, ps=ps, hp=hp: e.activation(out=attT[:, hp, NPT:T], in_=ps[:, 0:NST], func=AF.Copy), reads=[b_ps],
              writes=[b_attT])


def build(stage=99, n_pool=10240, dbg=False):
    nc = bass.Bass("TRN2", target_bir_lowering=False)
    P = Prog(nc)
    ins_, outs_ = {}, {}

    def din(name, shape, dt=F32):
        ins_[name] = nc.dram_tensor(name, list(shape), dt, kind="ExternalInput").ap()
        return ins_[name]

    def dout(name, shape, dt=F32):
        outs_[name] = nc.dram_tensor(name, list(shape), dt, kind="ExternalOutput").ap()
        return outs_[name]

    def dint(name, shape, dt):
        return nc.dram_tensor(name, list(shape), dt).ap()

    xT = din("xT", [D, T])
    small = din("small", [128, SL["_n"]])
    rope_c = din("rope_c", [96, T])
    rope_s = din("rope_s", [96, T])
    w_in = din("w_in", [D, IN_COLS])
    o_ckvT = dout("o_ckvT", [KV_LORA, T])
    o_krT = dout("o_krT", [QK_ROPE, T])

    w_in_b = dint("w_in_b", [D, IN_COLS], BF16)

    stores = []
    pool_q = "pool"

    def cast_w(dst, src, rows, cols, key):
        a = 1
        while cols // a > 2048 or cols % a:
            a += 1
        s2 = src.rearrange("k (a m) -> (k a) m", a=a) if a > 1 else src
        d2 = dst.rearrange("k (a m) -> (k a) m", a=a) if a > 1 else dst
        b = P.buf(key)
        P.add(pool_q, lambda e: e.dma_start(out=d2, in_=s2), writes=[b], dma=True, semkey=key)
        return b

    b_w_in_b = cast_w(w_in_b, w_in, D, IN_COLS, "c_w_in")
    w_glu = din("w_glu", [SSM_W, 2 * SSM_W])
    w_glu_b = dint("w_glu_b", [SSM_W, 2 * SSM_W], BF16)
    b_w_glu_b = cast_w(w_glu_b, w_glu, SSM_W, 2 * SSM_W, "c_w_glu")

    ones_bf = P.sbuf("ones_bf", [128, 128], BF16)
    b_ones = P.buf("ones")
    P.add("pool", lambda e: e.memset(ones_bf[:], 1.0), writes=[b_ones])
    sm = P.sbuf("sm", [128, SL["_n"]], F32)
    b_sm = P.buf("sm")
    P.add("sp", lambda e: e.dma_start(out=sm[:], in_=small), writes=[b_sm], dma=True, semkey="sm")

    def smc(name, i=0):
        o = SL[name][0] + i
        return sm[:, o:o + 1]

    NB = 8
    banks = [P.psum(f"ps{i}", [128, 512], F32) for i in range(NB)]
    bank_bufs = [P.buf(f"ps{i}") for i in range(NB)]
    bank_rr = [0]

    def next_bank():
        i = bank_rr[0] % NB
        bank_rr[0] += 1
        return banks[i], bank_bufs[i]

    cqn = P.sbuf("cqn", [128, 3, T], BF16)
    ckvn = P.sbuf("ckvn", [128, 2, T], BF16)
    krK = P.sbuf("krK", [96, T], F32)
    uT = P.sbuf("uT", [128, 4, T], BF16)
    b_cqn, b_ckvn, b_krT, b_uT = P.buf("cqn"), P.buf("ckvn"), P.buf("krT"), P.buf("uT")

    chunks = _token_chunks()
    AR = Arena(P, "arena", 142 * 1024)

    NA = OFF_GA
    wA = AR.alloc([128, KD, NA], BF16)
    b_wA = P.buf("wA")
    P.add("sp", lambda e: e.dma_start(out=wA[:], in_=w_in_b.rearrange("(k p) m -> p k m", p=128)[:, :, 0:NA]),
          reads=[b_w_in_b], writes=[b_wA], dma=True, semkey="wA")

    xc = [AR.alloc([128, KD, 512], F32) for i in range(2)]
    b_xc = [P.buf(f"xc{i}") for i in range(2)]
    sq = AR.alloc([128, KD, 512], BF16)
    b_sq = P.buf("sq")
    lnv = AR.alloc([128, 512], F32)
    b_lnv = P.buf("lnv")
    rstd = AR.alloc([128, 512], F32)
    b_rstd = P.buf("rstd")
    xn = AR.alloc([128, KD, 512], BF16)
    b_xn = P.buf("xn")
    cqf = AR.alloc([128, 3, 512], F32)
    b_cqf = P.buf("cqf")
    ckvf = AR.alloc([128, 2, 512], F32)
    b_ckvf = P.buf("ckvf")
    ckvo = AR.alloc([128, 2, 512], F32)
    b_ckvo = P.buf("ckvo")

    xT_v = xT.rearrange("(k p) t -> p k t", p=128)

    def rms_rstd(src, b_src, nk, n, nfeat, dst=rstd, b_dst=b_rstd):
        P.add("dve", lambda e: e.tensor_tensor(out=sq[:, 0:nk, 0:n], in0=src[:, 0:nk, 0:n], in1=src[:, 0:nk, 0:n],
                                               op=ALU.mult), reads=[b_src], writes=[b_sq])
        ps, b_ps = next_bank()
        for k in range(nk):
            P.add("pe", lambda e, k=k: e.matmul(ps[:, 0:n], lhsT=ones_bf[:], rhs=sq[:, k, 0:n], start=(k == 0),
                                                stop=(k == nk - 1)), reads=[b_sq, b_ones], writes=[b_ps])
        P.add("act", lambda e: e.activation(out=lnv[:, 0:n], in_=ps[:, 0:n], func=AF.Ln, scale=1.0 / nfeat, bias=EPS),
              reads=[b_ps], writes=[b_lnv])
        P.add("act", lambda e: e.activation(out=dst[:, 0:n], in_=lnv[:, 0:n], func=AF.Exp, scale=-0.5),
              reads=[b_lnv], writes=[b_dst])

    for ci, (n0, n) in enumerate(chunks):
        xb, b_x = xc[ci % 2], b_xc[ci % 2]
        P.add("sp", lambda e, xb=xb, n0=n0, n=n: e.dma_start(out=xb[:, :, 0:n], in_=xT_v[:, :, n0:n0 + n]),
              writes=[b_x], dma=True, semkey=f"xc{ci % 2}")
        rms_rstd(xb, b_x, KD, n, D)
        for k in range(KD):
            P.add("dve", lambda e, k=k, xb=xb, n=n: e.scalar_tensor_tensor(
                out=xn[:, k, 0:n], in0=xb[:, k, 0:n], scalar=smc("g_mix", k), in1=rstd[:, 0:n],
                op0=ALU.mult, op1=ALU.mult), reads=[b_x, b_rstd, b_sm], writes=[b_xn])
        groups = [("cq", i, OFF_CKV * 0 + 128 * i, 128) for i in range(3)] + \
                 [("ckv", i, OFF_CKV + 128 * i, 128) for i in range(2)] + \
                 [("kr", 0, OFF_KR, 32)] + [("u", i, OFF_U + 128 * i, 128) for i in range(4)]
        for kind, i, c0, m in groups:
            ps, b_ps = next_bank()
            for k in range(KD):
                if kind == "kr":
                    P.add("pe", lambda e, k=k, c0=c0, m=m, ps=ps, n=n: e.matmul(
                        ps[64:96, 0:n], lhsT=wA[:, k, c0:c0 + m], rhs=xn[:, k, 0:n], start=(k == 0), stop=(k == KD - 1),
                        tile_position=(0, 64)), reads=[b_wA, b_xn], writes=[b_ps])
                    continue
                P.add("pe", lambda e, k=k, c0=c0, m=m, ps=ps, n=n: e.matmul(
                    ps[0:m, 0:n], lhsT=wA[:, k, c0:c0 + m], rhs=xn[:, k, 0:n], start=(k == 0), stop=(k == KD - 1)),
                    reads=[b_wA, b_xn], writes=[b_ps])
            if kind == "cq":
                P.add("act", lambda e, i=i, ps=ps, n=n: e.activation(out=cqf[:, i, 0:n], in_=ps[:, 0:n], func=AF.Copy),
                      reads=[b_ps], writes=[b_cqf])
            elif kind == "ckv":
                P.add("act", lambda e, i=i, ps=ps, n=n: e.activation(out=ckvf[:, i, 0:n], in_=ps[:, 0:n], func=AF.Copy),
                      reads=[b_ps], writes=[b_ckvf])
            elif kind == "kr":
                P.add("act", lambda e, ps=ps, n=n, n0=n0: e.activation(out=krK[64:96, n0:n0 + n], in_=ps[64:96, 0:n], func=AF.Copy),
                      reads=[b_ps], writes=[b_krT])
            else:
                P.add("act", lambda e, i=i, ps=ps, n=n, n0=n0: e.activation(out=uT[:, i, n0:n0 + n], in_=ps[:, 0:n], func=AF.Copy),
                      reads=[b_ps], writes=[b_uT])
        rms_rstd(cqf, b_cqf, 3, n, Q_LORA)
        for i in range(3):
            P.add("dve", lambda e, i=i, n=n, n0=n0: e.scalar_tensor_tensor(
                out=cqn[:, i, n0:n0 + n], in0=cqf[:, i, 0:n], scalar=smc("g_cq", i), in1=rstd[:, 0:n],
                op0=ALU.mult, op1=ALU.mult), reads=[b_cqf, b_rstd, b_sm], writes=[b_cqn])
        rms_rstd(ckvf, b_ckvf, 2, n, KV_LORA)
        for i in range(2):
            P.add("dve", lambda e, i=i, n=n: e.scalar_tensor_tensor(
                out=ckvo[:, i, 0:n], in0=ckvf[:, i, 0:n], scalar=smc("g_ckv", i), in1=rstd[:, 0:n],
                op0=ALU.mult, op1=ALU.mult), reads=[b_ckvf, b_rstd, b_sm], writes=[b_ckvo])
        P.add("pool", lambda e, n=n, n0=n0: e.tensor_copy(out=ckvn[:, :, n0:n0 + n], in_=ckvo[:, :, 0:n]),
              reads=[b_ckvo], writes=[b_ckvn])
        stores.append(P.add("sp", lambda e, n=n, n0=n0: e.dma_start(
            out=o_ckvT.rearrange("(k p) t -> p k t", p=128)[:, :, n0:n0 + n], in_=ckvo[:, :, 0:n]),
            reads=[b_ckvo], dma=True, semkey="st_ckv"))
    stores.append(P.add("sp", lambda e: e.dma_start(out=o_krT, in_=krK[64:96, :]), reads=[b_krT], dma=True, semkey="st_kr"))


    if stage >= 2:
        ssm = build_ssm(P, AR, nc, din, dout, dint, stores, next_bank, uT, b_uT, smc, b_sm, sm, w_glu_b, b_w_glu_b, chunks)
        if dbg:
            o_dbg_ys = dout("o_dbg_ys", [128, 4, T], BF16)
            stores.append(P.add("sp", lambda e: e.dma_start(out=o_dbg_ys, in_=uT[:]), reads=[b_uT], dma=True, semkey="dbg_ys"))


    if stage >= 3:
        attT = P.sbuf("attT", [128, 4, T], BF16)
        b_attT = P.buf("attT")
        kS = P.sbuf("kS", [96, 8, NST], BF16)
        b_kS = P.buf("kS")
        att = build_attn(P, AR, nc, din, dout, dint, stores, banks, bank_bufs, cast_w, cqn, b_cqn, ckvn, b_ckvn, krK, b_krT,
                         sm, b_sm, smc, ones_bf, b_ones, rope_c, rope_s, attT, b_attT, chunks, kS, b_kS)
        if stage >= 5:
            build_sample_attn(P, AR, nc, din, dout, dint, stores, banks, bank_bufs, sm, b_sm, smc, ones_bf, b_ones, ckvn, b_ckvn,
                              krK, b_krT, att["qT"], att["b_qT"], attT, b_attT, n_pool, att["w_uk_b"], att["b_wkb"], att["w_uv_b"],
                              att["b_wvb"], rope_c, rope_s, att["rotm_d"], att["mC1"])
        if dbg:
            o_dbg_att = dout("o_dbg_att", [128, 4, T], BF16)
            stores.append(P.add("sp", lambda e: e.dma_start(out=o_dbg_att, in_=attT[:]), reads=[b_attT], dma=True, semkey="dbg_att"))


    if stage >= 4:
        build_tail(P, AR, nc, din, dout, dint, stores, banks, bank_bufs, cast_w, xT, w_in_b, b_w_in_b, sm, b_sm, smc, ones_bf, b_ones,
                   attT, b_attT, ssm["ysT"], ssm["b_ys"], chunks)

    P.add("sp", lambda e: None, after=stores)
    P.finalize()
    return nc, ins_, outs_, P


def _rope_tables(pos):
    inv_freq = np.power(np.float32(10000.0), -np.arange(0, QK_ROPE, 2, dtype=np.float32) / np.float32(QK_ROPE)).astype(np.float32)
    ang = pos.astype(np.float32)[:, None] * inv_freq[None, :]
    return np.cos(ang).astype(np.float32), np.sin(ang).astype(np.float32)


def _prep_core(c, inp):
    b, j = c // 4, c % 4
    m = {}
    xp = inp["x_prompt"][b, NPT * j:NPT * (j + 1)]
    xs = inp["x_sample"][SEQ_PER_CORE * c:SEQ_PER_CORE * (c + 1)].reshape(NST, D)
    m["xT"] = np.ascontiguousarray(np.concatenate([xp, xs], 0).T)
    pos = np.concatenate([NPT * j + np.arange(NPT), np.tile(PAST + np.arange(4), SEQ_PER_CORE)])
    cs, sn = _rope_tables(pos)
    rc = np.ones((96, T), np.float32)
    rs = np.zeros((96, T), np.float32)
    rc[64:80] = cs.T
    rc[80:96] = cs.T
    rs[64:80] = sn.T
    rs[80:96] = sn.T
    m["rope_c"], m["rope_s"] = rc, rs
    sm = np.zeros((128, SL["_n"]), np.float32)

    def put(name, arr):
        o, n = SL[name]
        sm[:arr.shape[0], o:o + arr.shape[1]] = arr
    put("g_mix", inp["g_mix"][0].reshape(8, 128).T)
    put("g_cq", inp["g_cq"][0].reshape(3, 128).T)
    put("g_ckv", inp["g_ckv"][0].reshape(2, 128).T)
    put("g_ffn", inp["g_ffn"][0].reshape(8, 128).T)
    put("g_ple", inp["g_ple"][0].reshape(8, 128).T)
    cw = inp["conv_w"][0].reshape(3, 44, 128)
    put("conv_w", cw.transpose(2, 0, 1).reshape(128, 132))
    put("conv_b", inp["conv_b"][0].reshape(44, 128).T)
    put("g_q", inp["g_q"][0].reshape(96, 1))
    put("g_k", inp["g_k"][0].reshape(96, 1))
    put("d_skip", inp["d_skip"][0].reshape(4, 128).T)
    put("vis", np.tile((np.arange(4) <= j).astype(np.float32)[None], (128, 1)))
    put("full", np.tile((np.arange(4) < j).astype(np.float32)[None], (128, 1)))
    put("ident", np.eye(128, dtype=np.float32))
    put("hsel", np.tile((np.arange(4) == j - 1).astype(np.float32)[None], (128, 1)))
    m["small"] = sm
    m["w_in"] = inp["w_in"][0]
    m["w_glu"] = inp["w_glu"][0]
    m["w_uq"] = inp["w_uq"][0].reshape(Q_LORA, 768)
    m["w_oa"], m["w_os"], m["w_out"] = inp["w_oa"][0], inp["w_os"][0], inp["w_out"][0]
    m["w_up"], m["w_down"] = inp["w_up"][0], inp["w_down"][0]
    m["w_pg"], m["w_pp"] = inp["w_ple_gate"][0], inp["w_ple_proj"][0]
    pp_ = inp["p_prompt"][0, b, NPT * j:NPT * (j + 1)]
    ps_ = inp["p_sample"][0, SEQ_PER_CORE * c:SEQ_PER_CORE * (c + 1)].reshape(NST, PLE)
    m["pT"] = np.ascontiguousarray(np.concatenate([pp_, ps_], 0).T)
    sc = inp["state_conv"][0, SEQ_PER_CORE * c:SEQ_PER_CORE * (c + 1)]
    m["scT"] = np.ascontiguousarray(sc.reshape(16, 2, 44, 128).transpose(3, 2, 0, 1))
    m["w_uk"] = inp["w_uk"][0].reshape(KV_LORA, 512)
    m["w_uv"] = inp["w_uv"][0].reshape(KV_LORA, 512)
    tri = (np.arange(128)[:, None] <= np.arange(128)[None, :]).astype(np.float32)
    md = np.zeros((128, 4, 128), np.float32)
    for r in range(4):
        vis, full = float(r <= j), float(r < j)
        md[:, r, :] = full + (vis - full) * tri
    m["maskd"] = md
    pt_ = inp["page_table"][SEQ_PER_CORE * c:SEQ_PER_CORE * (c + 1)].astype(np.int32).reshape(16, 16, 4)
    m["ptab"] = np.ascontiguousarray(np.repeat(pt_.transpose(2, 0, 1).reshape(4, 256), 32, axis=0))
    m["p32c"] = (np.arange(128, dtype=np.int32) % 32).reshape(128, 1)
    m["w_ukT"] = np.ascontiguousarray(inp["w_uk"][0].transpose(2, 1, 0).reshape(64, 8 * KV_LORA))
    cs_p, sn_p = _rope_tables(np.arange(PAST))
    rp = np.stack([cs_p, sn_p], 0).reshape(2, 16, 4, 32, 4, 16)
    m["ropeP"] = np.ascontiguousarray(rp.transpose(2, 3, 0, 1, 4, 5).reshape(128, 2, NPAGES, 16))
    m["gk_rep"] = np.tile(inp["g_k"][0, 64:96][None], (128, 1)).astype(np.float32)
    hs_ = np.zeros((128, 4, 32), np.float32)
    for mm in range(4):
        for hh in range(2):
            hs_[64 * hh:64 * hh + 64, mm, 4 * (2 * mm + hh):4 * (2 * mm + hh) + 4] = 1.0
    m["hselm"] = hs_
    cm = np.zeros((32, 4), np.float32)
    for h_ in range(8):
        for t_ in range(4):
            cm[4 * h_ + t_, :t_ + 1] = 1.0
    m["cmask"] = cm
    rot = np.zeros((96, 96), np.float32)
    for i in range(16):
        rot[80 + i, 64 + i] = -1.0
        rot[64 + i, 80 + i] = 1.0
    m["rotm"] = rot
    m["ssm_s"], m["ssm_r"] = _ssm_packs(c, inp)
    return m


def _ssm_packs(c, inp):
    j = c % 4
    a_re, a_im, logdt = inp["a_re"][0], inp["a_im"][0], inp["log_dt"][0]
    b_re, b_im, c_re, c_im = inp["b_re"][0], inp["b_im"][0], inp["c_re"][0], inp["c_im"][0]

    def st(a):
        return a.reshape(16, 2, 64).transpose(1, 2, 0).reshape(128, 16)
    ss = np.zeros((128, SSL["_n"]), np.float32)

    def put(lay, arr_, name, arr):
        o, n = lay[name]
        arr_[:, o:o + n] = arr.reshape(128, n)
    put(SSL, ss, "a_re", st(a_re))
    put(SSL, ss, "a_im", st(a_im))
    put(SSL, ss, "logdt", st(np.repeat(logdt[:, None], 64, 1)))
    for nm, cc in (("c_re", c_re), ("c_im", c_im)):
        c4 = cc.reshape(16, 2, 16, 64)
        pad = np.zeros((2, 64, 16, 2, 16), np.float32)
        for g2 in range(2):
            pad[g2, :, :, g2, :] = c4[:, g2].transpose(2, 0, 1)
        put(SSL, ss, nm, pad)
    for nm, bb in (("b_re", b_re), ("b_im", b_im)):
        b4 = bb.reshape(16, 2, 64, 16)
        pad = np.zeros((2, 64, 16, 2, 16), np.float32)
        for g2 in range(2):
            pad[g2, :, :, g2, :] = b4[:, g2].transpose(1, 0, 2)
        put(SSL, ss, nm, pad)
    for nm, key in (("h0_re", "state_ssm_re"), ("h0_im", "state_ssm_im")):
        h = inp[key][0, SEQ_PER_CORE * c:SEQ_PER_CORE * (c + 1)]
        h4 = h.reshape(16, 16, 2, 64)
        put(SSL, ss, nm, h4.transpose(2, 3, 1, 0))
    m = np.zeros((128, 12), np.float32)
    for i in range(4):
        n = j - 1 - i
        if 0 <= n <= 2:
            m[:, 3 * i + n] = 1.0
    put(SSL, ss, "msk", m)
    blk = np.zeros((128, 4), np.float32)
    for k4 in range(4):
        blk[32 * k4:32 * k4 + 32, k4] = 1.0
    put(SSL, ss, "blk", blk)
    sr = np.zeros((128, SRL["_n"]), np.float32)

    def rowrep(a):
        a4 = a.reshape(4, 4, 2, 64)
        out = np.zeros((4, 2, 16, 4, 64), np.float32)
        out[:] = a4.transpose(1, 2, 0, 3)[:, :, None, :, :]
        return out
    put(SRL, sr, "a_re", rowrep(a_re))
    put(SRL, sr, "a_im", rowrep(a_im))
    put(SRL, sr, "logdt", rowrep(np.repeat(logdt[:, None], 64, 1)))
    for nm, bb in (("b_re", b_re), ("b_im", b_im)):
        b5 = bb.reshape(4, 4, 2, 64, 16)
        pad = np.zeros((4, 2, 16, 4, 2, 64), np.float32)
        for g2 in range(2):
            pad[:, g2, :, :, g2, :] = b5[:, :, g2].transpose(1, 3, 0, 2)
        put(SRL, sr, nm, pad)
    put(SRL, sr, "ident", np.eye(128, dtype=np.float32))
    return ss, sr


_CACHE = {}


def kernel(**inputs):
    inp = {k: np.asarray(v) for k, v in inputs.items()}
    if "nc" not in _CACHE:
        _CACHE["nc"] = build()
    nc, ins_, outs_, P = _CACHE["nc"]
    in_maps = []
    cache = None
    if "cache" in ins_:
        cache = np.concatenate([inp["cache_ckv"][0], inp["cache_kr"][0]], axis=-1).reshape(-1, 4 * (KV_LORA + QK_ROPE))
    for c in range(8):
        m = _prep_core(c, inp)
        if cache is not None:
            m["cache"] = cache
        in_maps.append({k: np.ascontiguousarray(m[k]) for k in ins_})
    res = run_bass_kernel_spmd(nc, in_maps, core_ids=list(range(8)))
    R = res.results
    f32 = np.float32
    ckv_p = np.zeros((1, 2, 8192, KV_LORA), f32)
    kr_p = np.zeros((1, 2, 8192, QK_ROPE), f32)
    ckv_s = np.zeros((1, 128, 4, KV_LORA), f32)
    kr_s = np.zeros((1, 128, 4, QK_ROPE), f32)
    for c in range(8):
        b, j = c // 4, c % 4
        ck = R[c]["o_ckvT"].T
        kr = R[c]["o_krT"].T
        ckv_p[0, b, NPT * j:NPT * (j + 1)] = ck[:NPT]
        kr_p[0, b, NPT * j:NPT * (j + 1)] = kr[:NPT]
        ckv_s[0, 16 * c:16 * c + 16] = ck[NPT:].reshape(16, 4, KV_LORA)
        kr_s[0, 16 * c:16 * c + 16] = kr[NPT:].reshape(16, 4, QK_ROPE)
    yp = np.zeros((2, 8192, D), f32)
    ys = np.zeros((128, 4, D), f32)
    cv_p = np.zeros((1, 2, 2, 2 * D_FF), f32)
    cv_s = np.zeros((1, 128, 2, 2 * D_FF), f32)
    if "o_yT" in R[0]:
        for c in range(8):
            b, j = c // 4, c % 4
            y = R[c]["o_yT"].T
            yp[b, NPT * j:NPT * (j + 1)] = y[:NPT]
            ys[16 * c:16 * c + 16] = y[NPT:].reshape(16, 4, D)
            cs_ = R[c]["o_cvs"]
            cv_s[0, 16 * c:16 * c + 16] = cs_.transpose(2, 3, 1, 0).reshape(16, 2, 2 * D_FF)
            if j == 3:
                cv_p[0, b] = R[c]["o_cvp"].transpose(2, 1, 0).reshape(2, 2 * D_FF)
    z = lambda *s: np.zeros(s, f32)
    sre_p, sim_p, sre_s, sim_s = z(1, 2, 32, 64), z(1, 2, 32, 64), z(1, 128, 32, 64), z(1, 128, 32, 64)
    if "o_hp" in R[0]:
        for c in range(8):
            b, j = c // 4, c % 4
            hs_ = R[c]["o_hs"].reshape(2, 64, 2, 16, 16)
            hs_ = hs_.transpose(2, 4, 3, 0, 1).reshape(2, 16, 32, 64)
            sre_s[0, 16 * c:16 * c + 16] = hs_[0]
            sim_s[0, 16 * c:16 * c + 16] = hs_[1]
            if j == 3:
                hp_ = R[c]["o_hp"].reshape(2, 64, 2, 16).transpose(2, 3, 0, 1).reshape(2, 32, 64)
                sre_p[0, b] = hp_[0]
                sim_p[0, b] = hp_[1]
    return (yp, ys, ckv_p, kr_p, ckv_s, kr_s, sre_p, sim_p, sre_s, sim_s, cv_p, cv_s)
```
